# Optimizing a Trainium2 kernel written in Bass

```python
import math
import jax, jax.numpy as jnp
from jax import lax
import numpy as np

D_MODEL = 1024
BATCH = 4
SEQ = 8192
DEPTH = 4

N_MIXERS = 4
CONV_WIDTH = 3
CONV_DIM = D_MODEL
DIL_WINDOWS = (128, 512, 2048)
DIL_RATES = (1, 4, 16)
DIL_HEADS = 8
DIL_HEAD_DIM = D_MODEL // DIL_HEADS
DIL_DIM = DIL_HEADS * DIL_HEAD_DIM
SWA_HALF = 128
SWA_Q_HEADS = 16
SWA_KV_HEADS = 4
SWA_HEAD_DIM = 64
SWA_DIM = SWA_Q_HEADS * SWA_HEAD_DIM
DIFF_HEADS = 8
DIFF_HEAD_DIM = D_MODEL // (2 * DIFF_HEADS)
DIFF_DIM = DIFF_HEADS * 2 * DIFF_HEAD_DIM
Q_BLOCK = 128

ROPE_THETA = 10000.0
NORM_EPS = 1e-6
SUBLN_EPS = 1e-5
NEG_INF = -1e30

kernel_name = "hybrid_interleaved_bidir_encoder"


def _layers_of_type(t):
    return len(range(t, DEPTH, N_MIXERS))


def rms_norm(x, g, eps):
    xf = x.astype(jnp.float32)
    y = xf * lax.rsqrt(jnp.mean(xf * xf, axis=-1, keepdims=True) + eps)
    return (y * g.astype(jnp.float32)).astype(x.dtype)


def rope_tables(seq, dim):
    inv = ROPE_THETA ** (-jnp.arange(0, dim, 2, dtype=jnp.float32) / dim)
    ang = jnp.arange(seq, dtype=jnp.float32)[:, None] * inv[None, :]
    return jnp.cos(ang), jnp.sin(ang)


def apply_rope(x, cos, sin):
    shp = (cos.shape[0],) + (1,) * (x.ndim - 3) + (cos.shape[1],)
    cos, sin = cos.reshape(shp), sin.reshape(shp)
    x1, x2 = jnp.split(x.astype(jnp.float32), 2, axis=-1)
    return jnp.concatenate([x1 * cos - x2 * sin, x2 * cos + x1 * sin], axis=-1).astype(x.dtype)


def adaln_pre_norm(x, c, g, w_mod, b_mod):
    mod = jax.nn.silu(c) @ w_mod + b_mod
    shift, scale, gate = jnp.split(mod, 3, axis=-1)
    h = rms_norm(x, g, NORM_EPS) * (1.0 + scale[:, None, :]) + shift[:, None, :]
    return h, gate


def banded_attention(q, k, v, half, sink=None):
    N, L, Hk, G, D = q.shape
    Dv = v.shape[-1]
    blk = half
    nb = -(-L // blk)
    Lp = nb * blk
    qp = jnp.pad(q, ((0, 0), (0, Lp - L), (0, 0), (0, 0), (0, 0))).reshape(N, nb, blk, Hk, G, D)
    kv_pad = ((0, 0), (blk, Lp - L + blk), (0, 0), (0, 0))
    kp = jnp.pad(k, kv_pad).reshape(N, nb + 2, blk, Hk, D)
    vp = jnp.pad(v, kv_pad).reshape(N, nb + 2, blk, Hk, Dv)
    band = lambda t: jnp.concatenate([t[:, :-2], t[:, 1:-1], t[:, 2:]], axis=2)
    kw, vw = band(kp), band(vp)
    qpos = jnp.arange(Lp).reshape(nb, blk)
    kpos = (jnp.arange(nb)[:, None] - 1) * blk + jnp.arange(3 * blk)[None, :]
    kk = kpos[:, None, :]
    valid = (jnp.abs(kk - qpos[:, :, None]) <= half) & (kk >= 0) & (kk < L)
    scale = D ** -0.5
    sink_f = None if sink is None else sink.astype(jnp.float32)[None, :, :, None]

    def one(args):
        qs, ks, vs = args
        s = jnp.einsum('bqhgd,bkhd->bhgqk', qs, ks, preferred_element_type=jnp.float32) * scale
        s = jnp.where(valid[:, None, None], s, NEG_INF)
        m = jnp.max(s, axis=-1)
        if sink_f is not None:
            m = jnp.maximum(m, sink_f)
        p = jnp.exp(s - m[..., None])
        l = jnp.sum(p, axis=-1)
        if sink_f is not None:
            l = l + jnp.exp(sink_f - m)
        o = jnp.einsum('bhgqk,bkhe->bqhge', p, vs.astype(jnp.float32))
        l_t = jnp.moveaxis(l, -1, 1)
        return o / l_t[..., None], jnp.moveaxis(m, -1, 1) + jnp.log(l_t)

    o, lse = lax.map(one, (qp, kw, vw))
    o = o.reshape(N, Lp, Hk, G, Dv)[:, :L]
    lse = lse.reshape(N, Lp, Hk, G)[:, :L]
    return o, lse


def to_strided(x, r):
    B, S = x.shape[:2]
    rest = x.shape[2:]
    x = x.reshape((B, S // r, r) + rest)
    return jnp.moveaxis(x, 2, 1).reshape((B * r, S // r) + rest)


def from_strided(x, r, B):
    L = x.shape[1]
    rest = x.shape[2:]
    x = x.reshape((B, r, L) + rest)
    return jnp.moveaxis(x, 1, 2).reshape((B, L * r) + rest)


def short_conv_mixer(h, w_in, conv_k, w_out):
    b_gate, c_gate, xin, z = jnp.split(h @ w_in, 4, axis=-1)
    t = c_gate * xin
    E = t.shape[-1]
    pad = (CONV_WIDTH - 1) // 2
    conv = lax.conv_general_dilated(
        t, conv_k.astype(t.dtype)[:, None, :], window_strides=(1,), padding=[(pad, pad)],
        dimension_numbers=('NWC', 'WIO', 'NWC'), feature_group_count=E)
    return (b_gate * conv * jax.nn.silu(z)) @ w_out


def dilated_mixer(h, w_in, w_out, cos, sin):
    B, S, _ = h.shape
    G, H, Dh = len(DIL_RATES), DIL_HEADS, DIL_HEAD_DIM
    E = H * Dh
    q, k, v, z = jnp.split(h @ w_in, [G * E, 2 * G * E, 2 * G * E + E], axis=-1)
    q = apply_rope(q.reshape(B, S, G, H, Dh), cos, sin)
    k = apply_rope(k.reshape(B, S, G, H, Dh), cos, sin)
    v = v.reshape(B, S, H, Dh)
    outs, lses = [], []
    for g in range(G):
        r = DIL_RATES[g]
        half = DIL_WINDOWS[g] // (2 * r)
        o, lse = banded_attention(to_strided(q[:, :, g], r)[:, :, :, None],
                                  to_strided(k[:, :, g], r), to_strided(v, r), half)
        outs.append(from_strided(o[:, :, :, 0], r, B))
        lses.append(from_strided(lse[..., 0], r, B))
    wts = jax.nn.softmax(jnp.stack(lses, axis=0), axis=0)
    o = jnp.sum(wts[..., None] * jnp.stack(outs, axis=0), axis=0)
    y = o.reshape(B, S, E).astype(h.dtype) * jax.nn.silu(z)
    return y @ w_out


def window_gqa_mixer(h, w_in, sink, w_out, cos, sin):
    B, S, _ = h.shape
    Hq, Hk, Dh = SWA_Q_HEADS, SWA_KV_HEADS, SWA_HEAD_DIM
    G = Hq // Hk
    q, k, v, z = jnp.split(h @ w_in, [Hq * Dh, Hq * Dh + Hk * Dh, Hq * Dh + 2 * Hk * Dh], axis=-1)
    q = apply_rope(q.reshape(B, S, Hk, G, Dh), cos, sin)
    k = apply_rope(k.reshape(B, S, Hk, Dh), cos, sin)
    v = v.reshape(B, S, Hk, Dh)
    o, _ = banded_attention(q, k, v, SWA_HALF, sink.reshape(Hk, G))
    y = o.reshape(B, S, Hq * Dh).astype(h.dtype) * jax.nn.silu(z)
    return y @ w_out


def diff_attention_mixer(h, w_in, lam_vecs, subln_g, w_out, cos, sin, layer_idx):
    B, S, _ = h.shape
    H, d = DIFF_HEADS, DIFF_HEAD_DIM
    q, k, v, z = jnp.split(h @ w_in, 4, axis=-1)
    q = apply_rope(q.reshape(B, S, H, 2, d), cos, sin)
    k = apply_rope(k.reshape(B, S, H, 2, d), cos, sin)
    v = v.reshape(B, S, H, 2 * d).astype(jnp.float32)
    lam_init = 0.8 - 0.6 * math.exp(-0.3 * layer_idx)
    lv = lam_vecs.astype(jnp.float32)
    lam = jnp.exp(jnp.sum(lv[0] * lv[1])) - jnp.exp(jnp.sum(lv[2] * lv[3])) + lam_init
    nb = S // Q_BLOCK
    qb = jnp.moveaxis(q.reshape(B, nb, Q_BLOCK, H, 2, d), 1, 0)
    scale = d ** -0.5

    def block(qs):
        s = jnp.einsum('bqhcd,bkhcd->bhcqk', qs, k, preferred_element_type=jnp.float32) * scale
        p = jax.nn.softmax(s, axis=-1)
        a = p[:, :, 0] - lam * p[:, :, 1]
        return jnp.einsum('bhqk,bkhe->bqhe', a, v)

    o = jnp.moveaxis(lax.map(block, qb), 0, 1).reshape(B, S, H, 2 * d)
    o = rms_norm(o, subln_g, SUBLN_EPS) * (1.0 - lam_init)
    y = o.reshape(B, S, H * 2 * d).astype(h.dtype) * jax.nn.silu(z)
    return y @ w_out


def setup_inputs(seed: int = 0) -> dict:
    key = jax.random.key(seed)
    ks = iter(jax.random.split(key, 32))
    nrm = lambda shape, s: jax.random.normal(next(ks), shape, jnp.float32) * s
    D = D_MODEL
    nA, nB, nC, nD = (_layers_of_type(t) for t in range(N_MIXERS))
    G = len(DIL_RATES)
    swa_in = 2 * SWA_DIM + 2 * SWA_KV_HEADS * SWA_HEAD_DIM
    return dict(
        x=nrm((BATCH, SEQ, D), 1.0),
        c=nrm((BATCH, D), 1.0),
        norm_g=1.0 + nrm((DEPTH, D), 0.02),
        w_mod=nrm((DEPTH, D, 3 * D), 0.5 * D ** -0.5),
        b_mod=nrm((DEPTH, 3 * D), 0.02),
        conv_w_in=nrm((nA, D, 4 * CONV_DIM), D ** -0.5),
        conv_k=nrm((nA, CONV_WIDTH, CONV_DIM), CONV_WIDTH ** -0.5),
        conv_w_out=nrm((nA, CONV_DIM, D), CONV_DIM ** -0.5),
        dil_w_in=nrm((nB, D, (2 * G + 2) * DIL_DIM), D ** -0.5),
        dil_w_out=nrm((nB, DIL_DIM, D), DIL_DIM ** -0.5),
        swa_w_in=nrm((nC, D, swa_in), D ** -0.5),
        swa_sink=nrm((nC, SWA_Q_HEADS), 1.0),
        swa_w_out=nrm((nC, SWA_DIM, D), SWA_DIM ** -0.5),
        diff_w_in=nrm((nD, D, 4 * DIFF_DIM), D ** -0.5),
        diff_lambda=nrm((nD, 4, DIFF_HEAD_DIM), 0.1),
        diff_subln_g=1.0 + nrm((nD, 2 * DIFF_HEAD_DIM), 0.02),
        diff_w_out=nrm((nD, DIFF_DIM, D), DIFF_DIM ** -0.5),
        final_g=1.0 + nrm((D,), 0.02),
    )


def reference(x, c, norm_g, w_mod, b_mod, conv_w_in, conv_k, conv_w_out, dil_w_in, dil_w_out,
              swa_w_in, swa_sink, swa_w_out, diff_w_in, diff_lambda, diff_subln_g, diff_w_out,
              final_g):
    S = x.shape[1]
    cos_dil, sin_dil = rope_tables(S, DIL_HEAD_DIM)
    cos_swa, sin_swa = rope_tables(S, SWA_HEAD_DIM)
    cos_diff, sin_diff = rope_tables(S, DIFF_HEAD_DIM)
    for i in range(DEPTH):
        kind, j = i % N_MIXERS, i // N_MIXERS
        h, gate = adaln_pre_norm(x, c, norm_g[i], w_mod[i], b_mod[i])
        if kind == 0:
            y = short_conv_mixer(h, conv_w_in[j], conv_k[j], conv_w_out[j])
        elif kind == 1:
            y = dilated_mixer(h, dil_w_in[j], dil_w_out[j], cos_dil, sin_dil)
        elif kind == 2:
            y = window_gqa_mixer(h, swa_w_in[j], swa_sink[j], swa_w_out[j], cos_swa, sin_swa)
        else:
            y = diff_attention_mixer(h, diff_w_in[j], diff_lambda[j], diff_subln_g[j],
                                     diff_w_out[j], cos_diff, sin_diff, i)
        x = x + gate[:, None, :] * y
    return rms_norm(x, final_g, NORM_EPS)
```

```python
import contextlib
import math
import numpy as np
import ml_dtypes
import concourse.bass as bass
import concourse.mybir as mybir
from concourse.bass_utils import run_bass_kernel_spmd

F32 = mybir.dt.float32
BF16 = mybir.dt.bfloat16
ALU = mybir.AluOpType
AF = mybir.ActivationFunctionType
AX = mybir.AxisListType

D = 1024
S = 8192
NB = 4
TPC = 4096
NT = TPC // 128
ENGS = ("pe", "act", "dve", "pool", "sp")
EPOCH_LIMIT = 20000
NDMASEM = 10
NCCSEM = 8
NORM_EPS = 1e-6
_DEBUG_OUT = set()
SUBLN_EPS = 1e-5


class Buf:
    __slots__ = ("name", "w", "r")

    def __init__(self, name=""):
        self.name = name
        self.w = None
        self.r = []


class T:
    __slots__ = ("t", "b", "name_is_dram")

    def __init__(self, t, name="", dram=False, buf=None):
        self.t = t
        self.b = buf if buf is not None else Buf(name)
        self.name_is_dram = dram


class Prog:
    def __init__(self, nc, stack):
        self.nc = nc
        self.stack = stack
        self.q = {e: [] for e in ENGS}
        self.cnt = {e: 0 for e in ENGS}
        self.nsem = 0
        self.esem = {e: self._newsem("c_" + e) for e in ENGS}
        self.waited = {e: {} for e in ENGS}
        self.dsem = {e: [self._newsem(f"d_{e}{i}") for i in range(NDMASEM)] for e in ("sp", "pool", "act")}
        self.dval = {e: [0] * NDMASEM for e in self.dsem}
        self.dnext = {e: 0 for e in self.dsem}
        self.latest = {}
        self.nid = 0
        self.cur_stack = stack
        self.csem = [self._newsem(f"cc{i}") for i in range(NCCSEM)]
        self.cval = [0] * NCCSEM
        self.cnext = 0

    def _newsem(self, name):
        self.nsem += 1
        return self.stack.enter_context(self.nc.semaphore(name + f"_{self.nsem}"))

    def sb(self, shape, dt, name=None):
        self.nid += 1
        name = (name or "t") + f"_{self.nid}"
        return T(self.cur_stack.enter_context(self.nc.sbuf_tensor(name, list(shape), dt)), name)

    def ps(self, shape, dt, name=None):
        self.nid += 1
        name = (name or "p") + f"_{self.nid}"
        return T(self.cur_stack.enter_context(self.nc.psum_tensor(name, list(shape), dt)), name)

    def _deps(self, eng, reads, writes):
        deps = {}

        def add(tok):
            if tok is None:
                return
            s, v = tok
            if eng == "pe" and s is self.esem["pe"]:
                return
            k = id(s)
            if self.waited[eng].get(k, 0) >= v:
                return
            if k not in deps or deps[k][1] < v:
                deps[k] = (s, v)
        for b in reads:
            add(b.w)
        for b in writes:
            add(b.w)
            for t in b.r:
                add(t)
        out = list(deps.values())
        for s, v in out:
            self.waited[eng][id(s)] = v
        return out

    def _commit(self, tok, reads, writes):
        for b in reads:
            b.r = [t for t in b.r if t[0] is not tok[0]] + [tok]
        for b in writes:
            b.w = tok
            b.r = []
        self.latest[id(tok[0])] = tok

    @staticmethod
    def _bufs(xs):
        return [x.b if isinstance(x, T) else x for x in xs]

    def op(self, eng, fn, reads=(), writes=()):
        reads, writes = self._bufs(reads), self._bufs(writes)
        deps = self._deps(eng, reads, writes)
        self.cnt[eng] += 1
        tok = (self.esem[eng], self.cnt[eng])
        self.q[eng].append((deps, fn, tok[0], 1))
        self._commit(tok, reads, writes)
        return tok

    def dma(self, eng, fn, reads=(), writes=()):
        reads, writes = self._bufs(reads), self._bufs(writes)
        i = self.dnext[eng]
        self.dnext[eng] = (i + 1) % NDMASEM
        s = self.dsem[eng][i]
        deps = self._deps(eng, reads, writes)
        prev = self.dval[eng][i]
        if prev > 0 and self.waited[eng].get(id(s), 0) < prev:
            deps.append((s, prev))
            self.waited[eng][id(s)] = prev
        self.dval[eng][i] = prev + 16
        tok = (s, prev + 16)
        self.q[eng].append((deps, fn, s, 16))
        self._commit(tok, reads, writes)
        return tok

    def cc(self, fn, reads=(), writes=()):
        i = self.cnext
        self.cnext = (i + 1) % NCCSEM
        s = self.csem[i]
        deps = self._deps("pool", list(reads), list(writes))
        prev = self.cval[i]
        if prev > 0 and self.waited["pool"].get(id(s), 0) < prev:
            deps.append((s, prev))
            self.waited["pool"][id(s)] = prev
        self.cval[i] = prev + 1
        tok = (s, prev + 1)
        self.q["pool"].append((deps, fn, s, 1))
        self._commit(tok, list(reads), list(writes))
        return tok

    def barrier(self):
        toks = list(self.latest.values())
        for e in ENGS:
            deps = []
            for s, v in toks:
                if self.waited[e].get(id(s), 0) < v:
                    deps.append((s, v))
                    self.waited[e][id(s)] = v
            if deps:
                self.q[e].append((deps, None, None, 0))
        if max(self.cnt.values()) > EPOCH_LIMIT:
            dm = set(id(s) for ss in self.dsem.values() for s in ss) | set(id(s) for s in self.csem)
            for e in ENGS:
                self.esem[e] = self._newsem("c_" + e)
                self.cnt[e] = 0
            self.latest = {k: t for k, t in self.latest.items() if k in dm}

    def maybe_epoch(self):
        if max(self.cnt.values()) > EPOCH_LIMIT:
            self.barrier()

    def emit(self):
        qs = self.q
        self.q = {e: [] for e in ENGS}
        with self.nc.Block() as block:
            def run(engname):
                def _f(e):
                    for deps, fn, s, inc in qs[engname]:
                        for ds, dv in deps:
                            e.wait_ge(ds, dv)
                        if fn is not None:
                            fn(e).then_inc(s, inc)
                return _f
            block.tensor(run("pe"))
            block.scalar(run("act"))
            block.vector(run("dve"))
            block.gpsimd(run("pool"))
            block.sync(run("sp"))


class Rot:
    def __init__(self, items):
        self.items = items
        self.i = 0

    def next(self):
        x = self.items[self.i % len(self.items)]
        self.i += 1
        return x


class KB:
    def __init__(self, nc, P, shared, nf32=8, layout="std", nbf=4):
        self.nc = nc
        self.P = P
        self.dram = shared["dram"]
        if layout == "std":
            self.ptr = Rot([P.ps([128, 8, 128], BF16, "ptr") for _ in range(2)])
            self.pg = [P.ps([128, 512], F32, "pg") for _ in range(3)]
            self.po = [P.ps([128, 512], F32, "po") for _ in range(3)]
        else:
            self.sc = [P.ps([128, 2, 512], F32, "sc") for _ in range(2)]
            self.acc = [P.ps([128, 512], F32, "acc") for _ in range(4)]
            self.ptr = Rot([T(x.t[:, 0, :].bitcast(BF16).rearrange("p (c q) -> p c q", q=128), "ptrv", buf=x.b) for x in self.sc])
            self.pg = [T(self.sc[0].t[:, 0, :], "pgv", buf=self.sc[0].b), T(self.sc[1].t[:, 0, :], "pgv", buf=self.sc[1].b),
                       T(self.sc[0].t[:, 1, :], "pgv", buf=self.sc[0].b)]
            self.po = None
        self.pgr = Rot(self.pg[0:2])
        self.dq = Rot(["sp", "act"])
        self.f32p = Rot([P.sb([128, D], F32, "f32p") for _ in range(nf32)])
        self.bfp = Rot([P.sb([128, 2048], BF16, "bfp") for _ in range(nbf)])
        self.b1p = Rot([P.sb([128, D], BF16, "b1p") for _ in range(4)])
        self.t3p = Rot([P.sb([128, 8, 128], BF16, "t3p") for _ in range(4)])
        self.ssp = Rot([P.sb([128, 4], F32, "ssp") for _ in range(4)])
        self.w_stage = Rot([P.sb([128, 8, 256], F32, "wst") for _ in range(2)])
        self.WCH = 256
        self.wm_stage = self.w_stage

    def din(self, name, shape, dt=F32):
        if name not in self.dram:
            self.dram[name] = T(self.nc.dram_tensor(name, list(shape), dt, kind="ExternalInput").ap(), name, True)
        return self.dram[name]

    def dout(self, name, shape, dt=F32):
        if name not in self.dram:
            self.dram[name] = T(self.nc.dram_tensor(name, list(shape), dt, kind="ExternalOutput").ap(), name, True)
        return self.dram[name]

    def dint(self, name, shape, dt=F32):
        if name in _DEBUG_OUT:
            return self.dout(name, shape, dt)
        if name not in self.dram:
            self.dram[name] = T(self.nc.dram_tensor(name, list(shape), dt).ap(), name, True)
        return self.dram[name]

    def _untracked(self, x):
        if x is None:
            return Buf()
        return Buf() if (isinstance(x, T) and x.name_is_dram) else x

    def load(self, dst, dst_ap, src, src_ap, q="sp", slow=False):
        src, dst = self._untracked(src), self._untracked(dst)
        if slow:
            self.P.dma(q, lambda e: e.dma_start(out=dst_ap, in_=src_ap, allow_slow_non_contiguous=True), reads=[src], writes=[dst])
        else:
            self.P.dma(q, lambda e: e.dma_start(out=dst_ap, in_=src_ap), reads=[src], writes=[dst])

    def store(self, dst, dst_ap, src, src_ap, q="pool"):
        src, dst = self._untracked(src), self._untracked(dst)
        self.P.dma(q, lambda e: e.dma_start(out=dst_ap, in_=src_ap), reads=[src], writes=[dst])

    def copy(self, eng, dst, dst_ap, src, src_ap):
        if eng == "act":
            self.P.op("act", lambda e: e.activation(out=dst_ap, in_=src_ap, func=AF.Copy), reads=[src], writes=[dst])
        else:
            self.P.op(eng, lambda e: e.tensor_copy(out=dst_ap, in_=src_ap), reads=[src], writes=[dst])

    def act(self, dst, dst_ap, src, src_ap, func, scale=1.0, accum=None, extra_w=()):
        if accum is None:
            self.P.op("act", lambda e: e.activation(out=dst_ap, in_=src_ap, func=func, scale=scale),
                      reads=[src], writes=[dst] + list(extra_w))
        else:
            self.P.op("act", lambda e: e.activation(out=dst_ap, in_=src_ap, func=func, scale=scale, accum_out=accum),
                      reads=[src], writes=[dst] + list(extra_w))

    def tt(self, eng, dst, dst_ap, a, a_ap, b, b_ap, op):
        self.P.op(eng, lambda e: e.tensor_tensor(out=dst_ap, in0=a_ap, in1=b_ap, op=op), reads=[a, b], writes=[dst])

    def ts(self, eng, dst, dst_ap, a, a_ap, s1, s2, op0, op1=None, sreads=()):
        if op1 is None:
            self.P.op(eng, lambda e: e.tensor_scalar(out=dst_ap, in0=a_ap, scalar1=s1, scalar2=None, op0=op0),
                      reads=[a] + list(sreads), writes=[dst])
        else:
            self.P.op(eng, lambda e: e.tensor_scalar(out=dst_ap, in0=a_ap, scalar1=s1, scalar2=s2, op0=op0, op1=op1),
                      reads=[a] + list(sreads), writes=[dst])

    def recip(self, dst, dst_ap, src, src_ap):
        self.P.op("dve", lambda e: e.reciprocal(out=dst_ap, in_=src_ap), reads=[src], writes=[dst])

    def memset(self, eng, dst, dst_ap, val):
        self.P.op(eng, lambda e: e.memset(dst_ap, val), writes=[dst])

    def transposes(self, dst_ps, src, src_aps, ident):
        def fn(e):
            for i, ap in enumerate(src_aps):
                ins = e.transpose(out=dst_ps.t[:, i, :], in_=ap, identity=ident.t[:])
            return ins
        self.P.op("pe", fn, reads=[src, ident], writes=[dst_ps])

    def matmuls(self, dst, specs, reads):
        def fn(e):
            for sp_ in specs:
                o, l, r, st, sp = sp_[:5]
                if len(sp_) > 5 and sp_[5]:
                    ins = e.matmul(o, lhsT=l, rhs=r, start=st, stop=sp, skip_group_check=True)
                else:
                    ins = e.matmul(o, lhsT=l, rhs=r, start=st, stop=sp)
            return ins
        self.P.op("pe", fn, reads=reads, writes=(dst if isinstance(dst, (list, tuple)) else [dst]))

    def setup_consts(self):
        P = self.P
        idf = P.sb([128, 128], F32, "idf")
        self.ident = P.sb([128, 128], BF16, "ident")
        d = self.din("ident", [128, 128])
        self.load(idf, idf.t[:], d, d.t)
        self.copy("dve", self.ident, self.ident.t[:], idf, idf.t[:])
        self.ones_row = P.sb([1, 128], F32, "ones_row")
        self.memset("dve", self.ones_row, self.ones_row.t[:], 1.0)

    def mod_vectors(self, l, need_ab, need_gate):
        P = self.P
        wmod = self.din(f"wmod{l}", [D, 3 * D])
        bT = self.din(f"bmodT{l}", [128, 24])
        res = {}
        if not hasattr(self, "silu_c"):
            cc = self.din("ccol", [128, 8])
            ct = P.sb([128, 8], F32, "ccol")
            self.load(ct, ct.t[:], cc, cc.t)
            self.silu_c = P.sb([128, 8], F32, "siluc")
            self.act(self.silu_c, self.silu_c.t[:], ct, ct.t[:], AF.Silu)
        sc = self.silu_c
        bt = P.sb([128, 24], F32, "bT")
        self.load(bt, bt.t[:], bT, bT.t)
        if need_ab:
            gT = self.din(f"ngT{l}", [128, 8])
            gt = P.sb([128, 8], F32, "gT")
            self.load(gt, gt.t[:], gT, gT.t)
            modT = P.sb([128, 16], F32, "modT")
            pg = self.pg[2]
            for g4 in range(8):
                st = self.wm_stage.next()
                self.load(st, st.t[:], wmod, wmod.t[:, g4 * 256:(g4 + 1) * 256].rearrange("(c p) n -> p c n", p=128),
                          q=self.dq.next())
                specs = []
                for jj in range(2):
                    j = g4 * 2 + jj
                    for k in range(8):
                        specs.append((pg.t[:, j:j + 1], st.t[:, k, jj * 128:(jj + 1) * 128], sc.t[:, k:k + 1], k == 0, k == 7))
                self.matmuls(pg, specs, [st, sc])
            self.tt("dve", modT, modT.t[:], pg, pg.t[:, 0:16], bt, bt.t[:, 0:16], ALU.add)
            A = P.sb([128, 8], F32, "A")
            self.P.op("dve", lambda e: e.scalar_tensor_tensor(out=A.t[:], in0=modT.t[:, 8:16], scalar=1.0, in1=gt.t[:],
                                                               op0=ALU.add, op1=ALU.mult), reads=[modT, gt], writes=[A])
            res["A"] = A
            res["B"] = modT
        if need_gate:
            bg = self.din(f"bmodg{l}", [1, D])
            bgt = P.sb([1, D], F32, "bg")
            self.load(bgt, bgt.t[:], bg, bg.t)
            grow = P.sb([1, D], F32, "grow")
            gbc = P.sb([128, D], F32, "gbc")
            for g2 in range(4):
                st = self.wm_stage.next()
                c0 = 2048 + g2 * 256
                self.load(st, st.t[:], wmod, wmod.t[:, c0:c0 + 256].rearrange("(c p) n -> p c n", p=128), q=self.dq.next())
                pg = self.pg[g2 % 2]
                specs = [(pg.t[0:1, 0:256], sc.t[:, k:k + 1], st.t[:, k, :], k == 0, k == 7) for k in range(8)]
                self.matmuls(pg, specs, [st, sc])
                self.tt("dve", grow, grow.t[:, g2 * 256:(g2 + 1) * 256], pg, pg.t[0:1, 0:256], bgt, bgt.t[:, g2 * 256:(g2 + 1) * 256], ALU.add)
            for g2 in range(2):
                pg = self.pg[g2]
                self.matmuls(pg, [(pg.t[:], self.ones_row.t[:], grow.t[:, g2 * 512:(g2 + 1) * 512], True, True)],
                             [self.ones_row, grow])
                self.copy("act", gbc, gbc.t[:, g2 * 512:(g2 + 1) * 512], pg, pg.t[:])
            res["gate_bc"] = gbc
        return res

    def load_weight(self, wsb, wdram, c0, ncols, scale_bc=None):
        P = self.P
        for g in range(0, ncols, 256):
            n = min(256, ncols - g)
            st = self.w_stage.next()
            self.load(st, st.t[:, :, 0:n], wdram, wdram.t[:, c0 + g:c0 + g + n].rearrange("(c p) n -> p c n", p=128),
                      q=self.dq.next())
            if scale_bc is None:
                self.copy("dve" if (g // 256) % 2 == 0 else "act", wsb, wsb.t[:, :, g:g + n], st, st.t[:, :, 0:n])
            else:
                for k in range(8):
                    self.tt("dve", wsb, wsb.t[:, k, g:g + n], st, st.t[:, k, 0:n], scale_bc, scale_bc.t[:, g:g + n], ALU.mult)

    def phase_a(self, ntiles, xsrc, xrow0, pdst, prow0, pcol0, wsb, ncols, A, Bm, cs, csrow0, hd, rope_cols, also=None):
        P = self.P
        if not hasattr(self, "pa_bufs"):
            self.pa_bufs = dict(
                tA=Rot([P.sb([128, 512], F32, "patA") for _ in range(3)]),
                tB=Rot([P.sb([128, 512], F32, "patB") for _ in range(3)]),
                csF=Rot([P.sb([128, 2, 512], F32, "pacsF") for _ in range(3)]),
            )
        bufs = self.pa_bufs
        h2 = hd // 2
        nhf = 512 // hd

        def prep_a(i):
            xt = self.f32p.next()
            self.load(xt, xt.t[:], xsrc, xsrc.t[xrow0 + i * 128: xrow0 + (i + 1) * 128, :], q="sp")
            ss = self.ssp.next()
            sq = self.f32p.next()
            self.act(sq, sq.t[:], xt, xt.t[:], AF.Square, accum=ss.t[:, 0:1], extra_w=[ss])
            self.ts("dve", ss, ss.t[:, 1:2], ss, ss.t[:, 0:1], 1.0 / D, NORM_EPS, ALU.mult, ALU.add)
            self.act(ss, ss.t[:, 2:3], ss, ss.t[:, 1:2], AF.Sqrt)
            self.recip(ss, ss.t[:, 3:4], ss, ss.t[:, 2:3])
            xn = self.b1p.next()
            self.ts("dve", xn, xn.t[:], xt, xt.t[:], ss.t[:, 3:4], None, ALU.mult, sreads=[ss])
            csF = None
            if rope_cols > 0:
                csF = bufs["csF"].next()
                self.load(csF, csF.t[:], cs, cs.t[csrow0 + i * 128: csrow0 + (i + 1) * 128, :, :], q="sp")
            return xn, csF

        def prep_b(pa):
            xn, csF = pa
            ptr = self.ptr.next()
            self.transposes(ptr, xn, [xn.t[:, c * 128:(c + 1) * 128] for c in range(8)], self.ident)
            hT = self.t3p.next()
            for c in range(8):
                if rope_cols == 0 and c % 2 == 1:
                    self.ts("dve", hT, hT.t[:, c, :], ptr, ptr.t[:, c, :], A.t[:, c:c + 1], Bm.t[:, c:c + 1],
                            ALU.mult, ALU.add, sreads=[A, Bm])
                else:
                    self._act_affine(hT, hT.t[:, c, :], ptr, ptr.t[:, c, :], A, A.t[:, c:c + 1], Bm, Bm.t[:, c:c + 1])
            cosF = sinS = None
            if csF is not None:
                cosF = T(csF.t[:, 0, :], "cosF", buf=csF.b)
                sinS = T(csF.t[:, 1, :], "sinS", buf=csF.b)
            return hT, cosF, sinS

        def mm(i, st):
            hT, cosF, sinS = st
            for og in range(0, ncols, 2048):
                on = min(2048, ncols - og)
                ot = self.bfp.next()
                for g in range(og, og + on, 512):
                    pg = self.pgr.next()
                    specs = [(pg.t[:], hT.t[:, k, :], wsb.t[:, k, g:g + 512], k == 0, k == 7) for k in range(8)]
                    self.matmuls(pg, specs, [hT, wsb])
                    lo = g - og
                    rc = max(0, min(512, rope_cols - g))
                    if rc > 0:
                        tA, tB = bufs["tA"].next(), bufs["tB"].next()
                        self.tt("dve", tA, tA.t[:, 0:rc], pg, pg.t[:, 0:rc], cosF, cosF.t[:, 0:rc], ALU.mult)
                        q4 = pg.t[:, 0:rc].rearrange("p (h two d) -> p h two d", two=2, d=h2)
                        b4 = tB.t[:, 0:rc].rearrange("p (h two d) -> p h two d", two=2, d=h2)
                        s4 = sinS.t[:, 0:rc].rearrange("p (h two d) -> p h two d", two=2, d=h2)
                        self.tt("dve", tB, b4[:, :, 0, :], pg, q4[:, :, 1, :], sinS, s4[:, :, 0, :], ALU.mult)
                        self.tt("dve", tB, b4[:, :, 1, :], pg, q4[:, :, 0, :], sinS, s4[:, :, 1, :], ALU.mult)
                        self.tt("pool", ot, ot.t[:, lo:lo + rc], tA, tA.t[:, 0:rc], tB, tB.t[:, 0:rc], ALU.add)
                        if rc < 512:
                            self.copy("act", ot, ot.t[:, lo + rc:lo + 512], pg, pg.t[:, rc:512])
                    else:
                        self.copy("act", ot, ot.t[:, lo:lo + 512], pg, pg.t[:])
                self.store(pdst, pdst.t[prow0 + i * 128: prow0 + (i + 1) * 128, pcol0 + og: pcol0 + og + on], ot, ot.t[:, 0:on])
                if also is not None:
                    adst, arow0, c_lo, c_hi, dcol0 = also
                    lo_, hi_ = max(c_lo, og), min(c_hi, og + on)
                    if hi_ > lo_:
                        self.store(adst, adst.t[arow0 + i * 128: arow0 + (i + 1) * 128, dcol0 + lo_ - c_lo: dcol0 + hi_ - c_lo],
                                   ot, ot.t[:, lo_ - og:hi_ - og], q="pool")

        pa = {0: prep_a(0)}
        if ntiles > 1:
            pa[1] = prep_a(1)
        st = prep_b(pa.pop(0))
        for i in range(ntiles):
            if i + 2 < ntiles:
                pa[i + 2] = prep_a(i + 2)
            nxt = prep_b(pa.pop(i + 1)) if i + 1 < ntiles else None
            mm(i, st)
            st = nxt
            self.P.maybe_epoch()

    def _act_affine(self, dst, dst_ap, src, src_ap, a_t, a_ap, b_t, b_ap):
        self.P.op("act", lambda e: e.activation(out=dst_ap, in_=src_ap, func=AF.Identity, bias=b_ap, scale=a_ap),
                  reads=[src, a_t, b_t], writes=[dst])

    def phase_c_tile(self, ytile, xsrc, xrow, xdst, drow, wout, final_g=None):
        P = self.P
        if not hasattr(self, "pc_bufs"):
            self.pc_bufs = dict(yT=self.t3p, x=self.f32p, xo=self.f32p, ss=self.ssp, sq=self.f32p)
        bufs = self.pc_bufs
        ptr = self.ptr.next()
        self.transposes(ptr, ytile, [ytile.t[:, c * 128:(c + 1) * 128] for c in range(8)], self.ident)
        yT = bufs["yT"].next()
        self.copy("act", yT, yT.t[:], ptr, ptr.t[:])
        xt = bufs["x"].next()
        self.load(xt, xt.t[:], xsrc, xsrc.t[xrow:xrow + 128, :], q="sp")
        xo = bufs["xo"].next()
        for half in range(2):
            pg = self.pgr.next()
            specs = [(pg.t[:], yT.t[:, k, :], wout.t[:, k, half * 512:(half + 1) * 512], k == 0, k == 7) for k in range(8)]
            self.matmuls(pg, specs, [yT, wout])
            self.tt("dve", xo, xo.t[:, half * 512:(half + 1) * 512], pg, pg.t[:], xt, xt.t[:, half * 512:(half + 1) * 512], ALU.add)
        if final_g is not None:
            ss = bufs["ss"].next()
            sq = bufs["sq"].next()
            self.act(sq, sq.t[:], xo, xo.t[:], AF.Square, accum=ss.t[:, 0:1], extra_w=[ss])
            self.ts("dve", ss, ss.t[:, 1:2], ss, ss.t[:, 0:1], 1.0 / D, NORM_EPS, ALU.mult, ALU.add)
            self.act(ss, ss.t[:, 2:3], ss, ss.t[:, 1:2], AF.Sqrt)
            self.recip(ss, ss.t[:, 3:4], ss, ss.t[:, 2:3])
            self.P.op("dve", lambda e: e.scalar_tensor_tensor(out=xt.t[:], in0=xo.t[:], scalar=ss.t[:, 3:4], in1=final_g.t[:],
                                                               op0=ALU.mult, op1=ALU.mult), reads=[xo, ss, final_g], writes=[xt])
            self.store(xdst, xdst.t[drow:drow + 128, :], xt, xt.t[:])
        else:
            self.store(xdst, xdst.t[drow:drow + 128, :], xo, xo.t[:])

    def conv_layer(self, p0, convk_bc, vmask, xsrc, xdst, wout):
        P = self.P
        cx = [self.bfp] * 3
        tj = [self.f32p] * 3
        bz, sz, acc, yt = self.bfp, self.f32p, self.f32p, self.b1p
        for i in range(NT):
            r0 = 128 + i * 128
            ts_ = []
            for j in range(3):
                c = cx[j].next()
                self.load(c, c.t[:], p0, p0.t[r0 + j - 1: r0 + j - 1 + 128, 1024:3072], q=self.dq.next())
                t = tj[j].next()
                self.tt("pool", t, t.t[:], c, c.t[:, 0:1024], c, c.t[:, 1024:2048], ALU.mult)
                if (i == 0 and j == 0):
                    self.ts("pool", t, t.t[:], t, t.t[:], vmask.t[:, 0:1], None, ALU.mult, sreads=[vmask])
                if (i == NT - 1 and j == 2):
                    self.ts("pool", t, t.t[:], t, t.t[:], vmask.t[:, 1:2], None, ALU.mult, sreads=[vmask])
                ts_.append(t)
            b = bz.next()
            self.load(b, b.t[:, 0:1024], p0, p0.t[r0:r0 + 128, 0:1024], q="sp")
            self.load(b, b.t[:, 1024:2048], p0, p0.t[r0:r0 + 128, 3072:4096], q="sp")
            s = sz.next()
            self.act(s, s.t[:], b, b.t[:, 1024:2048], AF.Silu)
            a = acc.next()
            self.tt("dve", a, a.t[:], ts_[0], ts_[0].t[:], convk_bc, convk_bc.t[:, 0, :], ALU.mult)
            self.tt("pool", ts_[1], ts_[1].t[:], ts_[1], ts_[1].t[:], convk_bc, convk_bc.t[:, 1, :], ALU.mult)
            self.tt("dve", ts_[2], ts_[2].t[:], ts_[2], ts_[2].t[:], convk_bc, convk_bc.t[:, 2, :], ALU.mult)
            self.tt("pool", a, a.t[:], a, a.t[:], ts_[1], ts_[1].t[:], ALU.add)
            self.tt("dve", a, a.t[:], a, a.t[:], ts_[2], ts_[2].t[:], ALU.add)
            self.tt("pool", s, s.t[:], s, s.t[:], b, b.t[:, 0:1024], ALU.mult)
            y = yt.next()
            self.tt("dve", y, y.t[:], a, a.t[:], s, s.t[:], ALU.mult)
            self.phase_c_tile(y, xsrc, r0, xdst, i * 128, wout)
            self.P.maybe_epoch()

    def band_attention(self, *, pq, kv, kvalid, masks, nheads, hd, ev, qcol0, kcol0, vcol0, nkvh, gq,
                       blocks, scale, finish):
        P = self.P
        qcols = nheads * hd
        kcols = nkvh * hd
        nqt = qcols // 128
        dup = (hd == 64)
        nkt = nkvh if dup else kcols // 128
        maxch = max(len(b["chunks"]) for b in blocks)
        hb = max(1, 512 // (128 * maxch))
        key = (qcols, nkt, nkvh, ev, maxch)
        if not hasattr(self, "_ba_cache"):
            self._ba_cache = {}
        if key not in self._ba_cache:
            self._ba_cache[key] = dict(
                qrow=Rot([P.sb([128, qcols], BF16, "ba_q") for _ in range(2)]),
                qT=Rot([P.sb([128, nqt, 128], BF16, "ba_qT") for _ in range(2)]),
                krow=Rot([P.sb([128, nkt * 128], BF16, "ba_k") for _ in range(2 * maxch)]),
                kT=Rot([P.sb([128, nkt, 128], BF16, "ba_kT") for _ in range(2 * maxch)]),
                vaug=Rot([P.sb([128, nkvh, ev + 1], BF16, "ba_v") for _ in range(2 * maxch)]),
                pt=Rot([P.sb([128, hb * maxch, 128], BF16, "ba_pt") for _ in range(3)]),
                kvc=Rot([P.sb([128, 1], F32, "ba_kvc") for _ in range(2 * maxch)]))
            for v_ in self._ba_cache[key]["vaug"].items:
                self.memset("pool", v_, v_.t[:], 1.0)
        c_ = self._ba_cache[key]
        kvcr = c_["kvc"]
        qrow, qT, krow, kT, vaug, pt = c_["qrow"], c_["qT"], c_["krow"], c_["kT"], c_["vaug"], c_["pt"]
        per_bank = 512 // (ev + 1)
        nch = maxch
        assert all(len(b_["chunks"]) == nch for b_ in blocks)
        if "maskb" not in c_:
            c_["maskb"] = P.sb([128, hb * nch, 128], BF16, "ba_maskb")
        maskb = c_["maskb"]
        self.memset("pool", maskb, maskb.t[:], 0.0)
        for ii in range(hb):
            for ci in range(nch):
                mid = blocks[0]["chunks"][ci][2]
                if mid is not None:
                    self.ts("pool", maskb, maskb.t[:, ii * nch + ci, :], masks, masks.t[:, mid, :], 30000.0, -30000.0, ALU.mult, ALU.add)
        oaps = []
        for h in range(nheads):
            bank = self.po[h // per_bank]
            sl = (h % per_bank) * (ev + 1)
            oaps.append((bank, bank.t[:, sl:sl + ev + 1]))

        def prep_loads(blk):
            qr = qrow.next()
            r0, st = blk["qrow0"], blk["qstep"]
            self.load(qr, qr.t[:], pq, pq.t[r0:r0 + 127 * st + 1:st, qcol0:qcol0 + qcols], q="sp")
            vas, kvcs, krs = [], [], []
            for (k0, kst, mid, halo) in blk["chunks"]:
                kr = krow.next()
                if dup:
                    ksrc = kv.t[k0:k0 + 127 * kst + 1:kst, kcol0:kcol0 + kcols].rearrange("p (h d) -> p h d", d=hd)
                    kr4 = kr.t[:].rearrange("p (h two d) -> p h two d", two=2, d=hd)
                    self.load(kr, kr4[:, :, 0, :], kv, ksrc, q="sp")
                    self.load(kr, kr4[:, :, 1, :], kv, ksrc, q="sp")
                else:
                    self.load(kr, kr.t[:], kv, kv.t[k0:k0 + 127 * kst + 1:kst, kcol0:kcol0 + kcols], q="sp")
                va = vaug.next()
                self.load(va, va.t[:, :, 0:ev], kv,
                          kv.t[k0:k0 + 127 * kst + 1:kst, vcol0:vcol0 + nkvh * ev].rearrange("p (h e) -> p h e", e=ev), q="sp")
                if halo:
                    kc_ = kvcr.next()
                    self.load(kc_, kc_.t[:], kvalid, kvalid.t[k0:k0 + 127 * kst + 1:kst, :], q="sp", slow=True)
                    kvcs.append(kc_)
                else:
                    kvcs.append(None)
                krs.append(kr)
                vas.append(va)
            return qr, krs, vas, kvcs

        def prep_tr(ld):
            qr, krs, vas, kvcs = ld
            kts = []
            ptr = self.ptr.next()
            self.transposes(ptr, qr, [qr.t[:, c * 128:(c + 1) * 128] for c in range(nqt)], self.ident)
            qt = qT.next()
            self.copy("dve", qt, qt.t[:], ptr, ptr.t[:, 0:nqt, :])
            for kr in krs:
                ptr = self.ptr.next()
                self.transposes(ptr, kr, [kr.t[:, c * 128:(c + 1) * 128] for c in range(nkt)], self.ident)
                kt = kT.next()
                self.copy("dve", kt, kt.t[:], ptr, ptr.t[:, 0:nkt, :])
                kts.append(kt)
            return qt, kts, vas, kvcs

        def emit_qk(st, hs):
            qt, kts, vas, kvcs = st
            pg = self.pgr.next()
            specs = []
            for ii, h in enumerate(hs):
                kvh = h // gq
                qd = h * hd
                po_ = qd % 128
                for ci in range(nch):
                    col = (ii * nch + ci) * 128
                    specs.append((pg.t[:, col:col + 128], kts[ci].t[po_: po_ + hd, kvh, :], qt.t[po_: po_ + hd, qd // 128, :],
                                  len(specs) == 0, False, True))
            ncol = len(hs) * nch
            specs.append((pg.t[:, 0:ncol * 128], self.ident.t[:], maskb.t[:, 0:ncol, :].rearrange("p c q -> p (c q)"), False, True, True))
            self.matmuls(pg, specs, kts + [qt, maskb, self.ident])
            p = pt.next()
            self.act(p, p.t[:, 0:ncol, :], pg, pg.t[:, 0:ncol * 128].rearrange("p (c q) -> p c q", q=128), AF.Exp, scale=scale)
            for ii, h in enumerate(hs):
                for ci in range(nch):
                    if kvcs[ci] is not None:
                        cc = ii * nch + ci
                        self.ts("dve", p, p.t[:, cc, :], p, p.t[:, cc, :], kvcs[ci].t[:, 0:1], None, ALU.mult, sreads=[kvcs[ci]])
            return p

        def emit_pv(st, hs, p):
            qt, kts, vas, kvcs = st
            banks = []
            per = {}
            for ii, h in enumerate(hs):
                kvh = h // gq
                bank, oap = oaps[h]
                if bank not in banks:
                    banks.append(bank)
                    per[id(bank)] = []
                for ci in range(nch):
                    cc = ii * nch + ci
                    per[id(bank)].append((oap, p.t[:, cc, :], vas[ci].t[:, kvh, :], ci == 0, ci == nch - 1))
            for bank in banks:
                self.matmuls(bank, per[id(bank)], [p] + vas)

        units = [list(range(h0, min(nheads, h0 + hb))) for h0 in range(0, nheads, hb)]
        prep_at = min(len(units) - 1, max(1, len(units) // 2))
        st = prep_tr(prep_loads(blocks[0]))
        deferred = None
        for bi, blk in enumerate(blocks):
            ld = prep_loads(blocks[bi + 1]) if bi + 1 < len(blocks) else None
            nxt = None
            ps_ = [emit_qk(st, units[0])]
            for ui, hs in enumerate(units):
                if ui + 1 < len(units):
                    ps_.append(emit_qk(st, units[ui + 1]))
                emit_pv(st, hs, ps_[ui])
                if ui == prep_at - 1 and ld is not None:
                    nxt = prep_tr(ld)
            if nxt is None and ld is not None:
                nxt = prep_tr(ld)
            d_ = finish(bi, blk, oaps)
            if deferred is not None:
                deferred()
            deferred = d_
            st = nxt
            self.P.maybe_epoch()
        if deferred is not None:
            deferred()


def _group(xs, n):
    return [xs[i:i + n] for i in range(0, len(xs), n)]


def d2d(kb, dst, dst_ap, src, src_ap, q="sp"):
    kb.load(dst, dst_ap, src, src_ap, q=q)


CC_MAX_BYTES = 2 << 20


def exchange_start(kb, name, pieces, cols):
    P = kb.P
    rmax = max(1, CC_MAX_BYTES // (cols * 2))
    groups = [[0, 1], [2, 3], [4, 5], [6, 7]]
    jobs = []
    ci = 0
    for (src_ap, rows, d0, d1) in pieces:
        for r0 in range(0, rows, rmax):
            n = min(rmax, rows - r0)
            sb_ = kb.dint(f"{name}_s{ci}", [n, cols], BF16)
            db_ = kb.dint(f"{name}_d{ci}", [2 * n, cols], BF16)
            sb_.name_is_dram = False
            db_.name_is_dram = False
            ci += 1
            d2d(kb, sb_, sb_.t, None, src_ap[r0:r0 + n, :], q="sp")
            jobs.append((sb_, db_, n, r0, d0, d1))
    for (sb_, db_, n, r0, d0, d1) in jobs:
        P.cc((lambda sb_=sb_, db_=db_: (lambda e: e.collective_compute(
            "AllGather", ALU.bypass, replica_groups=groups, ins=[sb_.t.opt()], outs=[db_.t.opt()])))(),
            reads=[sb_.b], writes=[db_.b])
    return jobs


def exchange_finish(kb, jobs):
    for (sb_, db_, n, r0, d0, d1) in jobs:
        if d0 is not None:
            d2d(kb, None, d0[r0:r0 + n, :], db_, db_.t[0:n, :], q=kb.dq.next())
        if d1 is not None:
            d2d(kb, None, d1[r0:r0 + n, :], db_, db_.t[n:2 * n, :], q=kb.dq.next())


def stage0(kb):
    P = kb.P
    xe = kb.din("xe", [TPC + 256, D])
    m0 = kb.mod_vectors(0, True, True)
    m1 = kb.mod_vectors(1, True, False)
    w_in0 = kb.din("w_in0", [D, 4096])
    w_out0 = kb.din("w_out0", [D, D])
    w_in1 = kb.din("w_in1", [D, 8192])
    convk = kb.din("convk_bc", [128, 3, D])
    vmask = kb.din("vmask", [128, 2])
    cs1 = kb.din("cs1", [TPC, 2, 512])
    X1 = kb.dint("X1", [TPC, D])
    P1 = kb.dint("P1", [TPC, 8192], BF16)
    p0 = kb.dint("p0", [TPC + 256, 4096], BF16)
    wsb = P.sb([128, 8, 2048], BF16, "wsb")
    wout = P.sb([128, 8, D], BF16, "wout")
    ck = P.sb([128, 3, D], F32, "convk")
    vm = P.sb([128, 2], F32, "vmask")
    kb.load(ck, ck.t[:], convk, convk.t)
    kb.load(vm, vm.t[:], vmask, vmask.t)
    kb.load_weight(wout, w_out0, 0, D, scale_bc=m0["gate_bc"])
    for ps_ in range(2):
        kb.load_weight(wsb, w_in0, ps_ * 2048, 2048)
        kb.phase_a(NT + 2, xe, 0, p0, 0, ps_ * 2048, wsb, 2048, m0["A"], m0["B"], None, 0, 128, 0)
    P.barrier()
    kb.conv_layer(p0, ck, vm, xe, X1, wout)
    P.barrier()
    HALO = 1024
    kv = kb.dint("kv1", [TPC + 2 * HALO, 4096], BF16)
    for (c0, nc_, rc_) in ((3072, 2048, 2048), (5120, 2048, 1024)):
        kb.load_weight(wsb, w_in1, c0, nc_)
        kb.phase_a(NT, X1, 0, P1, 0, c0, wsb, nc_, m1["A"], m1["B"], cs1, 0, 128, rc_, also=(kv, HALO, 0, nc_, c0 - 3072))
    P.barrier()
    jobs = exchange_start(kb, "ex1", [
        (P1.t[TPC - HALO:TPC, 3072:7168], HALO, kv.t[0:HALO, :], None),
        (P1.t[0:HALO, 3072:7168], HALO, None, kv.t[HALO + TPC:, :]),
    ], 4096)
    for (c0, nc_, rc_) in ((0, 2048, 2048), (2048, 1024, 1024), (7168, 1024, 0)):
        kb.load_weight(wsb, w_in1, c0, nc_)
        kb.phase_a(NT, X1, 0, P1, 0, c0, wsb, nc_, m1["A"], m1["B"], cs1, 0, 128, rc_)
    exchange_finish(kb, jobs)
    P.barrier()


def stage1(kb):
    P = kb.P
    HALO = 1024
    X1 = kb.dram["X1"]
    P1 = kb.dram["P1"]
    kv = kb.dram["kv1"]
    kvalid = kb.din("kvalid1", [TPC + 2 * HALO, 1])
    masksd = kb.din("masks", [128, 2, 128], BF16)
    m1 = kb.mod_vectors(1, False, True)
    m2 = kb.mod_vectors(2, True, False)
    w_out1 = kb.din("w_out1", [D, D])
    w_in2 = kb.din("w_in2", [D, 2560])
    cs2 = kb.din("cs2", [TPC, 2, 512])
    X2 = kb.dint("X2", [TPC, D])
    P2 = kb.dint("P2", [TPC, 2560], BF16)
    og = [kb.dint(f"og{g}", [TPC, 8 * 129]) for g in range(3)]
    wout = P.sb([128, 8, D], BF16, "wout")
    masks = P.sb([128, 2, 128], BF16, "masks")
    kb.load(masks, masks.t[:], masksd, masksd.t)
    kb.load_weight(wout, w_out1, 0, D, scale_bc=m1["gate_bc"])
    P.barrier()
    osb = Rot([P.sb([128, 8 * 129], F32, "osb") for _ in range(2)])
    for g, r in enumerate((1, 4, 16)):
        blocks = []
        for p in range(r):
            for j in range(TPC // (128 * r)):
                i0 = 128 * j
                q0 = p + r * i0
                ka = HALO + p + r * (i0 - 64)
                kb_ = HALO + p + r * (i0 + 64)
                ha = ka < HALO
                hb_ = kb_ + 127 * r >= HALO + TPC
                blocks.append(dict(qrow0=q0, qstep=r, chunks=[(ka, r, 0, ha), (kb_, r, 1, hb_)]))

        def finish(bi, blk, oaps, g=g, r=r):
            o = osb.next()
            for bnk in range(3):
                n = min(3, 8 - 3 * bnk) * 129
                kb.copy("act" if bnk == 1 else "dve", o, o.t[:, bnk * 387: bnk * 387 + n], kb.po[bnk], kb.po[bnk].t[:, 0:n])
            r0 = blk["qrow0"]
            kb.store(og[g], og[g].t[r0:r0 + 127 * r + 1:r, :], o, o.t[:])
        kb.band_attention(pq=P1, kv=kv, kvalid=kvalid, masks=masks, nheads=8, hd=128, ev=128,
                          qcol0=g * 1024, kcol0=g * 1024, vcol0=3072, nkvh=8, gq=1,
                          blocks=blocks, scale=128 ** -0.5, finish=finish)
    P.barrier()
    zt = Rot([P.sb([128, D], BF16, "zt") for _ in range(2)])
    rl = Rot([P.sb([128, 8], F32, "rl") for _ in range(2)])
    o3 = Rot(osb.items + [P.sb([128, 8 * 129], F32, "o3")])
    for i in range(NT):
        os_ = []
        for g in range(3):
            o = o3.next()
            kb.load(o, o.t[:], og[g], og[g].t[i * 128:(i + 1) * 128, :], q=kb.dq.next())
            os_.append(o)
        z = zt.next()
        kb.load(z, z.t[:], P1, P1.t[i * 128:(i + 1) * 128, 7168:8192], q="sp")
        kb.tt("pool", os_[0], os_[0].t[:], os_[0], os_[0].t[:], os_[1], os_[1].t[:], ALU.add)
        kb.tt("dve", os_[0], os_[0].t[:], os_[0], os_[0].t[:], os_[2], os_[2].t[:], ALU.add)
        o4 = os_[0].t[:].rearrange("p (h e) -> p h e", e=129)
        rr = rl.next()
        kb.recip(rr, rr.t[:], os_[0], o4[:, :, 128])
        sz = kb.f32p.next()
        kb.act(sz, sz.t[:], z, z.t[:], AF.Silu)
        on = kb.f32p.next()
        kb.tt("dve", on, on.t[:].rearrange("p (h e) -> p h e", e=128), os_[0], o4[:, :, 0:128],
              rr, rr.t[:].unsqueeze(2).to_broadcast([128, 8, 128]), ALU.mult)
        y = kb.b1p.next()
        kb.tt("pool", y, y.t[:], on, on.t[:], sz, sz.t[:], ALU.mult)
        kb.phase_c_tile(y, X1, i * 128, X2, i * 128, wout)
        P.maybe_epoch()
    P.barrier()
    wsb = P.sb([128, 8, 2048], BF16, "wsb")
    kb.load_weight(wsb, w_in2, 0, 2048)
    H2 = 128
    kv2 = kb.dint("kv2", [TPC + 2 * H2, 512], BF16)
    kb.phase_a(NT, X2, 0, P2, 0, 0, wsb, 2048, m2["A"], m2["B"], cs2, 0, 64, 1280, also=(kv2, H2, 1024, 1536, 0))
    P.barrier()
    jobs = exchange_start(kb, "ex2", [
        (P2.t[TPC - H2:TPC, 1024:1536], H2, kv2.t[0:H2, :], None),
        (P2.t[0:H2, 1024:1536], H2, None, kv2.t[H2 + TPC:, :]),
    ], 512)
    kb.load_weight(wsb, w_in2, 2048, 512)
    kb.phase_a(NT, X2, 0, P2, 0, 2048, wsb, 512, m2["A"], m2["B"], cs2, 0, 64, 0)
    exchange_finish(kb, jobs)
    P.barrier()


def stage2(kb):
    P = kb.P
    HALO = 128
    X2 = kb.dram["X2"]
    P2 = kb.dram["P2"]
    kv = kb.dram["kv2"]
    kvalid = kb.din("kvalid2", [TPC + 2 * HALO, 1])
    masksd = kb.din("masks", [128, 2, 128], BF16)
    sinkd = kb.din("sink_bc", [128, 16])
    m2 = kb.mod_vectors(2, False, True)
    m3 = kb.mod_vectors(3, True, False)
    w_out2 = kb.din("w_out2", [D, D])
    w_in3 = kb.din("w_in3", [D, 4096])
    cs3 = kb.din("cs2", [TPC, 2, 512])
    X3 = kb.dint("X3", [TPC, D])
    P3 = kb.dint("P3", [TPC, 4096], BF16)
    wout = P.sb([128, 8, D], BF16, "wout")
    masks = P.sb([128, 2, 128], BF16, "masks")
    kb.load(masks, masks.t[:], masksd, masksd.t)
    esink = P.sb([128, 16], F32, "esink")
    kb.load(esink, esink.t[:], sinkd, sinkd.t)
    kb.act(esink, esink.t[:], esink, esink.t[:], AF.Exp)
    kb.load_weight(wout, w_out2, 0, D, scale_bc=m2["gate_bc"])
    P.barrier()
    blocks = []
    for j in range(NT):
        q0 = 128 * j
        blocks.append(dict(qrow0=q0, qstep=1, chunks=[(HALO + q0 - 128, 1, 0, j == 0), (HALO + q0, 1, None, False),
                                                      (HALO + q0 + 128, 1, 1, j == NT - 1)]))
    zt = Rot([P.sb([128, D], BF16, "zt") for _ in range(2)])
    lt = Rot([P.sb([128, 16], F32, "lt") for _ in range(2)])

    oev = Rot([P.sb([128, 16 * 65], F32, "oev") for _ in range(2)])

    def finish(bi, blk, oaps):
        r0 = blk["qrow0"]
        oe = oev.next()
        for bnk in range(3):
            nh = min(7, 16 - 7 * bnk)
            kb.copy("act" if bnk == 1 else "dve", oe, oe.t[:, 7 * bnk * 65:(7 * bnk + nh) * 65], kb.po[bnk], kb.po[bnk].t[:, 0:nh * 65])

        def rest():
            z = zt.next()
            kb.load(z, z.t[:], P2, P2.t[r0:r0 + 128, 1536:2560], q="sp")
            sz = kb.f32p.next()
            kb.act(sz, sz.t[:], z, z.t[:], AF.Silu)
            l = lt.next()
            on = kb.f32p.next()
            o3_ = oe.t[:].rearrange("p (h e) -> p h e", e=65)
            kb.tt("dve", l, l.t[:], oe, o3_[:, :, 64], esink, esink.t[:], ALU.add)
            kb.recip(l, l.t[:], l, l.t[:])
            kb.tt("dve", on, on.t[:].rearrange("p (h e) -> p h e", e=64), oe, o3_[:, :, 0:64],
                  l, l.t[:].unsqueeze(2).to_broadcast([128, 16, 64]), ALU.mult)
            y = kb.b1p.next()
            kb.tt("pool", y, y.t[:], on, on.t[:], sz, sz.t[:], ALU.mult)
            kb.phase_c_tile(y, X2, r0, X3, r0, wout)
        return rest

    kb.band_attention(pq=P2, kv=kv, kvalid=kvalid, masks=masks, nheads=16, hd=64, ev=64,
                      qcol0=0, kcol0=0, vcol0=256, nkvh=4, gq=4, blocks=blocks, scale=64 ** -0.5, finish=finish)
    P.barrier()
    wsb = P.sb([128, 8, 2048], BF16, "wsb")
    kb.load_weight(wsb, w_in3, 1024, 2048)
    kb.phase_a(NT, X3, 0, P3, 0, 1024, wsb, 2048, m3["A"], m3["B"], cs3, 0, 64, 1024)
    P.barrier()
    kv3 = kb.dint("kv3", [S, 2048], BF16)
    jobs = exchange_start(kb, "ex3", [(P3.t[:, 1024:3072], TPC, kv3.t[0:TPC, :], kv3.t[TPC:S, :])], 2048)
    kb.load_weight(wsb, w_in3, 0, 1024)
    kb.phase_a(NT, X3, 0, P3, 0, 0, wsb, 1024, m3["A"], m3["B"], cs3, 0, 64, 1024)
    kb.load_weight(wsb, w_in3, 3072, 1024)
    kb.phase_a(NT, X3, 0, P3, 0, 3072, wsb, 1024, m3["A"], m3["B"], cs3, 0, 64, 0)
    exchange_finish(kb, jobs)
    P.barrier()


def stage3(kb):
    P = kb.P
    X3 = kb.dram["X3"]
    P3 = kb.dram["P3"]
    kv = kb.dram["kv3"]
    lamd = kb.din("lam_bc", [128, 4, 64])
    sgd = kb.din("subln_bc", [128, 128])
    fgd = kb.din("finalg_bc", [128, D])
    m3 = kb.mod_vectors(3, False, True)
    w_out3 = kb.din("w_out3", [D, D])
    OUT = kb.dout("OUT", [TPC, D])
    Y = kb.dint("Y3", [TPC, D], BF16)
    wout = P.sb([128, 8, D], BF16, "wout")
    kb.load_weight(wout, w_out3, 0, D, scale_bc=m3["gate_bc"])
    lam_init = 0.8 - 0.6 * math.exp(-0.3 * 3)
    lv = P.sb([128, 4, 64], F32, "lv")
    kb.load(lv, lv.t[:], lamd, lamd.t)
    lp = P.sb([128, 2, 64], F32, "lp")
    ls = P.sb([128, 4], F32, "ls")
    kb.tt("dve", lp, lp.t[:, 0, :], lv, lv.t[:, 0, :], lv, lv.t[:, 1, :], ALU.mult)
    kb.tt("dve", lp, lp.t[:, 1, :], lv, lv.t[:, 2, :], lv, lv.t[:, 3, :], ALU.mult)
    P.op("dve", lambda e: e.reduce_sum(out=ls.t[:, 0:2], in_=lp.t[:], axis=AX.X), reads=[lp.b], writes=[ls.b])
    kb.act(ls, ls.t[:, 0:2], ls, ls.t[:, 0:2], AF.Exp)
    kb.tt("dve", ls, ls.t[:, 2:3], ls, ls.t[:, 0:1], ls, ls.t[:, 1:2], ALU.subtract)
    kb.ts("dve", ls, ls.t[:, 3:4], ls, ls.t[:, 2:3], lam_init, -1.0, ALU.add, ALU.mult)
    sg = P.sb([128, 128], F32, "sg")
    kb.load(sg, sg.t[:], sgd, sgd.t)
    kb.ts("dve", sg, sg.t[:], sg, sg.t[:], 1.0 - lam_init, None, ALU.mult)
    fg = P.sb([128, D], F32, "fg")
    kb.load(fg, fg.t[:], fgd, fgd.t)

    NKC = S // 128
    kTh = Rot([P.sb([128, S], BF16, "kTh") for _ in range(2)])
    vah = Rot([P.sb([128, NKC, 129], BF16, "vah") for _ in range(2)])
    for v_ in vah.items:
        kb.memset("pool", v_, v_.t[:], 1.0)
    krow = Rot([P.sb([128, 8, 128], BF16, "krow") for _ in range(2)])
    qrow = Rot([P.sb([128, 4, 128], BF16, "qrow") for _ in range(2)])
    qTt = Rot([P.sb([128, 512], BF16, "qTt") for _ in range(2)])
    ptile = Rot([P.sb([128, 2, 512], BF16, "ptile") for _ in range(3)])
    zt = Rot([P.sb([128, 4, 128], BF16, "zt") for _ in range(2)])
    szt = Rot([P.sb([128, 4, 128], F32, "szt") for _ in range(2)])
    ezt = Rot([P.sb([128, 4, 128], F32, "ezt") for _ in range(2)])
    yh = Rot([P.sb([128, 4, 128], BF16, "yh") for _ in range(2)])
    sm = Rot([P.sb([128, 24], F32, "sm") for _ in range(2)])
    A0r = Rot([P.sb([128, 4, 129], F32, "A0") for _ in range(2)])
    A1r = Rot([P.sb([128, 4, 129], F32, "A1") for _ in range(2)])
    w0 = Rot([P.sb([128, 4, 128], F32, "w0") for _ in range(2)])
    w1 = Rot([P.sb([128, 4, 128], F32, "w1") for _ in range(2)])
    scr = Rot(kb.sc)
    acc = kb.acc
    scale = 64 ** -0.5
    sgb = sg.t[:].unsqueeze(1).to_broadcast([128, 4, 128])

    def emit_qk(kT, qT, kc):
        sc = scr.next()
        specs = [(sc.t[:, comp, :], kT.t[64 * comp:64 * comp + 64, kc * 128:(kc + 1) * 128], qT.t[64 * comp:64 * comp + 64, :], True, True)
                 for comp in range(2)]
        kb.matmuls(sc, specs, [kT, qT])
        pt = ptile.next()
        kb.act(pt, pt.t[:], sc, sc.t[:], AF.Exp, scale=scale)
        return pt

    def emit_pv(pt, va, kc):
        specs = []
        for comp in range(2):
            for j in range(4):
                bank = acc[2 * comp + j // 2]
                specs.append((bank.t[:, (j % 2) * 129:(j % 2) * 129 + 129], pt.t[:, comp, j * 128:(j + 1) * 128],
                              va.t[:, kc, :], kc == 0 and j % 2 == 0, kc == NKC - 1 and j % 2 == 1, True))
        kb.matmuls(list(acc), specs, [pt, va])

    def k_prep(h, kT, c8):
        kr = krow.next()
        kb.load(kr, kr.t[:], kv, kv.t[c8 * 1024:(c8 + 1) * 1024, h * 128:(h + 1) * 128].rearrange("(c p) e -> p c e", p=128), q="sp")
        ptr = kb.ptr.next()
        kb.transposes(ptr, kr, [kr.t[:, c, :] for c in range(8)], kb.ident)
        kb.copy("dve", kT, kT.t[:, c8 * 1024:(c8 + 1) * 1024], ptr, ptr.t[:].rearrange("p c q -> p (c q)"))

    def v_load(h, va):
        kb.load(va, va.t[:, :, 0:128], kv, kv.t[:, 1024 + h * 128: 1024 + (h + 1) * 128].rearrange("(c p) e -> p c e", p=128), q="sp")

    heads = [(kTh.next(), vah.next()) for _ in range(8)]
    v_load(0, heads[0][1])
    for c8 in range(NKC // 8):
        k_prep(0, heads[0][0], c8)
    def q_prep(h, qc):
        qr = qrow.next()
        kb.load(qr, qr.t[:], P3, P3.t[qc * 512:(qc + 1) * 512, h * 128:(h + 1) * 128].rearrange("(c p) e -> p c e", p=128), q="sp")
        z = zt.next()
        kb.load(z, z.t[:], P3, P3.t[qc * 512:(qc + 1) * 512, 3072 + h * 128: 3072 + (h + 1) * 128].rearrange("(c p) e -> p c e", p=128), q="sp")
        ptr = kb.ptr.next()
        kb.transposes(ptr, qr, [qr.t[:, c, :] for c in range(4)], kb.ident)
        qT = qTt.next()
        kb.copy("dve", qT, qT.t[:], ptr, ptr.t[:, 0:4, :].rearrange("p c q -> p (c q)"))
        return qT, z

    def silu_z(z):
        ez = ezt.next()
        kb.act(ez, ez.t[:], z, z.t[:], AF.Exp, scale=-1.0)
        kb.ts("dve", ez, ez.t[:], ez, ez.t[:], 1.0, None, ALU.add)
        sz = szt.next()
        kb.recip(sz, sz.t[:], ez, ez.t[:])
        kb.tt("pool", sz, sz.t[:], sz, sz.t[:], z, z.t[:], ALU.mult)
        return sz

    def epilogue(h, qc, A0, A1, sz):
        s_ = sm.next()
        kb.recip(s_, s_.t[:, 0:4], A0, A0.t[:, :, 128])
        kb.recip(s_, s_.t[:, 4:8], A1, A1.t[:, :, 128])
        kb.ts("dve", s_, s_.t[:, 8:12], s_, s_.t[:, 4:8], ls.t[:, 3:4], None, ALU.mult, sreads=[ls])
        a0, a1 = w0.next(), w1.next()
        kb.tt("pool", a0, a0.t[:], A0, A0.t[:, :, 0:128], s_, s_.t[:, 0:4].unsqueeze(2).to_broadcast([128, 4, 128]), ALU.mult)
        kb.tt("dve", a1, a1.t[:], A1, A1.t[:, :, 0:128], s_, s_.t[:, 8:12].unsqueeze(2).to_broadcast([128, 4, 128]), ALU.mult)
        kb.tt("pool", a0, a0.t[:], a0, a0.t[:], a1, a1.t[:], ALU.add)
        kb.tt("dve", a1, a1.t[:], a0, a0.t[:], a0, a0.t[:], ALU.mult)
        P.op("dve", lambda e: e.reduce_sum(out=s_.t[:, 12:16], in_=a1.t[:], axis=AX.X), reads=[a1.b], writes=[s_.b])
        kb.ts("dve", s_, s_.t[:, 12:16], s_, s_.t[:, 12:16], 1.0 / 128, SUBLN_EPS, ALU.mult, ALU.add)
        kb.act(s_, s_.t[:, 16:20], s_, s_.t[:, 12:16], AF.Ln)
        kb.act(s_, s_.t[:, 20:24], s_, s_.t[:, 16:20], AF.Exp, scale=-0.5)
        kb.tt("pool", a0, a0.t[:], a0, a0.t[:], s_, s_.t[:, 20:24].unsqueeze(2).to_broadcast([128, 4, 128]), ALU.mult)
        kb.tt("dve", a0, a0.t[:], a0, a0.t[:], sg, sgb, ALU.mult)
        y = yh.next()
        kb.tt("pool", y, y.t[:], a0, a0.t[:], sz, sz.t[:], ALU.mult)
        kb.store(Y, Y.t[qc * 512:(qc + 1) * 512, h * 128:(h + 1) * 128].rearrange("(c p) e -> p c e", p=128), y, y.t[:])

    items = [(h, qc) for h in range(8) for qc in range(TPC // 512)]
    cur = q_prep(*items[0])
    for ii, (h, qc) in enumerate(items):
        kT, va = heads[h]
        qT, z = cur
        sz = silu_z(z)
        pts = [emit_qk(kT, qT, 0)]
        for kc in range(NKC):
            if kc + 1 < NKC:
                pts.append(emit_qk(kT, qT, kc + 1))
            emit_pv(pts[kc], va, kc)
        if h + 1 < 8:
            if qc == 0:
                v_load(h + 1, heads[h + 1][1])
            k_prep(h + 1, heads[h + 1][0], qc)
        A0, A1 = A0r.next(), A1r.next()
        for half in range(2):
            kb.copy("dve", A0, A0.t[:, 2 * half:2 * half + 2, :], acc[half], acc[half].t[:, 0:258].rearrange("p (j e) -> p j e", e=129))
            kb.copy("dve", A1, A1.t[:, 2 * half:2 * half + 2, :], acc[2 + half], acc[2 + half].t[:, 0:258].rearrange("p (j e) -> p j e", e=129))
        if ii + 1 < len(items):
            cur = q_prep(*items[ii + 1])
        epilogue(h, qc, A0, A1, sz)
        P.maybe_epoch()
    P.barrier()
    for i in range(NT):
        yt = kb.b1p.next()
        kb.load(yt, yt.t[:], Y, Y.t[i * 128:(i + 1) * 128, :], q="act")
        kb.phase_c_tile(yt, X3, i * 128, OUT, i * 128, wout, final_g=fg)
        P.maybe_epoch()
    P.barrier()


STAGES = [stage0, stage1, stage2, stage3]
_NC_CACHE = {}


def build_program(stages=(0, 1, 2, 3)):
    key = tuple(stages)
    if key in _NC_CACHE:
        return _NC_CACHE[key]
    nc = bass.Bass("TRN2", target_bir_lowering=False)
    shared = {"dram": {}}
    with contextlib.ExitStack() as st:
        P = Prog(nc, st)
        for si in stages:
            with contextlib.ExitStack() as sst:
                P.cur_stack = sst
                kb = KB(nc, P, shared, nf32=8 if si == 0 else 5, layout="l3" if si == 3 else "std", nbf={0: 8, 3: 1}.get(si, 4))
                kb.setup_consts()
                STAGES[si](kb)
                P.barrier()
                P.emit()
    _NC_CACHE[key] = (nc, shared)
    return _NC_CACHE[key]


def _pp(v, n):
    return np.ascontiguousarray(np.asarray(v, np.float32).reshape(n, 128).T)


def _bc(v):
    v = np.asarray(v, np.float32)
    return np.ascontiguousarray(np.broadcast_to(v[None], (128,) + v.shape))


def _rope_table(dim):
    inv = (10000.0 ** (-(np.arange(0, dim, 2, dtype=np.float32) / np.float32(dim)))).astype(np.float32)
    ang = np.arange(S, dtype=np.float32)[:, None] * inv[None, :]
    cos, sin = np.cos(ang).astype(np.float32), np.sin(ang).astype(np.float32)
    nh = 512 // dim
    cosF = np.tile(np.concatenate([cos, cos], axis=1), (1, nh))
    sinS = np.tile(np.concatenate([-sin, sin], axis=1), (1, nh))
    return np.ascontiguousarray(np.stack([cosF, sinS], axis=1))


def _ext(rows, lo, hi, total):
    out = np.zeros((hi - lo,) + rows.shape[1:], rows.dtype)
    a, b = max(lo, 0), min(hi, total)
    out[a - lo:b - lo] = rows[a:b]
    return out


def kernel(x, c, norm_g, w_mod, b_mod, conv_w_in, conv_k, conv_w_out, dil_w_in, dil_w_out,
           swa_w_in, swa_sink, swa_w_out, diff_w_in, diff_lambda, diff_subln_g, diff_w_out, final_g,
           _stages=(0, 1, 2, 3), _debug_outs=()):
    f = lambda a: np.ascontiguousarray(np.asarray(a, np.float32))
    x, c = f(x), f(c)
    bf = ml_dtypes.bfloat16
    masks = np.zeros((128, 2, 128), np.float32)
    kk, qq = np.meshgrid(np.arange(128), np.arange(128), indexing="ij")
    masks[:, 0, :] = (kk >= qq)
    masks[:, 1, :] = (kk <= qq)
    masks = masks.astype(bf)
    cs128, cs64 = _rope_table(128), _rope_table(64)
    nc, shared = build_program(_stages)
    need = set(shared["dram"].keys())

    def valid(lo, hi):
        t = np.arange(lo, hi)
        return np.ascontiguousarray(((t >= 0) & (t < S)).astype(np.float32)[:, None])

    shared_in = dict(ident=np.eye(128, dtype=np.float32), masks=masks,
                     w_in0=f(conv_w_in[0]), w_out0=f(conv_w_out[0]), w_in1=f(dil_w_in[0]), w_out1=f(dil_w_out[0]),
                     w_in2=f(swa_w_in[0]), w_out2=f(swa_w_out[0]), w_in3=f(diff_w_in[0]), w_out3=f(diff_w_out[0]),
                     convk_bc=_bc(f(conv_k[0])), sink_bc=_bc(f(swa_sink[0])), lam_bc=_bc(f(diff_lambda[0])),
                     subln_bc=_bc(f(diff_subln_g[0])), finalg_bc=_bc(f(final_g)))
    for l in range(4):
        shared_in[f"wmod{l}"] = f(w_mod[l])
        shared_in[f"bmodT{l}"] = _pp(b_mod[l], 24)
        shared_in[f"ngT{l}"] = _pp(norm_g[l], 8)
        shared_in[f"bmodg{l}"] = f(b_mod[l][2048:3072]).reshape(1, D)
    cores = [(b, h) for b in range(NB) for h in range(2)]
    maps = []
    for (b, h) in cores:
        t0 = h * TPC
        d = dict(shared_in)
        d["ccol"] = _pp(c[b], 8)
        d["xe"] = _ext(x[b], t0 - 128, t0 + TPC + 128, S)
        vm = np.ones((128, 2), np.float32)
        if h == 0:
            vm[0, 0] = 0.0
        if h == 1:
            vm[127, 1] = 0.0
        d["vmask"] = vm
        d["cs1"] = np.ascontiguousarray(cs128[t0:t0 + TPC])
        d["cs2"] = np.ascontiguousarray(cs64[t0:t0 + TPC])
        d["kvalid1"] = valid(t0 - 1024, t0 + TPC + 1024)
        d["kvalid2"] = valid(t0 - 128, t0 + TPC + 128)
        maps.append({k: v for k, v in d.items() if k in need})
    res = run_bass_kernel_spmd(nc, maps, core_ids=list(range(8)))
    if _debug_outs:
        return res.results
    out = np.zeros((NB, S, D), np.float32)
    for ci, (b, h) in enumerate(cores):
        out[b, h * TPC:(h + 1) * TPC] = res.results[ci]["OUT"]
    return out
```

```python
import contextlib
import math
import numpy as np
import ml_dtypes
import concourse.bass as bass
import concourse.mybir as mybir
from concourse.bass_utils import run_bass_kernel_spmd

F32 = mybir.dt.float32
BF16 = mybir.dt.bfloat16
ALU = mybir.AluOpType
AF = mybir.ActivationFunctionType
AX = mybir.AxisListType

D = 1024
S = 8192
NB = 4
TPC = 4096
NT = TPC // 128
ENGS = ("pe", "act", "dve", "pool", "sp")
EPOCH_LIMIT = 20000
NDMASEM = 10
NCCSEM = 8
NORM_EPS = 1e-6
_DEBUG_OUT = set()
SUBLN_EPS = 1e-5


class Buf:
    __slots__ = ("name", "w", "r")

    def __init__(self, name=""):
        self.name = name
        self.w = None
        self.r = []


class T:
    __slots__ = ("t", "b", "name_is_dram", "parts")

    def __init__(self, t, name="", dram=False, buf=None):
        self.t = t
        self.b = buf if buf is not None else Buf(name)
        self.name_is_dram = dram
        self.parts = None

    def part(self, col):
        if self.parts is None:
            self.parts = {}
        k = col // 512
        if k not in self.parts:
            self.parts[k] = T(self.t, "part", buf=Buf())
        return self.parts[k]


class Prog:
    def __init__(self, nc, stack):
        self.nc = nc
        self.stack = stack
        self.q = {e: [] for e in ENGS}
        self.cnt = {e: 0 for e in ENGS}
        self.nsem = 0
        self.esem = {e: self._newsem("c_" + e) for e in ENGS}
        self.waited = {e: {} for e in ENGS}
        self.dsem = {e: [self._newsem(f"d_{e}{i}") for i in range(NDMASEM)] for e in ("sp", "pool", "act")}
        self.dval = {e: [0] * NDMASEM for e in self.dsem}
        self.dnext = {e: 0 for e in self.dsem}
        self.latest = {}
        self.nid = 0
        self.cur_stack = stack
        self.csem = [self._newsem(f"cc{i}") for i in range(NCCSEM)]
        self.cval = [0] * NCCSEM
        self.cnext = 0

    def _newsem(self, name):
        self.nsem += 1
        return self.stack.enter_context(self.nc.semaphore(name + f"_{self.nsem}"))

    def sb(self, shape, dt, name=None):
        self.nid += 1
        name = (name or "t") + f"_{self.nid}"
        return T(self.cur_stack.enter_context(self.nc.sbuf_tensor(name, list(shape), dt)), name)

    def ps(self, shape, dt, name=None):
        self.nid += 1
        name = (name or "p") + f"_{self.nid}"
        return T(self.cur_stack.enter_context(self.nc.psum_tensor(name, list(shape), dt)), name)

    def _deps(self, eng, reads, writes):
        deps = {}

        def add(tok):
            if tok is None:
                return
            s, v = tok
            if eng == "pe" and s is self.esem["pe"]:
                return
            k = id(s)
            if self.waited[eng].get(k, 0) >= v:
                return
            if k not in deps or deps[k][1] < v:
                deps[k] = (s, v)
        for b in reads:
            add(b.w)
        for b in writes:
            add(b.w)
            for t in b.r:
                add(t)
        out = list(deps.values())
        for s, v in out:
            self.waited[eng][id(s)] = v
        return out

    def _commit(self, tok, reads, writes):
        for b in reads:
            b.r = [t for t in b.r if t[0] is not tok[0]] + [tok]
        for b in writes:
            b.w = tok
            b.r = []
        self.latest[id(tok[0])] = tok

    @staticmethod
    def _bufs(xs):
        return [x.b if isinstance(x, T) else x for x in xs]

    def op(self, eng, fn, reads=(), writes=()):
        reads, writes = self._bufs(reads), self._bufs(writes)
        deps = self._deps(eng, reads, writes)
        self.cnt[eng] += 1
        tok = (self.esem[eng], self.cnt[eng])
        self.q[eng].append((deps, fn, tok[0], 1))
        self._commit(tok, reads, writes)
        return tok

    def dma(self, eng, fn, reads=(), writes=()):
        reads, writes = self._bufs(reads), self._bufs(writes)
        i = self.dnext[eng]
        self.dnext[eng] = (i + 1) % NDMASEM
        s = self.dsem[eng][i]
        deps = self._deps(eng, reads, writes)
        prev = self.dval[eng][i]
        if prev > 0 and self.waited[eng].get(id(s), 0) < prev:
            deps.append((s, prev))
            self.waited[eng][id(s)] = prev
        self.dval[eng][i] = prev + 16
        tok = (s, prev + 16)
        self.q[eng].append((deps, fn, s, 16))
        self._commit(tok, reads, writes)
        return tok

    def cc(self, fn, reads=(), writes=()):
        i = self.cnext
        self.cnext = (i + 1) % NCCSEM
        s = self.csem[i]
        deps = self._deps("pool", list(reads), list(writes))
        prev = self.cval[i]
        if prev > 0 and self.waited["pool"].get(id(s), 0) < prev:
            deps.append((s, prev))
            self.waited["pool"][id(s)] = prev
        self.cval[i] = prev + 1
        tok = (s, prev + 1)
        self.q["pool"].append((deps, fn, s, 1))
        self._commit(tok, list(reads), list(writes))
        return tok

    def barrier(self):
        toks = list(self.latest.values())
        for e in ENGS:
            deps = []
            for s, v in toks:
                if self.waited[e].get(id(s), 0) < v:
                    deps.append((s, v))
                    self.waited[e][id(s)] = v
            if deps:
                self.q[e].append((deps, None, None, 0))
        if max(self.cnt.values()) > EPOCH_LIMIT:
            dm = set(id(s) for ss in self.dsem.values() for s in ss) | set(id(s) for s in self.csem)
            for e in ENGS:
                self.esem[e] = self._newsem("c_" + e)
                self.cnt[e] = 0
            self.latest = {k: t for k, t in self.latest.items() if k in dm}

    def maybe_epoch(self):
        if max(self.cnt.values()) > EPOCH_LIMIT:
            self.barrier()

    def emit(self):
        qs = self.q
        self.q = {e: [] for e in ENGS}
        with self.nc.Block() as block:
            def run(engname):
                def _f(e):
                    for deps, fn, s, inc in qs[engname]:
                        for ds, dv in deps:
                            e.wait_ge(ds, dv)
                        if fn is not None:
                            fn(e).then_inc(s, inc)
                return _f
            block.tensor(run("pe"))
            block.scalar(run("act"))
            block.vector(run("dve"))
            block.gpsimd(run("pool"))
            block.sync(run("sp"))


class Rot:
    def __init__(self, items):
        self.items = items
        self.i = 0

    def next(self):
        x = self.items[self.i % len(self.items)]
        self.i += 1
        return x


class KB:
    def __init__(self, nc, P, shared, nf32=8, layout="std", nbf=4):
        self.nc = nc
        self.P = P
        self.dram = shared["dram"]
        if layout == "std":
            self.ptr = Rot([P.ps([128, 8, 128], BF16, "ptr") for _ in range(2)])
            self.pg = [P.ps([128, 512], F32, "pg") for _ in range(3)]
            self.po = [P.ps([128, 512], F32, "po") for _ in range(3)]
        else:
            self.sc = [P.ps([128, 2, 512], F32, "sc") for _ in range(2)]
            self.acc = [P.ps([128, 512], F32, "acc") for _ in range(4)]
            self.ptr = Rot([T(x.t[:, 0, :].bitcast(BF16).rearrange("p (c q) -> p c q", q=128), "ptrv", buf=x.b) for x in self.sc])
            self.pg = [T(self.sc[0].t[:, 0, :], "pgv", buf=self.sc[0].b), T(self.sc[1].t[:, 0, :], "pgv", buf=self.sc[1].b),
                       T(self.sc[0].t[:, 1, :], "pgv", buf=self.sc[0].b)]
            self.po = None
        self.pgr = Rot(self.pg[0:2])
        self.dq = Rot(["sp", "act"])
        self.f32p = Rot([P.sb([128, D], F32, "f32p") for _ in range(nf32)])
        self.bfp = Rot([P.sb([128, 2048], BF16, "bfp") for _ in range(nbf)])
        self.b1p = Rot([P.sb([128, D], BF16, "b1p") for _ in range(4)])
        self.t3p = Rot([P.sb([128, 8, 128], BF16, "t3p") for _ in range(4)])
        self.ssp = Rot([P.sb([128, 4], F32, "ssp") for _ in range(4)])
        self.w_stage = Rot([P.sb([128, 8, 256], F32, "wst") for _ in range(2)])
        self.WCH = 256
        self.wm_stage = self.w_stage

    def din(self, name, shape, dt=F32):
        if name not in self.dram:
            self.dram[name] = T(self.nc.dram_tensor(name, list(shape), dt, kind="ExternalInput").ap(), name, True)
        return self.dram[name]

    def dout(self, name, shape, dt=F32):
        if name not in self.dram:
            self.dram[name] = T(self.nc.dram_tensor(name, list(shape), dt, kind="ExternalOutput").ap(), name, True)
        return self.dram[name]

    def dint(self, name, shape, dt=F32):
        if name in _DEBUG_OUT:
            return self.dout(name, shape, dt)
        if name not in self.dram:
            self.dram[name] = T(self.nc.dram_tensor(name, list(shape), dt).ap(), name, True)
        return self.dram[name]

    def _untracked(self, x):
        if x is None:
            return Buf()
        return Buf() if (isinstance(x, T) and x.name_is_dram) else x

    def load(self, dst, dst_ap, src, src_ap, q="sp", slow=False):
        src, dst = self._untracked(src), self._untracked(dst)
        if slow:
            self.P.dma(q, lambda e: e.dma_start(out=dst_ap, in_=src_ap, allow_slow_non_contiguous=True), reads=[src], writes=[dst])
        else:
            self.P.dma(q, lambda e: e.dma_start(out=dst_ap, in_=src_ap), reads=[src], writes=[dst])

    def store(self, dst, dst_ap, src, src_ap, q="pool"):
        src, dst = self._untracked(src), self._untracked(dst)
        self.P.dma(q, lambda e: e.dma_start(out=dst_ap, in_=src_ap), reads=[src], writes=[dst])

    def copy(self, eng, dst, dst_ap, src, src_ap):
        if eng == "act":
            self.P.op("act", lambda e: e.activation(out=dst_ap, in_=src_ap, func=AF.Copy), reads=[src], writes=[dst])
        else:
            self.P.op(eng, lambda e: e.tensor_copy(out=dst_ap, in_=src_ap), reads=[src], writes=[dst])

    def act(self, dst, dst_ap, src, src_ap, func, scale=1.0, accum=None, extra_w=()):
        if accum is None:
            self.P.op("act", lambda e: e.activation(out=dst_ap, in_=src_ap, func=func, scale=scale),
                      reads=[src], writes=[dst] + list(extra_w))
        else:
            self.P.op("act", lambda e: e.activation(out=dst_ap, in_=src_ap, func=func, scale=scale, accum_out=accum),
                      reads=[src], writes=[dst] + list(extra_w))

    def tt(self, eng, dst, dst_ap, a, a_ap, b, b_ap, op):
        self.P.op(eng, lambda e: e.tensor_tensor(out=dst_ap, in0=a_ap, in1=b_ap, op=op), reads=[a, b], writes=[dst])

    def ts(self, eng, dst, dst_ap, a, a_ap, s1, s2, op0, op1=None, sreads=()):
        if op1 is None:
            self.P.op(eng, lambda e: e.tensor_scalar(out=dst_ap, in0=a_ap, scalar1=s1, scalar2=None, op0=op0),
                      reads=[a] + list(sreads), writes=[dst])
        else:
            self.P.op(eng, lambda e: e.tensor_scalar(out=dst_ap, in0=a_ap, scalar1=s1, scalar2=s2, op0=op0, op1=op1),
                      reads=[a] + list(sreads), writes=[dst])

    def recip(self, dst, dst_ap, src, src_ap):
        self.P.op("dve", lambda e: e.reciprocal(out=dst_ap, in_=src_ap), reads=[src], writes=[dst])

    def memset(self, eng, dst, dst_ap, val):
        self.P.op(eng, lambda e: e.memset(dst_ap, val), writes=[dst])

    def transposes(self, dst_ps, src, src_aps, ident):
        def fn(e):
            for i, ap in enumerate(src_aps):
                ins = e.transpose(out=dst_ps.t[:, i, :], in_=ap, identity=ident.t[:])
            return ins
        self.P.op("pe", fn, reads=[src, ident], writes=[dst_ps])

    def matmuls(self, dst, specs, reads):
        def fn(e):
            for sp_ in specs:
                o, l, r, st, sp = sp_[:5]
                if len(sp_) > 5 and sp_[5]:
                    ins = e.matmul(o, lhsT=l, rhs=r, start=st, stop=sp, skip_group_check=True)
                else:
                    ins = e.matmul(o, lhsT=l, rhs=r, start=st, stop=sp)
            return ins
        self.P.op("pe", fn, reads=reads, writes=(dst if isinstance(dst, (list, tuple)) else [dst]))

    def setup_consts(self):
        P = self.P
        idf = P.sb([128, 128], F32, "idf")
        self.ident = P.sb([128, 128], BF16, "ident")
        d = self.din("ident", [128, 128])
        self.load(idf, idf.t[:], d, d.t)
        self.copy("dve", self.ident, self.ident.t[:], idf, idf.t[:])
        self.ones_row = P.sb([1, 128], F32, "ones_row")
        self.memset("dve", self.ones_row, self.ones_row.t[:], 1.0)

    def mod_vectors(self, l, need_ab, need_gate):
        P = self.P
        wmod = self.din(f"wmod{l}", [D, 3 * D])
        bT = self.din(f"bmodT{l}", [128, 24])
        res = {}
        if not hasattr(self, "silu_c"):
            cc = self.din("ccol", [128, 8])
            ct = P.sb([128, 8], F32, "ccol")
            self.load(ct, ct.t[:], cc, cc.t)
            self.silu_c = P.sb([128, 8], F32, "siluc")
            self.act(self.silu_c, self.silu_c.t[:], ct, ct.t[:], AF.Silu)
        sc = self.silu_c
        bt = P.sb([128, 24], F32, "bT")
        self.load(bt, bt.t[:], bT, bT.t)
        if need_ab:
            gT = self.din(f"ngT{l}", [128, 8])
            gt = P.sb([128, 8], F32, "gT")
            self.load(gt, gt.t[:], gT, gT.t)
            modT = P.sb([128, 16], F32, "modT")
            pg = self.pg[2]
            for g4 in range(8):
                st = self.wm_stage.next()
                self.load(st, st.t[:], wmod, wmod.t[:, g4 * 256:(g4 + 1) * 256].rearrange("(c p) n -> p c n", p=128),
                          q=self.dq.next())
                specs = []
                for jj in range(2):
                    j = g4 * 2 + jj
                    for k in range(8):
                        specs.append((pg.t[:, j:j + 1], st.t[:, k, jj * 128:(jj + 1) * 128], sc.t[:, k:k + 1], k == 0, k == 7))
                self.matmuls(pg, specs, [st, sc])
            self.tt("dve", modT, modT.t[:], pg, pg.t[:, 0:16], bt, bt.t[:, 0:16], ALU.add)
            A = P.sb([128, 8], F32, "A")
            self.P.op("dve", lambda e: e.scalar_tensor_tensor(out=A.t[:], in0=modT.t[:, 8:16], scalar=1.0, in1=gt.t[:],
                                                               op0=ALU.add, op1=ALU.mult), reads=[modT, gt], writes=[A])
            res["A"] = A
            res["B"] = modT
        if need_gate:
            bg = self.din(f"bmodg{l}", [1, D])
            bgt = P.sb([1, D], F32, "bg")
            self.load(bgt, bgt.t[:], bg, bg.t)
            grow = P.sb([1, D], F32, "grow")
            gbc = P.sb([128, D], F32, "gbc")
            for g2 in range(4):
                st = self.wm_stage.next()
                c0 = 2048 + g2 * 256
                self.load(st, st.t[:], wmod, wmod.t[:, c0:c0 + 256].rearrange("(c p) n -> p c n", p=128), q=self.dq.next())
                pg = self.pg[g2 % 2]
                specs = [(pg.t[0:1, 0:256], sc.t[:, k:k + 1], st.t[:, k, :], k == 0, k == 7) for k in range(8)]
                self.matmuls(pg, specs, [st, sc])
                self.tt("dve", grow, grow.t[:, g2 * 256:(g2 + 1) * 256], pg, pg.t[0:1, 0:256], bgt, bgt.t[:, g2 * 256:(g2 + 1) * 256], ALU.add)
            for g2 in range(2):
                pg = self.pg[g2]
                self.matmuls(pg, [(pg.t[:], self.ones_row.t[:], grow.t[:, g2 * 512:(g2 + 1) * 512], True, True)],
                             [self.ones_row, grow])
                self.copy("act", gbc, gbc.t[:, g2 * 512:(g2 + 1) * 512], pg, pg.t[:])
            res["gate_bc"] = gbc
        return res

    def load_weight(self, wsb, wdram, c0, ncols, scale_bc=None):
        P = self.P
        for g in range(0, ncols, 256):
            n = min(256, ncols - g)
            st = self.w_stage.next()
            self.load(st, st.t[:, :, 0:n], wdram, wdram.t[:, c0 + g:c0 + g + n].rearrange("(c p) n -> p c n", p=128),
                      q=self.dq.next())
            if scale_bc is None:
                self.copy("dve" if (g // 256) % 2 == 0 else "act", wsb.part(g), wsb.t[:, :, g:g + n], st, st.t[:, :, 0:n])
            else:
                for k in range(8):
                    self.tt("dve", wsb.part(g), wsb.t[:, k, g:g + n], st, st.t[:, k, 0:n], scale_bc, scale_bc.t[:, g:g + n], ALU.mult)

    def phase_a(self, ntiles, xsrc, xrow0, pdst, prow0, pcol0, wsb, ncols, A, Bm, cs, csrow0, hd, rope_cols, also=None):
        P = self.P
        if not hasattr(self, "pa_bufs"):
            self.pa_bufs = dict(
                tA=Rot([P.sb([128, 512], F32, "patA") for _ in range(3)]),
                tB=Rot([P.sb([128, 512], F32, "patB") for _ in range(3)]),
                csF=Rot([P.sb([128, 2, 512], F32, "pacsF") for _ in range(3)]),
            )
        bufs = self.pa_bufs
        h2 = hd // 2
        nhf = 512 // hd

        def prep_a(i):
            xt = self.f32p.next()
            self.load(xt, xt.t[:], xsrc, xsrc.t[xrow0 + i * 128: xrow0 + (i + 1) * 128, :], q="sp")
            ss = self.ssp.next()
            sq = self.f32p.next()
            self.act(sq, sq.t[:], xt, xt.t[:], AF.Square, accum=ss.t[:, 0:1], extra_w=[ss])
            self.ts("dve", ss, ss.t[:, 1:2], ss, ss.t[:, 0:1], 1.0 / D, NORM_EPS, ALU.mult, ALU.add)
            self.act(ss, ss.t[:, 2:3], ss, ss.t[:, 1:2], AF.Sqrt)
            self.recip(ss, ss.t[:, 3:4], ss, ss.t[:, 2:3])
            xn = self.b1p.next()
            self.ts("dve", xn, xn.t[:], xt, xt.t[:], ss.t[:, 3:4], None, ALU.mult, sreads=[ss])
            csF = None
            if rope_cols > 0:
                csF = bufs["csF"].next()
                self.load(csF, csF.t[:], cs, cs.t[csrow0 + i * 128: csrow0 + (i + 1) * 128, :, :], q="sp")
            return xn, csF

        def prep_b(pa):
            xn, csF = pa
            ptr = self.ptr.next()
            self.transposes(ptr, xn, [xn.t[:, c * 128:(c + 1) * 128] for c in range(8)], self.ident)
            hT = self.t3p.next()
            for c in range(8):
                if rope_cols == 0 and c % 2 == 1:
                    self.ts("dve", hT, hT.t[:, c, :], ptr, ptr.t[:, c, :], A.t[:, c:c + 1], Bm.t[:, c:c + 1],
                            ALU.mult, ALU.add, sreads=[A, Bm])
                else:
                    self._act_affine(hT, hT.t[:, c, :], ptr, ptr.t[:, c, :], A, A.t[:, c:c + 1], Bm, Bm.t[:, c:c + 1])
            cosF = sinS = None
            if csF is not None:
                cosF = T(csF.t[:, 0, :], "cosF", buf=csF.b)
                sinS = T(csF.t[:, 1, :], "sinS", buf=csF.b)
            return hT, cosF, sinS

        def mm(i, st):
            hT, cosF, sinS = st
            for og in range(0, ncols, 2048):
                on = min(2048, ncols - og)
                ot = self.bfp.next()
                for g in range(og, og + on, 512):
                    pg = self.pgr.next()
                    specs = [(pg.t[:], hT.t[:, k, :], wsb.t[:, k, g:g + 512], k == 0, k == 7) for k in range(8)]
                    self.matmuls(pg, specs, [hT, wsb.part(g)])
                    lo = g - og
                    rc = max(0, min(512, rope_cols - g))
                    if rc > 0:
                        tA, tB = bufs["tA"].next(), bufs["tB"].next()
                        self.tt("dve", tA, tA.t[:, 0:rc], pg, pg.t[:, 0:rc], cosF, cosF.t[:, 0:rc], ALU.mult)
                        q4 = pg.t[:, 0:rc].rearrange("p (h two d) -> p h two d", two=2, d=h2)
                        b4 = tB.t[:, 0:rc].rearrange("p (h two d) -> p h two d", two=2, d=h2)
                        s4 = sinS.t[:, 0:rc].rearrange("p (h two d) -> p h two d", two=2, d=h2)
                        self.tt("dve", tB, b4[:, :, 0, :], pg, q4[:, :, 1, :], sinS, s4[:, :, 0, :], ALU.mult)
                        self.tt("dve", tB, b4[:, :, 1, :], pg, q4[:, :, 0, :], sinS, s4[:, :, 1, :], ALU.mult)
                        self.tt("pool", ot, ot.t[:, lo:lo + rc], tA, tA.t[:, 0:rc], tB, tB.t[:, 0:rc], ALU.add)
                        if rc < 512:
                            self.copy("act", ot, ot.t[:, lo + rc:lo + 512], pg, pg.t[:, rc:512])
                    else:
                        self.copy("act", ot, ot.t[:, lo:lo + 512], pg, pg.t[:])
                self.store(pdst, pdst.t[prow0 + i * 128: prow0 + (i + 1) * 128, pcol0 + og: pcol0 + og + on], ot, ot.t[:, 0:on])
                if also is not None:
                    adst, arow0, c_lo, c_hi, dcol0 = also
                    lo_, hi_ = max(c_lo, og), min(c_hi, og + on)
                    if hi_ > lo_:
                        self.store(adst, adst.t[arow0 + i * 128: arow0 + (i + 1) * 128, dcol0 + lo_ - c_lo: dcol0 + hi_ - c_lo],
                                   ot, ot.t[:, lo_ - og:hi_ - og], q="pool")

        pa = {0: prep_a(0)}
        if ntiles > 1:
            pa[1] = prep_a(1)
        st = prep_b(pa.pop(0))
        for i in range(ntiles):
            if i + 2 < ntiles:
                pa[i + 2] = prep_a(i + 2)
            nxt = prep_b(pa.pop(i + 1)) if i + 1 < ntiles else None
            mm(i, st)
            st = nxt
            self.P.maybe_epoch()

    def _act_affine(self, dst, dst_ap, src, src_ap, a_t, a_ap, b_t, b_ap):
        self.P.op("act", lambda e: e.activation(out=dst_ap, in_=src_ap, func=AF.Identity, bias=b_ap, scale=a_ap),
                  reads=[src, a_t, b_t], writes=[dst])

    def phase_c_tile(self, ytile, xsrc, xrow, xdst, drow, wout, final_g=None, defer=False, banks=None):
        P = self.P
        if not hasattr(self, "pc_bufs"):
            self.pc_bufs = dict(yT=self.t3p, x=self.f32p, xo=self.f32p, ss=self.ssp, sq=self.f32p)
        bufs = self.pc_bufs
        banks = banks or self.pgr
        ptr = self.ptr.next()
        self.transposes(ptr, ytile, [ytile.t[:, c * 128:(c + 1) * 128] for c in range(8)], self.ident)
        yT = bufs["yT"].next()
        self.copy("act", yT, yT.t[:], ptr, ptr.t[:])
        xt = bufs["x"].next()
        self.load(xt, xt.t[:], xsrc, xsrc.t[xrow:xrow + 128, :], q="sp")
        pgs = []
        for half in range(2):
            pg = banks.next()
            specs = [(pg.t[:], yT.t[:, k, :], wout.t[:, k, half * 512:(half + 1) * 512], k == 0, k == 7) for k in range(8)]
            self.matmuls(pg, specs, [yT, wout.part(half * 512)])
            pgs.append(pg)

        def back():
            xo = bufs["xo"].next()
            for half in range(2):
                pg = pgs[half]
                self.tt("dve", xo, xo.t[:, half * 512:(half + 1) * 512], pg, pg.t[:], xt, xt.t[:, half * 512:(half + 1) * 512], ALU.add)
            if final_g is not None:
                ss = bufs["ss"].next()
                sq = bufs["sq"].next()
                self.act(sq, sq.t[:], xo, xo.t[:], AF.Square, accum=ss.t[:, 0:1], extra_w=[ss])
                self.ts("dve", ss, ss.t[:, 1:2], ss, ss.t[:, 0:1], 1.0 / D, NORM_EPS, ALU.mult, ALU.add)
                self.act(ss, ss.t[:, 2:3], ss, ss.t[:, 1:2], AF.Sqrt)
                self.recip(ss, ss.t[:, 3:4], ss, ss.t[:, 2:3])
                self.P.op("dve", lambda e: e.scalar_tensor_tensor(out=xt.t[:], in0=xo.t[:], scalar=ss.t[:, 3:4], in1=final_g.t[:],
                                                                   op0=ALU.mult, op1=ALU.mult), reads=[xo, ss, final_g], writes=[xt])
                self.store(xdst, xdst.t[drow:drow + 128, :], xt, xt.t[:])
            else:
                self.store(xdst, xdst.t[drow:drow + 128, :], xo, xo.t[:])
        if defer:
            return back
        back()
        return None

    def conv_layer(self, p0, convk_bc, vmask, xsrc, xdst, wout):
        P = self.P
        cx = [self.bfp] * 3
        tj = [self.f32p] * 3
        bz, sz, acc, yt = self.bfp, self.f32p, self.f32p, self.b1p
        pcb = Rot([self.pg[0], self.pg[1], self.pg[2], self.po[0]])
        prev_back = None
        for i in range(NT):
            r0 = 128 + i * 128
            ts_ = []
            for j in range(3):
                c = cx[j].next()
                self.load(c, c.t[:], p0, p0.t[r0 + j - 1: r0 + j - 1 + 128, 1024:3072], q=self.dq.next())
                t = tj[j].next()
                self.tt("pool", t, t.t[:], c, c.t[:, 0:1024], c, c.t[:, 1024:2048], ALU.mult)
                if (i == 0 and j == 0):
                    self.ts("pool", t, t.t[:], t, t.t[:], vmask.t[:, 0:1], None, ALU.mult, sreads=[vmask])
                if (i == NT - 1 and j == 2):
                    self.ts("pool", t, t.t[:], t, t.t[:], vmask.t[:, 1:2], None, ALU.mult, sreads=[vmask])
                ts_.append(t)
            b = bz.next()
            self.load(b, b.t[:, 0:1024], p0, p0.t[r0:r0 + 128, 0:1024], q="sp")
            self.load(b, b.t[:, 1024:2048], p0, p0.t[r0:r0 + 128, 3072:4096], q="sp")
            s = sz.next()
            self.act(s, s.t[:], b, b.t[:, 1024:2048], AF.Silu)
            a = acc.next()
            self.tt("dve", a, a.t[:], ts_[0], ts_[0].t[:], convk_bc, convk_bc.t[:, 0, :], ALU.mult)
            self.tt("pool", ts_[1], ts_[1].t[:], ts_[1], ts_[1].t[:], convk_bc, convk_bc.t[:, 1, :], ALU.mult)
            self.tt("dve", ts_[2], ts_[2].t[:], ts_[2], ts_[2].t[:], convk_bc, convk_bc.t[:, 2, :], ALU.mult)
            self.tt("pool", a, a.t[:], a, a.t[:], ts_[1], ts_[1].t[:], ALU.add)
            self.tt("dve", a, a.t[:], a, a.t[:], ts_[2], ts_[2].t[:], ALU.add)
            self.tt("pool", s, s.t[:], s, s.t[:], b, b.t[:, 0:1024], ALU.mult)
            y = yt.next()
            self.tt("dve", y, y.t[:], a, a.t[:], s, s.t[:], ALU.mult)
            bk = self.phase_c_tile(y, xsrc, r0, xdst, i * 128, wout, defer=True, banks=pcb)
            if prev_back is not None:
                prev_back()
            prev_back = bk
            self.P.maybe_epoch()
        prev_back()

    def band_attention(self, *, pq, kv, kvalid, masks, nheads, hd, ev, qcol0, kcol0, vcol0, nkvh, gq,
                       blocks, scale, finish):
        P = self.P
        qcols = nheads * hd
        kcols = nkvh * hd
        nqt = qcols // 128
        dup = (hd == 64)
        nkt = nkvh if dup else kcols // 128
        maxch = max(len(b["chunks"]) for b in blocks)
        hb = max(1, 512 // (128 * maxch))
        key = (qcols, nkt, nkvh, ev, maxch)
        if not hasattr(self, "_ba_cache"):
            self._ba_cache = {}
        if key not in self._ba_cache:
            self._ba_cache[key] = dict(
                qrow=Rot([P.sb([128, qcols], BF16, "ba_q") for _ in range(2)]),
                qT=Rot([P.sb([128, nqt, 128], BF16, "ba_qT") for _ in range(2)]),
                krow=Rot([P.sb([128, nkt * 128], BF16, "ba_k") for _ in range(2 * maxch)]),
                kT=Rot([P.sb([128, nkt, 128], BF16, "ba_kT") for _ in range(2 * maxch)]),
                vaug=Rot([P.sb([128, nkvh, ev + 1], BF16, "ba_v") for _ in range(2 * maxch)]),
                pt=Rot([P.sb([128, hb * maxch, 128], BF16, "ba_pt") for _ in range(3)]),
                kvc=Rot([P.sb([128, 1], F32, "ba_kvc") for _ in range(2 * maxch)]))
            for v_ in self._ba_cache[key]["vaug"].items:
                self.memset("pool", v_, v_.t[:], 1.0)
        c_ = self._ba_cache[key]
        kvcr = c_["kvc"]
        qrow, qT, krow, kT, vaug, pt = c_["qrow"], c_["qT"], c_["krow"], c_["kT"], c_["vaug"], c_["pt"]
        per_bank = 512 // (ev + 1)
        nch = maxch
        assert all(len(b_["chunks"]) == nch for b_ in blocks)
        if "maskb" not in c_:
            c_["maskb"] = P.sb([128, hb * nch, 128], BF16, "ba_maskb")
        maskb = c_["maskb"]
        self.memset("pool", maskb, maskb.t[:], 0.0)
        for ii in range(hb):
            for ci in range(nch):
                mid = blocks[0]["chunks"][ci][2]
                if mid is not None:
                    self.ts("pool", maskb, maskb.t[:, ii * nch + ci, :], masks, masks.t[:, mid, :], 30000.0, -30000.0, ALU.mult, ALU.add)
        oaps = []
        for h in range(nheads):
            bank = self.po[h // per_bank]
            sl = (h % per_bank) * (ev + 1)
            oaps.append((bank, bank.t[:, sl:sl + ev + 1]))

        def prep_loads(blk):
            qr = qrow.next()
            r0, st = blk["qrow0"], blk["qstep"]
            self.load(qr, qr.t[:], pq, pq.t[r0:r0 + 127 * st + 1:st, qcol0:qcol0 + qcols], q="sp")
            vas, kvcs, krs = [], [], []
            for (k0, kst, mid, halo) in blk["chunks"]:
                kr = krow.next()
                if dup:
                    ksrc = kv.t[k0:k0 + 127 * kst + 1:kst, kcol0:kcol0 + kcols].rearrange("p (h d) -> p h d", d=hd)
                    kr4 = kr.t[:].rearrange("p (h two d) -> p h two d", two=2, d=hd)
                    self.load(kr, kr4[:, :, 0, :], kv, ksrc, q="sp")
                    self.load(kr, kr4[:, :, 1, :], kv, ksrc, q="sp")
                else:
                    self.load(kr, kr.t[:], kv, kv.t[k0:k0 + 127 * kst + 1:kst, kcol0:kcol0 + kcols], q="sp")
                va = vaug.next()
                self.load(va, va.t[:, :, 0:ev], kv,
                          kv.t[k0:k0 + 127 * kst + 1:kst, vcol0:vcol0 + nkvh * ev].rearrange("p (h e) -> p h e", e=ev), q="sp")
                if halo:
                    kc_ = kvcr.next()
                    self.load(kc_, kc_.t[:], kvalid, kvalid.t[k0:k0 + 127 * kst + 1:kst, :], q="sp", slow=True)
                    kvcs.append(kc_)
                else:
                    kvcs.append(None)
                krs.append(kr)
                vas.append(va)
            return qr, krs, vas, kvcs

        def prep_tr(ld):
            qr, krs, vas, kvcs = ld
            kts = []
            ptr = self.ptr.next()
            self.transposes(ptr, qr, [qr.t[:, c * 128:(c + 1) * 128] for c in range(nqt)], self.ident)
            qt = qT.next()
            self.copy("dve", qt, qt.t[:], ptr, ptr.t[:, 0:nqt, :])
            for kr in krs:
                ptr = self.ptr.next()
                self.transposes(ptr, kr, [kr.t[:, c * 128:(c + 1) * 128] for c in range(nkt)], self.ident)
                kt = kT.next()
                self.copy("dve", kt, kt.t[:], ptr, ptr.t[:, 0:nkt, :])
                kts.append(kt)
            return qt, kts, vas, kvcs

        def emit_qk(st, hs):
            qt, kts, vas, kvcs = st
            pg = self.pgr.next()
            specs = []
            for ii, h in enumerate(hs):
                kvh = h // gq
                qd = h * hd
                po_ = qd % 128
                for ci in range(nch):
                    col = (ii * nch + ci) * 128
                    specs.append((pg.t[:, col:col + 128], kts[ci].t[po_: po_ + hd, kvh, :], qt.t[po_: po_ + hd, qd // 128, :],
                                  len(specs) == 0, False, True))
            ncol = len(hs) * nch
            specs.append((pg.t[:, 0:ncol * 128], self.ident.t[:], maskb.t[:, 0:ncol, :].rearrange("p c q -> p (c q)"), False, True, True))
            self.matmuls(pg, specs, kts + [qt, maskb, self.ident])
            p = pt.next()
            self.act(p, p.t[:, 0:ncol, :], pg, pg.t[:, 0:ncol * 128].rearrange("p (c q) -> p c q", q=128), AF.Exp, scale=scale)
            for ii, h in enumerate(hs):
                for ci in range(nch):
                    if kvcs[ci] is not None:
                        cc = ii * nch + ci
                        self.ts("dve", p, p.t[:, cc, :], p, p.t[:, cc, :], kvcs[ci].t[:, 0:1], None, ALU.mult, sreads=[kvcs[ci]])
            return p

        def emit_pv(st, hs, p):
            qt, kts, vas, kvcs = st
            banks = []
            per = {}
            for ii, h in enumerate(hs):
                kvh = h // gq
                bank, oap = oaps[h]
                if bank not in banks:
                    banks.append(bank)
                    per[id(bank)] = []
                for ci in range(nch):
                    cc = ii * nch + ci
                    per[id(bank)].append((oap, p.t[:, cc, :], vas[ci].t[:, kvh, :], ci == 0, ci == nch - 1))
            for bank in banks:
                self.matmuls(bank, per[id(bank)], [p] + vas)

        units = [list(range(h0, min(nheads, h0 + hb))) for h0 in range(0, nheads, hb)]
        prep_at = min(len(units) - 1, max(1, len(units) // 2))
        st = prep_tr(prep_loads(blocks[0]))
        deferred = None
        for bi, blk in enumerate(blocks):
            ld = prep_loads(blocks[bi + 1]) if bi + 1 < len(blocks) else None
            nxt = None
            ps_ = [emit_qk(st, units[0])]
            for ui, hs in enumerate(units):
                if ui + 1 < len(units):
                    ps_.append(emit_qk(st, units[ui + 1]))
                emit_pv(st, hs, ps_[ui])
                if ui == prep_at - 1 and ld is not None:
                    nxt = prep_tr(ld)
            if nxt is None and ld is not None:
                nxt = prep_tr(ld)
            d_ = finish(bi, blk, oaps)
            if deferred is not None:
                deferred()
            deferred = d_
            st = nxt
            self.P.maybe_epoch()
        if deferred is not None:
            deferred()


def _group(xs, n):
    return [xs[i:i + n] for i in range(0, len(xs), n)]


def d2d(kb, dst, dst_ap, src, src_ap, q="sp"):
    kb.load(dst, dst_ap, src, src_ap, q=q)


CC_MAX_BYTES = 2 << 20


def exchange_start(kb, name, pieces, cols):
    P = kb.P
    rmax = max(1, CC_MAX_BYTES // (cols * 2))
    groups = [[0, 1], [2, 3], [4, 5], [6, 7]]
    jobs = []
    ci = 0
    for (src_ap, rows, d0, d1) in pieces:
        for r0 in range(0, rows, rmax):
            n = min(rmax, rows - r0)
            sb_ = kb.dint(f"{name}_s{ci}", [n, cols], BF16)
            db_ = kb.dint(f"{name}_d{ci}", [2 * n, cols], BF16)
            sb_.name_is_dram = False
            db_.name_is_dram = False
            ci += 1
            d2d(kb, sb_, sb_.t, None, src_ap[r0:r0 + n, :], q="sp")
            jobs.append((sb_, db_, n, r0, d0, d1))
    for (sb_, db_, n, r0, d0, d1) in jobs:
        P.cc((lambda sb_=sb_, db_=db_: (lambda e: e.collective_compute(
            "AllGather", ALU.bypass, replica_groups=groups, ins=[sb_.t.opt()], outs=[db_.t.opt()])))(),
            reads=[sb_.b], writes=[db_.b])
    return jobs


def exchange_finish(kb, jobs):
    for (sb_, db_, n, r0, d0, d1) in jobs:
        if d0 is not None:
            d2d(kb, None, d0[r0:r0 + n, :], db_, db_.t[0:n, :], q=kb.dq.next())
        if d1 is not None:
            d2d(kb, None, d1[r0:r0 + n, :], db_, db_.t[n:2 * n, :], q=kb.dq.next())


def stage0(kb):
    P = kb.P
    xe = kb.din("xe", [TPC + 256, D])
    m0 = kb.mod_vectors(0, True, True)
    m1 = kb.mod_vectors(1, True, False)
    w_in0 = kb.din("w_in0", [D, 4096])
    w_out0 = kb.din("w_out0", [D, D])
    w_in1 = kb.din("w_in1", [D, 8192])
    convk = kb.din("convk_bc", [128, 3, D])
    vmask = kb.din("vmask", [128, 2])
    cs1 = kb.din("cs1", [TPC, 2, 512])
    X1 = kb.dint("X1", [TPC, D])
    P1 = kb.dint("P1", [TPC, 8192], BF16)
    p0 = kb.dint("p0", [TPC + 256, 4096], BF16)
    wsb = P.sb([128, 8, 2048], BF16, "wsb")
    wout = P.sb([128, 8, D], BF16, "wout")
    ck = P.sb([128, 3, D], F32, "convk")
    vm = P.sb([128, 2], F32, "vmask")
    kb.load(ck, ck.t[:], convk, convk.t)
    kb.load(vm, vm.t[:], vmask, vmask.t)
    kb.load_weight(wout, w_out0, 0, D, scale_bc=m0["gate_bc"])
    for ps_ in range(2):
        kb.load_weight(wsb, w_in0, ps_ * 2048, 2048)
        kb.phase_a(NT + 2, xe, 0, p0, 0, ps_ * 2048, wsb, 2048, m0["A"], m0["B"], None, 0, 128, 0)
    P.barrier()
    kb.conv_layer(p0, ck, vm, xe, X1, wout)
    P.barrier()
    HALO = 1024
    kv = kb.dint("kv1", [TPC + 2 * HALO, 4096], BF16)
    for (c0, nc_, rc_) in ((3072, 2048, 2048), (5120, 2048, 1024)):
        kb.load_weight(wsb, w_in1, c0, nc_)
        kb.phase_a(NT, X1, 0, P1, 0, c0, wsb, nc_, m1["A"], m1["B"], cs1, 0, 128, rc_, also=(kv, HALO, 0, nc_, c0 - 3072))
    P.barrier()
    jobs = exchange_start(kb, "ex1", [
        (P1.t[TPC - HALO:TPC, 3072:7168], HALO, kv.t[0:HALO, :], None),
        (P1.t[0:HALO, 3072:7168], HALO, None, kv.t[HALO + TPC:, :]),
    ], 4096)
    for (c0, nc_, rc_) in ((0, 2048, 2048), (2048, 1024, 1024), (7168, 1024, 0)):
        kb.load_weight(wsb, w_in1, c0, nc_)
        kb.phase_a(NT, X1, 0, P1, 0, c0, wsb, nc_, m1["A"], m1["B"], cs1, 0, 128, rc_)
    exchange_finish(kb, jobs)
    P.barrier()


def stage1(kb):
    P = kb.P
    HALO = 1024
    X1 = kb.dram["X1"]
    P1 = kb.dram["P1"]
    kv = kb.dram["kv1"]
    kvalid = kb.din("kvalid1", [TPC + 2 * HALO, 1])
    masksd = kb.din("masks", [128, 2, 128], BF16)
    m1 = kb.mod_vectors(1, False, True)
    m2 = kb.mod_vectors(2, True, False)
    w_out1 = kb.din("w_out1", [D, D])
    w_in2 = kb.din("w_in2", [D, 2560])
    cs2 = kb.din("cs2", [TPC, 2, 512])
    X2 = kb.dint("X2", [TPC, D])
    P2 = kb.dint("P2", [TPC, 2560], BF16)
    og = [kb.dint(f"og{g}", [TPC, 8 * 129]) for g in range(3)]
    wout = P.sb([128, 8, D], BF16, "wout")
    masks = P.sb([128, 2, 128], BF16, "masks")
    kb.load(masks, masks.t[:], masksd, masksd.t)
    kb.load_weight(wout, w_out1, 0, D, scale_bc=m1["gate_bc"])
    P.barrier()
    osb = Rot([P.sb([128, 8 * 129], F32, "osb") for _ in range(2)])
    for g, r in enumerate((1, 4, 16)):
        blocks = []
        for p in range(r):
            for j in range(TPC // (128 * r)):
                i0 = 128 * j
                q0 = p + r * i0
                ka = HALO + p + r * (i0 - 64)
                kb_ = HALO + p + r * (i0 + 64)
                ha = ka < HALO
                hb_ = kb_ + 127 * r >= HALO + TPC
                blocks.append(dict(qrow0=q0, qstep=r, chunks=[(ka, r, 0, ha), (kb_, r, 1, hb_)]))

        def finish(bi, blk, oaps, g=g, r=r):
            o = osb.next()
            for bnk in range(3):
                n = min(3, 8 - 3 * bnk) * 129
                kb.copy("act" if bnk == 1 else "dve", o, o.t[:, bnk * 387: bnk * 387 + n], kb.po[bnk], kb.po[bnk].t[:, 0:n])
            r0 = blk["qrow0"]
            kb.store(og[g], og[g].t[r0:r0 + 127 * r + 1:r, :], o, o.t[:])
        kb.band_attention(pq=P1, kv=kv, kvalid=kvalid, masks=masks, nheads=8, hd=128, ev=128,
                          qcol0=g * 1024, kcol0=g * 1024, vcol0=3072, nkvh=8, gq=1,
                          blocks=blocks, scale=128 ** -0.5, finish=finish)
    P.barrier()
    zt = Rot([P.sb([128, D], BF16, "zt") for _ in range(2)])
    rl = Rot([P.sb([128, 8], F32, "rl") for _ in range(2)])
    o3 = Rot(osb.items + [P.sb([128, 8 * 129], F32, "o3")])
    pcb = Rot([kb.pg[0], kb.pg[1], kb.pg[2], kb.po[0]])
    prev_back = None
    for i in range(NT):
        os_ = []
        for g in range(3):
            o = o3.next()
            kb.load(o, o.t[:], og[g], og[g].t[i * 128:(i + 1) * 128, :], q=kb.dq.next())
            os_.append(o)
        z = zt.next()
        kb.load(z, z.t[:], P1, P1.t[i * 128:(i + 1) * 128, 7168:8192], q="sp")
        kb.tt("pool", os_[0], os_[0].t[:], os_[0], os_[0].t[:], os_[1], os_[1].t[:], ALU.add)
        kb.tt("dve", os_[0], os_[0].t[:], os_[0], os_[0].t[:], os_[2], os_[2].t[:], ALU.add)
        o4 = os_[0].t[:].rearrange("p (h e) -> p h e", e=129)
        rr = rl.next()
        kb.recip(rr, rr.t[:], os_[0], o4[:, :, 128])
        sz = kb.f32p.next()
        kb.act(sz, sz.t[:], z, z.t[:], AF.Silu)
        on = kb.f32p.next()
        kb.tt("dve", on, on.t[:].rearrange("p (h e) -> p h e", e=128), os_[0], o4[:, :, 0:128],
              rr, rr.t[:].unsqueeze(2).to_broadcast([128, 8, 128]), ALU.mult)
        y = kb.b1p.next()
        kb.tt("pool", y, y.t[:], on, on.t[:], sz, sz.t[:], ALU.mult)
        bk = kb.phase_c_tile(y, X1, i * 128, X2, i * 128, wout, defer=True, banks=pcb)
        if prev_back is not None:
            prev_back()
        prev_back = bk
        P.maybe_epoch()
    prev_back()
    P.barrier()
    wsb = P.sb([128, 8, 2048], BF16, "wsb")
    kb.load_weight(wsb, w_in2, 0, 2048)
    H2 = 128
    kv2 = kb.dint("kv2", [TPC + 2 * H2, 512], BF16)
    kb.phase_a(NT, X2, 0, P2, 0, 0, wsb, 2048, m2["A"], m2["B"], cs2, 0, 64, 1280, also=(kv2, H2, 1024, 1536, 0))
    P.barrier()
    jobs = exchange_start(kb, "ex2", [
        (P2.t[TPC - H2:TPC, 1024:1536], H2, kv2.t[0:H2, :], None),
        (P2.t[0:H2, 1024:1536], H2, None, kv2.t[H2 + TPC:, :]),
    ], 512)
    kb.load_weight(wsb, w_in2, 2048, 512)
    kb.phase_a(NT, X2, 0, P2, 0, 2048, wsb, 512, m2["A"], m2["B"], cs2, 0, 64, 0)
    exchange_finish(kb, jobs)
    P.barrier()


def stage2(kb):
    P = kb.P
    HALO = 128
    X2 = kb.dram["X2"]
    P2 = kb.dram["P2"]
    kv = kb.dram["kv2"]
    kvalid = kb.din("kvalid2", [TPC + 2 * HALO, 1])
    masksd = kb.din("masks", [128, 2, 128], BF16)
    sinkd = kb.din("sink_bc", [128, 16])
    m2 = kb.mod_vectors(2, False, True)
    m3 = kb.mod_vectors(3, True, False)
    w_out2 = kb.din("w_out2", [D, D])
    w_in3 = kb.din("w_in3", [D, 4096])
    cs3 = kb.din("cs2", [TPC, 2, 512])
    X3 = kb.dint("X3", [TPC, D])
    P3 = kb.dint("P3", [TPC, 4096], BF16)
    wout = P.sb([128, 8, D], BF16, "wout")
    masks = P.sb([128, 2, 128], BF16, "masks")
    kb.load(masks, masks.t[:], masksd, masksd.t)
    esink = P.sb([128, 16], F32, "esink")
    kb.load(esink, esink.t[:], sinkd, sinkd.t)
    kb.act(esink, esink.t[:], esink, esink.t[:], AF.Exp)
    kb.load_weight(wout, w_out2, 0, D, scale_bc=m2["gate_bc"])
    P.barrier()
    blocks = []
    for j in range(NT):
        q0 = 128 * j
        blocks.append(dict(qrow0=q0, qstep=1, chunks=[(HALO + q0 - 128, 1, 0, j == 0), (HALO + q0, 1, None, False),
                                                      (HALO + q0 + 128, 1, 1, j == NT - 1)]))
    zt = Rot([P.sb([128, D], BF16, "zt") for _ in range(2)])
    lt = Rot([P.sb([128, 16], F32, "lt") for _ in range(2)])

    oev = Rot([P.sb([128, 16 * 65], F32, "oev") for _ in range(2)])

    def finish(bi, blk, oaps):
        r0 = blk["qrow0"]
        oe = oev.next()
        for bnk in range(3):
            nh = min(7, 16 - 7 * bnk)
            kb.copy("act" if bnk == 1 else "dve", oe, oe.t[:, 7 * bnk * 65:(7 * bnk + nh) * 65], kb.po[bnk], kb.po[bnk].t[:, 0:nh * 65])

        def rest():
            z = zt.next()
            kb.load(z, z.t[:], P2, P2.t[r0:r0 + 128, 1536:2560], q="sp")
            sz = kb.f32p.next()
            kb.act(sz, sz.t[:], z, z.t[:], AF.Silu)
            l = lt.next()
            on = kb.f32p.next()
            o3_ = oe.t[:].rearrange("p (h e) -> p h e", e=65)
            kb.tt("dve", l, l.t[:], oe, o3_[:, :, 64], esink, esink.t[:], ALU.add)
            kb.recip(l, l.t[:], l, l.t[:])
            kb.tt("dve", on, on.t[:].rearrange("p (h e) -> p h e", e=64), oe, o3_[:, :, 0:64],
                  l, l.t[:].unsqueeze(2).to_broadcast([128, 16, 64]), ALU.mult)
            y = kb.b1p.next()
            kb.tt("pool", y, y.t[:], on, on.t[:], sz, sz.t[:], ALU.mult)
            kb.phase_c_tile(y, X2, r0, X3, r0, wout)
        return rest

    kb.band_attention(pq=P2, kv=kv, kvalid=kvalid, masks=masks, nheads=16, hd=64, ev=64,
                      qcol0=0, kcol0=0, vcol0=256, nkvh=4, gq=4, blocks=blocks, scale=64 ** -0.5, finish=finish)
    P.barrier()
    wsb = P.sb([128, 8, 2048], BF16, "wsb")
    kb.load_weight(wsb, w_in3, 1024, 2048)
    kb.phase_a(NT, X3, 0, P3, 0, 1024, wsb, 2048, m3["A"], m3["B"], cs3, 0, 64, 1024)
    P.barrier()
    kv3 = kb.dint("kv3", [S, 2048], BF16)
    jobs = exchange_start(kb, "ex3", [(P3.t[:, 1024:3072], TPC, kv3.t[0:TPC, :], kv3.t[TPC:S, :])], 2048)
    kb.load_weight(wsb, w_in3, 0, 1024)
    kb.phase_a(NT, X3, 0, P3, 0, 0, wsb, 1024, m3["A"], m3["B"], cs3, 0, 64, 1024)
    kb.load_weight(wsb, w_in3, 3072, 1024)
    kb.phase_a(NT, X3, 0, P3, 0, 3072, wsb, 1024, m3["A"], m3["B"], cs3, 0, 64, 0)
    exchange_finish(kb, jobs)
    P.barrier()


def stage3(kb):
    P = kb.P
    X3 = kb.dram["X3"]
    P3 = kb.dram["P3"]
    kv = kb.dram["kv3"]
    lamd = kb.din("lam_bc", [128, 4, 64])
    sgd = kb.din("subln_bc", [128, 128])
    fgd = kb.din("finalg_bc", [128, D])
    m3 = kb.mod_vectors(3, False, True)
    w_out3 = kb.din("w_out3", [D, D])
    OUT = kb.dout("OUT", [TPC, D])
    Y = kb.dint("Y3", [TPC, D], BF16)
    wout = P.sb([128, 8, D], BF16, "wout")
    kb.load_weight(wout, w_out3, 0, D, scale_bc=m3["gate_bc"])
    lam_init = 0.8 - 0.6 * math.exp(-0.3 * 3)
    lv = P.sb([128, 4, 64], F32, "lv")
    kb.load(lv, lv.t[:], lamd, lamd.t)
    lp = P.sb([128, 2, 64], F32, "lp")
    ls = P.sb([128, 4], F32, "ls")
    kb.tt("dve", lp, lp.t[:, 0, :], lv, lv.t[:, 0, :], lv, lv.t[:, 1, :], ALU.mult)
    kb.tt("dve", lp, lp.t[:, 1, :], lv, lv.t[:, 2, :], lv, lv.t[:, 3, :], ALU.mult)
    P.op("dve", lambda e: e.reduce_sum(out=ls.t[:, 0:2], in_=lp.t[:], axis=AX.X), reads=[lp.b], writes=[ls.b])
    kb.act(ls, ls.t[:, 0:2], ls, ls.t[:, 0:2], AF.Exp)
    kb.tt("dve", ls, ls.t[:, 2:3], ls, ls.t[:, 0:1], ls, ls.t[:, 1:2], ALU.subtract)
    kb.ts("dve", ls, ls.t[:, 3:4], ls, ls.t[:, 2:3], lam_init, -1.0, ALU.add, ALU.mult)
    sg = P.sb([128, 128], F32, "sg")
    kb.load(sg, sg.t[:], sgd, sgd.t)
    kb.ts("dve", sg, sg.t[:], sg, sg.t[:], 1.0 - lam_init, None, ALU.mult)
    fg = P.sb([128, D], F32, "fg")
    kb.load(fg, fg.t[:], fgd, fgd.t)

    NKC = S // 128
    kTh = Rot([P.sb([128, S], BF16, "kTh") for _ in range(2)])
    vah = Rot([P.sb([128, NKC, 129], BF16, "vah") for _ in range(2)])
    for v_ in vah.items:
        kb.memset("pool", v_, v_.t[:], 1.0)
    krow = Rot([P.sb([128, 8, 128], BF16, "krow") for _ in range(2)])
    qrow = Rot([P.sb([128, 4, 128], BF16, "qrow") for _ in range(2)])
    qTt = Rot([P.sb([128, 512], BF16, "qTt") for _ in range(2)])
    ptile = Rot([P.sb([128, 2, 512], BF16, "ptile") for _ in range(3)])
    zt = Rot([P.sb([128, 4, 128], BF16, "zt") for _ in range(2)])
    szt = Rot([P.sb([128, 4, 128], F32, "szt") for _ in range(2)])
    ezt = Rot([P.sb([128, 4, 128], F32, "ezt") for _ in range(2)])
    yh = Rot([P.sb([128, 4, 128], BF16, "yh") for _ in range(2)])
    sm = Rot([P.sb([128, 24], F32, "sm") for _ in range(2)])
    A0r = Rot([P.sb([128, 4, 129], F32, "A0") for _ in range(2)])
    A1r = Rot([P.sb([128, 4, 129], F32, "A1") for _ in range(2)])
    w0 = Rot([P.sb([128, 4, 128], F32, "w0") for _ in range(2)])
    w1 = Rot([P.sb([128, 4, 128], F32, "w1") for _ in range(2)])
    scr = Rot(kb.sc)
    acc = kb.acc
    scale = 64 ** -0.5
    sgb = sg.t[:].unsqueeze(1).to_broadcast([128, 4, 128])

    def emit_qk(kT, qT, kc):
        sc = scr.next()
        specs = [(sc.t[:, comp, :], kT.t[64 * comp:64 * comp + 64, kc * 128:(kc + 1) * 128], qT.t[64 * comp:64 * comp + 64, :], True, True)
                 for comp in range(2)]
        kb.matmuls(sc, specs, [kT, qT])
        pt = ptile.next()
        kb.act(pt, pt.t[:], sc, sc.t[:], AF.Exp, scale=scale)
        return pt

    def emit_pv(pt, va, kc):
        specs = []
        for comp in range(2):
            for j in range(4):
                bank = acc[2 * comp + j // 2]
                specs.append((bank.t[:, (j % 2) * 129:(j % 2) * 129 + 129], pt.t[:, comp, j * 128:(j + 1) * 128],
                              va.t[:, kc, :], kc == 0 and j % 2 == 0, kc == NKC - 1 and j % 2 == 1, True))
        kb.matmuls(list(acc), specs, [pt, va])

    def k_prep(h, kT, c8):
        kr = krow.next()
        kb.load(kr, kr.t[:], kv, kv.t[c8 * 1024:(c8 + 1) * 1024, h * 128:(h + 1) * 128].rearrange("(c p) e -> p c e", p=128), q="sp")
        ptr = kb.ptr.next()
        kb.transposes(ptr, kr, [kr.t[:, c, :] for c in range(8)], kb.ident)
        kb.copy("dve", kT, kT.t[:, c8 * 1024:(c8 + 1) * 1024], ptr, ptr.t[:].rearrange("p c q -> p (c q)"))

    def v_load(h, va):
        kb.load(va, va.t[:, :, 0:128], kv, kv.t[:, 1024 + h * 128: 1024 + (h + 1) * 128].rearrange("(c p) e -> p c e", p=128), q="sp")

    heads = [(kTh.next(), vah.next()) for _ in range(8)]
    v_load(0, heads[0][1])
    for c8 in range(NKC // 8):
        k_prep(0, heads[0][0], c8)
    def q_prep(h, qc):
        qr = qrow.next()
        kb.load(qr, qr.t[:], P3, P3.t[qc * 512:(qc + 1) * 512, h * 128:(h + 1) * 128].rearrange("(c p) e -> p c e", p=128), q="sp")
        z = zt.next()
        kb.load(z, z.t[:], P3, P3.t[qc * 512:(qc + 1) * 512, 3072 + h * 128: 3072 + (h + 1) * 128].rearrange("(c p) e -> p c e", p=128), q="sp")
        ptr = kb.ptr.next()
        kb.transposes(ptr, qr, [qr.t[:, c, :] for c in range(4)], kb.ident)
        qT = qTt.next()
        kb.copy("dve", qT, qT.t[:], ptr, ptr.t[:, 0:4, :].rearrange("p c q -> p (c q)"))
        return qT, z

    def silu_z(z):
        ez = ezt.next()
        kb.act(ez, ez.t[:], z, z.t[:], AF.Exp, scale=-1.0)
        kb.ts("dve", ez, ez.t[:], ez, ez.t[:], 1.0, None, ALU.add)
        sz = szt.next()
        kb.recip(sz, sz.t[:], ez, ez.t[:])
        kb.tt("pool", sz, sz.t[:], sz, sz.t[:], z, z.t[:], ALU.mult)
        return sz

    def epilogue(h, qc, A0, A1, sz):
        s_ = sm.next()
        kb.recip(s_, s_.t[:, 0:4], A0, A0.t[:, :, 128])
        kb.recip(s_, s_.t[:, 4:8], A1, A1.t[:, :, 128])
        kb.ts("dve", s_, s_.t[:, 8:12], s_, s_.t[:, 4:8], ls.t[:, 3:4], None, ALU.mult, sreads=[ls])
        a0, a1 = w0.next(), w1.next()
        kb.tt("pool", a0, a0.t[:], A0, A0.t[:, :, 0:128], s_, s_.t[:, 0:4].unsqueeze(2).to_broadcast([128, 4, 128]), ALU.mult)
        kb.tt("dve", a1, a1.t[:], A1, A1.t[:, :, 0:128], s_, s_.t[:, 8:12].unsqueeze(2).to_broadcast([128, 4, 128]), ALU.mult)
        kb.tt("pool", a0, a0.t[:], a0, a0.t[:], a1, a1.t[:], ALU.add)
        kb.tt("dve", a1, a1.t[:], a0, a0.t[:], a0, a0.t[:], ALU.mult)
        P.op("dve", lambda e: e.reduce_sum(out=s_.t[:, 12:16], in_=a1.t[:], axis=AX.X), reads=[a1.b], writes=[s_.b])
        kb.ts("dve", s_, s_.t[:, 12:16], s_, s_.t[:, 12:16], 1.0 / 128, SUBLN_EPS, ALU.mult, ALU.add)
        kb.act(s_, s_.t[:, 16:20], s_, s_.t[:, 12:16], AF.Ln)
        kb.act(s_, s_.t[:, 20:24], s_, s_.t[:, 16:20], AF.Exp, scale=-0.5)
        kb.tt("pool", a0, a0.t[:], a0, a0.t[:], s_, s_.t[:, 20:24].unsqueeze(2).to_broadcast([128, 4, 128]), ALU.mult)
        kb.tt("dve", a0, a0.t[:], a0, a0.t[:], sg, sgb, ALU.mult)
        y = yh.next()
        kb.tt("pool", y, y.t[:], a0, a0.t[:], sz, sz.t[:], ALU.mult)
        kb.store(Y, Y.t[qc * 512:(qc + 1) * 512, h * 128:(h + 1) * 128].rearrange("(c p) e -> p c e", p=128), y, y.t[:])

    items = [(h, qc) for h in range(8) for qc in range(TPC // 512)]
    cur = q_prep(*items[0])
    for ii, (h, qc) in enumerate(items):
        kT, va = heads[h]
        qT, z = cur
        sz = silu_z(z)
        pts = [emit_qk(kT, qT, 0)]
        for kc in range(NKC):
            if kc + 1 < NKC:
                pts.append(emit_qk(kT, qT, kc + 1))
            emit_pv(pts[kc], va, kc)
        if h + 1 < 8:
            if qc == 0:
                v_load(h + 1, heads[h + 1][1])
            k_prep(h + 1, heads[h + 1][0], qc)
        A0, A1 = A0r.next(), A1r.next()
        for half in range(2):
            kb.copy("dve", A0, A0.t[:, 2 * half:2 * half + 2, :], acc[half], acc[half].t[:, 0:258].rearrange("p (j e) -> p j e", e=129))
            kb.copy("dve", A1, A1.t[:, 2 * half:2 * half + 2, :], acc[2 + half], acc[2 + half].t[:, 0:258].rearrange("p (j e) -> p j e", e=129))
        if ii + 1 < len(items):
            cur = q_prep(*items[ii + 1])
        epilogue(h, qc, A0, A1, sz)
        P.maybe_epoch()
    P.barrier()
    pcb = Rot(list(kb.acc))
    prev_back = None
    for i in range(NT):
        yt = kb.b1p.next()
        kb.load(yt, yt.t[:], Y, Y.t[i * 128:(i + 1) * 128, :], q="sp")
        bk = kb.phase_c_tile(yt, X3, i * 128, OUT, i * 128, wout, final_g=fg, defer=True, banks=pcb)
        if prev_back is not None:
            prev_back()
        prev_back = bk
        P.maybe_epoch()
    prev_back()
    P.barrier()


STAGES = [stage0, stage1, stage2, stage3]
_NC_CACHE = {}


def build_program(stages=(0, 1, 2, 3)):
    key = tuple(stages)
    if key in _NC_CACHE:
        return _NC_CACHE[key]
    nc = bass.Bass("TRN2", target_bir_lowering=False)
    shared = {"dram": {}}
    with contextlib.ExitStack() as st:
        P = Prog(nc, st)
        for si in stages:
            with contextlib.ExitStack() as sst:
                P.cur_stack = sst
                kb = KB(nc, P, shared, nf32=8 if si == 0 else 5, layout="l3" if si == 3 else "std", nbf={0: 8, 3: 1}.get(si, 4))
                kb.setup_consts()
                STAGES[si](kb)
                P.barrier()
                P.emit()
    _NC_CACHE[key] = (nc, shared)
    return _NC_CACHE[key]


def _pp(v, n):
    return np.ascontiguousarray(np.asarray(v, np.float32).reshape(n, 128).T)


def _bc(v):
    v = np.asarray(v, np.float32)
    return np.ascontiguousarray(np.broadcast_to(v[None], (128,) + v.shape))


def _rope_table(dim):
    inv = (10000.0 ** (-(np.arange(0, dim, 2, dtype=np.float32) / np.float32(dim)))).astype(np.float32)
    ang = np.arange(S, dtype=np.float32)[:, None] * inv[None, :]
    cos, sin = np.cos(ang).astype(np.float32), np.sin(ang).astype(np.float32)
    nh = 512 // dim
    cosF = np.tile(np.concatenate([cos, cos], axis=1), (1, nh))
    sinS = np.tile(np.concatenate([-sin, sin], axis=1), (1, nh))
    return np.ascontiguousarray(np.stack([cosF, sinS], axis=1))


def _ext(rows, lo, hi, total):
    out = np.zeros((hi - lo,) + rows.shape[1:], rows.dtype)
    a, b = max(lo, 0), min(hi, total)
    out[a - lo:b - lo] = rows[a:b]
    return out


def kernel(x, c, norm_g, w_mod, b_mod, conv_w_in, conv_k, conv_w_out, dil_w_in, dil_w_out,
           swa_w_in, swa_sink, swa_w_out, diff_w_in, diff_lambda, diff_subln_g, diff_w_out, final_g,
           _stages=(0, 1, 2, 3), _debug_outs=()):
    f = lambda a: np.ascontiguousarray(np.asarray(a, np.float32))
    x, c = f(x), f(c)
    bf = ml_dtypes.bfloat16
    masks = np.zeros((128, 2, 128), np.float32)
    kk, qq = np.meshgrid(np.arange(128), np.arange(128), indexing="ij")
    masks[:, 0, :] = (kk >= qq)
    masks[:, 1, :] = (kk <= qq)
    masks = masks.astype(bf)
    cs128, cs64 = _rope_table(128), _rope_table(64)
    nc, shared = build_program(_stages)
    need = set(shared["dram"].keys())

    def valid(lo, hi):
        t = np.arange(lo, hi)
        return np.ascontiguousarray(((t >= 0) & (t < S)).astype(np.float32)[:, None])

    shared_in = dict(ident=np.eye(128, dtype=np.float32), masks=masks,
                     w_in0=f(conv_w_in[0]), w_out0=f(conv_w_out[0]), w_in1=f(dil_w_in[0]), w_out1=f(dil_w_out[0]),
                     w_in2=f(swa_w_in[0]), w_out2=f(swa_w_out[0]), w_in3=f(diff_w_in[0]), w_out3=f(diff_w_out[0]),
                     convk_bc=_bc(f(conv_k[0])), sink_bc=_bc(f(swa_sink[0])), lam_bc=_bc(f(diff_lambda[0])),
                     subln_bc=_bc(f(diff_subln_g[0])), finalg_bc=_bc(f(final_g)))
    for l in range(4):
        shared_in[f"wmod{l}"] = f(w_mod[l])
        shared_in[f"bmodT{l}"] = _pp(b_mod[l], 24)
        shared_in[f"ngT{l}"] = _pp(norm_g[l], 8)
        shared_in[f"bmodg{l}"] = f(b_mod[l][2048:3072]).reshape(1, D)
    cores = [(b, h) for b in range(NB) for h in range(2)]
    maps = []
    for (b, h) in cores:
        t0 = h * TPC
        d = dict(shared_in)
        d["ccol"] = _pp(c[b], 8)
        d["xe"] = _ext(x[b], t0 - 128, t0 + TPC + 128, S)
        vm = np.ones((128, 2), np.float32)
        if h == 0:
            vm[0, 0] = 0.0
        if h == 1:
            vm[127, 1] = 0.0
        d["vmask"] = vm
        d["cs1"] = np.ascontiguousarray(cs128[t0:t0 + TPC])
        d["cs2"] = np.ascontiguousarray(cs64[t0:t0 + TPC])
        d["kvalid1"] = valid(t0 - 1024, t0 + TPC + 1024)
        d["kvalid2"] = valid(t0 - 128, t0 + TPC + 128)
        maps.append({k: v for k, v in d.items() if k in need})
    res = run_bass_kernel_spmd(nc, maps, core_ids=list(range(8)))
    if _debug_outs:
        return res.results
    out = np.zeros((NB, S, D), np.float32)
    for ci, (b, h) in enumerate(cores):
        out[b, h * TPC:(h + 1) * TPC] = res.results[ci]["OUT"]
    return out
```

```python
import contextlib
import math
import numpy as np
import ml_dtypes
import concourse.bass as bass
import concourse.mybir as mybir
from concourse.bass_utils import run_bass_kernel_spmd

F32 = mybir.dt.float32
BF16 = mybir.dt.bfloat16
ALU = mybir.AluOpType
AF = mybir.ActivationFunctionType
AX = mybir.AxisListType

D = 1024
S = 8192
NB = 4
TPC = 4096
NT = TPC // 128
ENGS = ("pe", "act", "dve", "pool", "sp")
EPOCH_LIMIT = 20000
NDMASEM = 10
NCCSEM = 8
NORM_EPS = 1e-6
_DEBUG_OUT = set()
SUBLN_EPS = 1e-5


class Buf:
    __slots__ = ("name", "w", "r")

    def __init__(self, name=""):
        self.name = name
        self.w = None
        self.r = []


class T:
    __slots__ = ("t", "b", "name_is_dram", "parts")

    def __init__(self, t, name="", dram=False, buf=None):
        self.t = t
        self.b = buf if buf is not None else Buf(name)
        self.name_is_dram = dram
        self.parts = None

    def part(self, col):
        if self.parts is None:
            self.parts = {}
        k = col // 512
        if k not in self.parts:
            self.parts[k] = T(self.t, "part", buf=Buf())
        return self.parts[k]


class Prog:
    def __init__(self, nc, stack):
        self.nc = nc
        self.stack = stack
        self.q = {e: [] for e in ENGS}
        self.cnt = {e: 0 for e in ENGS}
        self.nsem = 0
        self.esem = {e: self._newsem("c_" + e) for e in ENGS}
        self.waited = {e: {} for e in ENGS}
        self.dsem = {e: [self._newsem(f"d_{e}{i}") for i in range(NDMASEM)] for e in ("sp", "pool", "act")}
        self.dval = {e: [0] * NDMASEM for e in self.dsem}
        self.dnext = {e: 0 for e in self.dsem}
        self.latest = {}
        self.nid = 0
        self.cur_stack = stack
        self.csem = [self._newsem(f"cc{i}") for i in range(NCCSEM)]
        self.cval = [0] * NCCSEM
        self.cnext = 0

    def _newsem(self, name):
        self.nsem += 1
        return self.stack.enter_context(self.nc.semaphore(name + f"_{self.nsem}"))

    def sb(self, shape, dt, name=None):
        self.nid += 1
        name = (name or "t") + f"_{self.nid}"
        return T(self.cur_stack.enter_context(self.nc.sbuf_tensor(name, list(shape), dt)), name)

    def ps(self, shape, dt, name=None):
        self.nid += 1
        name = (name or "p") + f"_{self.nid}"
        return T(self.cur_stack.enter_context(self.nc.psum_tensor(name, list(shape), dt)), name)

    def _deps(self, eng, reads, writes):
        deps = {}

        def add(tok):
            if tok is None:
                return
            s, v = tok
            if eng == "pe" and s is self.esem["pe"]:
                return
            k = id(s)
            if self.waited[eng].get(k, 0) >= v:
                return
            if k not in deps or deps[k][1] < v:
                deps[k] = (s, v)
        for b in reads:
            add(b.w)
        for b in writes:
            add(b.w)
            for t in b.r:
                add(t)
        out = list(deps.values())
        for s, v in out:
            self.waited[eng][id(s)] = v
        return out

    def _commit(self, tok, reads, writes):
        for b in reads:
            b.r = [t for t in b.r if t[0] is not tok[0]] + [tok]
        for b in writes:
            b.w = tok
            b.r = []
        self.latest[id(tok[0])] = tok

    @staticmethod
    def _bufs(xs):
        return [x.b if isinstance(x, T) else x for x in xs]

    def op(self, eng, fn, reads=(), writes=()):
        reads, writes = self._bufs(reads), self._bufs(writes)
        deps = self._deps(eng, reads, writes)
        self.cnt[eng] += 1
        tok = (self.esem[eng], self.cnt[eng])
        self.q[eng].append((deps, fn, tok[0], 1))
        self._commit(tok, reads, writes)
        return tok

    def dma(self, eng, fn, reads=(), writes=()):
        reads, writes = self._bufs(reads), self._bufs(writes)
        i = self.dnext[eng]
        self.dnext[eng] = (i + 1) % NDMASEM
        s = self.dsem[eng][i]
        deps = self._deps(eng, reads, writes)
        prev = self.dval[eng][i]
        if prev > 0 and self.waited[eng].get(id(s), 0) < prev:
            deps.append((s, prev))
            self.waited[eng][id(s)] = prev
        self.dval[eng][i] = prev + 16
        tok = (s, prev + 16)
        self.q[eng].append((deps, fn, s, 16))
        self._commit(tok, reads, writes)
        return tok

    def cc(self, fn, reads=(), writes=()):
        i = self.cnext
        self.cnext = (i + 1) % NCCSEM
        s = self.csem[i]
        deps = self._deps("pool", list(reads), list(writes))
        prev = self.cval[i]
        if prev > 0 and self.waited["pool"].get(id(s), 0) < prev:
            deps.append((s, prev))
            self.waited["pool"][id(s)] = prev
        self.cval[i] = prev + 1
        tok = (s, prev + 1)
        self.q["pool"].append((deps, fn, s, 1))
        self._commit(tok, list(reads), list(writes))
        return tok

    def barrier(self):
        toks = list(self.latest.values())
        for e in ENGS:
            deps = []
            for s, v in toks:
                if self.waited[e].get(id(s), 0) < v:
                    deps.append((s, v))
                    self.waited[e][id(s)] = v
            if deps:
                self.q[e].append((deps, None, None, 0))
        if max(self.cnt.values()) > EPOCH_LIMIT:
            dm = set(id(s) for ss in self.dsem.values() for s in ss) | set(id(s) for s in self.csem)
            for e in ENGS:
                self.esem[e] = self._newsem("c_" + e)
                self.cnt[e] = 0
            self.latest = {k: t for k, t in self.latest.items() if k in dm}

    def maybe_epoch(self):
        if max(self.cnt.values()) > EPOCH_LIMIT:
            self.barrier()

    def emit(self):
        qs = self.q
        self.q = {e: [] for e in ENGS}
        with self.nc.Block() as block:
            def run(engname):
                def _f(e):
                    for deps, fn, s, inc in qs[engname]:
                        for ds, dv in deps:
                            e.wait_ge(ds, dv)
                        if fn is not None:
                            fn(e).then_inc(s, inc)
                return _f
            block.tensor(run("pe"))
            block.scalar(run("act"))
            block.vector(run("dve"))
            block.gpsimd(run("pool"))
            block.sync(run("sp"))


class Rot:
    def __init__(self, items):
        self.items = items
        self.i = 0

    def next(self):
        x = self.items[self.i % len(self.items)]
        self.i += 1
        return x


class KB:
    def __init__(self, nc, P, shared, nf32=8, layout="std", nbf=4):
        self.nc = nc
        self.P = P
        self.dram = shared["dram"]
        self.shared = shared
        if layout == "std":
            self.ptr = Rot([P.ps([128, 8, 128], BF16, "ptr") for _ in range(2)])
            self.pg = [P.ps([128, 512], F32, "pg") for _ in range(3)]
            self.po = [P.ps([128, 512], F32, "po") for _ in range(3)]
        else:
            self.sc = [P.ps([128, 2, 512], F32, "sc") for _ in range(2)]
            self.acc = [P.ps([128, 512], F32, "acc") for _ in range(4)]
            self.ptr = Rot([T(x.t[:, 0, :].bitcast(BF16).rearrange("p (c q) -> p c q", q=128), "ptrv", buf=x.b) for x in self.sc])
            self.pg = [T(self.sc[0].t[:, 0, :], "pgv", buf=self.sc[0].b), T(self.sc[1].t[:, 0, :], "pgv", buf=self.sc[1].b),
                       T(self.sc[0].t[:, 1, :], "pgv", buf=self.sc[0].b)]
            self.po = None
        self.pgr = Rot(self.pg[0:2])
        self.dq = Rot(["sp", "act"])
        self.f32p = Rot([P.sb([128, D], F32, "f32p") for _ in range(nf32)])
        self.bfp = Rot([P.sb([128, 2048], BF16, "bfp") for _ in range(nbf)])
        self.b1p = Rot([P.sb([128, D], BF16, "b1p") for _ in range(4)])
        self.t3p = Rot([P.sb([128, 8, 128], BF16, "t3p") for _ in range(4)])
        self.ssp = Rot([P.sb([128, 4], F32, "ssp") for _ in range(4)])
        self.w_stage = Rot([P.sb([128, 8, 256], F32, "wst") for _ in range(2)])
        self.WCH = 256
        self.wm_stage = self.w_stage

    def din(self, name, shape, dt=F32):
        if name not in self.dram:
            self.dram[name] = T(self.nc.dram_tensor(name, list(shape), dt, kind="ExternalInput").ap(), name, True)
        return self.dram[name]

    def dout(self, name, shape, dt=F32):
        if name not in self.dram:
            self.dram[name] = T(self.nc.dram_tensor(name, list(shape), dt, kind="ExternalOutput").ap(), name, True)
        return self.dram[name]

    def dint(self, name, shape, dt=F32):
        if name in _DEBUG_OUT:
            return self.dout(name, shape, dt)
        if name not in self.dram:
            self.dram[name] = T(self.nc.dram_tensor(name, list(shape), dt).ap(), name, True)
        return self.dram[name]

    def _untracked(self, x):
        if x is None:
            return Buf()
        return Buf() if (isinstance(x, T) and x.name_is_dram) else x

    def load(self, dst, dst_ap, src, src_ap, q="sp", slow=False):
        src, dst = self._untracked(src), self._untracked(dst)
        if slow:
            self.P.dma(q, lambda e: e.dma_start(out=dst_ap, in_=src_ap, allow_slow_non_contiguous=True), reads=[src], writes=[dst])
        else:
            self.P.dma(q, lambda e: e.dma_start(out=dst_ap, in_=src_ap), reads=[src], writes=[dst])

    def store(self, dst, dst_ap, src, src_ap, q="pool"):
        src, dst = self._untracked(src), self._untracked(dst)
        self.P.dma(q, lambda e: e.dma_start(out=dst_ap, in_=src_ap), reads=[src], writes=[dst])

    def copy(self, eng, dst, dst_ap, src, src_ap):
        if eng == "act":
            self.P.op("act", lambda e: e.activation(out=dst_ap, in_=src_ap, func=AF.Copy), reads=[src], writes=[dst])
        else:
            self.P.op(eng, lambda e: e.tensor_copy(out=dst_ap, in_=src_ap), reads=[src], writes=[dst])

    def act(self, dst, dst_ap, src, src_ap, func, scale=1.0, accum=None, extra_w=()):
        if accum is None:
            self.P.op("act", lambda e: e.activation(out=dst_ap, in_=src_ap, func=func, scale=scale),
                      reads=[src], writes=[dst] + list(extra_w))
        else:
            self.P.op("act", lambda e: e.activation(out=dst_ap, in_=src_ap, func=func, scale=scale, accum_out=accum),
                      reads=[src], writes=[dst] + list(extra_w))

    def tt(self, eng, dst, dst_ap, a, a_ap, b, b_ap, op):
        self.P.op(eng, lambda e: e.tensor_tensor(out=dst_ap, in0=a_ap, in1=b_ap, op=op), reads=[a, b], writes=[dst])

    def ts(self, eng, dst, dst_ap, a, a_ap, s1, s2, op0, op1=None, sreads=()):
        if op1 is None:
            self.P.op(eng, lambda e: e.tensor_scalar(out=dst_ap, in0=a_ap, scalar1=s1, scalar2=None, op0=op0),
                      reads=[a] + list(sreads), writes=[dst])
        else:
            self.P.op(eng, lambda e: e.tensor_scalar(out=dst_ap, in0=a_ap, scalar1=s1, scalar2=s2, op0=op0, op1=op1),
                      reads=[a] + list(sreads), writes=[dst])

    def recip(self, dst, dst_ap, src, src_ap):
        self.P.op("dve", lambda e: e.reciprocal(out=dst_ap, in_=src_ap), reads=[src], writes=[dst])

    def memset(self, eng, dst, dst_ap, val):
        self.P.op(eng, lambda e: e.memset(dst_ap, val), writes=[dst])

    def transposes(self, dst_ps, src, src_aps, ident):
        def fn(e):
            for i, ap in enumerate(src_aps):
                ins = e.transpose(out=dst_ps.t[:, i, :], in_=ap, identity=ident.t[:])
            return ins
        self.P.op("pe", fn, reads=[src, ident], writes=[dst_ps])

    def matmuls(self, dst, specs, reads):
        def fn(e):
            for sp_ in specs:
                o, l, r, st, sp = sp_[:5]
                if len(sp_) > 5 and sp_[5]:
                    ins = e.matmul(o, lhsT=l, rhs=r, start=st, stop=sp, skip_group_check=True)
                else:
                    ins = e.matmul(o, lhsT=l, rhs=r, start=st, stop=sp)
            return ins
        self.P.op("pe", fn, reads=reads, writes=(dst if isinstance(dst, (list, tuple)) else [dst]))

    def setup_consts(self):
        P = self.P
        idf = P.sb([128, 128], F32, "idf")
        self.ident = P.sb([128, 128], BF16, "ident")
        d = self.din("ident", [128, 128])
        self.load(idf, idf.t[:], d, d.t)
        self.copy("dve", self.ident, self.ident.t[:], idf, idf.t[:])
        self.ones_row = P.sb([1, 128], F32, "ones_row")
        self.memset("dve", self.ones_row, self.ones_row.t[:], 1.0)

    def mod_vectors(self, l, need_ab, need_gate):
        P = self.P
        wmod = self.din(f"wmod{l}", [D, 3 * D])
        bT = self.din(f"bmodT{l}", [128, 24])
        res = {}
        if not hasattr(self, "silu_c"):
            cc = self.din("ccol", [128, 8])
            ct = P.sb([128, 8], F32, "ccol")
            self.load(ct, ct.t[:], cc, cc.t)
            self.silu_c = P.sb([128, 8], F32, "siluc")
            self.act(self.silu_c, self.silu_c.t[:], ct, ct.t[:], AF.Silu)
        sc = self.silu_c
        bt = P.sb([128, 24], F32, "bT")
        self.load(bt, bt.t[:], bT, bT.t)
        if need_ab:
            gT = self.din(f"ngT{l}", [128, 8])
            gt = P.sb([128, 8], F32, "gT")
            self.load(gt, gt.t[:], gT, gT.t)
            modT = P.sb([128, 16], F32, "modT")
            pg = self.pg[2]
            for g4 in range(8):
                st = self.wm_stage.next()
                self.load(st, st.t[:], wmod, wmod.t[:, g4 * 256:(g4 + 1) * 256].rearrange("(c p) n -> p c n", p=128),
                          q=self.dq.next())
                specs = []
                for jj in range(2):
                    j = g4 * 2 + jj
                    for k in range(8):
                        specs.append((pg.t[:, j:j + 1], st.t[:, k, jj * 128:(jj + 1) * 128], sc.t[:, k:k + 1], k == 0, k == 7))
                self.matmuls(pg, specs, [st, sc])
            self.tt("dve", modT, modT.t[:], pg, pg.t[:, 0:16], bt, bt.t[:, 0:16], ALU.add)
            A = P.sb([128, 8], F32, "A")
            self.P.op("dve", lambda e: e.scalar_tensor_tensor(out=A.t[:], in0=modT.t[:, 8:16], scalar=1.0, in1=gt.t[:],
                                                               op0=ALU.add, op1=ALU.mult), reads=[modT, gt], writes=[A])
            res["A"] = A
            res["B"] = modT
        if need_gate:
            bg = self.din(f"bmodg{l}", [1, D])
            bgt = P.sb([1, D], F32, "bg")
            self.load(bgt, bgt.t[:], bg, bg.t)
            grow = P.sb([1, D], F32, "grow")
            gbc = P.sb([128, D], F32, "gbc")
            for g2 in range(4):
                st = self.wm_stage.next()
                c0 = 2048 + g2 * 256
                self.load(st, st.t[:], wmod, wmod.t[:, c0:c0 + 256].rearrange("(c p) n -> p c n", p=128), q=self.dq.next())
                pg = self.pg[g2 % 2]
                specs = [(pg.t[0:1, 0:256], sc.t[:, k:k + 1], st.t[:, k, :], k == 0, k == 7) for k in range(8)]
                self.matmuls(pg, specs, [st, sc])
                self.tt("dve", grow, grow.t[:, g2 * 256:(g2 + 1) * 256], pg, pg.t[0:1, 0:256], bgt, bgt.t[:, g2 * 256:(g2 + 1) * 256], ALU.add)
            for g2 in range(2):
                pg = self.pg[g2]
                self.matmuls(pg, [(pg.t[:], self.ones_row.t[:], grow.t[:, g2 * 512:(g2 + 1) * 512], True, True)],
                             [self.ones_row, grow])
                self.copy("act", gbc, gbc.t[:, g2 * 512:(g2 + 1) * 512], pg, pg.t[:])
            res["gate_bc"] = gbc
        return res

    def load_weight(self, wsb, wdram, c0, ncols, scale_bc=None):
        P = self.P
        for g in range(0, ncols, 256):
            n = min(256, ncols - g)
            st = self.w_stage.next()
            self.load(st, st.t[:, :, 0:n], wdram, wdram.t[:, c0 + g:c0 + g + n].rearrange("(c p) n -> p c n", p=128),
                      q=self.dq.next())
            if scale_bc is None:
                self.copy("dve" if (g // 256) % 2 == 0 else "act", wsb.part(g), wsb.t[:, :, g:g + n], st, st.t[:, :, 0:n])
            else:
                for k in range(8):
                    self.tt("dve", wsb.part(g), wsb.t[:, k, g:g + n], st, st.t[:, k, 0:n], scale_bc, scale_bc.t[:, g:g + n], ALU.mult)

    def phase_a(self, ntiles, xsrc, xrow0, pdst, prow0, pcol0, wsb, ncols, A, Bm, cs, csrow0, hd, rope_cols, also=None):
        P = self.P
        if not hasattr(self, "pa_bufs"):
            self.pa_bufs = dict(
                tA=Rot([P.sb([128, 512], F32, "patA") for _ in range(3)]),
                tB=Rot([P.sb([128, 512], F32, "patB") for _ in range(3)]),
                csF=Rot([P.sb([128, 2, 512], F32, "pacsF") for _ in range(3)]),
            )
        bufs = self.pa_bufs
        h2 = hd // 2
        nhf = 512 // hd

        def prep_a(i):
            xt = self.f32p.next()
            self.load(xt, xt.t[:], xsrc, xsrc.t[xrow0 + i * 128: xrow0 + (i + 1) * 128, :], q="sp")
            ss = self.ssp.next()
            sq = self.f32p.next()
            self.act(sq, sq.t[:], xt, xt.t[:], AF.Square, accum=ss.t[:, 0:1], extra_w=[ss])
            self.ts("dve", ss, ss.t[:, 1:2], ss, ss.t[:, 0:1], 1.0 / D, NORM_EPS, ALU.mult, ALU.add)
            self.act(ss, ss.t[:, 2:3], ss, ss.t[:, 1:2], AF.Sqrt)
            self.recip(ss, ss.t[:, 3:4], ss, ss.t[:, 2:3])
            xn = self.b1p.next()
            self.ts("dve", xn, xn.t[:], xt, xt.t[:], ss.t[:, 3:4], None, ALU.mult, sreads=[ss])
            csF = None
            if rope_cols > 0:
                csF = bufs["csF"].next()
                self.load(csF, csF.t[:], cs, cs.t[csrow0 + i * 128: csrow0 + (i + 1) * 128, :, :], q="sp")
            return xn, csF

        def prep_b(pa):
            xn, csF = pa
            ptr = self.ptr.next()
            self.transposes(ptr, xn, [xn.t[:, c * 128:(c + 1) * 128] for c in range(8)], self.ident)
            hT = self.t3p.next()
            for c in range(8):
                if rope_cols == 0 and c % 2 == 1:
                    self.ts("dve", hT, hT.t[:, c, :], ptr, ptr.t[:, c, :], A.t[:, c:c + 1], Bm.t[:, c:c + 1],
                            ALU.mult, ALU.add, sreads=[A, Bm])
                else:
                    self._act_affine(hT, hT.t[:, c, :], ptr, ptr.t[:, c, :], A, A.t[:, c:c + 1], Bm, Bm.t[:, c:c + 1])
            cosF = sinS = None
            if csF is not None:
                cosF = T(csF.t[:, 0, :], "cosF", buf=csF.b)
                sinS = T(csF.t[:, 1, :], "sinS", buf=csF.b)
            return hT, cosF, sinS

        def mm(i, st):
            hT, cosF, sinS = st
            for og in range(0, ncols, 2048):
                on = min(2048, ncols - og)
                ot = self.bfp.next()
                for g in range(og, og + on, 512):
                    pg = self.pgr.next()
                    specs = [(pg.t[:], hT.t[:, k, :], wsb.t[:, k, g:g + 512], k == 0, k == 7) for k in range(8)]
                    self.matmuls(pg, specs, [hT, wsb.part(g)])
                    lo = g - og
                    rc = max(0, min(512, rope_cols - g))
                    if rc > 0:
                        tA, tB = bufs["tA"].next(), bufs["tB"].next()
                        self.tt("dve", tA, tA.t[:, 0:rc], pg, pg.t[:, 0:rc], cosF, cosF.t[:, 0:rc], ALU.mult)
                        q4 = pg.t[:, 0:rc].rearrange("p (h two d) -> p h two d", two=2, d=h2)
                        b4 = tB.t[:, 0:rc].rearrange("p (h two d) -> p h two d", two=2, d=h2)
                        s4 = sinS.t[:, 0:rc].rearrange("p (h two d) -> p h two d", two=2, d=h2)
                        self.tt("dve", tB, b4[:, :, 0, :], pg, q4[:, :, 1, :], sinS, s4[:, :, 0, :], ALU.mult)
                        self.tt("dve", tB, b4[:, :, 1, :], pg, q4[:, :, 0, :], sinS, s4[:, :, 1, :], ALU.mult)
                        self.tt("pool", ot, ot.t[:, lo:lo + rc], tA, tA.t[:, 0:rc], tB, tB.t[:, 0:rc], ALU.add)
                        if rc < 512:
                            self.copy("act", ot, ot.t[:, lo + rc:lo + 512], pg, pg.t[:, rc:512])
                    else:
                        self.copy("act", ot, ot.t[:, lo:lo + 512], pg, pg.t[:])
                self.store(pdst, pdst.t[prow0 + i * 128: prow0 + (i + 1) * 128, pcol0 + og: pcol0 + og + on], ot, ot.t[:, 0:on])
                for (adst, atile, arow, c_lo, c_hi, dcol0) in (also or ()):
                    if atile != i:
                        continue
                    lo_, hi_ = max(c_lo, og), min(c_hi, og + on)
                    if hi_ > lo_:
                        self.store(None, adst.t[arow: arow + 128, dcol0 + lo_ - c_lo: dcol0 + hi_ - c_lo],
                                   ot, ot.t[:, lo_ - og:hi_ - og], q="pool")

        pa = {0: prep_a(0)}
        if ntiles > 1:
            pa[1] = prep_a(1)
        st = prep_b(pa.pop(0))
        for i in range(ntiles):
            if i + 2 < ntiles:
                pa[i + 2] = prep_a(i + 2)
            nxt = prep_b(pa.pop(i + 1)) if i + 1 < ntiles else None
            mm(i, st)
            st = nxt
            self.P.maybe_epoch()

    def _act_affine(self, dst, dst_ap, src, src_ap, a_t, a_ap, b_t, b_ap):
        self.P.op("act", lambda e: e.activation(out=dst_ap, in_=src_ap, func=AF.Identity, bias=b_ap, scale=a_ap),
                  reads=[src, a_t, b_t], writes=[dst])

    def phase_c_tile(self, ytile, xsrc, xrow, xdst, drow, wout, final_g=None, defer=False, banks=None):
        P = self.P
        if not hasattr(self, "pc_bufs"):
            self.pc_bufs = dict(yT=self.t3p, x=self.f32p, xo=self.f32p, ss=self.ssp, sq=self.f32p)
        bufs = self.pc_bufs
        banks = banks or self.pgr
        ptr = self.ptr.next()
        self.transposes(ptr, ytile, [ytile.t[:, c * 128:(c + 1) * 128] for c in range(8)], self.ident)
        yT = bufs["yT"].next()
        self.copy("act", yT, yT.t[:], ptr, ptr.t[:])
        xt = bufs["x"].next()
        self.load(xt, xt.t[:], xsrc, xsrc.t[xrow:xrow + 128, :], q="sp")
        pgs = []
        for half in range(2):
            pg = banks.next()
            specs = [(pg.t[:], yT.t[:, k, :], wout.t[:, k, half * 512:(half + 1) * 512], k == 0, k == 7) for k in range(8)]
            self.matmuls(pg, specs, [yT, wout.part(half * 512)])
            pgs.append(pg)

        def back():
            xo = bufs["xo"].next()
            for half in range(2):
                pg = pgs[half]
                self.tt("dve", xo, xo.t[:, half * 512:(half + 1) * 512], pg, pg.t[:], xt, xt.t[:, half * 512:(half + 1) * 512], ALU.add)
            if final_g is not None:
                ss = bufs["ss"].next()
                sq = bufs["sq"].next()
                self.act(sq, sq.t[:], xo, xo.t[:], AF.Square, accum=ss.t[:, 0:1], extra_w=[ss])
                self.ts("dve", ss, ss.t[:, 1:2], ss, ss.t[:, 0:1], 1.0 / D, NORM_EPS, ALU.mult, ALU.add)
                self.act(ss, ss.t[:, 2:3], ss, ss.t[:, 1:2], AF.Sqrt)
                self.recip(ss, ss.t[:, 3:4], ss, ss.t[:, 2:3])
                self.P.op("dve", lambda e: e.scalar_tensor_tensor(out=xt.t[:], in0=xo.t[:], scalar=ss.t[:, 3:4], in1=final_g.t[:],
                                                                   op0=ALU.mult, op1=ALU.mult), reads=[xo, ss, final_g], writes=[xt])
                self.store(xdst, xdst.t[drow:drow + 128, :], xt, xt.t[:])
            else:
                self.store(xdst, xdst.t[drow:drow + 128, :], xo, xo.t[:])
        if defer:
            return back
        back()
        return None

    def conv_layer(self, p0, convk_bc, vmask, xsrc, xdst, wout):
        P = self.P
        cx = [self.bfp] * 3
        tj = [self.f32p] * 3
        bz, sz, acc, yt = self.bfp, self.f32p, self.f32p, self.b1p
        pcb = Rot([self.pg[0], self.pg[1], self.pg[2], self.po[0]])
        prev_back = None
        for i in range(NT):
            r0 = 128 + i * 128
            ts_ = []
            for j in range(3):
                c = cx[j].next()
                self.load(c, c.t[:], p0, p0.t[r0 + j - 1: r0 + j - 1 + 128, 1024:3072], q=self.dq.next())
                t = tj[j].next()
                self.tt("pool", t, t.t[:], c, c.t[:, 0:1024], c, c.t[:, 1024:2048], ALU.mult)
                if (i == 0 and j == 0):
                    self.ts("pool", t, t.t[:], t, t.t[:], vmask.t[:, 0:1], None, ALU.mult, sreads=[vmask])
                if (i == NT - 1 and j == 2):
                    self.ts("pool", t, t.t[:], t, t.t[:], vmask.t[:, 1:2], None, ALU.mult, sreads=[vmask])
                ts_.append(t)
            b = bz.next()
            self.load(b, b.t[:, 0:1024], p0, p0.t[r0:r0 + 128, 0:1024], q="sp")
            self.load(b, b.t[:, 1024:2048], p0, p0.t[r0:r0 + 128, 3072:4096], q="sp")
            s = sz.next()
            self.act(s, s.t[:], b, b.t[:, 1024:2048], AF.Silu)
            a = acc.next()
            self.tt("dve", a, a.t[:], ts_[0], ts_[0].t[:], convk_bc, convk_bc.t[:, 0, :], ALU.mult)
            self.tt("pool", ts_[1], ts_[1].t[:], ts_[1], ts_[1].t[:], convk_bc, convk_bc.t[:, 1, :], ALU.mult)
            self.tt("dve", ts_[2], ts_[2].t[:], ts_[2], ts_[2].t[:], convk_bc, convk_bc.t[:, 2, :], ALU.mult)
            self.tt("pool", a, a.t[:], a, a.t[:], ts_[1], ts_[1].t[:], ALU.add)
            self.tt("dve", a, a.t[:], a, a.t[:], ts_[2], ts_[2].t[:], ALU.add)
            self.tt("pool", s, s.t[:], s, s.t[:], b, b.t[:, 0:1024], ALU.mult)
            y = yt.next()
            self.tt("dve", y, y.t[:], a, a.t[:], s, s.t[:], ALU.mult)
            bk = self.phase_c_tile(y, xsrc, r0, xdst, i * 128, wout, defer=True, banks=pcb)
            if prev_back is not None:
                prev_back()
            prev_back = bk
            self.P.maybe_epoch()
        prev_back()

    def band_attention(self, *, pq, kv, kvalid, masks, nheads, hd, ev, qcol0, kcol0, vcol0, nkvh, gq,
                       blocks, scale, finish):
        P = self.P
        qcols = nheads * hd
        kcols = nkvh * hd
        nqt = qcols // 128
        dup = (hd == 64)
        nkt = nkvh if dup else kcols // 128
        maxch = max(len(b["chunks"]) for b in blocks)
        hb = max(1, 512 // (128 * maxch))
        key = (qcols, nkt, nkvh, ev, maxch)
        if not hasattr(self, "_ba_cache"):
            self._ba_cache = {}
        if key not in self._ba_cache:
            self._ba_cache[key] = dict(
                qrow=Rot([P.sb([128, qcols], BF16, "ba_q") for _ in range(2)]),
                qT=Rot([P.sb([128, nqt, 128], BF16, "ba_qT") for _ in range(2)]),
                krow=Rot([P.sb([128, nkt * 128], BF16, "ba_k") for _ in range(2 * maxch)]),
                kT=Rot([P.sb([128, nkt, 128], BF16, "ba_kT") for _ in range(2 * maxch)]),
                vaug=Rot([P.sb([128, nkvh, ev + 1], BF16, "ba_v") for _ in range(2 * maxch)]),
                pt=Rot([P.sb([128, hb * maxch, 128], BF16, "ba_pt") for _ in range(3)]),
                kvc=Rot([P.sb([128, 1], F32, "ba_kvc") for _ in range(2 * maxch)]))
            for v_ in self._ba_cache[key]["vaug"].items:
                self.memset("pool", v_, v_.t[:], 1.0)
        c_ = self._ba_cache[key]
        kvcr = c_["kvc"]
        qrow, qT, krow, kT, vaug, pt = c_["qrow"], c_["qT"], c_["krow"], c_["kT"], c_["vaug"], c_["pt"]
        per_bank = 512 // (ev + 1)
        nch = maxch
        assert all(len(b_["chunks"]) == nch for b_ in blocks)
        if "maskb" not in c_:
            c_["maskb"] = P.sb([128, hb * nch, 128], BF16, "ba_maskb")
        maskb = c_["maskb"]
        self.memset("pool", maskb, maskb.t[:], 0.0)
        for ii in range(hb):
            for ci in range(nch):
                mid = blocks[0]["chunks"][ci][2]
                if mid is not None:
                    self.ts("pool", maskb, maskb.t[:, ii * nch + ci, :], masks, masks.t[:, mid, :], 30000.0, -30000.0, ALU.mult, ALU.add)
        oaps = []
        for h in range(nheads):
            bank = self.po[h // per_bank]
            sl = (h % per_bank) * (ev + 1)
            oaps.append((bank, bank.t[:, sl:sl + ev + 1]))

        def prep_loads(blk):
            qr = qrow.next()
            r0, st = blk["qrow0"], blk["qstep"]
            self.load(qr, qr.t[:], pq, pq.t[r0:r0 + 127 * st + 1:st, qcol0:qcol0 + qcols], q="sp")
            vas, kvcs, krs = [], [], []
            for (k0, kst, mid, halo) in blk["chunks"]:
                kr = krow.next()
                if dup:
                    ksrc = kv.t[k0:k0 + 127 * kst + 1:kst, kcol0:kcol0 + kcols].rearrange("p (h d) -> p h d", d=hd)
                    kr4 = kr.t[:].rearrange("p (h two d) -> p h two d", two=2, d=hd)
                    self.load(kr, kr4[:, :, 0, :], kv, ksrc, q="sp")
                    self.load(kr, kr4[:, :, 1, :], kv, ksrc, q="sp")
                else:
                    self.load(kr, kr.t[:], kv, kv.t[k0:k0 + 127 * kst + 1:kst, kcol0:kcol0 + kcols], q="sp")
                va = vaug.next()
                self.load(va, va.t[:, :, 0:ev], kv,
                          kv.t[k0:k0 + 127 * kst + 1:kst, vcol0:vcol0 + nkvh * ev].rearrange("p (h e) -> p h e", e=ev), q="sp")
                if halo:
                    kc_ = kvcr.next()
                    self.load(kc_, kc_.t[:], kvalid, kvalid.t[k0:k0 + 127 * kst + 1:kst, :], q="sp", slow=True)
                    kvcs.append(kc_)
                else:
                    kvcs.append(None)
                krs.append(kr)
                vas.append(va)
            return qr, krs, vas, kvcs

        def prep_tr(ld):
            qr, krs, vas, kvcs = ld
            kts = []
            ptr = self.ptr.next()
            self.transposes(ptr, qr, [qr.t[:, c * 128:(c + 1) * 128] for c in range(nqt)], self.ident)
            qt = qT.next()
            self.copy("dve", qt, qt.t[:], ptr, ptr.t[:, 0:nqt, :])
            for kr in krs:
                ptr = self.ptr.next()
                self.transposes(ptr, kr, [kr.t[:, c * 128:(c + 1) * 128] for c in range(nkt)], self.ident)
                kt = kT.next()
                self.copy("dve", kt, kt.t[:], ptr, ptr.t[:, 0:nkt, :])
                kts.append(kt)
            return qt, kts, vas, kvcs

        def emit_qk(st, hs):
            qt, kts, vas, kvcs = st
            pg = self.pgr.next()
            specs = []
            for ii, h in enumerate(hs):
                kvh = h // gq
                qd = h * hd
                po_ = qd % 128
                for ci in range(nch):
                    col = (ii * nch + ci) * 128
                    specs.append((pg.t[:, col:col + 128], kts[ci].t[po_: po_ + hd, kvh, :], qt.t[po_: po_ + hd, qd // 128, :],
                                  len(specs) == 0, False, True))
            ncol = len(hs) * nch
            specs.append((pg.t[:, 0:ncol * 128], self.ident.t[:], maskb.t[:, 0:ncol, :].rearrange("p c q -> p (c q)"), False, True, True))
            self.matmuls(pg, specs, kts + [qt, maskb, self.ident])
            p = pt.next()
            self.act(p, p.t[:, 0:ncol, :], pg, pg.t[:, 0:ncol * 128].rearrange("p (c q) -> p c q", q=128), AF.Exp, scale=scale)
            for ii, h in enumerate(hs):
                for ci in range(nch):
                    if kvcs[ci] is not None:
                        cc = ii * nch + ci
                        self.ts("dve", p, p.t[:, cc, :], p, p.t[:, cc, :], kvcs[ci].t[:, 0:1], None, ALU.mult, sreads=[kvcs[ci]])
            return p

        def emit_pv(st, hs, p):
            qt, kts, vas, kvcs = st
            banks = []
            per = {}
            for ii, h in enumerate(hs):
                kvh = h // gq
                bank, oap = oaps[h]
                if bank not in banks:
                    banks.append(bank)
                    per[id(bank)] = []
                for ci in range(nch):
                    cc = ii * nch + ci
                    per[id(bank)].append((oap, p.t[:, cc, :], vas[ci].t[:, kvh, :], ci == 0, ci == nch - 1))
            for bank in banks:
                self.matmuls(bank, per[id(bank)], [p] + vas)

        units = [list(range(h0, min(nheads, h0 + hb))) for h0 in range(0, nheads, hb)]
        prep_at = min(len(units) - 1, max(1, len(units) // 2))
        st = prep_tr(prep_loads(blocks[0]))
        deferred = None
        for bi, blk in enumerate(blocks):
            ld = prep_loads(blocks[bi + 1]) if bi + 1 < len(blocks) else None
            nxt = None
            ps_ = [emit_qk(st, units[0])]
            for ui, hs in enumerate(units):
                if ui + 1 < len(units):
                    ps_.append(emit_qk(st, units[ui + 1]))
                emit_pv(st, hs, ps_[ui])
                if ui == prep_at - 1 and ld is not None:
                    nxt = prep_tr(ld)
            if nxt is None and ld is not None:
                nxt = prep_tr(ld)
            d_ = finish(bi, blk, oaps)
            if deferred is not None:
                deferred()
            deferred = d_
            st = nxt
            self.P.maybe_epoch()
        if deferred is not None:
            deferred()


def _group(xs, n):
    return [xs[i:i + n] for i in range(0, len(xs), n)]


def d2d(kb, dst, dst_ap, src, src_ap, q="sp"):
    kb.load(dst, dst_ap, src, src_ap, q=q)


CC_MAX_BYTES = 2 << 20


def exchange_alloc(kb, name, pieces, cols):
    rmax = max(1, CC_MAX_BYTES // (cols * 2))
    jobs = []
    ci = 0
    for pi, (rows, d0, d1) in enumerate(pieces):
        for r0 in range(0, rows, rmax):
            n = min(rmax, rows - r0)
            sb_ = kb.dint(f"{name}_s{ci}", [n, cols], BF16)
            db_ = kb.dint(f"{name}_d{ci}", [2 * n, cols], BF16)
            sb_.name_is_dram = False
            db_.name_is_dram = False
            ci += 1
            jobs.append(dict(src=sb_, dst=db_, n=n, r0=r0, d0=d0, d1=d1, piece=pi))
    return jobs


def exchange_also(jobs, piece, tile0, c_lo, c_hi, dcol0):
    out = []
    for jb in jobs:
        if jb["piece"] != piece:
            continue
        for t in range(jb["n"] // 128):
            out.append((jb["src"], tile0 + jb["r0"] // 128 + t, t * 128, c_lo, c_hi, dcol0))
    return out


def exchange_cc(kb, jobs):
    groups = [[0, 1], [2, 3], [4, 5], [6, 7]]
    for jb in jobs:
        sb_, db_ = jb["src"], jb["dst"]
        kb.P.cc((lambda sb_=sb_, db_=db_: (lambda e: e.collective_compute(
            "AllGather", ALU.bypass, replica_groups=groups, ins=[sb_.t.opt()], outs=[db_.t.opt()])))(),
            reads=[sb_.b], writes=[db_.b])


def exchange_finish(kb, jobs):
    for jb in jobs:
        db_, n, r0 = jb["dst"], jb["n"], jb["r0"]
        if jb["d0"] is not None:
            d2d(kb, None, jb["d0"][r0:r0 + n, :], db_, db_.t[0:n, :], q=kb.dq.next())
        if jb["d1"] is not None:
            d2d(kb, None, jb["d1"][r0:r0 + n, :], db_, db_.t[n:2 * n, :], q=kb.dq.next())


def stage0(kb):
    P = kb.P
    xe = kb.din("xe", [TPC + 256, D])
    m0 = kb.mod_vectors(0, True, True)
    m1 = kb.mod_vectors(1, True, False)
    w_in0 = kb.din("w_in0", [D, 4096])
    w_out0 = kb.din("w_out0", [D, D])
    w_in1 = kb.din("w_in1", [D, 8192])
    convk = kb.din("convk_bc", [128, 3, D])
    vmask = kb.din("vmask", [128, 2])
    cs1 = kb.din("cs1", [TPC, 2, 512])
    X1 = kb.dint("X1", [TPC, D])
    P1 = kb.dint("P1", [TPC, 8192], BF16)
    p0 = kb.dint("p0", [TPC + 256, 4096], BF16)
    wsb = P.sb([128, 8, 2048], BF16, "wsb")
    wout = P.sb([128, 8, D], BF16, "wout")
    ck = P.sb([128, 3, D], F32, "convk")
    vm = P.sb([128, 2], F32, "vmask")
    kb.load(ck, ck.t[:], convk, convk.t)
    kb.load(vm, vm.t[:], vmask, vmask.t)
    kb.load_weight(wout, w_out0, 0, D, scale_bc=m0["gate_bc"])
    for ps_ in range(2):
        kb.load_weight(wsb, w_in0, ps_ * 2048, 2048)
        kb.phase_a(NT + 2, xe, 0, p0, 0, ps_ * 2048, wsb, 2048, m0["A"], m0["B"], None, 0, 128, 0)
    P.barrier()
    kb.conv_layer(p0, ck, vm, xe, X1, wout)
    P.barrier()
    HALO = 1024
    kv = kb.dint("kv1", [TPC + 2 * HALO, 4096], BF16)
    jobs = exchange_alloc(kb, "ex1", [
        (HALO, kv.t[0:HALO, :], None),
        (HALO, None, kv.t[HALO + TPC:, :]),
    ], 4096)
    for (c0, nc_, rc_) in ((3072, 2048, 2048), (5120, 2048, 1024)):
        kb.load_weight(wsb, w_in1, c0, nc_)
        also = [(kv, i, HALO + i * 128, 0, nc_, c0 - 3072) for i in range(NT)]
        also += exchange_also(jobs, 0, NT - HALO // 128, 0, nc_, c0 - 3072)
        also += exchange_also(jobs, 1, 0, 0, nc_, c0 - 3072)
        kb.phase_a(NT, X1, 0, P1, 0, c0, wsb, nc_, m1["A"], m1["B"], cs1, 0, 128, rc_, also=also)
    P.barrier()
    exchange_cc(kb, jobs)
    for (c0, nc_, rc_) in ((0, 2048, 2048), (2048, 1024, 1024), (7168, 1024, 0)):
        kb.load_weight(wsb, w_in1, c0, nc_)
        kb.phase_a(NT, X1, 0, P1, 0, c0, wsb, nc_, m1["A"], m1["B"], cs1, 0, 128, rc_)
    exchange_finish(kb, jobs)
    P.barrier()


def stage1(kb):
    P = kb.P
    HALO = 1024
    X1 = kb.dram["X1"]
    P1 = kb.dram["P1"]
    kv = kb.dram["kv1"]
    kvalid = kb.din("kvalid1", [TPC + 2 * HALO, 1])
    masksd = kb.din("masks", [128, 2, 128], BF16)
    m1 = kb.mod_vectors(1, False, True)
    m2 = kb.mod_vectors(2, True, False)
    w_out1 = kb.din("w_out1", [D, D])
    w_in2 = kb.din("w_in2", [D, 2560])
    cs2 = kb.din("cs2", [TPC, 2, 512])
    X2 = kb.dint("X2", [TPC, D])
    P2 = kb.dint("P2", [TPC, 2560], BF16)
    og = [kb.dint(f"og{g}", [TPC, 8 * 129]) for g in range(3)]
    wout = P.sb([128, 8, D], BF16, "wout")
    masks = P.sb([128, 2, 128], BF16, "masks")
    kb.load(masks, masks.t[:], masksd, masksd.t)
    kb.load_weight(wout, w_out1, 0, D, scale_bc=m1["gate_bc"])
    P.barrier()
    osb = Rot([P.sb([128, 8 * 129], F32, "osb") for _ in range(2)])
    for g, r in enumerate((1, 4, 16)):
        blocks = []
        for p in range(r):
            for j in range(TPC // (128 * r)):
                i0 = 128 * j
                q0 = p + r * i0
                ka = HALO + p + r * (i0 - 64)
                kb_ = HALO + p + r * (i0 + 64)
                ha = ka < HALO
                hb_ = kb_ + 127 * r >= HALO + TPC
                blocks.append(dict(qrow0=q0, qstep=r, chunks=[(ka, r, 0, ha), (kb_, r, 1, hb_)]))

        def finish(bi, blk, oaps, g=g, r=r):
            o = osb.next()
            for bnk in range(3):
                n = min(3, 8 - 3 * bnk) * 129
                kb.copy("act" if bnk == 1 else "dve", o, o.t[:, bnk * 387: bnk * 387 + n], kb.po[bnk], kb.po[bnk].t[:, 0:n])
            r0 = blk["qrow0"]
            kb.store(og[g], og[g].t[r0:r0 + 127 * r + 1:r, :], o, o.t[:])
        kb.band_attention(pq=P1, kv=kv, kvalid=kvalid, masks=masks, nheads=8, hd=128, ev=128,
                          qcol0=g * 1024, kcol0=g * 1024, vcol0=3072, nkvh=8, gq=1,
                          blocks=blocks, scale=128 ** -0.5, finish=finish)
    P.barrier()
    zt = Rot([P.sb([128, D], BF16, "zt") for _ in range(2)])
    rl = Rot([P.sb([128, 8], F32, "rl") for _ in range(2)])
    o3 = Rot(osb.items + [P.sb([128, 8 * 129], F32, "o3")])
    pcb = Rot([kb.pg[0], kb.pg[1], kb.pg[2], kb.po[0]])
    prev_back = None
    for i in range(NT):
        os_ = []
        for g in range(3):
            o = o3.next()
            kb.load(o, o.t[:], og[g], og[g].t[i * 128:(i + 1) * 128, :], q=kb.dq.next())
            os_.append(o)
        z = zt.next()
        kb.load(z, z.t[:], P1, P1.t[i * 128:(i + 1) * 128, 7168:8192], q="sp")
        kb.tt("pool", os_[0], os_[0].t[:], os_[0], os_[0].t[:], os_[1], os_[1].t[:], ALU.add)
        kb.tt("dve", os_[0], os_[0].t[:], os_[0], os_[0].t[:], os_[2], os_[2].t[:], ALU.add)
        o4 = os_[0].t[:].rearrange("p (h e) -> p h e", e=129)
        rr = rl.next()
        kb.recip(rr, rr.t[:], os_[0], o4[:, :, 128])
        sz = kb.f32p.next()
        kb.act(sz, sz.t[:], z, z.t[:], AF.Silu)
        on = kb.f32p.next()
        kb.tt("dve", on, on.t[:].rearrange("p (h e) -> p h e", e=128), os_[0], o4[:, :, 0:128],
              rr, rr.t[:].unsqueeze(2).to_broadcast([128, 8, 128]), ALU.mult)
        y = kb.b1p.next()
        kb.tt("pool", y, y.t[:], on, on.t[:], sz, sz.t[:], ALU.mult)
        bk = kb.phase_c_tile(y, X1, i * 128, X2, i * 128, wout, defer=True, banks=pcb)
        if prev_back is not None:
            prev_back()
        prev_back = bk
        P.maybe_epoch()
    prev_back()
    P.barrier()
    wsb = P.sb([128, 8, 2048], BF16, "wsb")
    kb.load_weight(wsb, w_in2, 0, 2048)
    H2 = 128
    kv2 = kb.dint("kv2", [TPC + 2 * H2, 512], BF16)
    jobs = exchange_alloc(kb, "ex2", [(H2, kv2.t[0:H2, :], None), (H2, None, kv2.t[H2 + TPC:, :])], 512)
    also = [(kv2, i, H2 + i * 128, 1024, 1536, 0) for i in range(NT)]
    also += exchange_also(jobs, 0, NT - 1, 1024, 1536, 0)
    also += exchange_also(jobs, 1, 0, 1024, 1536, 0)
    kb.phase_a(NT, X2, 0, P2, 0, 0, wsb, 2048, m2["A"], m2["B"], cs2, 0, 64, 1280, also=also)
    P.barrier()
    exchange_cc(kb, jobs)
    kb.load_weight(wsb, w_in2, 2048, 512)
    kb.phase_a(NT, X2, 0, P2, 0, 2048, wsb, 512, m2["A"], m2["B"], cs2, 0, 64, 0)
    exchange_finish(kb, jobs)
    P.barrier()


def stage2(kb):
    P = kb.P
    HALO = 128
    X2 = kb.dram["X2"]
    P2 = kb.dram["P2"]
    kv = kb.dram["kv2"]
    kvalid = kb.din("kvalid2", [TPC + 2 * HALO, 1])
    masksd = kb.din("masks", [128, 2, 128], BF16)
    sinkd = kb.din("sink_bc", [128, 16])
    m2 = kb.mod_vectors(2, False, True)
    m3 = kb.mod_vectors(3, True, False)
    w_out2 = kb.din("w_out2", [D, D])
    w_in3 = kb.din("w_in3", [D, 4096])
    cs3 = kb.din("cs2", [TPC, 2, 512])
    X3 = kb.dint("X3", [TPC, D])
    P3 = kb.dint("P3", [TPC, 4096], BF16)
    wout = P.sb([128, 8, D], BF16, "wout")
    masks = P.sb([128, 2, 128], BF16, "masks")
    kb.load(masks, masks.t[:], masksd, masksd.t)
    esink = P.sb([128, 16], F32, "esink")
    kb.load(esink, esink.t[:], sinkd, sinkd.t)
    kb.act(esink, esink.t[:], esink, esink.t[:], AF.Exp)
    kb.load_weight(wout, w_out2, 0, D, scale_bc=m2["gate_bc"])
    P.barrier()
    blocks = []
    for j in range(NT):
        q0 = 128 * j
        blocks.append(dict(qrow0=q0, qstep=1, chunks=[(HALO + q0 - 128, 1, 0, j == 0), (HALO + q0, 1, None, False),
                                                      (HALO + q0 + 128, 1, 1, j == NT - 1)]))
    zt = Rot([P.sb([128, D], BF16, "zt") for _ in range(2)])
    lt = Rot([P.sb([128, 16], F32, "lt") for _ in range(2)])

    oev = Rot([P.sb([128, 16 * 65], F32, "oev") for _ in range(2)])

    def finish(bi, blk, oaps):
        r0 = blk["qrow0"]
        oe = oev.next()
        for bnk in range(3):
            nh = min(7, 16 - 7 * bnk)
            kb.copy("act" if bnk == 1 else "dve", oe, oe.t[:, 7 * bnk * 65:(7 * bnk + nh) * 65], kb.po[bnk], kb.po[bnk].t[:, 0:nh * 65])

        def rest():
            z = zt.next()
            kb.load(z, z.t[:], P2, P2.t[r0:r0 + 128, 1536:2560], q="sp")
            sz = kb.f32p.next()
            kb.act(sz, sz.t[:], z, z.t[:], AF.Silu)
            l = lt.next()
            on = kb.f32p.next()
            o3_ = oe.t[:].rearrange("p (h e) -> p h e", e=65)
            kb.tt("dve", l, l.t[:], oe, o3_[:, :, 64], esink, esink.t[:], ALU.add)
            kb.recip(l, l.t[:], l, l.t[:])
            kb.tt("dve", on, on.t[:].rearrange("p (h e) -> p h e", e=64), oe, o3_[:, :, 0:64],
                  l, l.t[:].unsqueeze(2).to_broadcast([128, 16, 64]), ALU.mult)
            y = kb.b1p.next()
            kb.tt("pool", y, y.t[:], on, on.t[:], sz, sz.t[:], ALU.mult)
            kb.phase_c_tile(y, X2, r0, X3, r0, wout)
        return rest

    kb.band_attention(pq=P2, kv=kv, kvalid=kvalid, masks=masks, nheads=16, hd=64, ev=64,
                      qcol0=0, kcol0=0, vcol0=256, nkvh=4, gq=4, blocks=blocks, scale=64 ** -0.5, finish=finish)
    P.barrier()
    wsb = P.sb([128, 8, 2048], BF16, "wsb")
    kb.load_weight(wsb, w_in3, 1024, 2048)
    jobs = exchange_alloc(kb, "ex3", [(TPC, None, None)], 2048)
    kb.shared["ex3_jobs"] = jobs
    kb.phase_a(NT, X3, 0, P3, 0, 1024, wsb, 2048, m3["A"], m3["B"], cs3, 0, 64, 1024,
               also=exchange_also(jobs, 0, 0, 0, 2048, 0))
    P.barrier()
    exchange_cc(kb, jobs)
    kb.load_weight(wsb, w_in3, 0, 1024)
    kb.phase_a(NT, X3, 0, P3, 0, 0, wsb, 1024, m3["A"], m3["B"], cs3, 0, 64, 1024)
    kb.load_weight(wsb, w_in3, 3072, 1024)
    kb.phase_a(NT, X3, 0, P3, 0, 3072, wsb, 1024, m3["A"], m3["B"], cs3, 0, 64, 0)
    exchange_finish(kb, jobs)
    P.barrier()


def stage3(kb):
    P = kb.P
    X3 = kb.dram["X3"]
    P3 = kb.dram["P3"]
    ex3 = kb.shared["ex3_jobs"]
    lamd = kb.din("lam_bc", [128, 4, 64])
    sgd = kb.din("subln_bc", [128, 128])
    fgd = kb.din("finalg_bc", [128, D])
    m3 = kb.mod_vectors(3, False, True)
    w_out3 = kb.din("w_out3", [D, D])
    OUT = kb.dout("OUT", [TPC, D])
    Y = kb.dint("Y3", [TPC, D], BF16)
    wout = P.sb([128, 8, D], BF16, "wout")
    kb.load_weight(wout, w_out3, 0, D, scale_bc=m3["gate_bc"])
    lam_init = 0.8 - 0.6 * math.exp(-0.3 * 3)
    lv = P.sb([128, 4, 64], F32, "lv")
    kb.load(lv, lv.t[:], lamd, lamd.t)
    lp = P.sb([128, 2, 64], F32, "lp")
    ls = P.sb([128, 4], F32, "ls")
    kb.tt("dve", lp, lp.t[:, 0, :], lv, lv.t[:, 0, :], lv, lv.t[:, 1, :], ALU.mult)
    kb.tt("dve", lp, lp.t[:, 1, :], lv, lv.t[:, 2, :], lv, lv.t[:, 3, :], ALU.mult)
    P.op("dve", lambda e: e.reduce_sum(out=ls.t[:, 0:2], in_=lp.t[:], axis=AX.X), reads=[lp.b], writes=[ls.b])
    kb.act(ls, ls.t[:, 0:2], ls, ls.t[:, 0:2], AF.Exp)
    kb.tt("dve", ls, ls.t[:, 2:3], ls, ls.t[:, 0:1], ls, ls.t[:, 1:2], ALU.subtract)
    kb.ts("dve", ls, ls.t[:, 3:4], ls, ls.t[:, 2:3], lam_init, -1.0, ALU.add, ALU.mult)
    sg = P.sb([128, 128], F32, "sg")
    kb.load(sg, sg.t[:], sgd, sgd.t)
    kb.ts("dve", sg, sg.t[:], sg, sg.t[:], 1.0 - lam_init, None, ALU.mult)
    fg = P.sb([128, D], F32, "fg")
    kb.load(fg, fg.t[:], fgd, fgd.t)

    NKC = S // 128
    kTh = Rot([P.sb([128, S], BF16, "kTh") for _ in range(2)])
    vah = Rot([P.sb([128, NKC, 129], BF16, "vah") for _ in range(2)])
    for v_ in vah.items:
        kb.memset("pool", v_, v_.t[:], 1.0)
    krow = Rot([P.sb([128, 8, 128], BF16, "krow") for _ in range(2)])
    qrow = Rot([P.sb([128, 4, 128], BF16, "qrow") for _ in range(2)])
    qTt = Rot([P.sb([128, 512], BF16, "qTt") for _ in range(2)])
    ptile = Rot([P.sb([128, 2, 512], BF16, "ptile") for _ in range(3)])
    zt = Rot([P.sb([128, 4, 128], BF16, "zt") for _ in range(2)])
    szt = Rot([P.sb([128, 4, 128], F32, "szt") for _ in range(2)])
    ezt = Rot([P.sb([128, 4, 128], F32, "ezt") for _ in range(2)])
    yh = Rot([P.sb([128, 4, 128], BF16, "yh") for _ in range(2)])
    sm = Rot([P.sb([128, 24], F32, "sm") for _ in range(2)])
    A0r = Rot([P.sb([128, 4, 129], F32, "A0") for _ in range(2)])
    A1r = Rot([P.sb([128, 4, 129], F32, "A1") for _ in range(2)])
    w0 = Rot([P.sb([128, 4, 128], F32, "w0") for _ in range(2)])
    w1 = Rot([P.sb([128, 4, 128], F32, "w1") for _ in range(2)])
    scr = Rot(kb.sc)
    acc = kb.acc
    scale = 64 ** -0.5
    sgb = sg.t[:].unsqueeze(1).to_broadcast([128, 4, 128])

    def emit_qk(kT, qT, kc):
        sc = scr.next()
        specs = [(sc.t[:, comp, :], kT.t[64 * comp:64 * comp + 64, kc * 128:(kc + 1) * 128], qT.t[64 * comp:64 * comp + 64, :], True, True)
                 for comp in range(2)]
        kb.matmuls(sc, specs, [kT, qT])
        pt = ptile.next()
        kb.act(pt, pt.t[:], sc, sc.t[:], AF.Exp, scale=scale)
        return pt

    def emit_pv(pt, va, kc):
        specs = []
        for comp in range(2):
            for j in range(4):
                bank = acc[2 * comp + j // 2]
                specs.append((bank.t[:, (j % 2) * 129:(j % 2) * 129 + 129], pt.t[:, comp, j * 128:(j + 1) * 128],
                              va.t[:, kc, :], kc == 0 and j % 2 == 0, kc == NKC - 1 and j % 2 == 1, True))
        kb.matmuls(list(acc), specs, [pt, va])

    def k_prep(h, kT, c8):
        kr = krow.next()
        r = c8 // 4
        for j in range(2):
            db_ = ex3[2 * (c8 % 4) + j]["dst"]
            kb.load(kr, kr.t[:, 4 * j:4 * j + 4, :], db_,
                    db_.t[r * 512:(r + 1) * 512, h * 128:(h + 1) * 128].rearrange("(c p) e -> p c e", p=128), q="sp")
        ptr = kb.ptr.next()
        kb.transposes(ptr, kr, [kr.t[:, c, :] for c in range(8)], kb.ident)
        kb.copy("dve", kT, kT.t[:, c8 * 1024:(c8 + 1) * 1024], ptr, ptr.t[:].rearrange("p c q -> p (c q)"))

    def v_load(h, va):
        for r in range(2):
            for c in range(8):
                db_ = ex3[c]["dst"]
                b0 = r * 32 + c * 4
                kb.load(va, va.t[:, b0:b0 + 4, 0:128], db_,
                        db_.t[r * 512:(r + 1) * 512, 1024 + h * 128: 1024 + (h + 1) * 128].rearrange("(c p) e -> p c e", p=128), q="sp")

    heads = [(kTh.next(), vah.next()) for _ in range(8)]
    v_load(0, heads[0][1])
    for c8 in range(NKC // 8):
        k_prep(0, heads[0][0], c8)
    def q_prep(h, qc):
        qr = qrow.next()
        kb.load(qr, qr.t[:], P3, P3.t[qc * 512:(qc + 1) * 512, h * 128:(h + 1) * 128].rearrange("(c p) e -> p c e", p=128), q="sp")
        z = zt.next()
        kb.load(z, z.t[:], P3, P3.t[qc * 512:(qc + 1) * 512, 3072 + h * 128: 3072 + (h + 1) * 128].rearrange("(c p) e -> p c e", p=128), q="sp")
        ptr = kb.ptr.next()
        kb.transposes(ptr, qr, [qr.t[:, c, :] for c in range(4)], kb.ident)
        qT = qTt.next()
        kb.copy("dve", qT, qT.t[:], ptr, ptr.t[:, 0:4, :].rearrange("p c q -> p (c q)"))
        return qT, z

    def silu_z(z):
        ez = ezt.next()
        kb.act(ez, ez.t[:], z, z.t[:], AF.Exp, scale=-1.0)
        kb.ts("dve", ez, ez.t[:], ez, ez.t[:], 1.0, None, ALU.add)
        sz = szt.next()
        kb.recip(sz, sz.t[:], ez, ez.t[:])
        kb.tt("pool", sz, sz.t[:], sz, sz.t[:], z, z.t[:], ALU.mult)
        return sz

    def epilogue(h, qc, A0, A1, sz):
        s_ = sm.next()
        kb.recip(s_, s_.t[:, 0:4], A0, A0.t[:, :, 128])
        kb.recip(s_, s_.t[:, 4:8], A1, A1.t[:, :, 128])
        kb.ts("dve", s_, s_.t[:, 8:12], s_, s_.t[:, 4:8], ls.t[:, 3:4], None, ALU.mult, sreads=[ls])
        a0, a1 = w0.next(), w1.next()
        kb.tt("pool", a0, a0.t[:], A0, A0.t[:, :, 0:128], s_, s_.t[:, 0:4].unsqueeze(2).to_broadcast([128, 4, 128]), ALU.mult)
        kb.tt("dve", a1, a1.t[:], A1, A1.t[:, :, 0:128], s_, s_.t[:, 8:12].unsqueeze(2).to_broadcast([128, 4, 128]), ALU.mult)
        kb.tt("pool", a0, a0.t[:], a0, a0.t[:], a1, a1.t[:], ALU.add)
        kb.tt("dve", a1, a1.t[:], a0, a0.t[:], a0, a0.t[:], ALU.mult)
        P.op("dve", lambda e: e.reduce_sum(out=s_.t[:, 12:16], in_=a1.t[:], axis=AX.X), reads=[a1.b], writes=[s_.b])
        kb.ts("dve", s_, s_.t[:, 12:16], s_, s_.t[:, 12:16], 1.0 / 128, SUBLN_EPS, ALU.mult, ALU.add)
        kb.act(s_, s_.t[:, 16:20], s_, s_.t[:, 12:16], AF.Ln)
        kb.act(s_, s_.t[:, 20:24], s_, s_.t[:, 16:20], AF.Exp, scale=-0.5)
        kb.tt("pool", a0, a0.t[:], a0, a0.t[:], s_, s_.t[:, 20:24].unsqueeze(2).to_broadcast([128, 4, 128]), ALU.mult)
        kb.tt("dve", a0, a0.t[:], a0, a0.t[:], sg, sgb, ALU.mult)
        y = yh.next()
        kb.tt("pool", y, y.t[:], a0, a0.t[:], sz, sz.t[:], ALU.mult)
        kb.store(Y, Y.t[qc * 512:(qc + 1) * 512, h * 128:(h + 1) * 128].rearrange("(c p) e -> p c e", p=128), y, y.t[:])

    items = [(h, qc) for h in range(8) for qc in range(TPC // 512)]
    cur = q_prep(*items[0])
    for ii, (h, qc) in enumerate(items):
        kT, va = heads[h]
        qT, z = cur
        sz = silu_z(z)
        pts = [emit_qk(kT, qT, 0)]
        for kc in range(NKC):
            if kc + 1 < NKC:
                pts.append(emit_qk(kT, qT, kc + 1))
            emit_pv(pts[kc], va, kc)
        if h + 1 < 8:
            if qc == 0:
                v_load(h + 1, heads[h + 1][1])
            k_prep(h + 1, heads[h + 1][0], qc)
        A0, A1 = A0r.next(), A1r.next()
        for half in range(2):
            kb.copy("dve", A0, A0.t[:, 2 * half:2 * half + 2, :], acc[half], acc[half].t[:, 0:258].rearrange("p (j e) -> p j e", e=129))
            kb.copy("dve", A1, A1.t[:, 2 * half:2 * half + 2, :], acc[2 + half], acc[2 + half].t[:, 0:258].rearrange("p (j e) -> p j e", e=129))
        if ii + 1 < len(items):
            cur = q_prep(*items[ii + 1])
        epilogue(h, qc, A0, A1, sz)
        P.maybe_epoch()
    P.barrier()
    pcb = Rot(list(kb.acc))
    prev_back = None
    for i in range(NT):
        yt = kb.b1p.next()
        kb.load(yt, yt.t[:], Y, Y.t[i * 128:(i + 1) * 128, :], q="sp")
        bk = kb.phase_c_tile(yt, X3, i * 128, OUT, i * 128, wout, final_g=fg, defer=True, banks=pcb)
        if prev_back is not None:
            prev_back()
        prev_back = bk
        P.maybe_epoch()
    prev_back()
    P.barrier()


STAGES = [stage0, stage1, stage2, stage3]
_NC_CACHE = {}


def build_program(stages=(0, 1, 2, 3)):
    key = tuple(stages)
    if key in _NC_CACHE:
        return _NC_CACHE[key]
    nc = bass.Bass("TRN2", target_bir_lowering=False)
    shared = {"dram": {}}
    with contextlib.ExitStack() as st:
        P = Prog(nc, st)
        for si in stages:
            with contextlib.ExitStack() as sst:
                P.cur_stack = sst
                kb = KB(nc, P, shared, nf32=8 if si == 0 else 5, layout="l3" if si == 3 else "std", nbf={0: 8, 3: 1}.get(si, 4))
                kb.setup_consts()
                STAGES[si](kb)
                P.barrier()
                P.emit()
    _NC_CACHE[key] = (nc, shared)
    return _NC_CACHE[key]


def _pp(v, n):
    return np.ascontiguousarray(np.asarray(v, np.float32).reshape(n, 128).T)


def _bc(v):
    v = np.asarray(v, np.float32)
    return np.ascontiguousarray(np.broadcast_to(v[None], (128,) + v.shape))


def _rope_table(dim):
    inv = (10000.0 ** (-(np.arange(0, dim, 2, dtype=np.float32) / np.float32(dim)))).astype(np.float32)
    ang = np.arange(S, dtype=np.float32)[:, None] * inv[None, :]
    cos, sin = np.cos(ang).astype(np.float32), np.sin(ang).astype(np.float32)
    nh = 512 // dim
    cosF = np.tile(np.concatenate([cos, cos], axis=1), (1, nh))
    sinS = np.tile(np.concatenate([-sin, sin], axis=1), (1, nh))
    return np.ascontiguousarray(np.stack([cosF, sinS], axis=1))


def _ext(rows, lo, hi, total):
    out = np.zeros((hi - lo,) + rows.shape[1:], rows.dtype)
    a, b = max(lo, 0), min(hi, total)
    out[a - lo:b - lo] = rows[a:b]
    return out


def kernel(x, c, norm_g, w_mod, b_mod, conv_w_in, conv_k, conv_w_out, dil_w_in, dil_w_out,
           swa_w_in, swa_sink, swa_w_out, diff_w_in, diff_lambda, diff_subln_g, diff_w_out, final_g,
           _stages=(0, 1, 2, 3), _debug_outs=()):
    f = lambda a: np.ascontiguousarray(np.asarray(a, np.float32))
    x, c = f(x), f(c)
    bf = ml_dtypes.bfloat16
    masks = np.zeros((128, 2, 128), np.float32)
    kk, qq = np.meshgrid(np.arange(128), np.arange(128), indexing="ij")
    masks[:, 0, :] = (kk >= qq)
    masks[:, 1, :] = (kk <= qq)
    masks = masks.astype(bf)
    cs128, cs64 = _rope_table(128), _rope_table(64)
    nc, shared = build_program(_stages)
    need = set(shared["dram"].keys())

    def valid(lo, hi):
        t = np.arange(lo, hi)
        return np.ascontiguousarray(((t >= 0) & (t < S)).astype(np.float32)[:, None])

    shared_in = dict(ident=np.eye(128, dtype=np.float32), masks=masks,
                     w_in0=f(conv_w_in[0]), w_out0=f(conv_w_out[0]), w_in1=f(dil_w_in[0]), w_out1=f(dil_w_out[0]),
                     w_in2=f(swa_w_in[0]), w_out2=f(swa_w_out[0]), w_in3=f(diff_w_in[0]), w_out3=f(diff_w_out[0]),
                     convk_bc=_bc(f(conv_k[0])), sink_bc=_bc(f(swa_sink[0])), lam_bc=_bc(f(diff_lambda[0])),
                     subln_bc=_bc(f(diff_subln_g[0])), finalg_bc=_bc(f(final_g)))
    for l in range(4):
        shared_in[f"wmod{l}"] = f(w_mod[l])
        shared_in[f"bmodT{l}"] = _pp(b_mod[l], 24)
        shared_in[f"ngT{l}"] = _pp(norm_g[l], 8)
        shared_in[f"bmodg{l}"] = f(b_mod[l][2048:3072]).reshape(1, D)
    cores = [(b, h) for b in range(NB) for h in range(2)]
    maps = []
    for (b, h) in cores:
        t0 = h * TPC
        d = dict(shared_in)
        d["ccol"] = _pp(c[b], 8)
        d["xe"] = _ext(x[b], t0 - 128, t0 + TPC + 128, S)
        vm = np.ones((128, 2), np.float32)
        if h == 0:
            vm[0, 0] = 0.0
        if h == 1:
            vm[127, 1] = 0.0
        d["vmask"] = vm
        d["cs1"] = np.ascontiguousarray(cs128[t0:t0 + TPC])
        d["cs2"] = np.ascontiguousarray(cs64[t0:t0 + TPC])
        d["kvalid1"] = valid(t0 - 1024, t0 + TPC + 1024)
        d["kvalid2"] = valid(t0 - 128, t0 + TPC + 128)
        maps.append({k: v for k, v in d.items() if k in need})
    res = run_bass_kernel_spmd(nc, maps, core_ids=list(range(8)))
    if _debug_outs:
        return res.results
    out = np.zeros((NB, S, D), np.float32)
    for ci, (b, h) in enumerate(cores):
        out[b, h * TPC:(h + 1) * TPC] = res.results[ci]["OUT"]
    return out
```

```python
import contextlib
import math
import numpy as np
import ml_dtypes
import concourse.bass as bass
import concourse.mybir as mybir
from concourse.bass_utils import run_bass_kernel_spmd

F32 = mybir.dt.float32
BF16 = mybir.dt.bfloat16
ALU = mybir.AluOpType
AF = mybir.ActivationFunctionType
AX = mybir.AxisListType

D = 1024
S = 8192
NB = 4
TPC = 4096
NT = TPC // 128
ENGS = ("pe", "act", "dve", "pool", "sp")
EPOCH_LIMIT = 20000
NDMASEM = 16
NCCSEM = 8
NORM_EPS = 1e-6
_DEBUG_OUT = set()
SUBLN_EPS = 1e-5


class Buf:
    __slots__ = ("name", "w", "r")

    def __init__(self, name=""):
        self.name = name
        self.w = None
        self.r = []


class T:
    __slots__ = ("t", "b", "name_is_dram", "parts")

    def __init__(self, t, name="", dram=False, buf=None):
        self.t = t
        self.b = buf if buf is not None else Buf(name)
        self.name_is_dram = dram
        self.parts = None

    def part(self, col):
        if self.parts is None:
            self.parts = {}
        k = col // 512
        if k not in self.parts:
            self.parts[k] = T(self.t, "part", buf=Buf())
        return self.parts[k]


class Prog:
    def __init__(self, nc, stack):
        self.nc = nc
        self.stack = stack
        self.q = {e: [] for e in ENGS}
        self.cnt = {e: 0 for e in ENGS}
        self.nsem = 0
        self.esem = {e: self._newsem("c_" + e) for e in ENGS}
        self.waited = {e: {} for e in ENGS}
        self.dsem = {e: [self._newsem(f"d_{e}{i}") for i in range(NDMASEM)] for e in ("sp", "pool", "act")}
        self.dval = {e: [0] * NDMASEM for e in self.dsem}
        self.dnext = {e: 0 for e in self.dsem}
        self.latest = {}
        self.nid = 0
        self.cur_stack = stack
        self.csem = [self._newsem(f"cc{i}") for i in range(NCCSEM)]
        self.cval = [0] * NCCSEM
        self.cnext = 0

    def _newsem(self, name):
        self.nsem += 1
        return self.stack.enter_context(self.nc.semaphore(name + f"_{self.nsem}"))

    def sb(self, shape, dt, name=None):
        self.nid += 1
        name = (name or "t") + f"_{self.nid}"
        return T(self.cur_stack.enter_context(self.nc.sbuf_tensor(name, list(shape), dt)), name)

    def ps(self, shape, dt, name=None):
        self.nid += 1
        name = (name or "p") + f"_{self.nid}"
        return T(self.cur_stack.enter_context(self.nc.psum_tensor(name, list(shape), dt)), name)

    def _deps(self, eng, reads, writes):
        deps = {}

        def add(tok):
            if tok is None:
                return
            s, v = tok
            if eng == "pe" and s is self.esem["pe"]:
                return
            k = id(s)
            if self.waited[eng].get(k, 0) >= v:
                return
            if k not in deps or deps[k][1] < v:
                deps[k] = (s, v)
        for b in reads:
            add(b.w)
        for b in writes:
            add(b.w)
            for t in b.r:
                add(t)
        out = list(deps.values())
        for s, v in out:
            self.waited[eng][id(s)] = v
        return out

    def _commit(self, tok, reads, writes):
        for b in reads:
            b.r = [t for t in b.r if t[0] is not tok[0]] + [tok]
        for b in writes:
            b.w = tok
            b.r = []
        self.latest[id(tok[0])] = tok

    @staticmethod
    def _bufs(xs):
        return [x.b if isinstance(x, T) else x for x in xs]

    def op(self, eng, fn, reads=(), writes=()):
        reads, writes = self._bufs(reads), self._bufs(writes)
        deps = self._deps(eng, reads, writes)
        self.cnt[eng] += 1
        tok = (self.esem[eng], self.cnt[eng])
        self.q[eng].append((deps, fn, tok[0], 1))
        self._commit(tok, reads, writes)
        return tok

    def dma(self, eng, fn, reads=(), writes=()):
        reads, writes = self._bufs(reads), self._bufs(writes)
        i = self.dnext[eng]
        self.dnext[eng] = (i + 1) % NDMASEM
        s = self.dsem[eng][i]
        deps = self._deps(eng, reads, writes)
        prev = self.dval[eng][i]
        if prev > 0 and self.waited[eng].get(id(s), 0) < prev:
            deps.append((s, prev))
            self.waited[eng][id(s)] = prev
        self.dval[eng][i] = prev + 16
        tok = (s, prev + 16)
        self.q[eng].append((deps, fn, s, 16))
        self._commit(tok, reads, writes)
        return tok

    def cc(self, fn, reads=(), writes=()):
        i = self.cnext
        self.cnext = (i + 1) % NCCSEM
        s = self.csem[i]
        deps = self._deps("pool", list(reads), list(writes))
        prev = self.cval[i]
        if prev > 0 and self.waited["pool"].get(id(s), 0) < prev:
            deps.append((s, prev))
            self.waited["pool"][id(s)] = prev
        self.cval[i] = prev + 1
        tok = (s, prev + 1)
        self.q["pool"].append((deps, fn, s, 1))
        self._commit(tok, list(reads), list(writes))
        return tok

    def barrier(self):
        toks = list(self.latest.values())
        for e in ENGS:
            deps = []
            for s, v in toks:
                if self.waited[e].get(id(s), 0) < v:
                    deps.append((s, v))
                    self.waited[e][id(s)] = v
            if deps:
                self.q[e].append((deps, None, None, 0))
        if max(self.cnt.values()) > EPOCH_LIMIT:
            dm = set(id(s) for ss in self.dsem.values() for s in ss) | set(id(s) for s in self.csem)
            for e in ENGS:
                self.esem[e] = self._newsem("c_" + e)
                self.cnt[e] = 0
            self.latest = {k: t for k, t in self.latest.items() if k in dm}

    def maybe_epoch(self):
        if max(self.cnt.values()) > EPOCH_LIMIT:
            self.barrier()

    def emit(self):
        qs = self.q
        self.q = {e: [] for e in ENGS}
        with self.nc.Block() as block:
            def run(engname):
                def _f(e):
                    for deps, fn, s, inc in qs[engname]:
                        for ds, dv in deps:
                            e.wait_ge(ds, dv)
                        if fn is not None:
                            fn(e).then_inc(s, inc)
                return _f
            block.tensor(run("pe"))
            block.scalar(run("act"))
            block.vector(run("dve"))
            block.gpsimd(run("pool"))
            block.sync(run("sp"))


class Rot:
    def __init__(self, items):
        self.items = items
        self.i = 0

    def next(self):
        x = self.items[self.i % len(self.items)]
        self.i += 1
        return x


class KB:
    def __init__(self, nc, P, shared, nf32=8, layout="std", nbf=4):
        self.nc = nc
        self.P = P
        self.dram = shared["dram"]
        self.shared = shared
        if layout == "std":
            self.ptr = Rot([P.ps([128, 8, 128], BF16, "ptr") for _ in range(2)])
            self.pg = [P.ps([128, 512], F32, "pg") for _ in range(3)]
            self.po = [P.ps([128, 512], F32, "po") for _ in range(3)]
        else:
            self.sc = [P.ps([128, 2, 512], F32, "sc") for _ in range(2)]
            self.acc = [P.ps([128, 512], F32, "acc") for _ in range(4)]
            self.ptr = Rot([T(x.t[:, 0, :].bitcast(BF16).rearrange("p (c q) -> p c q", q=128), "ptrv", buf=x.b) for x in self.sc])
            self.pg = [T(self.sc[0].t[:, 0, :], "pgv", buf=self.sc[0].b), T(self.sc[1].t[:, 0, :], "pgv", buf=self.sc[1].b),
                       T(self.sc[0].t[:, 1, :], "pgv", buf=self.sc[0].b)]
            self.po = None
        self.pgr = Rot(self.pg[0:2])
        self.dq = Rot(["sp", "act"])
        self.f32p = Rot([P.sb([128, D], F32, "f32p") for _ in range(nf32)])
        self.bfp = Rot([P.sb([128, 2048], BF16, "bfp") for _ in range(nbf)])
        self.b1p = Rot([P.sb([128, D], BF16, "b1p") for _ in range(4)])
        self.t3p = Rot([P.sb([128, 8, 128], BF16, "t3p") for _ in range(4)])
        self.ssp = Rot([P.sb([128, 4], F32, "ssp") for _ in range(4)])
        self.w_stage = Rot([P.sb([128, 8, 256], F32, "wst") for _ in range(2)])
        self.WCH = 256
        self.wm_stage = self.w_stage

    def din(self, name, shape, dt=F32):
        if name not in self.dram:
            self.dram[name] = T(self.nc.dram_tensor(name, list(shape), dt, kind="ExternalInput").ap(), name, True)
        return self.dram[name]

    def dout(self, name, shape, dt=F32):
        if name not in self.dram:
            self.dram[name] = T(self.nc.dram_tensor(name, list(shape), dt, kind="ExternalOutput").ap(), name, True)
        return self.dram[name]

    def dint(self, name, shape, dt=F32):
        if name in _DEBUG_OUT:
            return self.dout(name, shape, dt)
        if name not in self.dram:
            self.dram[name] = T(self.nc.dram_tensor(name, list(shape), dt).ap(), name, True)
        return self.dram[name]

    def _untracked(self, x):
        if x is None:
            return Buf()
        return Buf() if (isinstance(x, T) and x.name_is_dram) else x

    def load(self, dst, dst_ap, src, src_ap, q="sp", slow=False):
        src, dst = self._untracked(src), self._untracked(dst)
        if slow:
            self.P.dma(q, lambda e: e.dma_start(out=dst_ap, in_=src_ap, allow_slow_non_contiguous=True), reads=[src], writes=[dst])
        else:
            self.P.dma(q, lambda e: e.dma_start(out=dst_ap, in_=src_ap), reads=[src], writes=[dst])

    def store(self, dst, dst_ap, src, src_ap, q="pool"):
        src, dst = self._untracked(src), self._untracked(dst)
        self.P.dma(q, lambda e: e.dma_start(out=dst_ap, in_=src_ap), reads=[src], writes=[dst])

    def copy(self, eng, dst, dst_ap, src, src_ap):
        if eng == "act":
            self.P.op("act", lambda e: e.activation(out=dst_ap, in_=src_ap, func=AF.Copy), reads=[src], writes=[dst])
        else:
            self.P.op(eng, lambda e: e.tensor_copy(out=dst_ap, in_=src_ap), reads=[src], writes=[dst])

    def act(self, dst, dst_ap, src, src_ap, func, scale=1.0, accum=None, extra_w=()):
        if accum is None:
            self.P.op("act", lambda e: e.activation(out=dst_ap, in_=src_ap, func=func, scale=scale),
                      reads=[src], writes=[dst] + list(extra_w))
        else:
            self.P.op("act", lambda e: e.activation(out=dst_ap, in_=src_ap, func=func, scale=scale, accum_out=accum),
                      reads=[src], writes=[dst] + list(extra_w))

    def tt(self, eng, dst, dst_ap, a, a_ap, b, b_ap, op):
        self.P.op(eng, lambda e: e.tensor_tensor(out=dst_ap, in0=a_ap, in1=b_ap, op=op), reads=[a, b], writes=[dst])

    def ts(self, eng, dst, dst_ap, a, a_ap, s1, s2, op0, op1=None, sreads=()):
        if op1 is None:
            self.P.op(eng, lambda e: e.tensor_scalar(out=dst_ap, in0=a_ap, scalar1=s1, scalar2=None, op0=op0),
                      reads=[a] + list(sreads), writes=[dst])
        else:
            self.P.op(eng, lambda e: e.tensor_scalar(out=dst_ap, in0=a_ap, scalar1=s1, scalar2=s2, op0=op0, op1=op1),
                      reads=[a] + list(sreads), writes=[dst])

    def recip(self, dst, dst_ap, src, src_ap):
        self.P.op("dve", lambda e: e.reciprocal(out=dst_ap, in_=src_ap), reads=[src], writes=[dst])

    def memset(self, eng, dst, dst_ap, val):
        self.P.op(eng, lambda e: e.memset(dst_ap, val), writes=[dst])

    def transposes(self, dst_ps, src, src_aps, ident):
        def fn(e):
            for i, ap in enumerate(src_aps):
                ins = e.transpose(out=dst_ps.t[:, i, :], in_=ap, identity=ident.t[:])
            return ins
        self.P.op("pe", fn, reads=[src, ident], writes=[dst_ps])

    def matmuls(self, dst, specs, reads):
        def fn(e):
            for sp_ in specs:
                o, l, r, st, sp = sp_[:5]
                if len(sp_) > 5 and sp_[5]:
                    ins = e.matmul(o, lhsT=l, rhs=r, start=st, stop=sp, skip_group_check=True)
                else:
                    ins = e.matmul(o, lhsT=l, rhs=r, start=st, stop=sp)
            return ins
        self.P.op("pe", fn, reads=reads, writes=(dst if isinstance(dst, (list, tuple)) else [dst]))

    def setup_consts(self):
        P = self.P
        idf = P.sb([128, 128], F32, "idf")
        self.ident = P.sb([128, 128], BF16, "ident")
        d = self.din("ident", [128, 128])
        self.load(idf, idf.t[:], d, d.t)
        self.copy("dve", self.ident, self.ident.t[:], idf, idf.t[:])
        self.ones_row = P.sb([1, 128], F32, "ones_row")
        self.memset("dve", self.ones_row, self.ones_row.t[:], 1.0)

    def mod_vectors(self, l, need_ab, need_gate):
        P = self.P
        wmod = self.din(f"wmod{l}", [D, 3 * D])
        bT = self.din(f"bmodT{l}", [128, 24])
        res = {}
        if not hasattr(self, "silu_c"):
            cc = self.din("ccol", [128, 8])
            ct = P.sb([128, 8], F32, "ccol")
            self.load(ct, ct.t[:], cc, cc.t)
            self.silu_c = P.sb([128, 8], F32, "siluc")
            self.act(self.silu_c, self.silu_c.t[:], ct, ct.t[:], AF.Silu)
        sc = self.silu_c
        bt = P.sb([128, 24], F32, "bT")
        self.load(bt, bt.t[:], bT, bT.t)
        if need_ab:
            gT = self.din(f"ngT{l}", [128, 8])
            gt = P.sb([128, 8], F32, "gT")
            self.load(gt, gt.t[:], gT, gT.t)
            modT = P.sb([128, 16], F32, "modT")
            pg = self.pg[2]
            for g4 in range(8):
                st = self.wm_stage.next()
                self.load(st, st.t[:], wmod, wmod.t[:, g4 * 256:(g4 + 1) * 256].rearrange("(c p) n -> p c n", p=128),
                          q=self.dq.next())
                specs = []
                for jj in range(2):
                    j = g4 * 2 + jj
                    for k in range(8):
                        specs.append((pg.t[:, j:j + 1], st.t[:, k, jj * 128:(jj + 1) * 128], sc.t[:, k:k + 1], k == 0, k == 7))
                self.matmuls(pg, specs, [st, sc])
            self.tt("dve", modT, modT.t[:], pg, pg.t[:, 0:16], bt, bt.t[:, 0:16], ALU.add)
            A = P.sb([128, 8], F32, "A")
            self.P.op("dve", lambda e: e.scalar_tensor_tensor(out=A.t[:], in0=modT.t[:, 8:16], scalar=1.0, in1=gt.t[:],
                                                               op0=ALU.add, op1=ALU.mult), reads=[modT, gt], writes=[A])
            res["A"] = A
            res["B"] = modT
        if need_gate:
            bg = self.din(f"bmodg{l}", [1, D])
            bgt = P.sb([1, D], F32, "bg")
            self.load(bgt, bgt.t[:], bg, bg.t)
            grow = P.sb([1, D], F32, "grow")
            gbc = P.sb([128, D], F32, "gbc")
            for g2 in range(4):
                st = self.wm_stage.next()
                c0 = 2048 + g2 * 256
                self.load(st, st.t[:], wmod, wmod.t[:, c0:c0 + 256].rearrange("(c p) n -> p c n", p=128), q=self.dq.next())
                pg = self.pg[g2 % 2]
                specs = [(pg.t[0:1, 0:256], sc.t[:, k:k + 1], st.t[:, k, :], k == 0, k == 7) for k in range(8)]
                self.matmuls(pg, specs, [st, sc])
                self.tt("dve", grow, grow.t[:, g2 * 256:(g2 + 1) * 256], pg, pg.t[0:1, 0:256], bgt, bgt.t[:, g2 * 256:(g2 + 1) * 256], ALU.add)
            for g2 in range(2):
                pg = self.pg[g2]
                self.matmuls(pg, [(pg.t[:], self.ones_row.t[:], grow.t[:, g2 * 512:(g2 + 1) * 512], True, True)],
                             [self.ones_row, grow])
                self.copy("act", gbc, gbc.t[:, g2 * 512:(g2 + 1) * 512], pg, pg.t[:])
            res["gate_bc"] = gbc
        return res

    def load_weight(self, wsb, wdram, c0, ncols, scale_bc=None):
        P = self.P
        for g in range(0, ncols, 256):
            n = min(256, ncols - g)
            st = self.w_stage.next()
            self.load(st, st.t[:, :, 0:n], wdram, wdram.t[:, c0 + g:c0 + g + n].rearrange("(c p) n -> p c n", p=128),
                      q=self.dq.next())
            if scale_bc is None:
                self.copy("dve" if (g // 256) % 2 == 0 else "act", wsb.part(g), wsb.t[:, :, g:g + n], st, st.t[:, :, 0:n])
            else:
                for k in range(8):
                    self.tt("dve", wsb.part(g), wsb.t[:, k, g:g + n], st, st.t[:, k, 0:n], scale_bc, scale_bc.t[:, g:g + n], ALU.mult)

    def phase_a(self, ntiles, xsrc, xrow0, pdst, prow0, pcol0, wsb, ncols, A, Bm, cs, csrow0, hd, rope_cols, also=None):
        P = self.P
        if not hasattr(self, "pa_bufs"):
            self.pa_bufs = dict(
                tA=Rot([P.sb([128, 512], F32, "patA") for _ in range(3)]),
                tB=Rot([P.sb([128, 512], F32, "patB") for _ in range(3)]),
                csF=Rot([P.sb([128, 2, 512], F32, "pacsF") for _ in range(3)]),
            )
        bufs = self.pa_bufs
        h2 = hd // 2
        nhf = 512 // hd

        def prep_a(i):
            xt = self.f32p.next()
            self.load(xt, xt.t[:], xsrc, xsrc.t[xrow0 + i * 128: xrow0 + (i + 1) * 128, :], q="sp")
            ss = self.ssp.next()
            sq = self.f32p.next()
            self.act(sq, sq.t[:], xt, xt.t[:], AF.Square, accum=ss.t[:, 0:1], extra_w=[ss])
            self.ts("dve", ss, ss.t[:, 1:2], ss, ss.t[:, 0:1], 1.0 / D, NORM_EPS, ALU.mult, ALU.add)
            self.act(ss, ss.t[:, 2:3], ss, ss.t[:, 1:2], AF.Sqrt)
            self.recip(ss, ss.t[:, 3:4], ss, ss.t[:, 2:3])
            xn = self.b1p.next()
            self.ts("dve", xn, xn.t[:], xt, xt.t[:], ss.t[:, 3:4], None, ALU.mult, sreads=[ss])
            csF = None
            if rope_cols > 0:
                csF = bufs["csF"].next()
                self.load(csF, csF.t[:], cs, cs.t[csrow0 + i * 128: csrow0 + (i + 1) * 128, :, :], q="sp")
            return xn, csF

        def prep_b(pa):
            xn, csF = pa
            ptr = self.ptr.next()
            self.transposes(ptr, xn, [xn.t[:, c * 128:(c + 1) * 128] for c in range(8)], self.ident)
            hT = self.t3p.next()
            for c in range(8):
                if rope_cols == 0 and c % 2 == 1:
                    self.ts("dve", hT, hT.t[:, c, :], ptr, ptr.t[:, c, :], A.t[:, c:c + 1], Bm.t[:, c:c + 1],
                            ALU.mult, ALU.add, sreads=[A, Bm])
                else:
                    self._act_affine(hT, hT.t[:, c, :], ptr, ptr.t[:, c, :], A, A.t[:, c:c + 1], Bm, Bm.t[:, c:c + 1])
            cosF = sinS = None
            if csF is not None:
                cosF = T(csF.t[:, 0, :], "cosF", buf=csF.b)
                sinS = T(csF.t[:, 1, :], "sinS", buf=csF.b)
            return hT, cosF, sinS

        def mm(i, st):
            hT, cosF, sinS = st
            for og in range(0, ncols, 2048):
                on = min(2048, ncols - og)
                ot = self.bfp.next()
                for g in range(og, og + on, 512):
                    pg = self.pgr.next()
                    specs = [(pg.t[:], hT.t[:, k, :], wsb.t[:, k, g:g + 512], k == 0, k == 7) for k in range(8)]
                    self.matmuls(pg, specs, [hT, wsb.part(g)])
                    lo = g - og
                    rc = max(0, min(512, rope_cols - g))
                    if rc > 0:
                        tA, tB = bufs["tA"].next(), bufs["tB"].next()
                        self.tt("dve", tA, tA.t[:, 0:rc], pg, pg.t[:, 0:rc], cosF, cosF.t[:, 0:rc], ALU.mult)
                        q4 = pg.t[:, 0:rc].rearrange("p (h two d) -> p h two d", two=2, d=h2)
                        b4 = tB.t[:, 0:rc].rearrange("p (h two d) -> p h two d", two=2, d=h2)
                        s4 = sinS.t[:, 0:rc].rearrange("p (h two d) -> p h two d", two=2, d=h2)
                        self.tt("dve", tB, b4[:, :, 0, :], pg, q4[:, :, 1, :], sinS, s4[:, :, 0, :], ALU.mult)
                        self.tt("dve", tB, b4[:, :, 1, :], pg, q4[:, :, 0, :], sinS, s4[:, :, 1, :], ALU.mult)
                        self.tt("pool", ot, ot.t[:, lo:lo + rc], tA, tA.t[:, 0:rc], tB, tB.t[:, 0:rc], ALU.add)
                        if rc < 512:
                            self.copy("act", ot, ot.t[:, lo + rc:lo + 512], pg, pg.t[:, rc:512])
                    else:
                        self.copy("act", ot, ot.t[:, lo:lo + 512], pg, pg.t[:])
                self.store(pdst, pdst.t[prow0 + i * 128: prow0 + (i + 1) * 128, pcol0 + og: pcol0 + og + on], ot, ot.t[:, 0:on])
                for (adst, atile, arow, c_lo, c_hi, dcol0) in (also or ()):
                    if atile != i:
                        continue
                    lo_, hi_ = max(c_lo, og), min(c_hi, og + on)
                    if hi_ > lo_:
                        self.store(None, adst.t[arow: arow + 128, dcol0 + lo_ - c_lo: dcol0 + hi_ - c_lo],
                                   ot, ot.t[:, lo_ - og:hi_ - og], q="pool")

        pa = {0: prep_a(0)}
        if ntiles > 1:
            pa[1] = prep_a(1)
        st = prep_b(pa.pop(0))
        for i in range(ntiles):
            if i + 2 < ntiles:
                pa[i + 2] = prep_a(i + 2)
            nxt = prep_b(pa.pop(i + 1)) if i + 1 < ntiles else None
            mm(i, st)
            st = nxt
            self.P.maybe_epoch()

    def _act_affine(self, dst, dst_ap, src, src_ap, a_t, a_ap, b_t, b_ap):
        self.P.op("act", lambda e: e.activation(out=dst_ap, in_=src_ap, func=AF.Identity, bias=b_ap, scale=a_ap),
                  reads=[src, a_t, b_t], writes=[dst])

    def phase_c_tile(self, ytile, xsrc, xrow, xdst, drow, wout, final_g=None, defer=False, banks=None):
        P = self.P
        if not hasattr(self, "pc_bufs"):
            self.pc_bufs = dict(yT=self.t3p, x=self.f32p, xo=self.f32p, ss=self.ssp, sq=self.f32p)
        bufs = self.pc_bufs
        banks = banks or self.pgr
        ptr = self.ptr.next()
        self.transposes(ptr, ytile, [ytile.t[:, c * 128:(c + 1) * 128] for c in range(8)], self.ident)
        yT = bufs["yT"].next()
        self.copy("act", yT, yT.t[:], ptr, ptr.t[:])
        xt = bufs["x"].next()
        self.load(xt, xt.t[:], xsrc, xsrc.t[xrow:xrow + 128, :], q="sp")
        pgs = []
        for half in range(2):
            pg = banks.next()
            specs = [(pg.t[:], yT.t[:, k, :], wout.t[:, k, half * 512:(half + 1) * 512], k == 0, k == 7) for k in range(8)]
            self.matmuls(pg, specs, [yT, wout.part(half * 512)])
            pgs.append(pg)

        def back():
            xo = bufs["xo"].next()
            for half in range(2):
                pg = pgs[half]
                self.tt("dve", xo, xo.t[:, half * 512:(half + 1) * 512], pg, pg.t[:], xt, xt.t[:, half * 512:(half + 1) * 512], ALU.add)
            if final_g is not None:
                ss = bufs["ss"].next()
                sq = bufs["sq"].next()
                self.act(sq, sq.t[:], xo, xo.t[:], AF.Square, accum=ss.t[:, 0:1], extra_w=[ss])
                self.ts("dve", ss, ss.t[:, 1:2], ss, ss.t[:, 0:1], 1.0 / D, NORM_EPS, ALU.mult, ALU.add)
                self.act(ss, ss.t[:, 2:3], ss, ss.t[:, 1:2], AF.Sqrt)
                self.recip(ss, ss.t[:, 3:4], ss, ss.t[:, 2:3])
                self.P.op("dve", lambda e: e.scalar_tensor_tensor(out=xt.t[:], in0=xo.t[:], scalar=ss.t[:, 3:4], in1=final_g.t[:],
                                                                   op0=ALU.mult, op1=ALU.mult), reads=[xo, ss, final_g], writes=[xt])
                self.store(xdst, xdst.t[drow:drow + 128, :], xt, xt.t[:])
            else:
                self.store(xdst, xdst.t[drow:drow + 128, :], xo, xo.t[:])
        if defer:
            return back
        back()
        return None

    def conv_layer(self, p0, convk_bc, vmask, xsrc, xdst, wout):
        P = self.P
        cx = [self.bfp] * 3
        tj = [self.f32p] * 3
        bz, sz, acc, yt = self.bfp, self.f32p, self.f32p, self.b1p
        pcb = Rot([self.pg[0], self.pg[1], self.pg[2], self.po[0]])
        prev_back = None
        for i in range(NT):
            r0 = 128 + i * 128
            ts_ = []
            for j in range(3):
                c = cx[j].next()
                self.load(c, c.t[:], p0, p0.t[r0 + j - 1: r0 + j - 1 + 128, 1024:3072], q=self.dq.next())
                t = tj[j].next()
                self.tt("pool", t, t.t[:], c, c.t[:, 0:1024], c, c.t[:, 1024:2048], ALU.mult)
                if (i == 0 and j == 0):
                    self.ts("pool", t, t.t[:], t, t.t[:], vmask.t[:, 0:1], None, ALU.mult, sreads=[vmask])
                if (i == NT - 1 and j == 2):
                    self.ts("pool", t, t.t[:], t, t.t[:], vmask.t[:, 1:2], None, ALU.mult, sreads=[vmask])
                ts_.append(t)
            b = bz.next()
            self.load(b, b.t[:, 0:1024], p0, p0.t[r0:r0 + 128, 0:1024], q="sp")
            self.load(b, b.t[:, 1024:2048], p0, p0.t[r0:r0 + 128, 3072:4096], q="sp")
            s = sz.next()
            self.act(s, s.t[:], b, b.t[:, 1024:2048], AF.Silu)
            a = acc.next()
            self.tt("dve", a, a.t[:], ts_[0], ts_[0].t[:], convk_bc, convk_bc.t[:, 0, :], ALU.mult)
            self.tt("pool", ts_[1], ts_[1].t[:], ts_[1], ts_[1].t[:], convk_bc, convk_bc.t[:, 1, :], ALU.mult)
            self.tt("dve", ts_[2], ts_[2].t[:], ts_[2], ts_[2].t[:], convk_bc, convk_bc.t[:, 2, :], ALU.mult)
            self.tt("pool", a, a.t[:], a, a.t[:], ts_[1], ts_[1].t[:], ALU.add)
            self.tt("dve", a, a.t[:], a, a.t[:], ts_[2], ts_[2].t[:], ALU.add)
            self.tt("dve", s, s.t[:], s, s.t[:], b, b.t[:, 0:1024], ALU.mult)
            y = yt.next()
            self.tt("dve", y, y.t[:], a, a.t[:], s, s.t[:], ALU.mult)
            bk = self.phase_c_tile(y, xsrc, r0, xdst, i * 128, wout, defer=True, banks=pcb)
            if prev_back is not None:
                prev_back()
            prev_back = bk
            self.P.maybe_epoch()
        prev_back()

    def band_attention(self, *, pq, kv, kvalid, masks, nheads, hd, ev, qcol0, kcol0, vcol0, nkvh, gq,
                       blocks, scale, finish):
        P = self.P
        qcols = nheads * hd
        kcols = nkvh * hd
        nqt = qcols // 128
        dup = (hd == 64)
        nkt = nkvh if dup else kcols // 128
        maxch = max(len(b["chunks"]) for b in blocks)
        hb = max(1, 512 // (128 * maxch))
        key = (qcols, nkt, nkvh, ev, maxch)
        if not hasattr(self, "_ba_cache"):
            self._ba_cache = {}
        if key not in self._ba_cache:
            self._ba_cache[key] = dict(
                qrow=Rot([P.sb([128, qcols], BF16, "ba_q") for _ in range(2)]),
                qT=Rot([P.sb([128, nqt, 128], BF16, "ba_qT") for _ in range(2)]),
                krow=Rot([P.sb([128, nkt * 128], BF16, "ba_k") for _ in range(2 * maxch)]),
                kT=Rot([P.sb([128, nkt, 128], BF16, "ba_kT") for _ in range(2 * maxch)]),
                vaug=Rot([P.sb([128, nkvh, ev + 1], BF16, "ba_v") for _ in range(2 * maxch)]),
                pt=Rot([P.sb([128, hb * maxch, 128], BF16, "ba_pt") for _ in range(3)]),
                kvc=Rot([P.sb([128, 1], F32, "ba_kvc") for _ in range(2 * maxch)]))
            for v_ in self._ba_cache[key]["vaug"].items:
                self.memset("pool", v_, v_.t[:], 1.0)
        c_ = self._ba_cache[key]
        kvcr = c_["kvc"]
        qrow, qT, krow, kT, vaug, pt = c_["qrow"], c_["qT"], c_["krow"], c_["kT"], c_["vaug"], c_["pt"]
        per_bank = 512 // (ev + 1)
        nch = maxch
        assert all(len(b_["chunks"]) == nch for b_ in blocks)
        if "maskb" not in c_:
            c_["maskb"] = P.sb([128, hb * nch, 128], BF16, "ba_maskb")
        maskb = c_["maskb"]
        self.memset("pool", maskb, maskb.t[:], 0.0)
        for ii in range(hb):
            for ci in range(nch):
                mid = blocks[0]["chunks"][ci][2]
                if mid is not None:
                    self.ts("pool", maskb, maskb.t[:, ii * nch + ci, :], masks, masks.t[:, mid, :], 30000.0, -30000.0, ALU.mult, ALU.add)
        oaps = []
        for h in range(nheads):
            bank = self.po[h // per_bank]
            sl = (h % per_bank) * (ev + 1)
            oaps.append((bank, bank.t[:, sl:sl + ev + 1]))

        def prep_loads(blk):
            qr = qrow.next()
            r0, st = blk["qrow0"], blk["qstep"]
            self.load(qr, qr.t[:], pq, pq.t[r0:r0 + 127 * st + 1:st, qcol0:qcol0 + qcols], q="sp")
            vas, kvcs, krs = [], [], []
            for (k0, kst, mid, halo) in blk["chunks"]:
                kr = krow.next()
                if dup:
                    ksrc = kv.t[k0:k0 + 127 * kst + 1:kst, kcol0:kcol0 + kcols].rearrange("p (h d) -> p h d", d=hd)
                    kr4 = kr.t[:].rearrange("p (h two d) -> p h two d", two=2, d=hd)
                    self.load(kr, kr4[:, :, 0, :], kv, ksrc, q="sp")
                    self.load(kr, kr4[:, :, 1, :], kv, ksrc, q="sp")
                else:
                    self.load(kr, kr.t[:], kv, kv.t[k0:k0 + 127 * kst + 1:kst, kcol0:kcol0 + kcols], q="sp")
                va = vaug.next()
                self.load(va, va.t[:, :, 0:ev], kv,
                          kv.t[k0:k0 + 127 * kst + 1:kst, vcol0:vcol0 + nkvh * ev].rearrange("p (h e) -> p h e", e=ev), q="sp")
                if halo:
                    kc_ = kvcr.next()
                    self.load(kc_, kc_.t[:], kvalid, kvalid.t[k0:k0 + 127 * kst + 1:kst, :], q="sp", slow=True)
                    kvcs.append(kc_)
                else:
                    kvcs.append(None)
                krs.append(kr)
                vas.append(va)
            return qr, krs, vas, kvcs

        def prep_tr(ld):
            qr, krs, vas, kvcs = ld
            kts = []
            ptr = self.ptr.next()
            self.transposes(ptr, qr, [qr.t[:, c * 128:(c + 1) * 128] for c in range(nqt)], self.ident)
            qt = qT.next()
            self.copy("dve", qt, qt.t[:], ptr, ptr.t[:, 0:nqt, :])
            for kr in krs:
                ptr = self.ptr.next()
                self.transposes(ptr, kr, [kr.t[:, c * 128:(c + 1) * 128] for c in range(nkt)], self.ident)
                kt = kT.next()
                self.copy("dve", kt, kt.t[:], ptr, ptr.t[:, 0:nkt, :])
                kts.append(kt)
            return qt, kts, vas, kvcs

        def emit_qk(st, hs):
            qt, kts, vas, kvcs = st
            pg = self.pgr.next()
            specs = []
            for ii, h in enumerate(hs):
                kvh = h // gq
                qd = h * hd
                po_ = qd % 128
                for ci in range(nch):
                    col = (ii * nch + ci) * 128
                    specs.append((pg.t[:, col:col + 128], kts[ci].t[po_: po_ + hd, kvh, :], qt.t[po_: po_ + hd, qd // 128, :],
                                  len(specs) == 0, False, True))
            ncol = len(hs) * nch
            specs.append((pg.t[:, 0:ncol * 128], self.ident.t[:], maskb.t[:, 0:ncol, :].rearrange("p c q -> p (c q)"), False, True, True))
            self.matmuls(pg, specs, kts + [qt, maskb, self.ident])
            p = pt.next()
            self.act(p, p.t[:, 0:ncol, :], pg, pg.t[:, 0:ncol * 128].rearrange("p (c q) -> p c q", q=128), AF.Exp, scale=scale)
            for ii, h in enumerate(hs):
                for ci in range(nch):
                    if kvcs[ci] is not None:
                        cc = ii * nch + ci
                        self.ts("dve", p, p.t[:, cc, :], p, p.t[:, cc, :], kvcs[ci].t[:, 0:1], None, ALU.mult, sreads=[kvcs[ci]])
            return p

        def emit_pv(st, hs, p):
            qt, kts, vas, kvcs = st
            banks = []
            per = {}
            for ii, h in enumerate(hs):
                kvh = h // gq
                bank, oap = oaps[h]
                if bank not in banks:
                    banks.append(bank)
                    per[id(bank)] = []
                for ci in range(nch):
                    cc = ii * nch + ci
                    per[id(bank)].append((oap, p.t[:, cc, :], vas[ci].t[:, kvh, :], ci == 0, ci == nch - 1))
            for bank in banks:
                self.matmuls(bank, per[id(bank)], [p] + vas)

        units = [list(range(h0, min(nheads, h0 + hb))) for h0 in range(0, nheads, hb)]
        prep_at = min(len(units) - 1, max(1, len(units) // 2))
        st = prep_tr(prep_loads(blocks[0]))
        deferred = None
        for bi, blk in enumerate(blocks):
            ld = prep_loads(blocks[bi + 1]) if bi + 1 < len(blocks) else None
            nxt = None
            ps_ = [emit_qk(st, units[0])]
            for ui, hs in enumerate(units):
                if ui + 1 < len(units):
                    ps_.append(emit_qk(st, units[ui + 1]))
                emit_pv(st, hs, ps_[ui])
                if ui == prep_at - 1 and ld is not None:
                    nxt = prep_tr(ld)
            if nxt is None and ld is not None:
                nxt = prep_tr(ld)
            d_ = finish(bi, blk, oaps)
            if deferred is not None:
                deferred()
            deferred = d_
            st = nxt
            self.P.maybe_epoch()
        if deferred is not None:
            deferred()


def _group(xs, n):
    return [xs[i:i + n] for i in range(0, len(xs), n)]


def d2d(kb, dst, dst_ap, src, src_ap, q="sp"):
    kb.load(dst, dst_ap, src, src_ap, q=q)


CC_MAX_BYTES = 2 << 20


def exchange_alloc(kb, name, pieces, cols):
    rmax = max(1, CC_MAX_BYTES // (cols * 2))
    jobs = []
    ci = 0
    for pi, (rows, d0, d1) in enumerate(pieces):
        for r0 in range(0, rows, rmax):
            n = min(rmax, rows - r0)
            sb_ = kb.dint(f"{name}_s{ci}", [n, cols], BF16)
            db_ = kb.dint(f"{name}_d{ci}", [2 * n, cols], BF16)
            sb_.name_is_dram = False
            db_.name_is_dram = False
            ci += 1
            jobs.append(dict(src=sb_, dst=db_, n=n, r0=r0, d0=d0, d1=d1, piece=pi))
    return jobs


def exchange_also(jobs, piece, tile0, c_lo, c_hi, dcol0):
    out = []
    for jb in jobs:
        if jb["piece"] != piece:
            continue
        for t in range(jb["n"] // 128):
            out.append((jb["src"], tile0 + jb["r0"] // 128 + t, t * 128, c_lo, c_hi, dcol0))
    return out


def exchange_cc(kb, jobs):
    groups = [[0, 1], [2, 3], [4, 5], [6, 7]]
    for jb in jobs:
        sb_, db_ = jb["src"], jb["dst"]
        kb.P.cc((lambda sb_=sb_, db_=db_: (lambda e: e.collective_compute(
            "AllGather", ALU.bypass, replica_groups=groups, ins=[sb_.t.opt()], outs=[db_.t.opt()])))(),
            reads=[sb_.b], writes=[db_.b])


def exchange_finish(kb, jobs):
    for jb in jobs:
        db_, n, r0 = jb["dst"], jb["n"], jb["r0"]
        if jb["d0"] is not None:
            d2d(kb, None, jb["d0"][r0:r0 + n, :], db_, db_.t[0:n, :], q=kb.dq.next())
        if jb["d1"] is not None:
            d2d(kb, None, jb["d1"][r0:r0 + n, :], db_, db_.t[n:2 * n, :], q=kb.dq.next())


def stage0(kb):
    P = kb.P
    xe = kb.din("xe", [TPC + 256, D])
    m0 = kb.mod_vectors(0, True, True)
    m1 = kb.mod_vectors(1, True, False)
    w_in0 = kb.din("w_in0", [D, 4096])
    w_out0 = kb.din("w_out0", [D, D])
    w_in1 = kb.din("w_in1", [D, 8192])
    convk = kb.din("convk_bc", [128, 3, D])
    vmask = kb.din("vmask", [128, 2])
    cs1 = kb.din("cs1", [TPC, 2, 512])
    X1 = kb.dint("X1", [TPC, D])
    P1 = kb.dint("P1", [TPC, 8192], BF16)
    p0 = kb.dint("p0", [TPC + 256, 4096], BF16)
    wsb = P.sb([128, 8, 2048], BF16, "wsb")
    wout = P.sb([128, 8, D], BF16, "wout")
    ck = P.sb([128, 3, D], F32, "convk")
    vm = P.sb([128, 2], F32, "vmask")
    kb.load(ck, ck.t[:], convk, convk.t)
    kb.load(vm, vm.t[:], vmask, vmask.t)
    kb.load_weight(wout, w_out0, 0, D, scale_bc=m0["gate_bc"])
    for ps_ in range(2):
        kb.load_weight(wsb, w_in0, ps_ * 2048, 2048)
        kb.phase_a(NT + 2, xe, 0, p0, 0, ps_ * 2048, wsb, 2048, m0["A"], m0["B"], None, 0, 128, 0)
    P.barrier()
    kb.conv_layer(p0, ck, vm, xe, X1, wout)
    P.barrier()
    HALO = 1024
    kv = kb.dint("kv1", [TPC + 2 * HALO, 4096], BF16)
    jobs = exchange_alloc(kb, "ex1", [
        (HALO, kv.t[0:HALO, :], None),
        (HALO, None, kv.t[HALO + TPC:, :]),
    ], 4096)
    for (c0, nc_, rc_) in ((3072, 2048, 2048), (5120, 2048, 1024)):
        kb.load_weight(wsb, w_in1, c0, nc_)
        also = [(kv, i, HALO + i * 128, 0, nc_, c0 - 3072) for i in range(NT)]
        also += exchange_also(jobs, 0, NT - HALO // 128, 0, nc_, c0 - 3072)
        also += exchange_also(jobs, 1, 0, 0, nc_, c0 - 3072)
        kb.phase_a(NT, X1, 0, P1, 0, c0, wsb, nc_, m1["A"], m1["B"], cs1, 0, 128, rc_, also=also)
    P.barrier()
    exchange_cc(kb, jobs)
    for (c0, nc_, rc_) in ((0, 2048, 2048), (2048, 1024, 1024), (7168, 1024, 0)):
        kb.load_weight(wsb, w_in1, c0, nc_)
        kb.phase_a(NT, X1, 0, P1, 0, c0, wsb, nc_, m1["A"], m1["B"], cs1, 0, 128, rc_)
    exchange_finish(kb, jobs)
    P.barrier()


def stage1(kb):
    P = kb.P
    HALO = 1024
    X1 = kb.dram["X1"]
    P1 = kb.dram["P1"]
    kv = kb.dram["kv1"]
    kvalid = kb.din("kvalid1", [TPC + 2 * HALO, 1])
    masksd = kb.din("masks", [128, 2, 128], BF16)
    m1 = kb.mod_vectors(1, False, True)
    m2 = kb.mod_vectors(2, True, False)
    w_out1 = kb.din("w_out1", [D, D])
    w_in2 = kb.din("w_in2", [D, 2560])
    cs2 = kb.din("cs2", [TPC, 2, 512])
    X2 = kb.dint("X2", [TPC, D])
    P2 = kb.dint("P2", [TPC, 2560], BF16)
    og = [kb.dint(f"og{g}", [TPC, 8 * 129]) for g in range(3)]
    wout = P.sb([128, 8, D], BF16, "wout")
    masks = P.sb([128, 2, 128], BF16, "masks")
    kb.load(masks, masks.t[:], masksd, masksd.t)
    kb.load_weight(wout, w_out1, 0, D, scale_bc=m1["gate_bc"])
    P.barrier()
    osb = Rot([P.sb([128, 8 * 129], F32, "osb") for _ in range(2)])
    for g, r in enumerate((1, 4, 16)):
        blocks = []
        for p in range(r):
            for j in range(TPC // (128 * r)):
                i0 = 128 * j
                q0 = p + r * i0
                ka = HALO + p + r * (i0 - 64)
                kb_ = HALO + p + r * (i0 + 64)
                ha = ka < HALO
                hb_ = kb_ + 127 * r >= HALO + TPC
                blocks.append(dict(qrow0=q0, qstep=r, chunks=[(ka, r, 0, ha), (kb_, r, 1, hb_)]))

        def finish(bi, blk, oaps, g=g, r=r):
            o = osb.next()
            for bnk in range(3):
                n = min(3, 8 - 3 * bnk) * 129
                kb.copy("act" if bnk == 1 else "dve", o, o.t[:, bnk * 387: bnk * 387 + n], kb.po[bnk], kb.po[bnk].t[:, 0:n])
            r0 = blk["qrow0"]
            kb.store(og[g], og[g].t[r0:r0 + 127 * r + 1:r, :], o, o.t[:])
        kb.band_attention(pq=P1, kv=kv, kvalid=kvalid, masks=masks, nheads=8, hd=128, ev=128,
                          qcol0=g * 1024, kcol0=g * 1024, vcol0=3072, nkvh=8, gq=1,
                          blocks=blocks, scale=128 ** -0.5, finish=finish)
    P.barrier()
    zt = Rot([P.sb([128, D], BF16, "zt") for _ in range(2)])
    rl = Rot([P.sb([128, 8], F32, "rl") for _ in range(2)])
    o3 = Rot(osb.items + [P.sb([128, 8 * 129], F32, "o3")])
    pcb = Rot([kb.pg[0], kb.pg[1], kb.pg[2], kb.po[0]])
    prev_back = None
    for i in range(NT):
        os_ = []
        for g in range(3):
            o = o3.next()
            kb.load(o, o.t[:], og[g], og[g].t[i * 128:(i + 1) * 128, :], q=kb.dq.next())
            os_.append(o)
        z = zt.next()
        kb.load(z, z.t[:], P1, P1.t[i * 128:(i + 1) * 128, 7168:8192], q="sp")
        kb.tt("pool", os_[0], os_[0].t[:], os_[0], os_[0].t[:], os_[1], os_[1].t[:], ALU.add)
        kb.tt("dve", os_[0], os_[0].t[:], os_[0], os_[0].t[:], os_[2], os_[2].t[:], ALU.add)
        o4 = os_[0].t[:].rearrange("p (h e) -> p h e", e=129)
        rr = rl.next()
        kb.recip(rr, rr.t[:], os_[0], o4[:, :, 128])
        sz = kb.f32p.next()
        kb.act(sz, sz.t[:], z, z.t[:], AF.Silu)
        on = kb.f32p.next()
        kb.tt("dve", on, on.t[:].rearrange("p (h e) -> p h e", e=128), os_[0], o4[:, :, 0:128],
              rr, rr.t[:].unsqueeze(2).to_broadcast([128, 8, 128]), ALU.mult)
        y = kb.b1p.next()
        kb.tt("pool", y, y.t[:], on, on.t[:], sz, sz.t[:], ALU.mult)
        bk = kb.phase_c_tile(y, X1, i * 128, X2, i * 128, wout, defer=True, banks=pcb)
        if prev_back is not None:
            prev_back()
        prev_back = bk
        P.maybe_epoch()
    prev_back()
    P.barrier()
    wsb = P.sb([128, 8, 2048], BF16, "wsb")
    kb.load_weight(wsb, w_in2, 0, 2048)
    H2 = 128
    kv2 = kb.dint("kv2", [TPC + 2 * H2, 512], BF16)
    jobs = exchange_alloc(kb, "ex2", [(H2, kv2.t[0:H2, :], None), (H2, None, kv2.t[H2 + TPC:, :])], 512)
    also = [(kv2, i, H2 + i * 128, 1024, 1536, 0) for i in range(NT)]
    also += exchange_also(jobs, 0, NT - 1, 1024, 1536, 0)
    also += exchange_also(jobs, 1, 0, 1024, 1536, 0)
    kb.phase_a(NT, X2, 0, P2, 0, 0, wsb, 2048, m2["A"], m2["B"], cs2, 0, 64, 1280, also=also)
    P.barrier()
    exchange_cc(kb, jobs)
    kb.load_weight(wsb, w_in2, 2048, 512)
    kb.phase_a(NT, X2, 0, P2, 0, 2048, wsb, 512, m2["A"], m2["B"], cs2, 0, 64, 0)
    exchange_finish(kb, jobs)
    P.barrier()


def stage2(kb):
    P = kb.P
    HALO = 128
    X2 = kb.dram["X2"]
    P2 = kb.dram["P2"]
    kv = kb.dram["kv2"]
    kvalid = kb.din("kvalid2", [TPC + 2 * HALO, 1])
    masksd = kb.din("masks", [128, 2, 128], BF16)
    sinkd = kb.din("sink_bc", [128, 16])
    m2 = kb.mod_vectors(2, False, True)
    m3 = kb.mod_vectors(3, True, False)
    w_out2 = kb.din("w_out2", [D, D])
    w_in3 = kb.din("w_in3", [D, 4096])
    cs3 = kb.din("cs2", [TPC, 2, 512])
    X3 = kb.dint("X3", [TPC, D])
    P3 = kb.dint("P3", [TPC, 4096], BF16)
    wout = P.sb([128, 8, D], BF16, "wout")
    masks = P.sb([128, 2, 128], BF16, "masks")
    kb.load(masks, masks.t[:], masksd, masksd.t)
    esink = P.sb([128, 16], F32, "esink")
    kb.load(esink, esink.t[:], sinkd, sinkd.t)
    kb.act(esink, esink.t[:], esink, esink.t[:], AF.Exp)
    kb.load_weight(wout, w_out2, 0, D, scale_bc=m2["gate_bc"])
    P.barrier()
    blocks = []
    for j in range(NT):
        q0 = 128 * j
        blocks.append(dict(qrow0=q0, qstep=1, chunks=[(HALO + q0 - 128, 1, 0, j == 0), (HALO + q0, 1, None, False),
                                                      (HALO + q0 + 128, 1, 1, j == NT - 1)]))
    zt = Rot([P.sb([128, D], BF16, "zt") for _ in range(2)])
    lt = Rot([P.sb([128, 16], F32, "lt") for _ in range(2)])

    oev = Rot([P.sb([128, 16 * 65], F32, "oev") for _ in range(2)])

    def finish(bi, blk, oaps):
        r0 = blk["qrow0"]
        oe = oev.next()
        for bnk in range(3):
            nh = min(7, 16 - 7 * bnk)
            kb.copy("act" if bnk == 1 else "dve", oe, oe.t[:, 7 * bnk * 65:(7 * bnk + nh) * 65], kb.po[bnk], kb.po[bnk].t[:, 0:nh * 65])

        def rest():
            z = zt.next()
            kb.load(z, z.t[:], P2, P2.t[r0:r0 + 128, 1536:2560], q="sp")
            sz = kb.f32p.next()
            kb.act(sz, sz.t[:], z, z.t[:], AF.Silu)
            l = lt.next()
            on = kb.f32p.next()
            o3_ = oe.t[:].rearrange("p (h e) -> p h e", e=65)
            kb.tt("dve", l, l.t[:], oe, o3_[:, :, 64], esink, esink.t[:], ALU.add)
            kb.recip(l, l.t[:], l, l.t[:])
            kb.tt("dve", on, on.t[:].rearrange("p (h e) -> p h e", e=64), oe, o3_[:, :, 0:64],
                  l, l.t[:].unsqueeze(2).to_broadcast([128, 16, 64]), ALU.mult)
            y = kb.b1p.next()
            kb.tt("pool", y, y.t[:], on, on.t[:], sz, sz.t[:], ALU.mult)
            kb.phase_c_tile(y, X2, r0, X3, r0, wout)
        return rest

    kb.band_attention(pq=P2, kv=kv, kvalid=kvalid, masks=masks, nheads=16, hd=64, ev=64,
                      qcol0=0, kcol0=0, vcol0=256, nkvh=4, gq=4, blocks=blocks, scale=64 ** -0.5, finish=finish)
    P.barrier()
    wsb = P.sb([128, 8, 2048], BF16, "wsb")
    kb.load_weight(wsb, w_in3, 1024, 2048)
    jobs = exchange_alloc(kb, "ex3", [(TPC, None, None)], 2048)
    kb.shared["ex3_jobs"] = jobs
    kb.phase_a(NT, X3, 0, P3, 0, 1024, wsb, 2048, m3["A"], m3["B"], cs3, 0, 64, 1024,
               also=exchange_also(jobs, 0, 0, 0, 2048, 0))
    P.barrier()
    exchange_cc(kb, jobs)
    kb.load_weight(wsb, w_in3, 0, 1024)
    kb.phase_a(NT, X3, 0, P3, 0, 0, wsb, 1024, m3["A"], m3["B"], cs3, 0, 64, 1024)
    kb.load_weight(wsb, w_in3, 3072, 1024)
    kb.phase_a(NT, X3, 0, P3, 0, 3072, wsb, 1024, m3["A"], m3["B"], cs3, 0, 64, 0)
    exchange_finish(kb, jobs)
    P.barrier()


def stage3(kb):
    P = kb.P
    X3 = kb.dram["X3"]
    P3 = kb.dram["P3"]
    ex3 = kb.shared["ex3_jobs"]
    lamd = kb.din("lam_bc", [128, 4, 64])
    sgd = kb.din("subln_bc", [128, 128])
    fgd = kb.din("finalg_bc", [128, D])
    m3 = kb.mod_vectors(3, False, True)
    w_out3 = kb.din("w_out3", [D, D])
    OUT = kb.dout("OUT", [TPC, D])
    Y = kb.dint("Y3", [TPC, D], BF16)
    wout = P.sb([128, 8, D], BF16, "wout")
    kb.load_weight(wout, w_out3, 0, D, scale_bc=m3["gate_bc"])
    lam_init = 0.8 - 0.6 * math.exp(-0.3 * 3)
    lv = P.sb([128, 4, 64], F32, "lv")
    kb.load(lv, lv.t[:], lamd, lamd.t)
    lp = P.sb([128, 2, 64], F32, "lp")
    ls = P.sb([128, 4], F32, "ls")
    kb.tt("dve", lp, lp.t[:, 0, :], lv, lv.t[:, 0, :], lv, lv.t[:, 1, :], ALU.mult)
    kb.tt("dve", lp, lp.t[:, 1, :], lv, lv.t[:, 2, :], lv, lv.t[:, 3, :], ALU.mult)
    P.op("dve", lambda e: e.reduce_sum(out=ls.t[:, 0:2], in_=lp.t[:], axis=AX.X), reads=[lp.b], writes=[ls.b])
    kb.act(ls, ls.t[:, 0:2], ls, ls.t[:, 0:2], AF.Exp)
    kb.tt("dve", ls, ls.t[:, 2:3], ls, ls.t[:, 0:1], ls, ls.t[:, 1:2], ALU.subtract)
    kb.ts("dve", ls, ls.t[:, 3:4], ls, ls.t[:, 2:3], lam_init, -1.0, ALU.add, ALU.mult)
    sg = P.sb([128, 128], F32, "sg")
    kb.load(sg, sg.t[:], sgd, sgd.t)
    kb.ts("dve", sg, sg.t[:], sg, sg.t[:], 1.0 - lam_init, None, ALU.mult)
    fg = P.sb([128, D], F32, "fg")
    kb.load(fg, fg.t[:], fgd, fgd.t)

    NKC = S // 128
    kTh = Rot([P.sb([128, S], BF16, "kTh") for _ in range(2)])
    vah = Rot([P.sb([128, NKC, 129], BF16, "vah") for _ in range(2)])
    for v_ in vah.items:
        kb.memset("pool", v_, v_.t[:], 1.0)
    krow = Rot([P.sb([128, 8, 128], BF16, "krow") for _ in range(2)])
    qrow = Rot([P.sb([128, 4, 128], BF16, "qrow") for _ in range(2)])
    qTt = Rot([P.sb([128, 512], BF16, "qTt") for _ in range(2)])
    ptile = Rot([P.sb([128, 2, 512], BF16, "ptile") for _ in range(3)])
    zt = Rot([P.sb([128, 4, 128], BF16, "zt") for _ in range(2)])
    szt = Rot([P.sb([128, 4, 128], F32, "szt") for _ in range(2)])
    ezt = Rot([P.sb([128, 4, 128], F32, "ezt") for _ in range(2)])
    yh = Rot([P.sb([128, 4, 128], BF16, "yh") for _ in range(2)])
    sm = Rot([P.sb([128, 24], F32, "sm") for _ in range(2)])
    A0r = Rot([P.sb([128, 4, 129], F32, "A0") for _ in range(2)])
    A1r = Rot([P.sb([128, 4, 129], F32, "A1") for _ in range(2)])
    w0 = Rot([P.sb([128, 4, 128], F32, "w0") for _ in range(2)])
    w1 = Rot([P.sb([128, 4, 128], F32, "w1") for _ in range(2)])
    scr = Rot(kb.sc)
    acc = kb.acc
    scale = 64 ** -0.5
    sgb = sg.t[:].unsqueeze(1).to_broadcast([128, 4, 128])

    def emit_qk(kT, qT, kc):
        sc = scr.next()
        specs = [(sc.t[:, comp, :], kT.t[64 * comp:64 * comp + 64, kc * 128:(kc + 1) * 128], qT.t[64 * comp:64 * comp + 64, :], True, True)
                 for comp in range(2)]
        kb.matmuls(sc, specs, [kT, qT])
        pt = ptile.next()
        kb.act(pt, pt.t[:], sc, sc.t[:], AF.Exp, scale=scale)
        return pt

    def emit_pv(pt, va, kc):
        specs = []
        for comp in range(2):
            for j in range(4):
                bank = acc[2 * comp + j // 2]
                specs.append((bank.t[:, (j % 2) * 129:(j % 2) * 129 + 129], pt.t[:, comp, j * 128:(j + 1) * 128],
                              va.t[:, kc, :], kc == 0 and j % 2 == 0, kc == NKC - 1 and j % 2 == 1, True))
        kb.matmuls(list(acc), specs, [pt, va])

    def k_prep(h, kT, c8):
        kr = krow.next()
        r = c8 // 4
        for j in range(2):
            db_ = ex3[2 * (c8 % 4) + j]["dst"]
            kb.load(kr, kr.t[:, 4 * j:4 * j + 4, :], db_,
                    db_.t[r * 512:(r + 1) * 512, h * 128:(h + 1) * 128].rearrange("(c p) e -> p c e", p=128), q="sp")
        ptr = kb.ptr.next()
        kb.transposes(ptr, kr, [kr.t[:, c, :] for c in range(8)], kb.ident)
        kb.copy("dve", kT, kT.t[:, c8 * 1024:(c8 + 1) * 1024], ptr, ptr.t[:].rearrange("p c q -> p (c q)"))

    def v_load(h, va):
        for r in range(2):
            for c in range(8):
                db_ = ex3[c]["dst"]
                b0 = r * 32 + c * 4
                kb.load(va, va.t[:, b0:b0 + 4, 0:128], db_,
                        db_.t[r * 512:(r + 1) * 512, 1024 + h * 128: 1024 + (h + 1) * 128].rearrange("(c p) e -> p c e", p=128), q="sp")

    heads = [(kTh.next(), vah.next()) for _ in range(8)]
    v_load(0, heads[0][1])
    for c8 in range(NKC // 8):
        k_prep(0, heads[0][0], c8)
    def q_prep(h, qc):
        qr = qrow.next()
        kb.load(qr, qr.t[:], P3, P3.t[qc * 512:(qc + 1) * 512, h * 128:(h + 1) * 128].rearrange("(c p) e -> p c e", p=128), q="sp")
        z = zt.next()
        kb.load(z, z.t[:], P3, P3.t[qc * 512:(qc + 1) * 512, 3072 + h * 128: 3072 + (h + 1) * 128].rearrange("(c p) e -> p c e", p=128), q="sp")
        ptr = kb.ptr.next()
        kb.transposes(ptr, qr, [qr.t[:, c, :] for c in range(4)], kb.ident)
        qT = qTt.next()
        kb.copy("dve", qT, qT.t[:], ptr, ptr.t[:, 0:4, :].rearrange("p c q -> p (c q)"))
        return qT, z

    def silu_z(z):
        ez = ezt.next()
        kb.act(ez, ez.t[:], z, z.t[:], AF.Exp, scale=-1.0)
        kb.ts("dve", ez, ez.t[:], ez, ez.t[:], 1.0, None, ALU.add)
        sz = szt.next()
        kb.recip(sz, sz.t[:], ez, ez.t[:])
        kb.tt("pool", sz, sz.t[:], sz, sz.t[:], z, z.t[:], ALU.mult)
        return sz

    def epilogue(h, qc, A0, A1, sz):
        s_ = sm.next()
        kb.recip(s_, s_.t[:, 0:4], A0, A0.t[:, :, 128])
        kb.recip(s_, s_.t[:, 4:8], A1, A1.t[:, :, 128])
        kb.ts("dve", s_, s_.t[:, 8:12], s_, s_.t[:, 4:8], ls.t[:, 3:4], None, ALU.mult, sreads=[ls])
        a0, a1 = w0.next(), w1.next()
        kb.tt("pool", a0, a0.t[:], A0, A0.t[:, :, 0:128], s_, s_.t[:, 0:4].unsqueeze(2).to_broadcast([128, 4, 128]), ALU.mult)
        kb.tt("dve", a1, a1.t[:], A1, A1.t[:, :, 0:128], s_, s_.t[:, 8:12].unsqueeze(2).to_broadcast([128, 4, 128]), ALU.mult)
        kb.tt("pool", a0, a0.t[:], a0, a0.t[:], a1, a1.t[:], ALU.add)
        kb.tt("dve", a1, a1.t[:], a0, a0.t[:], a0, a0.t[:], ALU.mult)
        P.op("dve", lambda e: e.reduce_sum(out=s_.t[:, 12:16], in_=a1.t[:], axis=AX.X), reads=[a1.b], writes=[s_.b])
        kb.ts("dve", s_, s_.t[:, 12:16], s_, s_.t[:, 12:16], 1.0 / 128, SUBLN_EPS, ALU.mult, ALU.add)
        kb.act(s_, s_.t[:, 16:20], s_, s_.t[:, 12:16], AF.Ln)
        kb.act(s_, s_.t[:, 20:24], s_, s_.t[:, 16:20], AF.Exp, scale=-0.5)
        kb.tt("pool", a0, a0.t[:], a0, a0.t[:], s_, s_.t[:, 20:24].unsqueeze(2).to_broadcast([128, 4, 128]), ALU.mult)
        kb.tt("dve", a0, a0.t[:], a0, a0.t[:], sg, sgb, ALU.mult)
        y = yh.next()
        kb.tt("pool", y, y.t[:], a0, a0.t[:], sz, sz.t[:], ALU.mult)
        kb.store(Y, Y.t[qc * 512:(qc + 1) * 512, h * 128:(h + 1) * 128].rearrange("(c p) e -> p c e", p=128), y, y.t[:])

    items = [(h, qc) for h in range(8) for qc in range(TPC // 512)]
    cur = q_prep(*items[0])
    for ii, (h, qc) in enumerate(items):
        kT, va = heads[h]
        qT, z = cur
        sz = silu_z(z)
        pts = [emit_qk(kT, qT, 0)]
        for kc in range(NKC):
            if kc + 1 < NKC:
                pts.append(emit_qk(kT, qT, kc + 1))
            emit_pv(pts[kc], va, kc)
        if h + 1 < 8:
            if qc == 0:
                v_load(h + 1, heads[h + 1][1])
            k_prep(h + 1, heads[h + 1][0], qc)
        A0, A1 = A0r.next(), A1r.next()
        for half in range(2):
            kb.copy("dve", A0, A0.t[:, 2 * half:2 * half + 2, :], acc[half], acc[half].t[:, 0:258].rearrange("p (j e) -> p j e", e=129))
            kb.copy("dve", A1, A1.t[:, 2 * half:2 * half + 2, :], acc[2 + half], acc[2 + half].t[:, 0:258].rearrange("p (j e) -> p j e", e=129))
        if ii + 1 < len(items):
            cur = q_prep(*items[ii + 1])
        epilogue(h, qc, A0, A1, sz)
        P.maybe_epoch()
    P.barrier()
    pcb = Rot(list(kb.acc))
    prev_back = None
    for i in range(NT):
        yt = kb.b1p.next()
        kb.load(yt, yt.t[:], Y, Y.t[i * 128:(i + 1) * 128, :], q="sp")
        bk = kb.phase_c_tile(yt, X3, i * 128, OUT, i * 128, wout, final_g=fg, defer=True, banks=pcb)
        if prev_back is not None:
            prev_back()
        prev_back = bk
        P.maybe_epoch()
    prev_back()
    P.barrier()


STAGES = [stage0, stage1, stage2, stage3]
_NC_CACHE = {}


def build_program(stages=(0, 1, 2, 3)):
    key = tuple(stages)
    if key in _NC_CACHE:
        return _NC_CACHE[key]
    nc = bass.Bass("TRN2", target_bir_lowering=False)
    shared = {"dram": {}}
    with contextlib.ExitStack() as st:
        P = Prog(nc, st)
        for si in stages:
            with contextlib.ExitStack() as sst:
                P.cur_stack = sst
                kb = KB(nc, P, shared, nf32=8 if si == 0 else 5, layout="l3" if si == 3 else "std", nbf={0: 8, 3: 1}.get(si, 4))
                kb.setup_consts()
                STAGES[si](kb)
                P.barrier()
                P.emit()
    _NC_CACHE[key] = (nc, shared)
    return _NC_CACHE[key]


def _pp(v, n):
    return np.ascontiguousarray(np.asarray(v, np.float32).reshape(n, 128).T)


def _bc(v):
    v = np.asarray(v, np.float32)
    return np.ascontiguousarray(np.broadcast_to(v[None], (128,) + v.shape))


def _rope_table(dim):
    inv = (10000.0 ** (-(np.arange(0, dim, 2, dtype=np.float32) / np.float32(dim)))).astype(np.float32)
    ang = np.arange(S, dtype=np.float32)[:, None] * inv[None, :]
    cos, sin = np.cos(ang).astype(np.float32), np.sin(ang).astype(np.float32)
    nh = 512 // dim
    cosF = np.tile(np.concatenate([cos, cos], axis=1), (1, nh))
    sinS = np.tile(np.concatenate([-sin, sin], axis=1), (1, nh))
    return np.ascontiguousarray(np.stack([cosF, sinS], axis=1))


def _ext(rows, lo, hi, total):
    out = np.zeros((hi - lo,) + rows.shape[1:], rows.dtype)
    a, b = max(lo, 0), min(hi, total)
    out[a - lo:b - lo] = rows[a:b]
    return out


def kernel(x, c, norm_g, w_mod, b_mod, conv_w_in, conv_k, conv_w_out, dil_w_in, dil_w_out,
           swa_w_in, swa_sink, swa_w_out, diff_w_in, diff_lambda, diff_subln_g, diff_w_out, final_g,
           _stages=(0, 1, 2, 3), _debug_outs=()):
    f = lambda a: np.ascontiguousarray(np.asarray(a, np.float32))
    x, c = f(x), f(c)
    bf = ml_dtypes.bfloat16
    masks = np.zeros((128, 2, 128), np.float32)
    kk, qq = np.meshgrid(np.arange(128), np.arange(128), indexing="ij")
    masks[:, 0, :] = (kk >= qq)
    masks[:, 1, :] = (kk <= qq)
    masks = masks.astype(bf)
    cs128, cs64 = _rope_table(128), _rope_table(64)
    nc, shared = build_program(_stages)
    need = set(shared["dram"].keys())

    def valid(lo, hi):
        t = np.arange(lo, hi)
        return np.ascontiguousarray(((t >= 0) & (t < S)).astype(np.float32)[:, None])

    shared_in = dict(ident=np.eye(128, dtype=np.float32), masks=masks,
                     w_in0=f(conv_w_in[0]), w_out0=f(conv_w_out[0]), w_in1=f(dil_w_in[0]), w_out1=f(dil_w_out[0]),
                     w_in2=f(swa_w_in[0]), w_out2=f(swa_w_out[0]), w_in3=f(diff_w_in[0]), w_out3=f(diff_w_out[0]),
                     convk_bc=_bc(f(conv_k[0])), sink_bc=_bc(f(swa_sink[0])), lam_bc=_bc(f(diff_lambda[0])),
                     subln_bc=_bc(f(diff_subln_g[0])), finalg_bc=_bc(f(final_g)))
    for l in range(4):
        shared_in[f"wmod{l}"] = f(w_mod[l])
        shared_in[f"bmodT{l}"] = _pp(b_mod[l], 24)
        shared_in[f"ngT{l}"] = _pp(norm_g[l], 8)
        shared_in[f"bmodg{l}"] = f(b_mod[l][2048:3072]).reshape(1, D)
    cores = [(b, h) for b in range(NB) for h in range(2)]
    maps = []
    for (b, h) in cores:
        t0 = h * TPC
        d = dict(shared_in)
        d["ccol"] = _pp(c[b], 8)
        d["xe"] = _ext(x[b], t0 - 128, t0 + TPC + 128, S)
        vm = np.ones((128, 2), np.float32)
        if h == 0:
            vm[0, 0] = 0.0
        if h == 1:
            vm[127, 1] = 0.0
        d["vmask"] = vm
        d["cs1"] = np.ascontiguousarray(cs128[t0:t0 + TPC])
        d["cs2"] = np.ascontiguousarray(cs64[t0:t0 + TPC])
        d["kvalid1"] = valid(t0 - 1024, t0 + TPC + 1024)
        d["kvalid2"] = valid(t0 - 128, t0 + TPC + 128)
        maps.append({k: v for k, v in d.items() if k in need})
    res = run_bass_kernel_spmd(nc, maps, core_ids=list(range(8)))
    if _debug_outs:
        return res.results
    out = np.zeros((NB, S, D), np.float32)
    for ci, (b, h) in enumerate(cores):
        out[b, h * TPC:(h + 1) * TPC] = res.results[ci]["OUT"]
    return out
```

```python
import contextlib
import math
import numpy as np
import ml_dtypes
import concourse.bass as bass
import concourse.mybir as mybir
from concourse.bass_utils import run_bass_kernel_spmd

F32 = mybir.dt.float32
BF16 = mybir.dt.bfloat16
ALU = mybir.AluOpType
AF = mybir.ActivationFunctionType
AX = mybir.AxisListType

D = 1024
S = 8192
NB = 4
TPC = 4096
NT = TPC // 128
ENGS = ("pe", "act", "dve", "pool", "sp")
EPOCH_LIMIT = 20000
NDMASEM = 16
NCCSEM = 8
NORM_EPS = 1e-6
_DEBUG_OUT = set()
SUBLN_EPS = 1e-5


class Buf:
    __slots__ = ("name", "w", "r")

    def __init__(self, name=""):
        self.name = name
        self.w = None
        self.r = []


class T:
    __slots__ = ("t", "b", "name_is_dram", "parts")

    def __init__(self, t, name="", dram=False, buf=None):
        self.t = t
        self.b = buf if buf is not None else Buf(name)
        self.name_is_dram = dram
        self.parts = None

    def part(self, col):
        if self.parts is None:
            self.parts = {}
        k = col // 512
        if k not in self.parts:
            self.parts[k] = T(self.t, "part", buf=Buf())
        return self.parts[k]


class Prog:
    def __init__(self, nc, stack):
        self.nc = nc
        self.stack = stack
        self.q = {e: [] for e in ENGS}
        self.cnt = {e: 0 for e in ENGS}
        self.nsem = 0
        self.esem = {e: self._newsem("c_" + e) for e in ENGS}
        self.waited = {e: {} for e in ENGS}
        self.dsem = {e: [self._newsem(f"d_{e}{i}") for i in range(NDMASEM)] for e in ("sp", "pool", "act")}
        self.dval = {e: [0] * NDMASEM for e in self.dsem}
        self.dnext = {e: 0 for e in self.dsem}
        self.latest = {}
        self.nid = 0
        self.cur_stack = stack
        self.csem = [self._newsem(f"cc{i}") for i in range(NCCSEM)]
        self.cval = [0] * NCCSEM
        self.cnext = 0

    def _newsem(self, name):
        self.nsem += 1
        return self.stack.enter_context(self.nc.semaphore(name + f"_{self.nsem}"))

    def sb(self, shape, dt, name=None):
        self.nid += 1
        name = (name or "t") + f"_{self.nid}"
        return T(self.cur_stack.enter_context(self.nc.sbuf_tensor(name, list(shape), dt)), name)

    def ps(self, shape, dt, name=None):
        self.nid += 1
        name = (name or "p") + f"_{self.nid}"
        return T(self.cur_stack.enter_context(self.nc.psum_tensor(name, list(shape), dt)), name)

    def _deps(self, eng, reads, writes):
        deps = {}

        def add(tok):
            if tok is None:
                return
            s, v = tok
            if eng == "pe" and s is self.esem["pe"]:
                return
            k = id(s)
            if self.waited[eng].get(k, 0) >= v:
                return
            if k not in deps or deps[k][1] < v:
                deps[k] = (s, v)
        for b in reads:
            add(b.w)
        for b in writes:
            add(b.w)
            for t in b.r:
                add(t)
        out = list(deps.values())
        for s, v in out:
            self.waited[eng][id(s)] = v
        return out

    def _commit(self, tok, reads, writes):
        for b in reads:
            b.r = [t for t in b.r if t[0] is not tok[0]] + [tok]
        for b in writes:
            b.w = tok
            b.r = []
        self.latest[id(tok[0])] = tok

    @staticmethod
    def _bufs(xs):
        return [x.b if isinstance(x, T) else x for x in xs]

    def op(self, eng, fn, reads=(), writes=()):
        reads, writes = self._bufs(reads), self._bufs(writes)
        deps = self._deps(eng, reads, writes)
        self.cnt[eng] += 1
        tok = (self.esem[eng], self.cnt[eng])
        self.q[eng].append((deps, fn, tok[0], 1))
        self._commit(tok, reads, writes)
        return tok

    def dma(self, eng, fn, reads=(), writes=()):
        reads, writes = self._bufs(reads), self._bufs(writes)
        i = self.dnext[eng]
        self.dnext[eng] = (i + 1) % NDMASEM
        s = self.dsem[eng][i]
        deps = self._deps(eng, reads, writes)
        prev = self.dval[eng][i]
        if prev > 0 and self.waited[eng].get(id(s), 0) < prev:
            deps.append((s, prev))
            self.waited[eng][id(s)] = prev
        self.dval[eng][i] = prev + 16
        tok = (s, prev + 16)
        self.q[eng].append((deps, fn, s, 16))
        self._commit(tok, reads, writes)
        return tok

    def cc(self, fn, reads=(), writes=()):
        i = self.cnext
        self.cnext = (i + 1) % NCCSEM
        s = self.csem[i]
        deps = self._deps("pool", list(reads), list(writes))
        prev = self.cval[i]
        if prev > 0 and self.waited["pool"].get(id(s), 0) < prev:
            deps.append((s, prev))
            self.waited["pool"][id(s)] = prev
        self.cval[i] = prev + 1
        tok = (s, prev + 1)
        self.q["pool"].append((deps, fn, s, 1))
        self._commit(tok, list(reads), list(writes))
        return tok

    def barrier(self):
        toks = list(self.latest.values())
        for e in ENGS:
            deps = []
            for s, v in toks:
                if self.waited[e].get(id(s), 0) < v:
                    deps.append((s, v))
                    self.waited[e][id(s)] = v
            if deps:
                self.q[e].append((deps, None, None, 0))
        if max(self.cnt.values()) > EPOCH_LIMIT:
            dm = set(id(s) for ss in self.dsem.values() for s in ss) | set(id(s) for s in self.csem)
            for e in ENGS:
                self.esem[e] = self._newsem("c_" + e)
                self.cnt[e] = 0
            self.latest = {k: t for k, t in self.latest.items() if k in dm}

    def maybe_epoch(self):
        if max(self.cnt.values()) > EPOCH_LIMIT:
            self.barrier()

    def emit(self):
        qs = self.q
        self.q = {e: [] for e in ENGS}
        with self.nc.Block() as block:
            def run(engname):
                def _f(e):
                    for deps, fn, s, inc in qs[engname]:
                        for ds, dv in deps:
                            e.wait_ge(ds, dv)
                        if fn is not None:
                            fn(e).then_inc(s, inc)
                return _f
            block.tensor(run("pe"))
            block.scalar(run("act"))
            block.vector(run("dve"))
            block.gpsimd(run("pool"))
            block.sync(run("sp"))


class Rot:
    def __init__(self, items):
        self.items = items
        self.i = 0

    def next(self):
        x = self.items[self.i % len(self.items)]
        self.i += 1
        return x


class KB:
    def __init__(self, nc, P, shared, nf32=8, layout="std", nbf=4):
        self.nc = nc
        self.P = P
        self.dram = shared["dram"]
        self.shared = shared
        if layout == "std":
            self.ptr = Rot([P.ps([128, 8, 128], BF16, "ptr") for _ in range(2)])
            self.pg = [P.ps([128, 512], F32, "pg") for _ in range(3)]
            self.po = [P.ps([128, 512], F32, "po") for _ in range(3)]
        else:
            self.sc = [P.ps([128, 2, 512], F32, "sc") for _ in range(2)]
            self.acc = [P.ps([128, 512], F32, "acc") for _ in range(4)]
            self.ptr = Rot([T(x.t[:, 0, :].bitcast(BF16).rearrange("p (c q) -> p c q", q=128), "ptrv", buf=x.b) for x in self.sc])
            self.pg = [T(self.sc[0].t[:, 0, :], "pgv", buf=self.sc[0].b), T(self.sc[1].t[:, 0, :], "pgv", buf=self.sc[1].b),
                       T(self.sc[0].t[:, 1, :], "pgv", buf=self.sc[0].b)]
            self.po = None
        self.pgr = Rot(self.pg[0:2])
        self.dq = Rot(["sp", "act"])
        self.f32p = Rot([P.sb([128, D], F32, "f32p") for _ in range(nf32)])
        self.bfp = Rot([P.sb([128, 2048], BF16, "bfp") for _ in range(nbf)])
        self.b1p = Rot([P.sb([128, D], BF16, "b1p") for _ in range(4)])
        self.t3p = Rot([P.sb([128, 8, 128], BF16, "t3p") for _ in range(4)])
        self.ssp = Rot([P.sb([128, 4], F32, "ssp") for _ in range(4)])
        self.w_stage = Rot([P.sb([128, 8, 256], F32, "wst") for _ in range(2)])
        self.WCH = 256
        self.wm_stage = self.w_stage

    def din(self, name, shape, dt=F32):
        if name not in self.dram:
            self.dram[name] = T(self.nc.dram_tensor(name, list(shape), dt, kind="ExternalInput").ap(), name, True)
        return self.dram[name]

    def dout(self, name, shape, dt=F32):
        if name not in self.dram:
            self.dram[name] = T(self.nc.dram_tensor(name, list(shape), dt, kind="ExternalOutput").ap(), name, True)
        return self.dram[name]

    def dint(self, name, shape, dt=F32):
        if name in _DEBUG_OUT:
            return self.dout(name, shape, dt)
        if name not in self.dram:
            self.dram[name] = T(self.nc.dram_tensor(name, list(shape), dt).ap(), name, True)
        return self.dram[name]

    def _untracked(self, x):
        if x is None:
            return Buf()
        return Buf() if (isinstance(x, T) and x.name_is_dram) else x

    def load(self, dst, dst_ap, src, src_ap, q="sp", slow=False):
        src, dst = self._untracked(src), self._untracked(dst)
        if slow:
            self.P.dma(q, lambda e: e.dma_start(out=dst_ap, in_=src_ap, allow_slow_non_contiguous=True), reads=[src], writes=[dst])
        else:
            self.P.dma(q, lambda e: e.dma_start(out=dst_ap, in_=src_ap), reads=[src], writes=[dst])

    def store(self, dst, dst_ap, src, src_ap, q="pool"):
        src, dst = self._untracked(src), self._untracked(dst)
        self.P.dma(q, lambda e: e.dma_start(out=dst_ap, in_=src_ap), reads=[src], writes=[dst])

    def copy(self, eng, dst, dst_ap, src, src_ap):
        if eng == "act":
            self.P.op("act", lambda e: e.activation(out=dst_ap, in_=src_ap, func=AF.Copy), reads=[src], writes=[dst])
        else:
            self.P.op(eng, lambda e: e.tensor_copy(out=dst_ap, in_=src_ap), reads=[src], writes=[dst])

    def act(self, dst, dst_ap, src, src_ap, func, scale=1.0, accum=None, extra_w=()):
        if accum is None:
            self.P.op("act", lambda e: e.activation(out=dst_ap, in_=src_ap, func=func, scale=scale),
                      reads=[src], writes=[dst] + list(extra_w))
        else:
            self.P.op("act", lambda e: e.activation(out=dst_ap, in_=src_ap, func=func, scale=scale, accum_out=accum),
                      reads=[src], writes=[dst] + list(extra_w))

    def tt(self, eng, dst, dst_ap, a, a_ap, b, b_ap, op):
        self.P.op(eng, lambda e: e.tensor_tensor(out=dst_ap, in0=a_ap, in1=b_ap, op=op), reads=[a, b], writes=[dst])

    def ts(self, eng, dst, dst_ap, a, a_ap, s1, s2, op0, op1=None, sreads=()):
        if op1 is None:
            self.P.op(eng, lambda e: e.tensor_scalar(out=dst_ap, in0=a_ap, scalar1=s1, scalar2=None, op0=op0),
                      reads=[a] + list(sreads), writes=[dst])
        else:
            self.P.op(eng, lambda e: e.tensor_scalar(out=dst_ap, in0=a_ap, scalar1=s1, scalar2=s2, op0=op0, op1=op1),
                      reads=[a] + list(sreads), writes=[dst])

    def recip(self, dst, dst_ap, src, src_ap):
        self.P.op("dve", lambda e: e.reciprocal(out=dst_ap, in_=src_ap), reads=[src], writes=[dst])

    def memset(self, eng, dst, dst_ap, val):
        self.P.op(eng, lambda e: e.memset(dst_ap, val), writes=[dst])

    def transposes(self, dst_ps, src, src_aps, ident):
        def fn(e):
            for i, ap in enumerate(src_aps):
                ins = e.transpose(out=dst_ps.t[:, i, :], in_=ap, identity=ident.t[:])
            return ins
        self.P.op("pe", fn, reads=[src, ident], writes=[dst_ps])

    def matmuls(self, dst, specs, reads):
        def fn(e):
            for sp_ in specs:
                o, l, r, st, sp = sp_[:5]
                if len(sp_) > 5 and sp_[5]:
                    ins = e.matmul(o, lhsT=l, rhs=r, start=st, stop=sp, skip_group_check=True)
                else:
                    ins = e.matmul(o, lhsT=l, rhs=r, start=st, stop=sp)
            return ins
        self.P.op("pe", fn, reads=reads, writes=(dst if isinstance(dst, (list, tuple)) else [dst]))

    def setup_consts(self):
        P = self.P
        idf = P.sb([128, 128], F32, "idf")
        self.ident = P.sb([128, 128], BF16, "ident")
        d = self.din("ident", [128, 128])
        self.load(idf, idf.t[:], d, d.t)
        self.copy("dve", self.ident, self.ident.t[:], idf, idf.t[:])
        self.ones_row = P.sb([1, 128], F32, "ones_row")
        self.memset("dve", self.ones_row, self.ones_row.t[:], 1.0)

    def mod_vectors(self, l, need_ab, need_gate):
        P = self.P
        wmod = self.din(f"wmod{l}", [D, 3 * D])
        bT = self.din(f"bmodT{l}", [128, 24])
        res = {}
        if not hasattr(self, "silu_c"):
            cc = self.din("ccol", [128, 8])
            ct = P.sb([128, 8], F32, "ccol")
            self.load(ct, ct.t[:], cc, cc.t)
            self.silu_c = P.sb([128, 8], F32, "siluc")
            self.act(self.silu_c, self.silu_c.t[:], ct, ct.t[:], AF.Silu)
        sc = self.silu_c
        bt = P.sb([128, 24], F32, "bT")
        self.load(bt, bt.t[:], bT, bT.t)
        if need_ab:
            gT = self.din(f"ngT{l}", [128, 8])
            gt = P.sb([128, 8], F32, "gT")
            self.load(gt, gt.t[:], gT, gT.t)
            modT = P.sb([128, 16], F32, "modT")
            pg = self.pg[2]
            for g4 in range(8):
                st = self.wm_stage.next()
                self.load(st, st.t[:], wmod, wmod.t[:, g4 * 256:(g4 + 1) * 256].rearrange("(c p) n -> p c n", p=128),
                          q=self.dq.next())
                specs = []
                for jj in range(2):
                    j = g4 * 2 + jj
                    for k in range(8):
                        specs.append((pg.t[:, j:j + 1], st.t[:, k, jj * 128:(jj + 1) * 128], sc.t[:, k:k + 1], k == 0, k == 7))
                self.matmuls(pg, specs, [st, sc])
            self.tt("dve", modT, modT.t[:], pg, pg.t[:, 0:16], bt, bt.t[:, 0:16], ALU.add)
            A = P.sb([128, 8], F32, "A")
            self.P.op("dve", lambda e: e.scalar_tensor_tensor(out=A.t[:], in0=modT.t[:, 8:16], scalar=1.0, in1=gt.t[:],
                                                               op0=ALU.add, op1=ALU.mult), reads=[modT, gt], writes=[A])
            res["A"] = A
            res["B"] = modT
        if need_gate:
            bg = self.din(f"bmodg{l}", [1, D])
            bgt = P.sb([1, D], F32, "bg")
            self.load(bgt, bgt.t[:], bg, bg.t)
            grow = P.sb([1, D], F32, "grow")
            gbc = P.sb([128, D], F32, "gbc")
            for g2 in range(4):
                st = self.wm_stage.next()
                c0 = 2048 + g2 * 256
                self.load(st, st.t[:], wmod, wmod.t[:, c0:c0 + 256].rearrange("(c p) n -> p c n", p=128), q=self.dq.next())
                pg = self.pg[g2 % 2]
                specs = [(pg.t[0:1, 0:256], sc.t[:, k:k + 1], st.t[:, k, :], k == 0, k == 7) for k in range(8)]
                self.matmuls(pg, specs, [st, sc])
                self.tt("dve", grow, grow.t[:, g2 * 256:(g2 + 1) * 256], pg, pg.t[0:1, 0:256], bgt, bgt.t[:, g2 * 256:(g2 + 1) * 256], ALU.add)
            for g2 in range(2):
                pg = self.pg[g2]
                self.matmuls(pg, [(pg.t[:], self.ones_row.t[:], grow.t[:, g2 * 512:(g2 + 1) * 512], True, True)],
                             [self.ones_row, grow])
                self.copy("act", gbc, gbc.t[:, g2 * 512:(g2 + 1) * 512], pg, pg.t[:])
            res["gate_bc"] = gbc
        return res

    def load_weight(self, wsb, wdram, c0, ncols, scale_bc=None):
        P = self.P
        for g in range(0, ncols, 256):
            n = min(256, ncols - g)
            st = self.w_stage.next()
            self.load(st, st.t[:, :, 0:n], wdram, wdram.t[:, c0 + g:c0 + g + n].rearrange("(c p) n -> p c n", p=128),
                      q=self.dq.next())
            if scale_bc is None:
                self.copy("dve" if (g // 256) % 2 == 0 else "act", wsb.part(g), wsb.t[:, :, g:g + n], st, st.t[:, :, 0:n])
            else:
                for k in range(8):
                    self.tt("dve", wsb.part(g), wsb.t[:, k, g:g + n], st, st.t[:, k, 0:n], scale_bc, scale_bc.t[:, g:g + n], ALU.mult)

    def phase_a(self, ntiles, xsrc, xrow0, pdst, prow0, pcol0, wsb, ncols, A, Bm, cs, csrow0, hd, rope_cols, also=None, hcache=None):
        P = self.P
        if not hasattr(self, "pa_bufs"):
            self.pa_bufs = dict(
                tA=Rot([P.sb([128, 512], F32, "patA") for _ in range(3)]),
                tB=Rot([P.sb([128, 512], F32, "patB") for _ in range(3)]),
                csF=Rot([P.sb([128, 2, 512], F32, "pacsF") for _ in range(3)]),
            )
        bufs = self.pa_bufs
        h2 = hd // 2
        nhf = 512 // hd

        hc_tiles, hc_mode = hcache if hcache is not None else (None, None)

        def prep_a(i):
            if hc_mode == "r":
                csF = None
                if rope_cols > 0:
                    csF = bufs["csF"].next()
                    self.load(csF, csF.t[:], cs, cs.t[csrow0 + i * 128: csrow0 + (i + 1) * 128, :, :], q="sp")
                return None, csF, i
            xt = self.f32p.next()
            self.load(xt, xt.t[:], xsrc, xsrc.t[xrow0 + i * 128: xrow0 + (i + 1) * 128, :], q="sp")
            ss = self.ssp.next()
            sq = self.f32p.next()
            self.act(sq, sq.t[:], xt, xt.t[:], AF.Square, accum=ss.t[:, 0:1], extra_w=[ss])
            self.ts("dve", ss, ss.t[:, 1:2], ss, ss.t[:, 0:1], 1.0 / D, NORM_EPS, ALU.mult, ALU.add)
            self.act(ss, ss.t[:, 2:3], ss, ss.t[:, 1:2], AF.Sqrt)
            self.recip(ss, ss.t[:, 3:4], ss, ss.t[:, 2:3])
            xn = self.b1p.next()
            self.ts("dve", xn, xn.t[:], xt, xt.t[:], ss.t[:, 3:4], None, ALU.mult, sreads=[ss])
            csF = None
            if rope_cols > 0:
                csF = bufs["csF"].next()
                self.load(csF, csF.t[:], cs, cs.t[csrow0 + i * 128: csrow0 + (i + 1) * 128, :, :], q="sp")
            return xn, csF, i

        def prep_b(pa):
            xn, csF, i = pa
            if hc_mode == "r":
                hT = self.t3p.next()
                self.load(hT, hT.t[:], hc_tiles[i], hc_tiles[i].t, q="sp")
                cosF = sinS = None
                if csF is not None:
                    cosF = T(csF.t[:, 0, :], "cosF", buf=csF.b)
                    sinS = T(csF.t[:, 1, :], "sinS", buf=csF.b)
                return hT, cosF, sinS
            ptr = self.ptr.next()
            self.transposes(ptr, xn, [xn.t[:, c * 128:(c + 1) * 128] for c in range(8)], self.ident)
            hT = self.t3p.next()
            for c in range(8):
                if rope_cols == 0 and c % 2 == 1:
                    self.ts("dve", hT, hT.t[:, c, :], ptr, ptr.t[:, c, :], A.t[:, c:c + 1], Bm.t[:, c:c + 1],
                            ALU.mult, ALU.add, sreads=[A, Bm])
                else:
                    self._act_affine(hT, hT.t[:, c, :], ptr, ptr.t[:, c, :], A, A.t[:, c:c + 1], Bm, Bm.t[:, c:c + 1])
            if hc_mode == "w":
                self.store(hc_tiles[i], hc_tiles[i].t, hT, hT.t[:], q="pool")
            cosF = sinS = None
            if csF is not None:
                cosF = T(csF.t[:, 0, :], "cosF", buf=csF.b)
                sinS = T(csF.t[:, 1, :], "sinS", buf=csF.b)
            return hT, cosF, sinS

        def mm(i, st):
            hT, cosF, sinS = st
            for og in range(0, ncols, 2048):
                on = min(2048, ncols - og)
                ot = self.bfp.next()
                for g in range(og, og + on, 512):
                    pg = self.pgr.next()
                    specs = [(pg.t[:], hT.t[:, k, :], wsb.t[:, k, g:g + 512], k == 0, k == 7) for k in range(8)]
                    self.matmuls(pg, specs, [hT, wsb.part(g)])
                    lo = g - og
                    rc = max(0, min(512, rope_cols - g))
                    if rc > 0:
                        tA, tB = bufs["tA"].next(), bufs["tB"].next()
                        self.tt("dve", tA, tA.t[:, 0:rc], pg, pg.t[:, 0:rc], cosF, cosF.t[:, 0:rc], ALU.mult)
                        q4 = pg.t[:, 0:rc].rearrange("p (h two d) -> p h two d", two=2, d=h2)
                        b4 = tB.t[:, 0:rc].rearrange("p (h two d) -> p h two d", two=2, d=h2)
                        s4 = sinS.t[:, 0:rc].rearrange("p (h two d) -> p h two d", two=2, d=h2)
                        self.tt("dve", tB, b4[:, :, 0, :], pg, q4[:, :, 1, :], sinS, s4[:, :, 0, :], ALU.mult)
                        self.tt("dve", tB, b4[:, :, 1, :], pg, q4[:, :, 0, :], sinS, s4[:, :, 1, :], ALU.mult)
                        self.tt("pool", ot, ot.t[:, lo:lo + rc], tA, tA.t[:, 0:rc], tB, tB.t[:, 0:rc], ALU.add)
                        if rc < 512:
                            self.copy("act", ot, ot.t[:, lo + rc:lo + 512], pg, pg.t[:, rc:512])
                    else:
                        self.copy("act", ot, ot.t[:, lo:lo + 512], pg, pg.t[:])
                self.store(pdst, pdst.t[prow0 + i * 128: prow0 + (i + 1) * 128, pcol0 + og: pcol0 + og + on], ot, ot.t[:, 0:on])
                for (adst, atile, arow, c_lo, c_hi, dcol0) in (also or ()):
                    if atile != i:
                        continue
                    lo_, hi_ = max(c_lo, og), min(c_hi, og + on)
                    if hi_ > lo_:
                        self.store(None, adst.t[arow: arow + 128, dcol0 + lo_ - c_lo: dcol0 + hi_ - c_lo],
                                   ot, ot.t[:, lo_ - og:hi_ - og], q="pool")

        pa = {0: prep_a(0)}
        if ntiles > 1:
            pa[1] = prep_a(1)
        st = prep_b(pa.pop(0))
        for i in range(ntiles):
            if i + 2 < ntiles:
                pa[i + 2] = prep_a(i + 2)
            nxt = prep_b(pa.pop(i + 1)) if i + 1 < ntiles else None
            mm(i, st)
            st = nxt
            self.P.maybe_epoch()

    def hcache_tiles(self, name, ntiles):
        full = self.dint(name, [ntiles, 128, 1024], BF16)
        return [T(full.t[i].rearrange("p (c q) -> p c q", q=128), "hc", dram=False) for i in range(ntiles)]

    def _act_affine(self, dst, dst_ap, src, src_ap, a_t, a_ap, b_t, b_ap):
        self.P.op("act", lambda e: e.activation(out=dst_ap, in_=src_ap, func=AF.Identity, bias=b_ap, scale=a_ap),
                  reads=[src, a_t, b_t], writes=[dst])

    def phase_c_tile(self, ytile, xsrc, xrow, xdst, drow, wout, final_g=None, defer=False, banks=None):
        P = self.P
        if not hasattr(self, "pc_bufs"):
            self.pc_bufs = dict(yT=self.t3p, x=self.f32p, xo=self.f32p, ss=self.ssp, sq=self.f32p)
        bufs = self.pc_bufs
        banks = banks or self.pgr
        ptr = self.ptr.next()
        self.transposes(ptr, ytile, [ytile.t[:, c * 128:(c + 1) * 128] for c in range(8)], self.ident)
        yT = bufs["yT"].next()
        self.copy("act", yT, yT.t[:], ptr, ptr.t[:])
        xt = bufs["x"].next()
        self.load(xt, xt.t[:], xsrc, xsrc.t[xrow:xrow + 128, :], q="sp")
        pgs = []
        for half in range(2):
            pg = banks.next()
            specs = [(pg.t[:], yT.t[:, k, :], wout.t[:, k, half * 512:(half + 1) * 512], k == 0, k == 7) for k in range(8)]
            self.matmuls(pg, specs, [yT, wout.part(half * 512)])
            pgs.append(pg)

        def back():
            xo = bufs["xo"].next()
            for half in range(2):
                pg = pgs[half]
                self.tt("dve", xo, xo.t[:, half * 512:(half + 1) * 512], pg, pg.t[:], xt, xt.t[:, half * 512:(half + 1) * 512], ALU.add)
            if final_g is not None:
                ss = bufs["ss"].next()
                sq = bufs["sq"].next()
                self.act(sq, sq.t[:], xo, xo.t[:], AF.Square, accum=ss.t[:, 0:1], extra_w=[ss])
                self.ts("dve", ss, ss.t[:, 1:2], ss, ss.t[:, 0:1], 1.0 / D, NORM_EPS, ALU.mult, ALU.add)
                self.act(ss, ss.t[:, 2:3], ss, ss.t[:, 1:2], AF.Sqrt)
                self.recip(ss, ss.t[:, 3:4], ss, ss.t[:, 2:3])
                self.P.op("dve", lambda e: e.scalar_tensor_tensor(out=xt.t[:], in0=xo.t[:], scalar=ss.t[:, 3:4], in1=final_g.t[:],
                                                                   op0=ALU.mult, op1=ALU.mult), reads=[xo, ss, final_g], writes=[xt])
                self.store(xdst, xdst.t[drow:drow + 128, :], xt, xt.t[:])
            else:
                self.store(xdst, xdst.t[drow:drow + 128, :], xo, xo.t[:])
        if defer:
            return back
        back()
        return None

    def conv_layer(self, p0, convk_bc, vmask, xsrc, xdst, wout):
        P = self.P
        cx = [self.bfp] * 3
        tj = [self.f32p] * 3
        bz, sz, acc, yt = self.bfp, self.f32p, self.f32p, self.b1p
        pcb = Rot([self.pg[0], self.pg[1], self.pg[2], self.po[0]])
        prev_back = None
        for i in range(NT):
            r0 = 128 + i * 128
            ts_ = []
            for j in range(3):
                c = cx[j].next()
                self.load(c, c.t[:], p0, p0.t[r0 + j - 1: r0 + j - 1 + 128, 1024:3072], q=self.dq.next())
                t = tj[j].next()
                self.tt("pool", t, t.t[:], c, c.t[:, 0:1024], c, c.t[:, 1024:2048], ALU.mult)
                if (i == 0 and j == 0):
                    self.ts("pool", t, t.t[:], t, t.t[:], vmask.t[:, 0:1], None, ALU.mult, sreads=[vmask])
                if (i == NT - 1 and j == 2):
                    self.ts("pool", t, t.t[:], t, t.t[:], vmask.t[:, 1:2], None, ALU.mult, sreads=[vmask])
                ts_.append(t)
            b = bz.next()
            self.load(b, b.t[:, 0:1024], p0, p0.t[r0:r0 + 128, 0:1024], q="sp")
            self.load(b, b.t[:, 1024:2048], p0, p0.t[r0:r0 + 128, 3072:4096], q="sp")
            s = sz.next()
            self.act(s, s.t[:], b, b.t[:, 1024:2048], AF.Silu)
            a = acc.next()
            self.tt("dve", a, a.t[:], ts_[0], ts_[0].t[:], convk_bc, convk_bc.t[:, 0, :], ALU.mult)
            self.tt("pool", ts_[1], ts_[1].t[:], ts_[1], ts_[1].t[:], convk_bc, convk_bc.t[:, 1, :], ALU.mult)
            self.tt("dve", ts_[2], ts_[2].t[:], ts_[2], ts_[2].t[:], convk_bc, convk_bc.t[:, 2, :], ALU.mult)
            self.tt("pool", a, a.t[:], a, a.t[:], ts_[1], ts_[1].t[:], ALU.add)
            self.tt("dve", a, a.t[:], a, a.t[:], ts_[2], ts_[2].t[:], ALU.add)
            self.tt("dve", s, s.t[:], s, s.t[:], b, b.t[:, 0:1024], ALU.mult)
            y = yt.next()
            self.tt("dve", y, y.t[:], a, a.t[:], s, s.t[:], ALU.mult)
            bk = self.phase_c_tile(y, xsrc, r0, xdst, i * 128, wout, defer=True, banks=pcb)
            if prev_back is not None:
                prev_back()
            prev_back = bk
            self.P.maybe_epoch()
        prev_back()

    def band_attention(self, *, pq, kv, kvalid, masks, nheads, hd, ev, qcol0, kcol0, vcol0, nkvh, gq,
                       blocks, scale, finish):
        P = self.P
        qcols = nheads * hd
        kcols = nkvh * hd
        nqt = qcols // 128
        dup = (hd == 64)
        nkt = nkvh if dup else kcols // 128
        maxch = max(len(b["chunks"]) for b in blocks)
        hb = max(1, 512 // (128 * maxch))
        key = (qcols, nkt, nkvh, ev, maxch)
        if not hasattr(self, "_ba_cache"):
            self._ba_cache = {}
        if key not in self._ba_cache:
            self._ba_cache[key] = dict(
                qrow=Rot([P.sb([128, qcols], BF16, "ba_q") for _ in range(2)]),
                qT=Rot([P.sb([128, nqt, 128], BF16, "ba_qT") for _ in range(2)]),
                krow=Rot([P.sb([128, nkt * 128], BF16, "ba_k") for _ in range(2 * maxch)]),
                kT=Rot([P.sb([128, nkt, 128], BF16, "ba_kT") for _ in range(2 * maxch)]),
                vaug=Rot([P.sb([128, nkvh, ev + 1], BF16, "ba_v") for _ in range(2 * maxch)]),
                pt=Rot([P.sb([128, hb * maxch, 128], BF16, "ba_pt") for _ in range(3)]),
                kvc=Rot([P.sb([128, 1], F32, "ba_kvc") for _ in range(2 * maxch)]))
            for v_ in self._ba_cache[key]["vaug"].items:
                self.memset("pool", v_, v_.t[:], 1.0)
        c_ = self._ba_cache[key]
        kvcr = c_["kvc"]
        qrow, qT, krow, kT, vaug, pt = c_["qrow"], c_["qT"], c_["krow"], c_["kT"], c_["vaug"], c_["pt"]
        per_bank = 512 // (ev + 1)
        nch = maxch
        assert all(len(b_["chunks"]) == nch for b_ in blocks)
        if "maskb" not in c_:
            c_["maskb"] = P.sb([128, hb * nch, 128], BF16, "ba_maskb")
        maskb = c_["maskb"]
        self.memset("pool", maskb, maskb.t[:], 0.0)
        for ii in range(hb):
            for ci in range(nch):
                mid = blocks[0]["chunks"][ci][2]
                if mid is not None:
                    self.ts("pool", maskb, maskb.t[:, ii * nch + ci, :], masks, masks.t[:, mid, :], 30000.0, -30000.0, ALU.mult, ALU.add)
        oaps = []
        for h in range(nheads):
            bank = self.po[h // per_bank]
            sl = (h % per_bank) * (ev + 1)
            oaps.append((bank, bank.t[:, sl:sl + ev + 1]))

        def prep_loads(blk):
            qr = qrow.next()
            r0, st = blk["qrow0"], blk["qstep"]
            self.load(qr, qr.t[:], pq, pq.t[r0:r0 + 127 * st + 1:st, qcol0:qcol0 + qcols], q="sp")
            vas, kvcs, krs = [], [], []
            for (k0, kst, mid, halo) in blk["chunks"]:
                kr = krow.next()
                if dup:
                    ksrc = kv.t[k0:k0 + 127 * kst + 1:kst, kcol0:kcol0 + kcols].rearrange("p (h d) -> p h d", d=hd)
                    kr4 = kr.t[:].rearrange("p (h two d) -> p h two d", two=2, d=hd)
                    self.load(kr, kr4[:, :, 0, :], kv, ksrc, q="sp")
                    self.load(kr, kr4[:, :, 1, :], kv, ksrc, q="sp")
                else:
                    self.load(kr, kr.t[:], kv, kv.t[k0:k0 + 127 * kst + 1:kst, kcol0:kcol0 + kcols], q="sp")
                va = vaug.next()
                self.load(va, va.t[:, :, 0:ev], kv,
                          kv.t[k0:k0 + 127 * kst + 1:kst, vcol0:vcol0 + nkvh * ev].rearrange("p (h e) -> p h e", e=ev), q="sp")
                if halo:
                    kc_ = kvcr.next()
                    self.load(kc_, kc_.t[:], kvalid, kvalid.t[k0:k0 + 127 * kst + 1:kst, :], q="sp", slow=True)
                    kvcs.append(kc_)
                else:
                    kvcs.append(None)
                krs.append(kr)
                vas.append(va)
            return qr, krs, vas, kvcs

        def prep_tr(ld):
            qr, krs, vas, kvcs = ld
            kts = []
            ptr = self.ptr.next()
            self.transposes(ptr, qr, [qr.t[:, c * 128:(c + 1) * 128] for c in range(nqt)], self.ident)
            qt = qT.next()
            self.copy("dve", qt, qt.t[:], ptr, ptr.t[:, 0:nqt, :])
            for kr in krs:
                ptr = self.ptr.next()
                self.transposes(ptr, kr, [kr.t[:, c * 128:(c + 1) * 128] for c in range(nkt)], self.ident)
                kt = kT.next()
                self.copy("dve", kt, kt.t[:], ptr, ptr.t[:, 0:nkt, :])
                kts.append(kt)
            return qt, kts, vas, kvcs

        def emit_qk(st, hs):
            qt, kts, vas, kvcs = st
            pg = self.pgr.next()
            specs = []
            for ii, h in enumerate(hs):
                kvh = h // gq
                qd = h * hd
                po_ = qd % 128
                for ci in range(nch):
                    col = (ii * nch + ci) * 128
                    specs.append((pg.t[:, col:col + 128], kts[ci].t[po_: po_ + hd, kvh, :], qt.t[po_: po_ + hd, qd // 128, :],
                                  len(specs) == 0, False, True))
            ncol = len(hs) * nch
            specs.append((pg.t[:, 0:ncol * 128], self.ident.t[:], maskb.t[:, 0:ncol, :].rearrange("p c q -> p (c q)"), False, True, True))
            self.matmuls(pg, specs, kts + [qt, maskb, self.ident])
            p = pt.next()
            self.act(p, p.t[:, 0:ncol, :], pg, pg.t[:, 0:ncol * 128].rearrange("p (c q) -> p c q", q=128), AF.Exp, scale=scale)
            for ii, h in enumerate(hs):
                for ci in range(nch):
                    if kvcs[ci] is not None:
                        cc = ii * nch + ci
                        self.ts("dve", p, p.t[:, cc, :], p, p.t[:, cc, :], kvcs[ci].t[:, 0:1], None, ALU.mult, sreads=[kvcs[ci]])
            return p

        def emit_pv(st, hs, p):
            qt, kts, vas, kvcs = st
            banks = []
            per = {}
            for ii, h in enumerate(hs):
                kvh = h // gq
                bank, oap = oaps[h]
                if bank not in banks:
                    banks.append(bank)
                    per[id(bank)] = []
                for ci in range(nch):
                    cc = ii * nch + ci
                    per[id(bank)].append((oap, p.t[:, cc, :], vas[ci].t[:, kvh, :], ci == 0, ci == nch - 1))
            for bank in banks:
                self.matmuls(bank, per[id(bank)], [p] + vas)

        units = [list(range(h0, min(nheads, h0 + hb))) for h0 in range(0, nheads, hb)]
        prep_at = min(len(units) - 1, max(1, len(units) // 2))
        st = prep_tr(prep_loads(blocks[0]))
        deferred = None
        for bi, blk in enumerate(blocks):
            ld = prep_loads(blocks[bi + 1]) if bi + 1 < len(blocks) else None
            nxt = None
            ps_ = [emit_qk(st, units[0])]
            for ui, hs in enumerate(units):
                if ui + 1 < len(units):
                    ps_.append(emit_qk(st, units[ui + 1]))
                emit_pv(st, hs, ps_[ui])
                if ui == prep_at - 1 and ld is not None:
                    nxt = prep_tr(ld)
            if nxt is None and ld is not None:
                nxt = prep_tr(ld)
            d_ = finish(bi, blk, oaps)
            if deferred is not None:
                deferred()
            deferred = d_
            st = nxt
            self.P.maybe_epoch()
        if deferred is not None:
            deferred()


def _group(xs, n):
    return [xs[i:i + n] for i in range(0, len(xs), n)]


def d2d(kb, dst, dst_ap, src, src_ap, q="sp"):
    kb.load(dst, dst_ap, src, src_ap, q=q)


CC_MAX_BYTES = 2 << 20


def exchange_alloc(kb, name, pieces, cols):
    rmax = max(1, CC_MAX_BYTES // (cols * 2))
    jobs = []
    ci = 0
    for pi, (rows, d0, d1) in enumerate(pieces):
        for r0 in range(0, rows, rmax):
            n = min(rmax, rows - r0)
            sb_ = kb.dint(f"{name}_s{ci}", [n, cols], BF16)
            db_ = kb.dint(f"{name}_d{ci}", [2 * n, cols], BF16)
            sb_.name_is_dram = False
            db_.name_is_dram = False
            ci += 1
            jobs.append(dict(src=sb_, dst=db_, n=n, r0=r0, d0=d0, d1=d1, piece=pi))
    return jobs


def exchange_also(jobs, piece, tile0, c_lo, c_hi, dcol0):
    out = []
    for jb in jobs:
        if jb["piece"] != piece:
            continue
        for t in range(jb["n"] // 128):
            out.append((jb["src"], tile0 + jb["r0"] // 128 + t, t * 128, c_lo, c_hi, dcol0))
    return out


def exchange_cc(kb, jobs):
    groups = [[0, 1], [2, 3], [4, 5], [6, 7]]
    for jb in jobs:
        sb_, db_ = jb["src"], jb["dst"]
        kb.P.cc((lambda sb_=sb_, db_=db_: (lambda e: e.collective_compute(
            "AllGather", ALU.bypass, replica_groups=groups, ins=[sb_.t.opt()], outs=[db_.t.opt()])))(),
            reads=[sb_.b], writes=[db_.b])


def exchange_finish(kb, jobs):
    for jb in jobs:
        db_, n, r0 = jb["dst"], jb["n"], jb["r0"]
        if jb["d0"] is not None:
            d2d(kb, None, jb["d0"][r0:r0 + n, :], db_, db_.t[0:n, :], q=kb.dq.next())
        if jb["d1"] is not None:
            d2d(kb, None, jb["d1"][r0:r0 + n, :], db_, db_.t[n:2 * n, :], q=kb.dq.next())


def stage0(kb):
    P = kb.P
    xe = kb.din("xe", [TPC + 256, D])
    m0 = kb.mod_vectors(0, True, True)
    m1 = kb.mod_vectors(1, True, False)
    w_in0 = kb.din("w_in0", [D, 4096])
    w_out0 = kb.din("w_out0", [D, D])
    w_in1 = kb.din("w_in1", [D, 8192])
    convk = kb.din("convk_bc", [128, 3, D])
    vmask = kb.din("vmask", [128, 2])
    cs1 = kb.din("cs1", [TPC, 2, 512])
    X1 = kb.dint("X1", [TPC, D])
    P1 = kb.dint("P1", [TPC, 8192], BF16)
    p0 = kb.dint("p0", [TPC + 256, 4096], BF16)
    wsb = P.sb([128, 8, 2048], BF16, "wsb")
    wout = P.sb([128, 8, D], BF16, "wout")
    ck = P.sb([128, 3, D], F32, "convk")
    vm = P.sb([128, 2], F32, "vmask")
    kb.load(ck, ck.t[:], convk, convk.t)
    kb.load(vm, vm.t[:], vmask, vmask.t)
    kb.load_weight(wout, w_out0, 0, D, scale_bc=m0["gate_bc"])
    hc0 = kb.hcache_tiles("hc0", NT + 2)
    for ps_ in range(2):
        kb.load_weight(wsb, w_in0, ps_ * 2048, 2048)
        kb.phase_a(NT + 2, xe, 0, p0, 0, ps_ * 2048, wsb, 2048, m0["A"], m0["B"], None, 0, 128, 0,
                   hcache=(hc0, "w" if ps_ == 0 else "r"))
    P.barrier()
    kb.conv_layer(p0, ck, vm, xe, X1, wout)
    P.barrier()
    HALO = 1024
    kv = kb.dint("kv1", [TPC + 2 * HALO, 4096], BF16)
    hc1 = kb.hcache_tiles("hc1", NT)
    jobs = exchange_alloc(kb, "ex1", [
        (HALO, kv.t[0:HALO, :], None),
        (HALO, None, kv.t[HALO + TPC:, :]),
    ], 4096)
    for (c0, nc_, rc_) in ((3072, 2048, 2048), (5120, 2048, 1024)):
        kb.load_weight(wsb, w_in1, c0, nc_)
        also = [(kv, i, HALO + i * 128, 0, nc_, c0 - 3072) for i in range(NT)]
        also += exchange_also(jobs, 0, NT - HALO // 128, 0, nc_, c0 - 3072)
        also += exchange_also(jobs, 1, 0, 0, nc_, c0 - 3072)
        kb.phase_a(NT, X1, 0, P1, 0, c0, wsb, nc_, m1["A"], m1["B"], cs1, 0, 128, rc_, also=also,
                   hcache=(hc1, "w" if c0 == 3072 else "r"))
    P.barrier()
    exchange_cc(kb, jobs)
    for (c0, nc_, rc_) in ((0, 2048, 2048), (2048, 1024, 1024), (7168, 1024, 0)):
        kb.load_weight(wsb, w_in1, c0, nc_)
        kb.phase_a(NT, X1, 0, P1, 0, c0, wsb, nc_, m1["A"], m1["B"], cs1, 0, 128, rc_, hcache=(hc1, "r"))
    exchange_finish(kb, jobs)
    P.barrier()


def stage1(kb):
    P = kb.P
    HALO = 1024
    X1 = kb.dram["X1"]
    P1 = kb.dram["P1"]
    kv = kb.dram["kv1"]
    kvalid = kb.din("kvalid1", [TPC + 2 * HALO, 1])
    masksd = kb.din("masks", [128, 2, 128], BF16)
    m1 = kb.mod_vectors(1, False, True)
    m2 = kb.mod_vectors(2, True, False)
    w_out1 = kb.din("w_out1", [D, D])
    w_in2 = kb.din("w_in2", [D, 2560])
    cs2 = kb.din("cs2", [TPC, 2, 512])
    X2 = kb.dint("X2", [TPC, D])
    P2 = kb.dint("P2", [TPC, 2560], BF16)
    og = [kb.dint(f"og{g}", [TPC, 8 * 129]) for g in range(3)]
    wout = P.sb([128, 8, D], BF16, "wout")
    masks = P.sb([128, 2, 128], BF16, "masks")
    kb.load(masks, masks.t[:], masksd, masksd.t)
    kb.load_weight(wout, w_out1, 0, D, scale_bc=m1["gate_bc"])
    P.barrier()
    osb = Rot([P.sb([128, 8 * 129], F32, "osb") for _ in range(2)])
    for g, r in enumerate((1, 4, 16)):
        blocks = []
        for p in range(r):
            for j in range(TPC // (128 * r)):
                i0 = 128 * j
                q0 = p + r * i0
                ka = HALO + p + r * (i0 - 64)
                kb_ = HALO + p + r * (i0 + 64)
                ha = ka < HALO
                hb_ = kb_ + 127 * r >= HALO + TPC
                blocks.append(dict(qrow0=q0, qstep=r, chunks=[(ka, r, 0, ha), (kb_, r, 1, hb_)]))

        def finish(bi, blk, oaps, g=g, r=r):
            o = osb.next()
            for bnk in range(3):
                n = min(3, 8 - 3 * bnk) * 129
                kb.copy("act" if bnk == 1 else "dve", o, o.t[:, bnk * 387: bnk * 387 + n], kb.po[bnk], kb.po[bnk].t[:, 0:n])
            r0 = blk["qrow0"]
            kb.store(og[g], og[g].t[r0:r0 + 127 * r + 1:r, :], o, o.t[:])
        kb.band_attention(pq=P1, kv=kv, kvalid=kvalid, masks=masks, nheads=8, hd=128, ev=128,
                          qcol0=g * 1024, kcol0=g * 1024, vcol0=3072, nkvh=8, gq=1,
                          blocks=blocks, scale=128 ** -0.5, finish=finish)
    P.barrier()
    zt = Rot([P.sb([128, D], BF16, "zt") for _ in range(2)])
    rl = Rot([P.sb([128, 8], F32, "rl") for _ in range(2)])
    o3 = Rot(osb.items + [P.sb([128, 8 * 129], F32, "o3")])
    pcb = Rot([kb.pg[0], kb.pg[1], kb.pg[2], kb.po[0]])
    prev_back = None
    for i in range(NT):
        os_ = []
        for g in range(3):
            o = o3.next()
            kb.load(o, o.t[:], og[g], og[g].t[i * 128:(i + 1) * 128, :], q=kb.dq.next())
            os_.append(o)
        z = zt.next()
        kb.load(z, z.t[:], P1, P1.t[i * 128:(i + 1) * 128, 7168:8192], q="sp")
        kb.tt("pool", os_[0], os_[0].t[:], os_[0], os_[0].t[:], os_[1], os_[1].t[:], ALU.add)
        kb.tt("dve", os_[0], os_[0].t[:], os_[0], os_[0].t[:], os_[2], os_[2].t[:], ALU.add)
        o4 = os_[0].t[:].rearrange("p (h e) -> p h e", e=129)
        rr = rl.next()
        kb.recip(rr, rr.t[:], os_[0], o4[:, :, 128])
        sz = kb.f32p.next()
        kb.act(sz, sz.t[:], z, z.t[:], AF.Silu)
        on = kb.f32p.next()
        kb.tt("dve", on, on.t[:].rearrange("p (h e) -> p h e", e=128), os_[0], o4[:, :, 0:128],
              rr, rr.t[:].unsqueeze(2).to_broadcast([128, 8, 128]), ALU.mult)
        y = kb.b1p.next()
        kb.tt("pool", y, y.t[:], on, on.t[:], sz, sz.t[:], ALU.mult)
        bk = kb.phase_c_tile(y, X1, i * 128, X2, i * 128, wout, defer=True, banks=pcb)
        if prev_back is not None:
            prev_back()
        prev_back = bk
        P.maybe_epoch()
    prev_back()
    P.barrier()
    wsb = P.sb([128, 8, 2048], BF16, "wsb")
    kb.load_weight(wsb, w_in2, 0, 2048)
    H2 = 128
    kv2 = kb.dint("kv2", [TPC + 2 * H2, 512], BF16)
    jobs = exchange_alloc(kb, "ex2", [(H2, kv2.t[0:H2, :], None), (H2, None, kv2.t[H2 + TPC:, :])], 512)
    also = [(kv2, i, H2 + i * 128, 1024, 1536, 0) for i in range(NT)]
    also += exchange_also(jobs, 0, NT - 1, 1024, 1536, 0)
    also += exchange_also(jobs, 1, 0, 1024, 1536, 0)
    hc2 = kb.hcache_tiles("hc2", NT)
    kb.phase_a(NT, X2, 0, P2, 0, 0, wsb, 2048, m2["A"], m2["B"], cs2, 0, 64, 1280, also=also, hcache=(hc2, "w"))
    P.barrier()
    exchange_cc(kb, jobs)
    kb.load_weight(wsb, w_in2, 2048, 512)
    kb.phase_a(NT, X2, 0, P2, 0, 2048, wsb, 512, m2["A"], m2["B"], cs2, 0, 64, 0, hcache=(hc2, "r"))
    exchange_finish(kb, jobs)
    P.barrier()


def stage2(kb):
    P = kb.P
    HALO = 128
    X2 = kb.dram["X2"]
    P2 = kb.dram["P2"]
    kv = kb.dram["kv2"]
    kvalid = kb.din("kvalid2", [TPC + 2 * HALO, 1])
    masksd = kb.din("masks", [128, 2, 128], BF16)
    sinkd = kb.din("sink_bc", [128, 16])
    m2 = kb.mod_vectors(2, False, True)
    m3 = kb.mod_vectors(3, True, False)
    w_out2 = kb.din("w_out2", [D, D])
    w_in3 = kb.din("w_in3", [D, 4096])
    cs3 = kb.din("cs2", [TPC, 2, 512])
    X3 = kb.dint("X3", [TPC, D])
    P3 = kb.dint("P3", [TPC, 4096], BF16)
    wout = P.sb([128, 8, D], BF16, "wout")
    masks = P.sb([128, 2, 128], BF16, "masks")
    kb.load(masks, masks.t[:], masksd, masksd.t)
    esink = P.sb([128, 16], F32, "esink")
    kb.load(esink, esink.t[:], sinkd, sinkd.t)
    kb.act(esink, esink.t[:], esink, esink.t[:], AF.Exp)
    kb.load_weight(wout, w_out2, 0, D, scale_bc=m2["gate_bc"])
    P.barrier()
    blocks = []
    for j in range(NT):
        q0 = 128 * j
        blocks.append(dict(qrow0=q0, qstep=1, chunks=[(HALO + q0 - 128, 1, 0, j == 0), (HALO + q0, 1, None, False),
                                                      (HALO + q0 + 128, 1, 1, j == NT - 1)]))
    zt = Rot([P.sb([128, D], BF16, "zt") for _ in range(2)])
    lt = Rot([P.sb([128, 16], F32, "lt") for _ in range(2)])

    oev = Rot([P.sb([128, 16 * 65], F32, "oev") for _ in range(2)])

    def finish(bi, blk, oaps):
        r0 = blk["qrow0"]
        oe = oev.next()
        for bnk in range(3):
            nh = min(7, 16 - 7 * bnk)
            kb.copy("act" if bnk == 1 else "dve", oe, oe.t[:, 7 * bnk * 65:(7 * bnk + nh) * 65], kb.po[bnk], kb.po[bnk].t[:, 0:nh * 65])

        def rest():
            z = zt.next()
            kb.load(z, z.t[:], P2, P2.t[r0:r0 + 128, 1536:2560], q="sp")
            sz = kb.f32p.next()
            kb.act(sz, sz.t[:], z, z.t[:], AF.Silu)
            l = lt.next()
            on = kb.f32p.next()
            o3_ = oe.t[:].rearrange("p (h e) -> p h e", e=65)
            kb.tt("dve", l, l.t[:], oe, o3_[:, :, 64], esink, esink.t[:], ALU.add)
            kb.recip(l, l.t[:], l, l.t[:])
            kb.tt("dve", on, on.t[:].rearrange("p (h e) -> p h e", e=64), oe, o3_[:, :, 0:64],
                  l, l.t[:].unsqueeze(2).to_broadcast([128, 16, 64]), ALU.mult)
            y = kb.b1p.next()
            kb.tt("pool", y, y.t[:], on, on.t[:], sz, sz.t[:], ALU.mult)
            kb.phase_c_tile(y, X2, r0, X3, r0, wout)
        return rest

    kb.band_attention(pq=P2, kv=kv, kvalid=kvalid, masks=masks, nheads=16, hd=64, ev=64,
                      qcol0=0, kcol0=0, vcol0=256, nkvh=4, gq=4, blocks=blocks, scale=64 ** -0.5, finish=finish)
    P.barrier()
    wsb = P.sb([128, 8, 2048], BF16, "wsb")
    kb.load_weight(wsb, w_in3, 1024, 2048)
    jobs = exchange_alloc(kb, "ex3", [(TPC, None, None)], 2048)
    kb.shared["ex3_jobs"] = jobs
    hc3 = kb.hcache_tiles("hc3", NT)
    kb.phase_a(NT, X3, 0, P3, 0, 1024, wsb, 2048, m3["A"], m3["B"], cs3, 0, 64, 1024,
               also=exchange_also(jobs, 0, 0, 0, 2048, 0), hcache=(hc3, "w"))
    P.barrier()
    exchange_cc(kb, jobs)
    kb.load_weight(wsb, w_in3, 0, 1024)
    kb.phase_a(NT, X3, 0, P3, 0, 0, wsb, 1024, m3["A"], m3["B"], cs3, 0, 64, 1024, hcache=(hc3, "r"))
    kb.load_weight(wsb, w_in3, 3072, 1024)
    kb.phase_a(NT, X3, 0, P3, 0, 3072, wsb, 1024, m3["A"], m3["B"], cs3, 0, 64, 0, hcache=(hc3, "r"))
    exchange_finish(kb, jobs)
    P.barrier()


def stage3(kb):
    P = kb.P
    X3 = kb.dram["X3"]
    P3 = kb.dram["P3"]
    ex3 = kb.shared["ex3_jobs"]
    lamd = kb.din("lam_bc", [128, 4, 64])
    sgd = kb.din("subln_bc", [128, 128])
    fgd = kb.din("finalg_bc", [128, D])
    m3 = kb.mod_vectors(3, False, True)
    w_out3 = kb.din("w_out3", [D, D])
    OUT = kb.dout("OUT", [TPC, D])
    Y = kb.dint("Y3", [TPC, D], BF16)
    wout = P.sb([128, 8, D], BF16, "wout")
    kb.load_weight(wout, w_out3, 0, D, scale_bc=m3["gate_bc"])
    lam_init = 0.8 - 0.6 * math.exp(-0.3 * 3)
    lv = P.sb([128, 4, 64], F32, "lv")
    kb.load(lv, lv.t[:], lamd, lamd.t)
    lp = P.sb([128, 2, 64], F32, "lp")
    ls = P.sb([128, 4], F32, "ls")
    kb.tt("dve", lp, lp.t[:, 0, :], lv, lv.t[:, 0, :], lv, lv.t[:, 1, :], ALU.mult)
    kb.tt("dve", lp, lp.t[:, 1, :], lv, lv.t[:, 2, :], lv, lv.t[:, 3, :], ALU.mult)
    P.op("dve", lambda e: e.reduce_sum(out=ls.t[:, 0:2], in_=lp.t[:], axis=AX.X), reads=[lp.b], writes=[ls.b])
    kb.act(ls, ls.t[:, 0:2], ls, ls.t[:, 0:2], AF.Exp)
    kb.tt("dve", ls, ls.t[:, 2:3], ls, ls.t[:, 0:1], ls, ls.t[:, 1:2], ALU.subtract)
    kb.ts("dve", ls, ls.t[:, 3:4], ls, ls.t[:, 2:3], lam_init, -1.0, ALU.add, ALU.mult)
    sg = P.sb([128, 128], F32, "sg")
    kb.load(sg, sg.t[:], sgd, sgd.t)
    kb.ts("dve", sg, sg.t[:], sg, sg.t[:], 1.0 - lam_init, None, ALU.mult)
    fg = P.sb([128, D], F32, "fg")
    kb.load(fg, fg.t[:], fgd, fgd.t)

    NKC = S // 128
    kTh = Rot([P.sb([128, S], BF16, "kTh") for _ in range(2)])
    vah = Rot([P.sb([128, NKC, 129], BF16, "vah") for _ in range(2)])
    for v_ in vah.items:
        kb.memset("pool", v_, v_.t[:], 1.0)
    krow = Rot([P.sb([128, 8, 128], BF16, "krow") for _ in range(2)])
    qrow = Rot([P.sb([128, 4, 128], BF16, "qrow") for _ in range(2)])
    qTt = Rot([P.sb([128, 512], BF16, "qTt") for _ in range(2)])
    ptile = Rot([P.sb([128, 2, 512], BF16, "ptile") for _ in range(3)])
    zt = Rot([P.sb([128, 4, 128], BF16, "zt") for _ in range(2)])
    szt = Rot([P.sb([128, 4, 128], F32, "szt") for _ in range(2)])
    ezt = Rot([P.sb([128, 4, 128], F32, "ezt") for _ in range(2)])
    yh = Rot([P.sb([128, 4, 128], BF16, "yh") for _ in range(2)])
    sm = Rot([P.sb([128, 24], F32, "sm") for _ in range(2)])
    A0r = Rot([P.sb([128, 4, 129], F32, "A0") for _ in range(2)])
    A1r = Rot([P.sb([128, 4, 129], F32, "A1") for _ in range(2)])
    w0 = Rot([P.sb([128, 4, 128], F32, "w0") for _ in range(2)])
    w1 = Rot([P.sb([128, 4, 128], F32, "w1") for _ in range(2)])
    scr = Rot(kb.sc)
    acc = kb.acc
    scale = 64 ** -0.5
    sgb = sg.t[:].unsqueeze(1).to_broadcast([128, 4, 128])

    def emit_qk(kT, qT, kc):
        sc = scr.next()
        specs = [(sc.t[:, comp, :], kT.t[64 * comp:64 * comp + 64, kc * 128:(kc + 1) * 128], qT.t[64 * comp:64 * comp + 64, :], True, True)
                 for comp in range(2)]
        kb.matmuls(sc, specs, [kT, qT])
        pt = ptile.next()
        kb.act(pt, pt.t[:], sc, sc.t[:], AF.Exp, scale=scale)
        return pt

    def emit_pv(pt, va, kc):
        specs = []
        for comp in range(2):
            for j in range(4):
                bank = acc[2 * comp + j // 2]
                specs.append((bank.t[:, (j % 2) * 129:(j % 2) * 129 + 129], pt.t[:, comp, j * 128:(j + 1) * 128],
                              va.t[:, kc, :], kc == 0 and j % 2 == 0, kc == NKC - 1 and j % 2 == 1, True))
        kb.matmuls(list(acc), specs, [pt, va])

    def k_prep(h, kT, c8):
        kr = krow.next()
        r = c8 // 4
        for j in range(2):
            db_ = ex3[2 * (c8 % 4) + j]["dst"]
            kb.load(kr, kr.t[:, 4 * j:4 * j + 4, :], db_,
                    db_.t[r * 512:(r + 1) * 512, h * 128:(h + 1) * 128].rearrange("(c p) e -> p c e", p=128), q="sp")
        ptr = kb.ptr.next()
        kb.transposes(ptr, kr, [kr.t[:, c, :] for c in range(8)], kb.ident)
        kb.copy("dve", kT, kT.t[:, c8 * 1024:(c8 + 1) * 1024], ptr, ptr.t[:].rearrange("p c q -> p (c q)"))

    def v_load(h, va):
        for r in range(2):
            for c in range(8):
                db_ = ex3[c]["dst"]
                b0 = r * 32 + c * 4
                kb.load(va, va.t[:, b0:b0 + 4, 0:128], db_,
                        db_.t[r * 512:(r + 1) * 512, 1024 + h * 128: 1024 + (h + 1) * 128].rearrange("(c p) e -> p c e", p=128), q="sp")

    heads = [(kTh.next(), vah.next()) for _ in range(8)]
    v_load(0, heads[0][1])
    for c8 in range(NKC // 8):
        k_prep(0, heads[0][0], c8)
    def q_prep(h, qc):
        qr = qrow.next()
        kb.load(qr, qr.t[:], P3, P3.t[qc * 512:(qc + 1) * 512, h * 128:(h + 1) * 128].rearrange("(c p) e -> p c e", p=128), q="sp")
        z = zt.next()
        kb.load(z, z.t[:], P3, P3.t[qc * 512:(qc + 1) * 512, 3072 + h * 128: 3072 + (h + 1) * 128].rearrange("(c p) e -> p c e", p=128), q="sp")
        ptr = kb.ptr.next()
        kb.transposes(ptr, qr, [qr.t[:, c, :] for c in range(4)], kb.ident)
        qT = qTt.next()
        kb.copy("dve", qT, qT.t[:], ptr, ptr.t[:, 0:4, :].rearrange("p c q -> p (c q)"))
        return qT, z

    def silu_z(z):
        ez = ezt.next()
        kb.act(ez, ez.t[:], z, z.t[:], AF.Exp, scale=-1.0)
        kb.ts("dve", ez, ez.t[:], ez, ez.t[:], 1.0, None, ALU.add)
        sz = szt.next()
        kb.recip(sz, sz.t[:], ez, ez.t[:])
        kb.tt("pool", sz, sz.t[:], sz, sz.t[:], z, z.t[:], ALU.mult)
        return sz

    def epilogue(h, qc, A0, A1, sz):
        s_ = sm.next()
        kb.recip(s_, s_.t[:, 0:4], A0, A0.t[:, :, 128])
        kb.recip(s_, s_.t[:, 4:8], A1, A1.t[:, :, 128])
        kb.ts("dve", s_, s_.t[:, 8:12], s_, s_.t[:, 4:8], ls.t[:, 3:4], None, ALU.mult, sreads=[ls])
        a0, a1 = w0.next(), w1.next()
        kb.tt("pool", a0, a0.t[:], A0, A0.t[:, :, 0:128], s_, s_.t[:, 0:4].unsqueeze(2).to_broadcast([128, 4, 128]), ALU.mult)
        kb.tt("dve", a1, a1.t[:], A1, A1.t[:, :, 0:128], s_, s_.t[:, 8:12].unsqueeze(2).to_broadcast([128, 4, 128]), ALU.mult)
        kb.tt("pool", a0, a0.t[:], a0, a0.t[:], a1, a1.t[:], ALU.add)
        kb.tt("dve", a1, a1.t[:], a0, a0.t[:], a0, a0.t[:], ALU.mult)
        P.op("dve", lambda e: e.reduce_sum(out=s_.t[:, 12:16], in_=a1.t[:], axis=AX.X), reads=[a1.b], writes=[s_.b])
        kb.ts("dve", s_, s_.t[:, 12:16], s_, s_.t[:, 12:16], 1.0 / 128, SUBLN_EPS, ALU.mult, ALU.add)
        kb.act(s_, s_.t[:, 16:20], s_, s_.t[:, 12:16], AF.Ln)
        kb.act(s_, s_.t[:, 20:24], s_, s_.t[:, 16:20], AF.Exp, scale=-0.5)
        kb.tt("pool", a0, a0.t[:], a0, a0.t[:], s_, s_.t[:, 20:24].unsqueeze(2).to_broadcast([128, 4, 128]), ALU.mult)
        kb.tt("dve", a0, a0.t[:], a0, a0.t[:], sg, sgb, ALU.mult)
        y = yh.next()
        kb.tt("pool", y, y.t[:], a0, a0.t[:], sz, sz.t[:], ALU.mult)
        kb.store(Y, Y.t[qc * 512:(qc + 1) * 512, h * 128:(h + 1) * 128].rearrange("(c p) e -> p c e", p=128), y, y.t[:])

    items = [(h, qc) for h in range(8) for qc in range(TPC // 512)]
    cur = q_prep(*items[0])
    for ii, (h, qc) in enumerate(items):
        kT, va = heads[h]
        qT, z = cur
        sz = silu_z(z)
        pts = [emit_qk(kT, qT, 0)]
        for kc in range(NKC):
            if kc + 1 < NKC:
                pts.append(emit_qk(kT, qT, kc + 1))
            emit_pv(pts[kc], va, kc)
        if h + 1 < 8:
            if qc == 0:
                v_load(h + 1, heads[h + 1][1])
            k_prep(h + 1, heads[h + 1][0], qc)
        A0, A1 = A0r.next(), A1r.next()
        for half in range(2):
            kb.copy("dve", A0, A0.t[:, 2 * half:2 * half + 2, :], acc[half], acc[half].t[:, 0:258].rearrange("p (j e) -> p j e", e=129))
            kb.copy("dve", A1, A1.t[:, 2 * half:2 * half + 2, :], acc[2 + half], acc[2 + half].t[:, 0:258].rearrange("p (j e) -> p j e", e=129))
        if ii + 1 < len(items):
            cur = q_prep(*items[ii + 1])
        epilogue(h, qc, A0, A1, sz)
        P.maybe_epoch()
    P.barrier()
    pcb = Rot(list(kb.acc))
    prev_back = None
    for i in range(NT):
        yt = kb.b1p.next()
        kb.load(yt, yt.t[:], Y, Y.t[i * 128:(i + 1) * 128, :], q="sp")
        bk = kb.phase_c_tile(yt, X3, i * 128, OUT, i * 128, wout, final_g=fg, defer=True, banks=pcb)
        if prev_back is not None:
            prev_back()
        prev_back = bk
        P.maybe_epoch()
    prev_back()
    P.barrier()


STAGES = [stage0, stage1, stage2, stage3]
_NC_CACHE = {}


def build_program(stages=(0, 1, 2, 3)):
    key = tuple(stages)
    if key in _NC_CACHE:
        return _NC_CACHE[key]
    nc = bass.Bass("TRN2", target_bir_lowering=False)
    shared = {"dram": {}}
    with contextlib.ExitStack() as st:
        P = Prog(nc, st)
        for si in stages:
            with contextlib.ExitStack() as sst:
                P.cur_stack = sst
                kb = KB(nc, P, shared, nf32=8 if si == 0 else 5, layout="l3" if si == 3 else "std", nbf={0: 8, 3: 1}.get(si, 4))
                kb.setup_consts()
                STAGES[si](kb)
                P.barrier()
                P.emit()
    _NC_CACHE[key] = (nc, shared)
    return _NC_CACHE[key]


def _pp(v, n):
    return np.ascontiguousarray(np.asarray(v, np.float32).reshape(n, 128).T)


def _bc(v):
    v = np.asarray(v, np.float32)
    return np.ascontiguousarray(np.broadcast_to(v[None], (128,) + v.shape))


def _rope_table(dim):
    inv = (10000.0 ** (-(np.arange(0, dim, 2, dtype=np.float32) / np.float32(dim)))).astype(np.float32)
    ang = np.arange(S, dtype=np.float32)[:, None] * inv[None, :]
    cos, sin = np.cos(ang).astype(np.float32), np.sin(ang).astype(np.float32)
    nh = 512 // dim
    cosF = np.tile(np.concatenate([cos, cos], axis=1), (1, nh))
    sinS = np.tile(np.concatenate([-sin, sin], axis=1), (1, nh))
    return np.ascontiguousarray(np.stack([cosF, sinS], axis=1))


def _ext(rows, lo, hi, total):
    out = np.zeros((hi - lo,) + rows.shape[1:], rows.dtype)
    a, b = max(lo, 0), min(hi, total)
    out[a - lo:b - lo] = rows[a:b]
    return out


def kernel(x, c, norm_g, w_mod, b_mod, conv_w_in, conv_k, conv_w_out, dil_w_in, dil_w_out,
           swa_w_in, swa_sink, swa_w_out, diff_w_in, diff_lambda, diff_subln_g, diff_w_out, final_g,
           _stages=(0, 1, 2, 3), _debug_outs=()):
    f = lambda a: np.ascontiguousarray(np.asarray(a, np.float32))
    x, c = f(x), f(c)
    bf = ml_dtypes.bfloat16
    masks = np.zeros((128, 2, 128), np.float32)
    kk, qq = np.meshgrid(np.arange(128), np.arange(128), indexing="ij")
    masks[:, 0, :] = (kk >= qq)
    masks[:, 1, :] = (kk <= qq)
    masks = masks.astype(bf)
    cs128, cs64 = _rope_table(128), _rope_table(64)
    nc, shared = build_program(_stages)
    need = set(shared["dram"].keys())

    def valid(lo, hi):
        t = np.arange(lo, hi)
        return np.ascontiguousarray(((t >= 0) & (t < S)).astype(np.float32)[:, None])

    shared_in = dict(ident=np.eye(128, dtype=np.float32), masks=masks,
                     w_in0=f(conv_w_in[0]), w_out0=f(conv_w_out[0]), w_in1=f(dil_w_in[0]), w_out1=f(dil_w_out[0]),
                     w_in2=f(swa_w_in[0]), w_out2=f(swa_w_out[0]), w_in3=f(diff_w_in[0]), w_out3=f(diff_w_out[0]),
                     convk_bc=_bc(f(conv_k[0])), sink_bc=_bc(f(swa_sink[0])), lam_bc=_bc(f(diff_lambda[0])),
                     subln_bc=_bc(f(diff_subln_g[0])), finalg_bc=_bc(f(final_g)))
    for l in range(4):
        shared_in[f"wmod{l}"] = f(w_mod[l])
        shared_in[f"bmodT{l}"] = _pp(b_mod[l], 24)
        shared_in[f"ngT{l}"] = _pp(norm_g[l], 8)
        shared_in[f"bmodg{l}"] = f(b_mod[l][2048:3072]).reshape(1, D)
    cores = [(b, h) for b in range(NB) for h in range(2)]
    maps = []
    for (b, h) in cores:
        t0 = h * TPC
        d = dict(shared_in)
        d["ccol"] = _pp(c[b], 8)
        d["xe"] = _ext(x[b], t0 - 128, t0 + TPC + 128, S)
        vm = np.ones((128, 2), np.float32)
        if h == 0:
            vm[0, 0] = 0.0
        if h == 1:
            vm[127, 1] = 0.0
        d["vmask"] = vm
        d["cs1"] = np.ascontiguousarray(cs128[t0:t0 + TPC])
        d["cs2"] = np.ascontiguousarray(cs64[t0:t0 + TPC])
        d["kvalid1"] = valid(t0 - 1024, t0 + TPC + 1024)
        d["kvalid2"] = valid(t0 - 128, t0 + TPC + 128)
        maps.append({k: v for k, v in d.items() if k in need})
    res = run_bass_kernel_spmd(nc, maps, core_ids=list(range(8)))
    if _debug_outs:
        return res.results
    out = np.zeros((NB, S, D), np.float32)
    for ci, (b, h) in enumerate(cores):
        out[b, h * TPC:(h + 1) * TPC] = res.results[ci]["OUT"]
    return out
```

```python
import contextlib
import math
import numpy as np
import ml_dtypes
import concourse.bass as bass
import concourse.mybir as mybir
from concourse.bass_utils import run_bass_kernel_spmd

F32 = mybir.dt.float32
BF16 = mybir.dt.bfloat16
ALU = mybir.AluOpType
AF = mybir.ActivationFunctionType
AX = mybir.AxisListType

D = 1024
S = 8192
NB = 4
TPC = 4096
NT = TPC // 128
ENGS = ("pe", "act", "dve", "pool", "sp")
EPOCH_LIMIT = 20000
NDMASEM = 16
NCCSEM = 8
NORM_EPS = 1e-6
_DEBUG_OUT = set()
SUBLN_EPS = 1e-5


class Buf:
    __slots__ = ("name", "w", "r")

    def __init__(self, name=""):
        self.name = name
        self.w = None
        self.r = []


class T:
    __slots__ = ("t", "b", "name_is_dram", "parts")

    def __init__(self, t, name="", dram=False, buf=None):
        self.t = t
        self.b = buf if buf is not None else Buf(name)
        self.name_is_dram = dram
        self.parts = None

    def part(self, col):
        if self.parts is None:
            self.parts = {}
        k = col // 512
        if k not in self.parts:
            self.parts[k] = T(self.t, "part", buf=Buf())
        return self.parts[k]


class Prog:
    def __init__(self, nc, stack):
        self.nc = nc
        self.stack = stack
        self.q = {e: [] for e in ENGS}
        self.cnt = {e: 0 for e in ENGS}
        self.nsem = 0
        self.esem = {e: self._newsem("c_" + e) for e in ENGS}
        self.waited = {e: {} for e in ENGS}
        self.dsem = {e: [self._newsem(f"d_{e}{i}") for i in range(NDMASEM)] for e in ("sp", "pool", "act")}
        self.dval = {e: [0] * NDMASEM for e in self.dsem}
        self.dnext = {e: 0 for e in self.dsem}
        self.latest = {}
        self.nid = 0
        self.cur_stack = stack
        self.csem = [self._newsem(f"cc{i}") for i in range(NCCSEM)]
        self.cval = [0] * NCCSEM
        self.cnext = 0

    def _newsem(self, name):
        self.nsem += 1
        return self.stack.enter_context(self.nc.semaphore(name + f"_{self.nsem}"))

    def sb(self, shape, dt, name=None):
        self.nid += 1
        name = (name or "t") + f"_{self.nid}"
        return T(self.cur_stack.enter_context(self.nc.sbuf_tensor(name, list(shape), dt)), name)

    def ps(self, shape, dt, name=None):
        self.nid += 1
        name = (name or "p") + f"_{self.nid}"
        return T(self.cur_stack.enter_context(self.nc.psum_tensor(name, list(shape), dt)), name)

    def _deps(self, eng, reads, writes):
        deps = {}

        def add(tok):
            if tok is None:
                return
            s, v = tok
            if eng == "pe" and s is self.esem["pe"]:
                return
            k = id(s)
            if self.waited[eng].get(k, 0) >= v:
                return
            if k not in deps or deps[k][1] < v:
                deps[k] = (s, v)
        for b in reads:
            add(b.w)
        for b in writes:
            add(b.w)
            for t in b.r:
                add(t)
        out = list(deps.values())
        for s, v in out:
            self.waited[eng][id(s)] = v
        return out

    def _commit(self, tok, reads, writes):
        for b in reads:
            b.r = [t for t in b.r if t[0] is not tok[0]] + [tok]
        for b in writes:
            b.w = tok
            b.r = []
        self.latest[id(tok[0])] = tok

    @staticmethod
    def _bufs(xs):
        return [x.b if isinstance(x, T) else x for x in xs]

    def op(self, eng, fn, reads=(), writes=()):
        reads, writes = self._bufs(reads), self._bufs(writes)
        deps = self._deps(eng, reads, writes)
        self.cnt[eng] += 1
        tok = (self.esem[eng], self.cnt[eng])
        self.q[eng].append((deps, fn, tok[0], 1))
        self._commit(tok, reads, writes)
        return tok

    def dma(self, eng, fn, reads=(), writes=()):
        reads, writes = self._bufs(reads), self._bufs(writes)
        i = self.dnext[eng]
        self.dnext[eng] = (i + 1) % NDMASEM
        s = self.dsem[eng][i]
        deps = self._deps(eng, reads, writes)
        prev = self.dval[eng][i]
        if prev > 0 and self.waited[eng].get(id(s), 0) < prev:
            deps.append((s, prev))
            self.waited[eng][id(s)] = prev
        self.dval[eng][i] = prev + 16
        tok = (s, prev + 16)
        self.q[eng].append((deps, fn, s, 16))
        self._commit(tok, reads, writes)
        return tok

    def cc(self, fn, reads=(), writes=()):
        i = self.cnext
        self.cnext = (i + 1) % NCCSEM
        s = self.csem[i]
        deps = self._deps("pool", list(reads), list(writes))
        prev = self.cval[i]
        if prev > 0 and self.waited["pool"].get(id(s), 0) < prev:
            deps.append((s, prev))
            self.waited["pool"][id(s)] = prev
        self.cval[i] = prev + 1
        tok = (s, prev + 1)
        self.q["pool"].append((deps, fn, s, 1))
        self._commit(tok, list(reads), list(writes))
        return tok

    def barrier(self):
        toks = list(self.latest.values())
        for e in ENGS:
            deps = []
            for s, v in toks:
                if self.waited[e].get(id(s), 0) < v:
                    deps.append((s, v))
                    self.waited[e][id(s)] = v
            if deps:
                self.q[e].append((deps, None, None, 0))
        if max(self.cnt.values()) > EPOCH_LIMIT:
            dm = set(id(s) for ss in self.dsem.values() for s in ss) | set(id(s) for s in self.csem)
            for e in ENGS:
                self.esem[e] = self._newsem("c_" + e)
                self.cnt[e] = 0
            self.latest = {k: t for k, t in self.latest.items() if k in dm}

    def maybe_epoch(self):
        if max(self.cnt.values()) > EPOCH_LIMIT:
            self.barrier()

    def emit(self):
        qs = self.q
        self.q = {e: [] for e in ENGS}
        with self.nc.Block() as block:
            def run(engname):
                def _f(e):
                    for deps, fn, s, inc in qs[engname]:
                        for ds, dv in deps:
                            e.wait_ge(ds, dv)
                        if fn is not None:
                            fn(e).then_inc(s, inc)
                return _f
            block.tensor(run("pe"))
            block.scalar(run("act"))
            block.vector(run("dve"))
            block.gpsimd(run("pool"))
            block.sync(run("sp"))


class Rot:
    def __init__(self, items):
        self.items = items
        self.i = 0

    def next(self):
        x = self.items[self.i % len(self.items)]
        self.i += 1
        return x


class KB:
    def __init__(self, nc, P, shared, nf32=8, layout="std", nbf=4):
        self.nc = nc
        self.P = P
        self.dram = shared["dram"]
        self.shared = shared
        if layout == "std":
            self.ptr = Rot([P.ps([128, 8, 128], BF16, "ptr") for _ in range(2)])
            self.pg = [P.ps([128, 512], F32, "pg") for _ in range(3)]
            self.po = [P.ps([128, 512], F32, "po") for _ in range(3)]
        else:
            self.sc = [P.ps([128, 2, 512], F32, "sc") for _ in range(2)]
            self.acc = [P.ps([128, 512], F32, "acc") for _ in range(4)]
            self.ptr = Rot([T(x.t[:, 0, :].bitcast(BF16).rearrange("p (c q) -> p c q", q=128), "ptrv", buf=x.b) for x in self.sc])
            self.pg = [T(self.sc[0].t[:, 0, :], "pgv", buf=self.sc[0].b), T(self.sc[1].t[:, 0, :], "pgv", buf=self.sc[1].b),
                       T(self.sc[0].t[:, 1, :], "pgv", buf=self.sc[0].b)]
            self.po = None
        self.pgr = Rot(self.pg[0:2])
        self.dq = Rot(["sp", "act"])
        self.f32p = Rot([P.sb([128, D], F32, "f32p") for _ in range(nf32)])
        self.bfp = Rot([P.sb([128, 2048], BF16, "bfp") for _ in range(nbf)])
        self.b1p = Rot([P.sb([128, D], BF16, "b1p") for _ in range(4)])
        self.t3p = Rot([P.sb([128, 8, 128], BF16, "t3p") for _ in range(4)])
        self.ssp = Rot([P.sb([128, 4], F32, "ssp") for _ in range(4)])
        self.w_stage = Rot([P.sb([128, 8, 256], F32, "wst") for _ in range(2)])
        self.WCH = 256
        self.wm_stage = self.w_stage

    def din(self, name, shape, dt=F32):
        if name not in self.dram:
            self.dram[name] = T(self.nc.dram_tensor(name, list(shape), dt, kind="ExternalInput").ap(), name, True)
        return self.dram[name]

    def dout(self, name, shape, dt=F32):
        if name not in self.dram:
            self.dram[name] = T(self.nc.dram_tensor(name, list(shape), dt, kind="ExternalOutput").ap(), name, True)
        return self.dram[name]

    def dint(self, name, shape, dt=F32):
        if name in _DEBUG_OUT:
            return self.dout(name, shape, dt)
        if name not in self.dram:
            self.dram[name] = T(self.nc.dram_tensor(name, list(shape), dt).ap(), name, True)
        return self.dram[name]

    def _untracked(self, x):
        if x is None:
            return Buf()
        return Buf() if (isinstance(x, T) and x.name_is_dram) else x

    def load(self, dst, dst_ap, src, src_ap, q="sp", slow=False):
        src, dst = self._untracked(src), self._untracked(dst)
        if slow:
            self.P.dma(q, lambda e: e.dma_start(out=dst_ap, in_=src_ap, allow_slow_non_contiguous=True), reads=[src], writes=[dst])
        else:
            self.P.dma(q, lambda e: e.dma_start(out=dst_ap, in_=src_ap), reads=[src], writes=[dst])

    def store(self, dst, dst_ap, src, src_ap, q="pool"):
        src, dst = self._untracked(src), self._untracked(dst)
        self.P.dma(q, lambda e: e.dma_start(out=dst_ap, in_=src_ap), reads=[src], writes=[dst])

    def copy(self, eng, dst, dst_ap, src, src_ap):
        if eng == "act":
            self.P.op("act", lambda e: e.activation(out=dst_ap, in_=src_ap, func=AF.Copy), reads=[src], writes=[dst])
        else:
            self.P.op(eng, lambda e: e.tensor_copy(out=dst_ap, in_=src_ap), reads=[src], writes=[dst])

    def act(self, dst, dst_ap, src, src_ap, func, scale=1.0, accum=None, extra_w=()):
        if accum is None:
            self.P.op("act", lambda e: e.activation(out=dst_ap, in_=src_ap, func=func, scale=scale),
                      reads=[src], writes=[dst] + list(extra_w))
        else:
            self.P.op("act", lambda e: e.activation(out=dst_ap, in_=src_ap, func=func, scale=scale, accum_out=accum),
                      reads=[src], writes=[dst] + list(extra_w))

    def tt(self, eng, dst, dst_ap, a, a_ap, b, b_ap, op):
        self.P.op(eng, lambda e: e.tensor_tensor(out=dst_ap, in0=a_ap, in1=b_ap, op=op), reads=[a, b], writes=[dst])

    def ts(self, eng, dst, dst_ap, a, a_ap, s1, s2, op0, op1=None, sreads=()):
        if op1 is None:
            self.P.op(eng, lambda e: e.tensor_scalar(out=dst_ap, in0=a_ap, scalar1=s1, scalar2=None, op0=op0),
                      reads=[a] + list(sreads), writes=[dst])
        else:
            self.P.op(eng, lambda e: e.tensor_scalar(out=dst_ap, in0=a_ap, scalar1=s1, scalar2=s2, op0=op0, op1=op1),
                      reads=[a] + list(sreads), writes=[dst])

    def recip(self, dst, dst_ap, src, src_ap):
        self.P.op("dve", lambda e: e.reciprocal(out=dst_ap, in_=src_ap), reads=[src], writes=[dst])

    def memset(self, eng, dst, dst_ap, val):
        self.P.op(eng, lambda e: e.memset(dst_ap, val), writes=[dst])

    def transposes(self, dst_ps, src, src_aps, ident):
        def fn(e):
            for i, ap in enumerate(src_aps):
                ins = e.transpose(out=dst_ps.t[:, i, :], in_=ap, identity=ident.t[:])
            return ins
        self.P.op("pe", fn, reads=[src, ident], writes=[dst_ps])

    def matmuls(self, dst, specs, reads):
        def fn(e):
            for sp_ in specs:
                o, l, r, st, sp = sp_[:5]
                if len(sp_) > 5 and sp_[5]:
                    ins = e.matmul(o, lhsT=l, rhs=r, start=st, stop=sp, skip_group_check=True)
                else:
                    ins = e.matmul(o, lhsT=l, rhs=r, start=st, stop=sp)
            return ins
        self.P.op("pe", fn, reads=reads, writes=(dst if isinstance(dst, (list, tuple)) else [dst]))

    def setup_consts(self):
        P = self.P
        idf = P.sb([128, 128], F32, "idf")
        self.ident = P.sb([128, 128], BF16, "ident")
        d = self.din("ident", [128, 128])
        self.load(idf, idf.t[:], d, d.t)
        self.copy("dve", self.ident, self.ident.t[:], idf, idf.t[:])
        self.ones_row = P.sb([1, 128], F32, "ones_row")
        self.memset("dve", self.ones_row, self.ones_row.t[:], 1.0)

    def mod_vectors(self, l, need_ab, need_gate):
        P = self.P
        wmod = self.din(f"wmod{l}", [D, 3 * D])
        bT = self.din(f"bmodT{l}", [128, 24])
        res = {}
        if not hasattr(self, "silu_c"):
            cc = self.din("ccol", [128, 8])
            ct = P.sb([128, 8], F32, "ccol")
            self.load(ct, ct.t[:], cc, cc.t)
            self.silu_c = P.sb([128, 8], F32, "siluc")
            self.act(self.silu_c, self.silu_c.t[:], ct, ct.t[:], AF.Silu)
        sc = self.silu_c
        bt = P.sb([128, 24], F32, "bT")
        self.load(bt, bt.t[:], bT, bT.t)
        if need_ab:
            gT = self.din(f"ngT{l}", [128, 8])
            gt = P.sb([128, 8], F32, "gT")
            self.load(gt, gt.t[:], gT, gT.t)
            modT = P.sb([128, 16], F32, "modT")
            pg = self.pg[2]
            for g4 in range(8):
                st = self.wm_stage.next()
                self.load(st, st.t[:], wmod, wmod.t[:, g4 * 256:(g4 + 1) * 256].rearrange("(c p) n -> p c n", p=128),
                          q=self.dq.next())
                specs = []
                for jj in range(2):
                    j = g4 * 2 + jj
                    for k in range(8):
                        specs.append((pg.t[:, j:j + 1], st.t[:, k, jj * 128:(jj + 1) * 128], sc.t[:, k:k + 1], k == 0, k == 7))
                self.matmuls(pg, specs, [st, sc])
            self.tt("dve", modT, modT.t[:], pg, pg.t[:, 0:16], bt, bt.t[:, 0:16], ALU.add)
            A = P.sb([128, 8], F32, "A")
            self.P.op("dve", lambda e: e.scalar_tensor_tensor(out=A.t[:], in0=modT.t[:, 8:16], scalar=1.0, in1=gt.t[:],
                                                               op0=ALU.add, op1=ALU.mult), reads=[modT, gt], writes=[A])
            res["A"] = A
            res["B"] = modT
        if need_gate:
            bg = self.din(f"bmodg{l}", [1, D])
            bgt = P.sb([1, D], F32, "bg")
            self.load(bgt, bgt.t[:], bg, bg.t)
            grow = P.sb([1, D], F32, "grow")
            gbc = P.sb([128, D], F32, "gbc")
            for g2 in range(4):
                st = self.wm_stage.next()
                c0 = 2048 + g2 * 256
                self.load(st, st.t[:], wmod, wmod.t[:, c0:c0 + 256].rearrange("(c p) n -> p c n", p=128), q=self.dq.next())
                pg = self.pg[g2 % 2]
                specs = [(pg.t[0:1, 0:256], sc.t[:, k:k + 1], st.t[:, k, :], k == 0, k == 7) for k in range(8)]
                self.matmuls(pg, specs, [st, sc])
                self.tt("dve", grow, grow.t[:, g2 * 256:(g2 + 1) * 256], pg, pg.t[0:1, 0:256], bgt, bgt.t[:, g2 * 256:(g2 + 1) * 256], ALU.add)
            for g2 in range(2):
                pg = self.pg[g2]
                self.matmuls(pg, [(pg.t[:], self.ones_row.t[:], grow.t[:, g2 * 512:(g2 + 1) * 512], True, True)],
                             [self.ones_row, grow])
                self.copy("act", gbc, gbc.t[:, g2 * 512:(g2 + 1) * 512], pg, pg.t[:])
            res["gate_bc"] = gbc
        return res

    def load_weight(self, wsb, wdram, c0, ncols, scale_bc=None):
        P = self.P
        for g in range(0, ncols, 256):
            n = min(256, ncols - g)
            st = self.w_stage.next()
            self.load(st, st.t[:, :, 0:n], wdram, wdram.t[:, c0 + g:c0 + g + n].rearrange("(c p) n -> p c n", p=128),
                      q=self.dq.next())
            if scale_bc is None:
                self.copy("dve" if (g // 256) % 2 == 0 else "act", wsb.part(g), wsb.t[:, :, g:g + n], st, st.t[:, :, 0:n])
            else:
                for k in range(8):
                    self.tt("dve", wsb.part(g), wsb.t[:, k, g:g + n], st, st.t[:, k, 0:n], scale_bc, scale_bc.t[:, g:g + n], ALU.mult)

    def phase_a(self, ntiles, xsrc, xrow0, pdst, prow0, pcol0, wsb, ncols, A, Bm, cs, csrow0, hd, rope_cols, also=None, hcache=None):
        P = self.P
        if not hasattr(self, "pa_bufs"):
            self.pa_bufs = dict(
                tA=Rot([P.sb([128, 512], F32, "patA") for _ in range(3)]),
                tB=Rot([P.sb([128, 512], F32, "patB") for _ in range(3)]),
                csF=Rot([P.sb([128, 2, 512], F32, "pacsF") for _ in range(3)]),
            )
        bufs = self.pa_bufs
        h2 = hd // 2
        nhf = 512 // hd

        hc_tiles, hc_mode = hcache if hcache is not None else (None, None)

        def prep_a(i):
            if hc_mode == "r":
                csF = None
                if rope_cols > 0:
                    csF = bufs["csF"].next()
                    self.load(csF, csF.t[:], cs, cs.t[csrow0 + i * 128: csrow0 + (i + 1) * 128, :, :], q="sp")
                return None, csF, i
            xt = self.f32p.next()
            self.load(xt, xt.t[:], xsrc, xsrc.t[xrow0 + i * 128: xrow0 + (i + 1) * 128, :], q="sp")
            ss = self.ssp.next()
            sq = self.f32p.next()
            self.act(sq, sq.t[:], xt, xt.t[:], AF.Square, accum=ss.t[:, 0:1], extra_w=[ss])
            self.ts("dve", ss, ss.t[:, 1:2], ss, ss.t[:, 0:1], 1.0 / D, NORM_EPS, ALU.mult, ALU.add)
            self.act(ss, ss.t[:, 2:3], ss, ss.t[:, 1:2], AF.Sqrt)
            self.recip(ss, ss.t[:, 3:4], ss, ss.t[:, 2:3])
            xn = self.b1p.next()
            self.ts("dve", xn, xn.t[:], xt, xt.t[:], ss.t[:, 3:4], None, ALU.mult, sreads=[ss])
            csF = None
            if rope_cols > 0:
                csF = bufs["csF"].next()
                self.load(csF, csF.t[:], cs, cs.t[csrow0 + i * 128: csrow0 + (i + 1) * 128, :, :], q="sp")
            return xn, csF, i

        def prep_b(pa):
            xn, csF, i = pa
            if hc_mode == "r":
                hT = self.t3p.next()
                self.load(hT, hT.t[:], hc_tiles[i], hc_tiles[i].t, q="sp")
                cosF = sinS = None
                if csF is not None:
                    cosF = T(csF.t[:, 0, :], "cosF", buf=csF.b)
                    sinS = T(csF.t[:, 1, :], "sinS", buf=csF.b)
                return hT, cosF, sinS
            ptr = self.ptr.next()
            self.transposes(ptr, xn, [xn.t[:, c * 128:(c + 1) * 128] for c in range(8)], self.ident)
            hT = self.t3p.next()
            for c in range(8):
                if rope_cols == 0 and c % 2 == 1:
                    self.ts("dve", hT, hT.t[:, c, :], ptr, ptr.t[:, c, :], A.t[:, c:c + 1], Bm.t[:, c:c + 1],
                            ALU.mult, ALU.add, sreads=[A, Bm])
                else:
                    self._act_affine(hT, hT.t[:, c, :], ptr, ptr.t[:, c, :], A, A.t[:, c:c + 1], Bm, Bm.t[:, c:c + 1])
            if hc_mode == "w":
                self.store(hc_tiles[i], hc_tiles[i].t, hT, hT.t[:], q="pool")
            cosF = sinS = None
            if csF is not None:
                cosF = T(csF.t[:, 0, :], "cosF", buf=csF.b)
                sinS = T(csF.t[:, 1, :], "sinS", buf=csF.b)
            return hT, cosF, sinS

        def mm(i, st):
            hT, cosF, sinS = st
            for og in range(0, ncols, 2048):
                on = min(2048, ncols - og)
                ot = self.bfp.next()
                for g in range(og, og + on, 512):
                    pg = self.pgr.next()
                    specs = [(pg.t[:], hT.t[:, k, :], wsb.t[:, k, g:g + 512], k == 0, k == 7) for k in range(8)]
                    self.matmuls(pg, specs, [hT, wsb.part(g)])
                    lo = g - og
                    rc = max(0, min(512, rope_cols - g))
                    if rc > 0:
                        tA, tB = bufs["tA"].next(), bufs["tB"].next()
                        self.tt("dve", tA, tA.t[:, 0:rc], pg, pg.t[:, 0:rc], cosF, cosF.t[:, 0:rc], ALU.mult)
                        q4 = pg.t[:, 0:rc].rearrange("p (h two d) -> p h two d", two=2, d=h2)
                        b4 = tB.t[:, 0:rc].rearrange("p (h two d) -> p h two d", two=2, d=h2)
                        s4 = sinS.t[:, 0:rc].rearrange("p (h two d) -> p h two d", two=2, d=h2)
                        self.tt("dve", tB, b4[:, :, 0, :], pg, q4[:, :, 1, :], sinS, s4[:, :, 0, :], ALU.mult)
                        self.tt("dve", tB, b4[:, :, 1, :], pg, q4[:, :, 0, :], sinS, s4[:, :, 1, :], ALU.mult)
                        self.tt("pool", ot, ot.t[:, lo:lo + rc], tA, tA.t[:, 0:rc], tB, tB.t[:, 0:rc], ALU.add)
                        if rc < 512:
                            self.copy("act", ot, ot.t[:, lo + rc:lo + 512], pg, pg.t[:, rc:512])
                    else:
                        self.copy("act", ot, ot.t[:, lo:lo + 512], pg, pg.t[:])
                self.store(pdst, pdst.t[prow0 + i * 128: prow0 + (i + 1) * 128, pcol0 + og: pcol0 + og + on], ot, ot.t[:, 0:on])
                for (adst, atile, arow, c_lo, c_hi, dcol0) in (also or ()):
                    if atile != i:
                        continue
                    lo_, hi_ = max(c_lo, og), min(c_hi, og + on)
                    if hi_ > lo_:
                        self.store(None, adst.t[arow: arow + 128, dcol0 + lo_ - c_lo: dcol0 + hi_ - c_lo],
                                   ot, ot.t[:, lo_ - og:hi_ - og], q="pool")

        pa = {0: prep_a(0)}
        if ntiles > 1:
            pa[1] = prep_a(1)
        st = prep_b(pa.pop(0))
        for i in range(ntiles):
            if i + 2 < ntiles:
                pa[i + 2] = prep_a(i + 2)
            nxt = prep_b(pa.pop(i + 1)) if i + 1 < ntiles else None
            mm(i, st)
            st = nxt
            self.P.maybe_epoch()

    def hcache_tiles(self, name, ntiles):
        full = self.dint(name, [ntiles, 128, 1024], BF16)
        return [T(full.t[i].rearrange("p (c q) -> p c q", q=128), "hc", dram=False) for i in range(ntiles)]

    def _act_affine(self, dst, dst_ap, src, src_ap, a_t, a_ap, b_t, b_ap):
        self.P.op("act", lambda e: e.activation(out=dst_ap, in_=src_ap, func=AF.Identity, bias=b_ap, scale=a_ap),
                  reads=[src, a_t, b_t], writes=[dst])

    def phase_c_tile(self, ytile, xsrc, xrow, xdst, drow, wout, final_g=None, defer=False, banks=None):
        P = self.P
        if not hasattr(self, "pc_bufs"):
            self.pc_bufs = dict(yT=self.t3p, x=self.f32p, xo=self.f32p, ss=self.ssp, sq=self.f32p)
        bufs = self.pc_bufs
        banks = banks or self.pgr
        ptr = self.ptr.next()
        self.transposes(ptr, ytile, [ytile.t[:, c * 128:(c + 1) * 128] for c in range(8)], self.ident)
        yT = bufs["yT"].next()
        self.copy("act", yT, yT.t[:], ptr, ptr.t[:])
        xt = bufs["x"].next()
        self.load(xt, xt.t[:], xsrc, xsrc.t[xrow:xrow + 128, :], q="sp")
        pgs = []
        for half in range(2):
            pg = banks.next()
            specs = [(pg.t[:], yT.t[:, k, :], wout.t[:, k, half * 512:(half + 1) * 512], k == 0, k == 7) for k in range(8)]
            self.matmuls(pg, specs, [yT, wout.part(half * 512)])
            pgs.append(pg)

        def back():
            xo = bufs["xo"].next()
            for half in range(2):
                pg = pgs[half]
                self.tt("dve", xo, xo.t[:, half * 512:(half + 1) * 512], pg, pg.t[:], xt, xt.t[:, half * 512:(half + 1) * 512], ALU.add)
            if final_g is not None:
                ss = bufs["ss"].next()
                sq = bufs["sq"].next()
                self.act(sq, sq.t[:], xo, xo.t[:], AF.Square, accum=ss.t[:, 0:1], extra_w=[ss])
                self.ts("dve", ss, ss.t[:, 1:2], ss, ss.t[:, 0:1], 1.0 / D, NORM_EPS, ALU.mult, ALU.add)
                self.act(ss, ss.t[:, 2:3], ss, ss.t[:, 1:2], AF.Sqrt)
                self.recip(ss, ss.t[:, 3:4], ss, ss.t[:, 2:3])
                self.P.op("dve", lambda e: e.scalar_tensor_tensor(out=xt.t[:], in0=xo.t[:], scalar=ss.t[:, 3:4], in1=final_g.t[:],
                                                                   op0=ALU.mult, op1=ALU.mult), reads=[xo, ss, final_g], writes=[xt])
                self.store(xdst, xdst.t[drow:drow + 128, :], xt, xt.t[:])
            else:
                self.store(xdst, xdst.t[drow:drow + 128, :], xo, xo.t[:])
        if defer:
            return back
        back()
        return None

    def conv_layer(self, p0, convk_bc, vmask, xsrc, xdst, wout):
        P = self.P
        cx = [self.bfp] * 3
        tj = [self.f32p] * 3
        bz, sz, acc, yt = self.bfp, self.f32p, self.f32p, self.b1p
        pcb = Rot([self.pg[0], self.pg[1], self.pg[2], self.po[0]])
        prev_back = None
        for i in range(NT):
            r0 = 128 + i * 128
            ts_ = []
            for j in range(3):
                c = cx[j].next()
                self.load(c, c.t[:], p0, p0.t[r0 + j - 1: r0 + j - 1 + 128, 1024:3072], q="sp")
                t = tj[j].next()
                self.tt("pool", t, t.t[:], c, c.t[:, 0:1024], c, c.t[:, 1024:2048], ALU.mult)
                if (i == 0 and j == 0):
                    self.ts("pool", t, t.t[:], t, t.t[:], vmask.t[:, 0:1], None, ALU.mult, sreads=[vmask])
                if (i == NT - 1 and j == 2):
                    self.ts("pool", t, t.t[:], t, t.t[:], vmask.t[:, 1:2], None, ALU.mult, sreads=[vmask])
                ts_.append(t)
            b = bz.next()
            self.load(b, b.t[:, 0:1024], p0, p0.t[r0:r0 + 128, 0:1024], q="sp")
            self.load(b, b.t[:, 1024:2048], p0, p0.t[r0:r0 + 128, 3072:4096], q="sp")
            s = sz.next()
            self.act(s, s.t[:], b, b.t[:, 1024:2048], AF.Silu)
            a = acc.next()
            self.tt("dve", a, a.t[:], ts_[0], ts_[0].t[:], convk_bc, convk_bc.t[:, 0, :], ALU.mult)
            self.tt("pool", ts_[1], ts_[1].t[:], ts_[1], ts_[1].t[:], convk_bc, convk_bc.t[:, 1, :], ALU.mult)
            self.tt("dve", ts_[2], ts_[2].t[:], ts_[2], ts_[2].t[:], convk_bc, convk_bc.t[:, 2, :], ALU.mult)
            self.tt("pool", a, a.t[:], a, a.t[:], ts_[1], ts_[1].t[:], ALU.add)
            self.tt("dve", a, a.t[:], a, a.t[:], ts_[2], ts_[2].t[:], ALU.add)
            self.tt("dve", s, s.t[:], s, s.t[:], b, b.t[:, 0:1024], ALU.mult)
            y = yt.next()
            self.tt("dve", y, y.t[:], a, a.t[:], s, s.t[:], ALU.mult)
            bk = self.phase_c_tile(y, xsrc, r0, xdst, i * 128, wout, defer=True, banks=pcb)
            if prev_back is not None:
                prev_back()
            prev_back = bk
            self.P.maybe_epoch()
        prev_back()

    def band_attention(self, *, pq, kv, kvalid, masks, nheads, hd, ev, qcol0, kcol0, vcol0, nkvh, gq,
                       blocks, scale, finish):
        P = self.P
        qcols = nheads * hd
        kcols = nkvh * hd
        nqt = qcols // 128
        dup = (hd == 64)
        nkt = nkvh if dup else kcols // 128
        maxch = max(len(b["chunks"]) for b in blocks)
        hb = max(1, 512 // (128 * maxch))
        key = (qcols, nkt, nkvh, ev, maxch)
        if not hasattr(self, "_ba_cache"):
            self._ba_cache = {}
        if key not in self._ba_cache:
            self._ba_cache[key] = dict(
                qrow=Rot([P.sb([128, qcols], BF16, "ba_q") for _ in range(2)]),
                qT=Rot([P.sb([128, nqt, 128], BF16, "ba_qT") for _ in range(2)]),
                krow=Rot([P.sb([128, nkt * 128], BF16, "ba_k") for _ in range(2 * maxch)]),
                kT=Rot([P.sb([128, nkt, 128], BF16, "ba_kT") for _ in range(2 * maxch)]),
                vaug=Rot([P.sb([128, nkvh, ev + 1], BF16, "ba_v") for _ in range(2 * maxch)]),
                pt=Rot([P.sb([128, hb * maxch, 128], BF16, "ba_pt") for _ in range(3)]),
                kvc=Rot([P.sb([128, 1], F32, "ba_kvc") for _ in range(2 * maxch)]))
            for v_ in self._ba_cache[key]["vaug"].items:
                self.memset("pool", v_, v_.t[:], 1.0)
        c_ = self._ba_cache[key]
        kvcr = c_["kvc"]
        qrow, qT, krow, kT, vaug, pt = c_["qrow"], c_["qT"], c_["krow"], c_["kT"], c_["vaug"], c_["pt"]
        per_bank = 512 // (ev + 1)
        nch = maxch
        assert all(len(b_["chunks"]) == nch for b_ in blocks)
        if "maskb" not in c_:
            c_["maskb"] = P.sb([128, hb * nch, 128], BF16, "ba_maskb")
        maskb = c_["maskb"]
        self.memset("pool", maskb, maskb.t[:], 0.0)
        for ii in range(hb):
            for ci in range(nch):
                mid = blocks[0]["chunks"][ci][2]
                if mid is not None:
                    self.ts("pool", maskb, maskb.t[:, ii * nch + ci, :], masks, masks.t[:, mid, :], 30000.0, -30000.0, ALU.mult, ALU.add)
        oaps = []
        for h in range(nheads):
            bank = self.po[h // per_bank]
            sl = (h % per_bank) * (ev + 1)
            oaps.append((bank, bank.t[:, sl:sl + ev + 1]))

        def prep_loads(blk):
            qr = qrow.next()
            r0, st = blk["qrow0"], blk["qstep"]
            self.load(qr, qr.t[:], pq, pq.t[r0:r0 + 127 * st + 1:st, qcol0:qcol0 + qcols], q="sp")
            vas, kvcs, krs = [], [], []
            for (k0, kst, mid, halo) in blk["chunks"]:
                kr = krow.next()
                if dup:
                    ksrc = kv.t[k0:k0 + 127 * kst + 1:kst, kcol0:kcol0 + kcols].rearrange("p (h d) -> p h d", d=hd)
                    kr4 = kr.t[:].rearrange("p (h two d) -> p h two d", two=2, d=hd)
                    self.load(kr, kr4[:, :, 0, :], kv, ksrc, q="sp")
                    self.load(kr, kr4[:, :, 1, :], kv, ksrc, q="sp")
                else:
                    self.load(kr, kr.t[:], kv, kv.t[k0:k0 + 127 * kst + 1:kst, kcol0:kcol0 + kcols], q="sp")
                va = vaug.next()
                self.load(va, va.t[:, :, 0:ev], kv,
                          kv.t[k0:k0 + 127 * kst + 1:kst, vcol0:vcol0 + nkvh * ev].rearrange("p (h e) -> p h e", e=ev), q="sp")
                if halo:
                    kc_ = kvcr.next()
                    self.load(kc_, kc_.t[:], kvalid, kvalid.t[k0:k0 + 127 * kst + 1:kst, :], q="sp", slow=True)
                    kvcs.append(kc_)
                else:
                    kvcs.append(None)
                krs.append(kr)
                vas.append(va)
            return qr, krs, vas, kvcs

        def prep_tr(ld):
            qr, krs, vas, kvcs = ld
            kts = []
            ptr = self.ptr.next()
            self.transposes(ptr, qr, [qr.t[:, c * 128:(c + 1) * 128] for c in range(nqt)], self.ident)
            qt = qT.next()
            self.copy("dve", qt, qt.t[:], ptr, ptr.t[:, 0:nqt, :])
            for kr in krs:
                ptr = self.ptr.next()
                self.transposes(ptr, kr, [kr.t[:, c * 128:(c + 1) * 128] for c in range(nkt)], self.ident)
                kt = kT.next()
                self.copy("dve", kt, kt.t[:], ptr, ptr.t[:, 0:nkt, :])
                kts.append(kt)
            return qt, kts, vas, kvcs

        def emit_qk(st, hs):
            qt, kts, vas, kvcs = st
            pg = self.pgr.next()
            specs = []
            for ii, h in enumerate(hs):
                kvh = h // gq
                qd = h * hd
                po_ = qd % 128
                for ci in range(nch):
                    col = (ii * nch + ci) * 128
                    specs.append((pg.t[:, col:col + 128], kts[ci].t[po_: po_ + hd, kvh, :], qt.t[po_: po_ + hd, qd // 128, :],
                                  len(specs) == 0, False, True))
            ncol = len(hs) * nch
            specs.append((pg.t[:, 0:ncol * 128], self.ident.t[:], maskb.t[:, 0:ncol, :].rearrange("p c q -> p (c q)"), False, True, True))
            self.matmuls(pg, specs, kts + [qt, maskb, self.ident])
            p = pt.next()
            self.act(p, p.t[:, 0:ncol, :], pg, pg.t[:, 0:ncol * 128].rearrange("p (c q) -> p c q", q=128), AF.Exp, scale=scale)
            for ii, h in enumerate(hs):
                for ci in range(nch):
                    if kvcs[ci] is not None:
                        cc = ii * nch + ci
                        self.ts("dve", p, p.t[:, cc, :], p, p.t[:, cc, :], kvcs[ci].t[:, 0:1], None, ALU.mult, sreads=[kvcs[ci]])
            return p

        def emit_pv(st, hs, p):
            qt, kts, vas, kvcs = st
            banks = []
            per = {}
            for ii, h in enumerate(hs):
                kvh = h // gq
                bank, oap = oaps[h]
                if bank not in banks:
                    banks.append(bank)
                    per[id(bank)] = []
                for ci in range(nch):
                    cc = ii * nch + ci
                    per[id(bank)].append((oap, p.t[:, cc, :], vas[ci].t[:, kvh, :], ci == 0, ci == nch - 1))
            for bank in banks:
                self.matmuls(bank, per[id(bank)], [p] + vas)

        units = [list(range(h0, min(nheads, h0 + hb))) for h0 in range(0, nheads, hb)]
        prep_at = min(len(units) - 1, max(1, len(units) // 2))
        st = prep_tr(prep_loads(blocks[0]))
        deferred = None
        for bi, blk in enumerate(blocks):
            ld = prep_loads(blocks[bi + 1]) if bi + 1 < len(blocks) else None
            nxt = None
            ps_ = [emit_qk(st, units[0])]
            for ui, hs in enumerate(units):
                if ui + 1 < len(units):
                    ps_.append(emit_qk(st, units[ui + 1]))
                emit_pv(st, hs, ps_[ui])
                if ui == prep_at - 1 and ld is not None:
                    nxt = prep_tr(ld)
            if nxt is None and ld is not None:
                nxt = prep_tr(ld)
            d_ = finish(bi, blk, oaps)
            if deferred is not None:
                deferred()
            deferred = d_
            st = nxt
            self.P.maybe_epoch()
        if deferred is not None:
            deferred()


def _group(xs, n):
    return [xs[i:i + n] for i in range(0, len(xs), n)]


def d2d(kb, dst, dst_ap, src, src_ap, q="sp"):
    kb.load(dst, dst_ap, src, src_ap, q=q)


CC_MAX_BYTES = 2 << 20


def exchange_alloc(kb, name, pieces, cols):
    rmax = max(1, CC_MAX_BYTES // (cols * 2))
    jobs = []
    ci = 0
    for pi, (rows, d0, d1) in enumerate(pieces):
        for r0 in range(0, rows, rmax):
            n = min(rmax, rows - r0)
            sb_ = kb.dint(f"{name}_s{ci}", [n, cols], BF16)
            db_ = kb.dint(f"{name}_d{ci}", [2 * n, cols], BF16)
            sb_.name_is_dram = False
            db_.name_is_dram = False
            ci += 1
            jobs.append(dict(src=sb_, dst=db_, n=n, r0=r0, d0=d0, d1=d1, piece=pi))
    return jobs


def exchange_also(jobs, piece, tile0, c_lo, c_hi, dcol0):
    out = []
    for jb in jobs:
        if jb["piece"] != piece:
            continue
        for t in range(jb["n"] // 128):
            out.append((jb["src"], tile0 + jb["r0"] // 128 + t, t * 128, c_lo, c_hi, dcol0))
    return out


def exchange_cc(kb, jobs):
    groups = [[0, 1], [2, 3], [4, 5], [6, 7]]
    for jb in jobs:
        sb_, db_ = jb["src"], jb["dst"]
        kb.P.cc((lambda sb_=sb_, db_=db_: (lambda e: e.collective_compute(
            "AllGather", ALU.bypass, replica_groups=groups, ins=[sb_.t.opt()], outs=[db_.t.opt()])))(),
            reads=[sb_.b], writes=[db_.b])


def exchange_finish(kb, jobs):
    for jb in jobs:
        db_, n, r0 = jb["dst"], jb["n"], jb["r0"]
        if jb["d0"] is not None:
            d2d(kb, None, jb["d0"][r0:r0 + n, :], db_, db_.t[0:n, :], q=kb.dq.next())
        if jb["d1"] is not None:
            d2d(kb, None, jb["d1"][r0:r0 + n, :], db_, db_.t[n:2 * n, :], q=kb.dq.next())


def stage0(kb):
    P = kb.P
    xe = kb.din("xe", [TPC + 256, D])
    m0 = kb.mod_vectors(0, True, True)
    m1 = kb.mod_vectors(1, True, False)
    w_in0 = kb.din("w_in0", [D, 4096])
    w_out0 = kb.din("w_out0", [D, D])
    w_in1 = kb.din("w_in1", [D, 8192])
    convk = kb.din("convk_bc", [128, 3, D])
    vmask = kb.din("vmask", [128, 2])
    cs1 = kb.din("cs1", [TPC, 2, 512])
    X1 = kb.dint("X1", [TPC, D])
    P1 = kb.dint("P1", [TPC, 8192], BF16)
    p0 = kb.dint("p0", [TPC + 256, 4096], BF16)
    wsb = P.sb([128, 8, 2048], BF16, "wsb")
    wout = P.sb([128, 8, D], BF16, "wout")
    ck = P.sb([128, 3, D], F32, "convk")
    vm = P.sb([128, 2], F32, "vmask")
    kb.load(ck, ck.t[:], convk, convk.t)
    kb.load(vm, vm.t[:], vmask, vmask.t)
    kb.load_weight(wout, w_out0, 0, D, scale_bc=m0["gate_bc"])
    hc0 = kb.hcache_tiles("hc0", NT + 2)
    for ps_ in range(2):
        kb.load_weight(wsb, w_in0, ps_ * 2048, 2048)
        kb.phase_a(NT + 2, xe, 0, p0, 0, ps_ * 2048, wsb, 2048, m0["A"], m0["B"], None, 0, 128, 0,
                   hcache=(hc0, "w" if ps_ == 0 else "r"))
    P.barrier()
    kb.conv_layer(p0, ck, vm, xe, X1, wout)
    P.barrier()
    HALO = 1024
    kv = kb.dint("kv1", [TPC + 2 * HALO, 4096], BF16)
    hc1 = kb.hcache_tiles("hc1", NT)
    jobs = exchange_alloc(kb, "ex1", [
        (HALO, kv.t[0:HALO, :], None),
        (HALO, None, kv.t[HALO + TPC:, :]),
    ], 4096)
    for (c0, nc_, rc_) in ((3072, 2048, 2048), (5120, 2048, 1024)):
        kb.load_weight(wsb, w_in1, c0, nc_)
        also = [(kv, i, HALO + i * 128, 0, nc_, c0 - 3072) for i in range(NT)]
        also += exchange_also(jobs, 0, NT - HALO // 128, 0, nc_, c0 - 3072)
        also += exchange_also(jobs, 1, 0, 0, nc_, c0 - 3072)
        kb.phase_a(NT, X1, 0, P1, 0, c0, wsb, nc_, m1["A"], m1["B"], cs1, 0, 128, rc_, also=also,
                   hcache=(hc1, "w" if c0 == 3072 else "r"))
    P.barrier()
    exchange_cc(kb, jobs)
    for (c0, nc_, rc_) in ((0, 2048, 2048), (2048, 1024, 1024), (7168, 1024, 0)):
        kb.load_weight(wsb, w_in1, c0, nc_)
        kb.phase_a(NT, X1, 0, P1, 0, c0, wsb, nc_, m1["A"], m1["B"], cs1, 0, 128, rc_, hcache=(hc1, "r"))
    exchange_finish(kb, jobs)
    P.barrier()


def stage1(kb):
    P = kb.P
    HALO = 1024
    X1 = kb.dram["X1"]
    P1 = kb.dram["P1"]
    kv = kb.dram["kv1"]
    kvalid = kb.din("kvalid1", [TPC + 2 * HALO, 1])
    masksd = kb.din("masks", [128, 2, 128], BF16)
    m1 = kb.mod_vectors(1, False, True)
    m2 = kb.mod_vectors(2, True, False)
    w_out1 = kb.din("w_out1", [D, D])
    w_in2 = kb.din("w_in2", [D, 2560])
    cs2 = kb.din("cs2", [TPC, 2, 512])
    X2 = kb.dint("X2", [TPC, D])
    P2 = kb.dint("P2", [TPC, 2560], BF16)
    og = [kb.dint(f"og{g}", [TPC, 8 * 129]) for g in range(3)]
    wout = P.sb([128, 8, D], BF16, "wout")
    masks = P.sb([128, 2, 128], BF16, "masks")
    kb.load(masks, masks.t[:], masksd, masksd.t)
    kb.load_weight(wout, w_out1, 0, D, scale_bc=m1["gate_bc"])
    P.barrier()
    osb = Rot([P.sb([128, 8 * 129], F32, "osb") for _ in range(2)])
    for g, r in enumerate((1, 4, 16)):
        blocks = []
        for p in range(r):
            for j in range(TPC // (128 * r)):
                i0 = 128 * j
                q0 = p + r * i0
                ka = HALO + p + r * (i0 - 64)
                kb_ = HALO + p + r * (i0 + 64)
                ha = ka < HALO
                hb_ = kb_ + 127 * r >= HALO + TPC
                blocks.append(dict(qrow0=q0, qstep=r, chunks=[(ka, r, 0, ha), (kb_, r, 1, hb_)]))

        def finish(bi, blk, oaps, g=g, r=r):
            o = osb.next()
            for bnk in range(3):
                n = min(3, 8 - 3 * bnk) * 129
                kb.copy("act" if bnk == 1 else "dve", o, o.t[:, bnk * 387: bnk * 387 + n], kb.po[bnk], kb.po[bnk].t[:, 0:n])
            r0 = blk["qrow0"]
            kb.store(og[g], og[g].t[r0:r0 + 127 * r + 1:r, :], o, o.t[:])
        kb.band_attention(pq=P1, kv=kv, kvalid=kvalid, masks=masks, nheads=8, hd=128, ev=128,
                          qcol0=g * 1024, kcol0=g * 1024, vcol0=3072, nkvh=8, gq=1,
                          blocks=blocks, scale=128 ** -0.5, finish=finish)
    P.barrier()
    zt = Rot([P.sb([128, D], BF16, "zt") for _ in range(2)])
    rl = Rot([P.sb([128, 8], F32, "rl") for _ in range(2)])
    o3 = Rot(osb.items + [P.sb([128, 8 * 129], F32, "o3")])
    pcb = Rot([kb.pg[0], kb.pg[1], kb.pg[2], kb.po[0]])
    prev_back = None
    for i in range(NT):
        os_ = []
        for g in range(3):
            o = o3.next()
            kb.load(o, o.t[:], og[g], og[g].t[i * 128:(i + 1) * 128, :], q="sp")
            os_.append(o)
        z = zt.next()
        kb.load(z, z.t[:], P1, P1.t[i * 128:(i + 1) * 128, 7168:8192], q="sp")
        kb.tt("pool", os_[0], os_[0].t[:], os_[0], os_[0].t[:], os_[1], os_[1].t[:], ALU.add)
        kb.tt("dve", os_[0], os_[0].t[:], os_[0], os_[0].t[:], os_[2], os_[2].t[:], ALU.add)
        o4 = os_[0].t[:].rearrange("p (h e) -> p h e", e=129)
        rr = rl.next()
        kb.recip(rr, rr.t[:], os_[0], o4[:, :, 128])
        sz = kb.f32p.next()
        kb.act(sz, sz.t[:], z, z.t[:], AF.Silu)
        on = kb.f32p.next()
        kb.tt("dve", on, on.t[:].rearrange("p (h e) -> p h e", e=128), os_[0], o4[:, :, 0:128],
              rr, rr.t[:].unsqueeze(2).to_broadcast([128, 8, 128]), ALU.mult)
        y = kb.b1p.next()
        kb.tt("pool", y, y.t[:], on, on.t[:], sz, sz.t[:], ALU.mult)
        bk = kb.phase_c_tile(y, X1, i * 128, X2, i * 128, wout, defer=True, banks=pcb)
        if prev_back is not None:
            prev_back()
        prev_back = bk
        P.maybe_epoch()
    prev_back()
    P.barrier()
    wsb = P.sb([128, 8, 2048], BF16, "wsb")
    kb.load_weight(wsb, w_in2, 0, 2048)
    H2 = 128
    kv2 = kb.dint("kv2", [TPC + 2 * H2, 512], BF16)
    jobs = exchange_alloc(kb, "ex2", [(H2, kv2.t[0:H2, :], None), (H2, None, kv2.t[H2 + TPC:, :])], 512)
    also = [(kv2, i, H2 + i * 128, 1024, 1536, 0) for i in range(NT)]
    also += exchange_also(jobs, 0, NT - 1, 1024, 1536, 0)
    also += exchange_also(jobs, 1, 0, 1024, 1536, 0)
    hc2 = kb.hcache_tiles("hc2", NT)
    kb.phase_a(NT, X2, 0, P2, 0, 0, wsb, 2048, m2["A"], m2["B"], cs2, 0, 64, 1280, also=also, hcache=(hc2, "w"))
    P.barrier()
    exchange_cc(kb, jobs)
    kb.load_weight(wsb, w_in2, 2048, 512)
    kb.phase_a(NT, X2, 0, P2, 0, 2048, wsb, 512, m2["A"], m2["B"], cs2, 0, 64, 0, hcache=(hc2, "r"))
    exchange_finish(kb, jobs)
    P.barrier()


def stage2(kb):
    P = kb.P
    HALO = 128
    X2 = kb.dram["X2"]
    P2 = kb.dram["P2"]
    kv = kb.dram["kv2"]
    kvalid = kb.din("kvalid2", [TPC + 2 * HALO, 1])
    masksd = kb.din("masks", [128, 2, 128], BF16)
    sinkd = kb.din("sink_bc", [128, 16])
    m2 = kb.mod_vectors(2, False, True)
    m3 = kb.mod_vectors(3, True, False)
    w_out2 = kb.din("w_out2", [D, D])
    w_in3 = kb.din("w_in3", [D, 4096])
    cs3 = kb.din("cs2", [TPC, 2, 512])
    X3 = kb.dint("X3", [TPC, D])
    P3 = kb.dint("P3", [TPC, 4096], BF16)
    wout = P.sb([128, 8, D], BF16, "wout")
    masks = P.sb([128, 2, 128], BF16, "masks")
    kb.load(masks, masks.t[:], masksd, masksd.t)
    esink = P.sb([128, 16], F32, "esink")
    kb.load(esink, esink.t[:], sinkd, sinkd.t)
    kb.act(esink, esink.t[:], esink, esink.t[:], AF.Exp)
    kb.load_weight(wout, w_out2, 0, D, scale_bc=m2["gate_bc"])
    P.barrier()
    blocks = []
    for j in range(NT):
        q0 = 128 * j
        blocks.append(dict(qrow0=q0, qstep=1, chunks=[(HALO + q0 - 128, 1, 0, j == 0), (HALO + q0, 1, None, False),
                                                      (HALO + q0 + 128, 1, 1, j == NT - 1)]))
    zt = Rot([P.sb([128, D], BF16, "zt") for _ in range(2)])
    lt = Rot([P.sb([128, 16], F32, "lt") for _ in range(2)])

    oev = Rot([P.sb([128, 16 * 65], F32, "oev") for _ in range(2)])

    def finish(bi, blk, oaps):
        r0 = blk["qrow0"]
        oe = oev.next()
        for bnk in range(3):
            nh = min(7, 16 - 7 * bnk)
            kb.copy("act" if bnk == 1 else "dve", oe, oe.t[:, 7 * bnk * 65:(7 * bnk + nh) * 65], kb.po[bnk], kb.po[bnk].t[:, 0:nh * 65])

        def rest():
            z = zt.next()
            kb.load(z, z.t[:], P2, P2.t[r0:r0 + 128, 1536:2560], q="sp")
            sz = kb.f32p.next()
            kb.act(sz, sz.t[:], z, z.t[:], AF.Silu)
            l = lt.next()
            on = kb.f32p.next()
            o3_ = oe.t[:].rearrange("p (h e) -> p h e", e=65)
            kb.tt("dve", l, l.t[:], oe, o3_[:, :, 64], esink, esink.t[:], ALU.add)
            kb.recip(l, l.t[:], l, l.t[:])
            kb.tt("dve", on, on.t[:].rearrange("p (h e) -> p h e", e=64), oe, o3_[:, :, 0:64],
                  l, l.t[:].unsqueeze(2).to_broadcast([128, 16, 64]), ALU.mult)
            y = kb.b1p.next()
            kb.tt("pool", y, y.t[:], on, on.t[:], sz, sz.t[:], ALU.mult)
            kb.phase_c_tile(y, X2, r0, X3, r0, wout)
        return rest

    kb.band_attention(pq=P2, kv=kv, kvalid=kvalid, masks=masks, nheads=16, hd=64, ev=64,
                      qcol0=0, kcol0=0, vcol0=256, nkvh=4, gq=4, blocks=blocks, scale=64 ** -0.5, finish=finish)
    P.barrier()
    wsb = P.sb([128, 8, 2048], BF16, "wsb")
    kb.load_weight(wsb, w_in3, 1024, 2048)
    jobs = exchange_alloc(kb, "ex3", [(TPC, None, None)], 2048)
    kb.shared["ex3_jobs"] = jobs
    hc3 = kb.hcache_tiles("hc3", NT)
    kb.phase_a(NT, X3, 0, P3, 0, 1024, wsb, 2048, m3["A"], m3["B"], cs3, 0, 64, 1024,
               also=exchange_also(jobs, 0, 0, 0, 2048, 0), hcache=(hc3, "w"))
    P.barrier()
    exchange_cc(kb, jobs)
    kb.load_weight(wsb, w_in3, 0, 1024)
    kb.phase_a(NT, X3, 0, P3, 0, 0, wsb, 1024, m3["A"], m3["B"], cs3, 0, 64, 1024, hcache=(hc3, "r"))
    kb.load_weight(wsb, w_in3, 3072, 1024)
    kb.phase_a(NT, X3, 0, P3, 0, 3072, wsb, 1024, m3["A"], m3["B"], cs3, 0, 64, 0, hcache=(hc3, "r"))
    exchange_finish(kb, jobs)
    P.barrier()


def stage3(kb):
    P = kb.P
    X3 = kb.dram["X3"]
    P3 = kb.dram["P3"]
    ex3 = kb.shared["ex3_jobs"]
    lamd = kb.din("lam_bc", [128, 4, 64])
    sgd = kb.din("subln_bc", [128, 128])
    fgd = kb.din("finalg_bc", [128, D])
    m3 = kb.mod_vectors(3, False, True)
    w_out3 = kb.din("w_out3", [D, D])
    OUT = kb.dout("OUT", [TPC, D])
    Y = kb.dint("Y3", [TPC, D], BF16)
    wout = P.sb([128, 8, D], BF16, "wout")
    kb.load_weight(wout, w_out3, 0, D, scale_bc=m3["gate_bc"])
    lam_init = 0.8 - 0.6 * math.exp(-0.3 * 3)
    lv = P.sb([128, 4, 64], F32, "lv")
    kb.load(lv, lv.t[:], lamd, lamd.t)
    lp = P.sb([128, 2, 64], F32, "lp")
    ls = P.sb([128, 4], F32, "ls")
    kb.tt("dve", lp, lp.t[:, 0, :], lv, lv.t[:, 0, :], lv, lv.t[:, 1, :], ALU.mult)
    kb.tt("dve", lp, lp.t[:, 1, :], lv, lv.t[:, 2, :], lv, lv.t[:, 3, :], ALU.mult)
    P.op("dve", lambda e: e.reduce_sum(out=ls.t[:, 0:2], in_=lp.t[:], axis=AX.X), reads=[lp.b], writes=[ls.b])
    kb.act(ls, ls.t[:, 0:2], ls, ls.t[:, 0:2], AF.Exp)
    kb.tt("dve", ls, ls.t[:, 2:3], ls, ls.t[:, 0:1], ls, ls.t[:, 1:2], ALU.subtract)
    kb.ts("dve", ls, ls.t[:, 3:4], ls, ls.t[:, 2:3], lam_init, -1.0, ALU.add, ALU.mult)
    sg = P.sb([128, 128], F32, "sg")
    kb.load(sg, sg.t[:], sgd, sgd.t)
    kb.ts("dve", sg, sg.t[:], sg, sg.t[:], 1.0 - lam_init, None, ALU.mult)
    fg = P.sb([128, D], F32, "fg")
    kb.load(fg, fg.t[:], fgd, fgd.t)

    NKC = S // 128
    kTh = Rot([P.sb([128, S], BF16, "kTh") for _ in range(2)])
    vah = Rot([P.sb([128, NKC, 129], BF16, "vah") for _ in range(2)])
    for v_ in vah.items:
        kb.memset("pool", v_, v_.t[:], 1.0)
    krow = Rot([P.sb([128, 8, 128], BF16, "krow") for _ in range(2)])
    qrow = Rot([P.sb([128, 4, 128], BF16, "qrow") for _ in range(2)])
    qTt = Rot([P.sb([128, 512], BF16, "qTt") for _ in range(2)])
    ptile = Rot([P.sb([128, 2, 512], BF16, "ptile") for _ in range(3)])
    zt = Rot([P.sb([128, 4, 128], BF16, "zt") for _ in range(2)])
    szt = Rot([P.sb([128, 4, 128], F32, "szt") for _ in range(2)])
    ezt = Rot([P.sb([128, 4, 128], F32, "ezt") for _ in range(2)])
    yh = Rot([P.sb([128, 4, 128], BF16, "yh") for _ in range(2)])
    sm = Rot([P.sb([128, 24], F32, "sm") for _ in range(2)])
    A0r = Rot([P.sb([128, 4, 129], F32, "A0") for _ in range(2)])
    A1r = Rot([P.sb([128, 4, 129], F32, "A1") for _ in range(2)])
    w0 = Rot([P.sb([128, 4, 128], F32, "w0") for _ in range(2)])
    w1 = Rot([P.sb([128, 4, 128], F32, "w1") for _ in range(2)])
    scr = Rot(kb.sc)
    acc = kb.acc
    scale = 64 ** -0.5
    sgb = sg.t[:].unsqueeze(1).to_broadcast([128, 4, 128])

    def emit_qk(kT, qT, kc):
        sc = scr.next()
        specs = [(sc.t[:, comp, :], kT.t[64 * comp:64 * comp + 64, kc * 128:(kc + 1) * 128], qT.t[64 * comp:64 * comp + 64, :], True, True)
                 for comp in range(2)]
        kb.matmuls(sc, specs, [kT, qT])
        pt = ptile.next()
        kb.act(pt, pt.t[:], sc, sc.t[:], AF.Exp, scale=scale)
        return pt

    def emit_pv(pt, va, kc):
        specs = []
        for comp in range(2):
            for j in range(4):
                bank = acc[2 * comp + j // 2]
                specs.append((bank.t[:, (j % 2) * 129:(j % 2) * 129 + 129], pt.t[:, comp, j * 128:(j + 1) * 128],
                              va.t[:, kc, :], kc == 0 and j % 2 == 0, kc == NKC - 1 and j % 2 == 1, True))
        kb.matmuls(list(acc), specs, [pt, va])

    def k_prep(h, kT, c8):
        kr = krow.next()
        r = c8 // 4
        for j in range(2):
            db_ = ex3[2 * (c8 % 4) + j]["dst"]
            kb.load(kr, kr.t[:, 4 * j:4 * j + 4, :], db_,
                    db_.t[r * 512:(r + 1) * 512, h * 128:(h + 1) * 128].rearrange("(c p) e -> p c e", p=128), q="sp")
        ptr = kb.ptr.next()
        kb.transposes(ptr, kr, [kr.t[:, c, :] for c in range(8)], kb.ident)
        kb.copy("dve", kT, kT.t[:, c8 * 1024:(c8 + 1) * 1024], ptr, ptr.t[:].rearrange("p c q -> p (c q)"))

    def v_load(h, va):
        for r in range(2):
            for c in range(8):
                db_ = ex3[c]["dst"]
                b0 = r * 32 + c * 4
                kb.load(va, va.t[:, b0:b0 + 4, 0:128], db_,
                        db_.t[r * 512:(r + 1) * 512, 1024 + h * 128: 1024 + (h + 1) * 128].rearrange("(c p) e -> p c e", p=128), q="sp")

    heads = [(kTh.next(), vah.next()) for _ in range(8)]
    v_load(0, heads[0][1])
    for c8 in range(NKC // 8):
        k_prep(0, heads[0][0], c8)
    def q_prep(h, qc):
        qr = qrow.next()
        kb.load(qr, qr.t[:], P3, P3.t[qc * 512:(qc + 1) * 512, h * 128:(h + 1) * 128].rearrange("(c p) e -> p c e", p=128), q="sp")
        z = zt.next()
        kb.load(z, z.t[:], P3, P3.t[qc * 512:(qc + 1) * 512, 3072 + h * 128: 3072 + (h + 1) * 128].rearrange("(c p) e -> p c e", p=128), q="sp")
        ptr = kb.ptr.next()
        kb.transposes(ptr, qr, [qr.t[:, c, :] for c in range(4)], kb.ident)
        qT = qTt.next()
        kb.copy("dve", qT, qT.t[:], ptr, ptr.t[:, 0:4, :].rearrange("p c q -> p (c q)"))
        return qT, z

    def silu_z(z):
        ez = ezt.next()
        kb.act(ez, ez.t[:], z, z.t[:], AF.Exp, scale=-1.0)
        kb.ts("dve", ez, ez.t[:], ez, ez.t[:], 1.0, None, ALU.add)
        sz = szt.next()
        kb.recip(sz, sz.t[:], ez, ez.t[:])
        kb.tt("pool", sz, sz.t[:], sz, sz.t[:], z, z.t[:], ALU.mult)
        return sz

    def epilogue(h, qc, A0, A1, sz):
        s_ = sm.next()
        kb.recip(s_, s_.t[:, 0:4], A0, A0.t[:, :, 128])
        kb.recip(s_, s_.t[:, 4:8], A1, A1.t[:, :, 128])
        kb.ts("dve", s_, s_.t[:, 8:12], s_, s_.t[:, 4:8], ls.t[:, 3:4], None, ALU.mult, sreads=[ls])
        a0, a1 = w0.next(), w1.next()
        kb.tt("pool", a0, a0.t[:], A0, A0.t[:, :, 0:128], s_, s_.t[:, 0:4].unsqueeze(2).to_broadcast([128, 4, 128]), ALU.mult)
        kb.tt("dve", a1, a1.t[:], A1, A1.t[:, :, 0:128], s_, s_.t[:, 8:12].unsqueeze(2).to_broadcast([128, 4, 128]), ALU.mult)
        kb.tt("pool", a0, a0.t[:], a0, a0.t[:], a1, a1.t[:], ALU.add)
        kb.tt("dve", a1, a1.t[:], a0, a0.t[:], a0, a0.t[:], ALU.mult)
        P.op("dve", lambda e: e.reduce_sum(out=s_.t[:, 12:16], in_=a1.t[:], axis=AX.X), reads=[a1.b], writes=[s_.b])
        kb.ts("dve", s_, s_.t[:, 12:16], s_, s_.t[:, 12:16], 1.0 / 128, SUBLN_EPS, ALU.mult, ALU.add)
        kb.act(s_, s_.t[:, 16:20], s_, s_.t[:, 12:16], AF.Ln)
        kb.act(s_, s_.t[:, 20:24], s_, s_.t[:, 16:20], AF.Exp, scale=-0.5)
        kb.tt("pool", a0, a0.t[:], a0, a0.t[:], s_, s_.t[:, 20:24].unsqueeze(2).to_broadcast([128, 4, 128]), ALU.mult)
        kb.tt("dve", a0, a0.t[:], a0, a0.t[:], sg, sgb, ALU.mult)
        y = yh.next()
        kb.tt("pool", y, y.t[:], a0, a0.t[:], sz, sz.t[:], ALU.mult)
        kb.store(Y, Y.t[qc * 512:(qc + 1) * 512, h * 128:(h + 1) * 128].rearrange("(c p) e -> p c e", p=128), y, y.t[:])

    items = [(h, qc) for h in range(8) for qc in range(TPC // 512)]
    cur = q_prep(*items[0])
    for ii, (h, qc) in enumerate(items):
        kT, va = heads[h]
        qT, z = cur
        sz = silu_z(z)
        pts = [emit_qk(kT, qT, 0)]
        for kc in range(NKC):
            if kc + 1 < NKC:
                pts.append(emit_qk(kT, qT, kc + 1))
            emit_pv(pts[kc], va, kc)
        if h + 1 < 8:
            if qc == 0:
                v_load(h + 1, heads[h + 1][1])
            k_prep(h + 1, heads[h + 1][0], qc)
        A0, A1 = A0r.next(), A1r.next()
        for half in range(2):
            kb.copy("dve", A0, A0.t[:, 2 * half:2 * half + 2, :], acc[half], acc[half].t[:, 0:258].rearrange("p (j e) -> p j e", e=129))
            kb.copy("dve", A1, A1.t[:, 2 * half:2 * half + 2, :], acc[2 + half], acc[2 + half].t[:, 0:258].rearrange("p (j e) -> p j e", e=129))
        if ii + 1 < len(items):
            cur = q_prep(*items[ii + 1])
        epilogue(h, qc, A0, A1, sz)
        P.maybe_epoch()
    P.barrier()
    pcb = Rot(list(kb.acc))
    prev_back = None
    for i in range(NT):
        yt = kb.b1p.next()
        kb.load(yt, yt.t[:], Y, Y.t[i * 128:(i + 1) * 128, :], q="sp")
        bk = kb.phase_c_tile(yt, X3, i * 128, OUT, i * 128, wout, final_g=fg, defer=True, banks=pcb)
        if prev_back is not None:
            prev_back()
        prev_back = bk
        P.maybe_epoch()
    prev_back()
    P.barrier()


STAGES = [stage0, stage1, stage2, stage3]
_NC_CACHE = {}


def build_program(stages=(0, 1, 2, 3)):
    key = tuple(stages)
    if key in _NC_CACHE:
        return _NC_CACHE[key]
    nc = bass.Bass("TRN2", target_bir_lowering=False)
    shared = {"dram": {}}
    with contextlib.ExitStack() as st:
        P = Prog(nc, st)
        for si in stages:
            with contextlib.ExitStack() as sst:
                P.cur_stack = sst
                kb = KB(nc, P, shared, nf32=8 if si == 0 else 5, layout="l3" if si == 3 else "std", nbf={0: 8, 3: 1}.get(si, 4))
                kb.setup_consts()
                STAGES[si](kb)
                P.barrier()
                P.emit()
    _NC_CACHE[key] = (nc, shared)
    return _NC_CACHE[key]


def _pp(v, n):
    return np.ascontiguousarray(np.asarray(v, np.float32).reshape(n, 128).T)


def _bc(v):
    v = np.asarray(v, np.float32)
    return np.ascontiguousarray(np.broadcast_to(v[None], (128,) + v.shape))


def _rope_table(dim):
    inv = (10000.0 ** (-(np.arange(0, dim, 2, dtype=np.float32) / np.float32(dim)))).astype(np.float32)
    ang = np.arange(S, dtype=np.float32)[:, None] * inv[None, :]
    cos, sin = np.cos(ang).astype(np.float32), np.sin(ang).astype(np.float32)
    nh = 512 // dim
    cosF = np.tile(np.concatenate([cos, cos], axis=1), (1, nh))
    sinS = np.tile(np.concatenate([-sin, sin], axis=1), (1, nh))
    return np.ascontiguousarray(np.stack([cosF, sinS], axis=1))


def _ext(rows, lo, hi, total):
    out = np.zeros((hi - lo,) + rows.shape[1:], rows.dtype)
    a, b = max(lo, 0), min(hi, total)
    out[a - lo:b - lo] = rows[a:b]
    return out


def kernel(x, c, norm_g, w_mod, b_mod, conv_w_in, conv_k, conv_w_out, dil_w_in, dil_w_out,
           swa_w_in, swa_sink, swa_w_out, diff_w_in, diff_lambda, diff_subln_g, diff_w_out, final_g,
           _stages=(0, 1, 2, 3), _debug_outs=()):
    f = lambda a: np.ascontiguousarray(np.asarray(a, np.float32))
    x, c = f(x), f(c)
    bf = ml_dtypes.bfloat16
    masks = np.zeros((128, 2, 128), np.float32)
    kk, qq = np.meshgrid(np.arange(128), np.arange(128), indexing="ij")
    masks[:, 0, :] = (kk >= qq)
    masks[:, 1, :] = (kk <= qq)
    masks = masks.astype(bf)
    cs128, cs64 = _rope_table(128), _rope_table(64)
    nc, shared = build_program(_stages)
    need = set(shared["dram"].keys())

    def valid(lo, hi):
        t = np.arange(lo, hi)
        return np.ascontiguousarray(((t >= 0) & (t < S)).astype(np.float32)[:, None])

    shared_in = dict(ident=np.eye(128, dtype=np.float32), masks=masks,
                     w_in0=f(conv_w_in[0]), w_out0=f(conv_w_out[0]), w_in1=f(dil_w_in[0]), w_out1=f(dil_w_out[0]),
                     w_in2=f(swa_w_in[0]), w_out2=f(swa_w_out[0]), w_in3=f(diff_w_in[0]), w_out3=f(diff_w_out[0]),
                     convk_bc=_bc(f(conv_k[0])), sink_bc=_bc(f(swa_sink[0])), lam_bc=_bc(f(diff_lambda[0])),
                     subln_bc=_bc(f(diff_subln_g[0])), finalg_bc=_bc(f(final_g)))
    for l in range(4):
        shared_in[f"wmod{l}"] = f(w_mod[l])
        shared_in[f"bmodT{l}"] = _pp(b_mod[l], 24)
        shared_in[f"ngT{l}"] = _pp(norm_g[l], 8)
        shared_in[f"bmodg{l}"] = f(b_mod[l][2048:3072]).reshape(1, D)
    cores = [(b, h) for b in range(NB) for h in range(2)]
    maps = []
    for (b, h) in cores:
        t0 = h * TPC
        d = dict(shared_in)
        d["ccol"] = _pp(c[b], 8)
        d["xe"] = _ext(x[b], t0 - 128, t0 + TPC + 128, S)
        vm = np.ones((128, 2), np.float32)
        if h == 0:
            vm[0, 0] = 0.0
        if h == 1:
            vm[127, 1] = 0.0
        d["vmask"] = vm
        d["cs1"] = np.ascontiguousarray(cs128[t0:t0 + TPC])
        d["cs2"] = np.ascontiguousarray(cs64[t0:t0 + TPC])
        d["kvalid1"] = valid(t0 - 1024, t0 + TPC + 1024)
        d["kvalid2"] = valid(t0 - 128, t0 + TPC + 128)
        maps.append({k: v for k, v in d.items() if k in need})
    res = run_bass_kernel_spmd(nc, maps, core_ids=list(range(8)))
    if _debug_outs:
        return res.results
    out = np.zeros((NB, S, D), np.float32)
    for ci, (b, h) in enumerate(cores):
        out[b, h * TPC:(h + 1) * TPC] = res.results[ci]["OUT"]
    return out
```

```python
import contextlib
import math
import numpy as np
import ml_dtypes
import concourse.bass as bass
import concourse.mybir as mybir
from concourse.bass_utils import run_bass_kernel_spmd

F32 = mybir.dt.float32
BF16 = mybir.dt.bfloat16
ALU = mybir.AluOpType
AF = mybir.ActivationFunctionType
AX = mybir.AxisListType

D = 1024
S = 8192
NB = 4
TPC = 4096
NT = TPC // 128
ENGS = ("pe", "act", "dve", "pool", "sp")
EPOCH_LIMIT = 20000
NDMASEM = 16
NCCSEM = 8
NORM_EPS = 1e-6
_DEBUG_OUT = set()
SUBLN_EPS = 1e-5


class Buf:
    __slots__ = ("name", "w", "r")

    def __init__(self, name=""):
        self.name = name
        self.w = None
        self.r = []


class T:
    __slots__ = ("t", "b", "name_is_dram", "parts")

    def __init__(self, t, name="", dram=False, buf=None):
        self.t = t
        self.b = buf if buf is not None else Buf(name)
        self.name_is_dram = dram
        self.parts = None

    def part(self, col):
        if self.parts is None:
            self.parts = {}
        k = col // 512
        if k not in self.parts:
            self.parts[k] = T(self.t, "part", buf=Buf())
        return self.parts[k]


class Prog:
    def __init__(self, nc, stack):
        self.nc = nc
        self.stack = stack
        self.q = {e: [] for e in ENGS}
        self.cnt = {e: 0 for e in ENGS}
        self.nsem = 0
        self.esem = {e: self._newsem("c_" + e) for e in ENGS}
        self.waited = {e: {} for e in ENGS}
        self.dsem = {e: [self._newsem(f"d_{e}{i}") for i in range(NDMASEM)] for e in ("sp", "pool", "act")}
        self.dval = {e: [0] * NDMASEM for e in self.dsem}
        self.dnext = {e: 0 for e in self.dsem}
        self.latest = {}
        self.nid = 0
        self.cur_stack = stack
        self.csem = [self._newsem(f"cc{i}") for i in range(NCCSEM)]
        self.cval = [0] * NCCSEM
        self.cnext = 0

    def _newsem(self, name):
        self.nsem += 1
        return self.stack.enter_context(self.nc.semaphore(name + f"_{self.nsem}"))

    def sb(self, shape, dt, name=None):
        self.nid += 1
        name = (name or "t") + f"_{self.nid}"
        return T(self.cur_stack.enter_context(self.nc.sbuf_tensor(name, list(shape), dt)), name)

    def ps(self, shape, dt, name=None):
        self.nid += 1
        name = (name or "p") + f"_{self.nid}"
        return T(self.cur_stack.enter_context(self.nc.psum_tensor(name, list(shape), dt)), name)

    def _deps(self, eng, reads, writes):
        deps = {}

        def add(tok):
            if tok is None:
                return
            s, v = tok
            if eng == "pe" and s is self.esem["pe"]:
                return
            k = id(s)
            if self.waited[eng].get(k, 0) >= v:
                return
            if k not in deps or deps[k][1] < v:
                deps[k] = (s, v)
        for b in reads:
            add(b.w)
        for b in writes:
            add(b.w)
            for t in b.r:
                add(t)
        out = list(deps.values())
        for s, v in out:
            self.waited[eng][id(s)] = v
        return out

    def _commit(self, tok, reads, writes):
        for b in reads:
            b.r = [t for t in b.r if t[0] is not tok[0]] + [tok]
        for b in writes:
            b.w = tok
            b.r = []
        self.latest[id(tok[0])] = tok

    @staticmethod
    def _bufs(xs):
        return [x.b if isinstance(x, T) else x for x in xs]

    def op(self, eng, fn, reads=(), writes=()):
        reads, writes = self._bufs(reads), self._bufs(writes)
        deps = self._deps(eng, reads, writes)
        self.cnt[eng] += 1
        tok = (self.esem[eng], self.cnt[eng])
        self.q[eng].append((deps, fn, tok[0], 1))
        self._commit(tok, reads, writes)
        return tok

    def dma(self, eng, fn, reads=(), writes=()):
        reads, writes = self._bufs(reads), self._bufs(writes)
        i = self.dnext[eng]
        self.dnext[eng] = (i + 1) % NDMASEM
        s = self.dsem[eng][i]
        deps = self._deps(eng, reads, writes)
        prev = self.dval[eng][i]
        if prev > 0 and self.waited[eng].get(id(s), 0) < prev:
            deps.append((s, prev))
            self.waited[eng][id(s)] = prev
        self.dval[eng][i] = prev + 16
        tok = (s, prev + 16)
        self.q[eng].append((deps, fn, s, 16))
        self._commit(tok, reads, writes)
        return tok

    def cc(self, fn, reads=(), writes=()):
        i = self.cnext
        self.cnext = (i + 1) % NCCSEM
        s = self.csem[i]
        deps = self._deps("pool", list(reads), list(writes))
        prev = self.cval[i]
        if prev > 0 and self.waited["pool"].get(id(s), 0) < prev:
            deps.append((s, prev))
            self.waited["pool"][id(s)] = prev
        self.cval[i] = prev + 1
        tok = (s, prev + 1)
        self.q["pool"].append((deps, fn, s, 1))
        self._commit(tok, list(reads), list(writes))
        return tok

    def barrier(self):
        toks = list(self.latest.values())
        for e in ENGS:
            deps = []
            for s, v in toks:
                if self.waited[e].get(id(s), 0) < v:
                    deps.append((s, v))
                    self.waited[e][id(s)] = v
            if deps:
                self.q[e].append((deps, None, None, 0))
        if max(self.cnt.values()) > EPOCH_LIMIT:
            dm = set(id(s) for ss in self.dsem.values() for s in ss) | set(id(s) for s in self.csem)
            for e in ENGS:
                self.esem[e] = self._newsem("c_" + e)
                self.cnt[e] = 0
            self.latest = {k: t for k, t in self.latest.items() if k in dm}

    def maybe_epoch(self):
        if max(self.cnt.values()) > EPOCH_LIMIT:
            self.barrier()

    def emit(self):
        qs = self.q
        self.q = {e: [] for e in ENGS}
        with self.nc.Block() as block:
            def run(engname):
                def _f(e):
                    for deps, fn, s, inc in qs[engname]:
                        for ds, dv in deps:
                            e.wait_ge(ds, dv)
                        if fn is not None:
                            fn(e).then_inc(s, inc)
                return _f
            block.tensor(run("pe"))
            block.scalar(run("act"))
            block.vector(run("dve"))
            block.gpsimd(run("pool"))
            block.sync(run("sp"))


class Rot:
    def __init__(self, items):
        self.items = items
        self.i = 0

    def next(self):
        x = self.items[self.i % len(self.items)]
        self.i += 1
        return x


class KB:
    def __init__(self, nc, P, shared, nf32=8, layout="std", nbf=4):
        self.nc = nc
        self.P = P
        self.dram = shared["dram"]
        self.shared = shared
        if layout == "std":
            self.ptr = Rot([P.ps([128, 8, 128], BF16, "ptr") for _ in range(2)])
            self.pg = [P.ps([128, 512], F32, "pg") for _ in range(3)]
            self.po = [P.ps([128, 512], F32, "po") for _ in range(3)]
        else:
            self.sc = [P.ps([128, 2, 512], F32, "sc") for _ in range(2)]
            self.acc = [P.ps([128, 512], F32, "acc") for _ in range(4)]
            self.ptr = Rot([T(x.t[:, 0, :].bitcast(BF16).rearrange("p (c q) -> p c q", q=128), "ptrv", buf=x.b) for x in self.sc])
            self.pg = [T(self.sc[0].t[:, 0, :], "pgv", buf=self.sc[0].b), T(self.sc[1].t[:, 0, :], "pgv", buf=self.sc[1].b),
                       T(self.sc[0].t[:, 1, :], "pgv", buf=self.sc[0].b)]
            self.po = None
        self.pgr = Rot(self.pg[0:2])
        self.dq = Rot(["sp", "act"])
        self.f32p = Rot([P.sb([128, D], F32, "f32p") for _ in range(nf32)])
        self.bfp = Rot([P.sb([128, 2048], BF16, "bfp") for _ in range(nbf)])
        self.b1p = Rot([P.sb([128, D], BF16, "b1p") for _ in range(4)])
        self.t3p = Rot([P.sb([128, 8, 128], BF16, "t3p") for _ in range(4)])
        self.ssp = Rot([P.sb([128, 4], F32, "ssp") for _ in range(4)])
        self.w_stage = Rot([P.sb([128, 8, 256], F32, "wst") for _ in range(2)])
        self.WCH = 256
        self.wm_stage = self.w_stage

    def din(self, name, shape, dt=F32):
        if name not in self.dram:
            self.dram[name] = T(self.nc.dram_tensor(name, list(shape), dt, kind="ExternalInput").ap(), name, True)
        return self.dram[name]

    def dout(self, name, shape, dt=F32):
        if name not in self.dram:
            self.dram[name] = T(self.nc.dram_tensor(name, list(shape), dt, kind="ExternalOutput").ap(), name, True)
        return self.dram[name]

    def dint(self, name, shape, dt=F32):
        if name in _DEBUG_OUT:
            return self.dout(name, shape, dt)
        if name not in self.dram:
            self.dram[name] = T(self.nc.dram_tensor(name, list(shape), dt).ap(), name, True)
        return self.dram[name]

    def _untracked(self, x):
        if x is None:
            return Buf()
        return Buf() if (isinstance(x, T) and x.name_is_dram) else x

    def load(self, dst, dst_ap, src, src_ap, q="sp", slow=False):
        src, dst = self._untracked(src), self._untracked(dst)
        if slow:
            self.P.dma(q, lambda e: e.dma_start(out=dst_ap, in_=src_ap, allow_slow_non_contiguous=True), reads=[src], writes=[dst])
        else:
            self.P.dma(q, lambda e: e.dma_start(out=dst_ap, in_=src_ap), reads=[src], writes=[dst])

    def store(self, dst, dst_ap, src, src_ap, q="pool"):
        src, dst = self._untracked(src), self._untracked(dst)
        self.P.dma(q, lambda e: e.dma_start(out=dst_ap, in_=src_ap), reads=[src], writes=[dst])

    def copy(self, eng, dst, dst_ap, src, src_ap):
        if eng == "act":
            self.P.op("act", lambda e: e.activation(out=dst_ap, in_=src_ap, func=AF.Copy), reads=[src], writes=[dst])
        else:
            self.P.op(eng, lambda e: e.tensor_copy(out=dst_ap, in_=src_ap), reads=[src], writes=[dst])

    def act(self, dst, dst_ap, src, src_ap, func, scale=1.0, accum=None, extra_w=()):
        if accum is None:
            self.P.op("act", lambda e: e.activation(out=dst_ap, in_=src_ap, func=func, scale=scale),
                      reads=[src], writes=[dst] + list(extra_w))
        else:
            self.P.op("act", lambda e: e.activation(out=dst_ap, in_=src_ap, func=func, scale=scale, accum_out=accum),
                      reads=[src], writes=[dst] + list(extra_w))

    def tt(self, eng, dst, dst_ap, a, a_ap, b, b_ap, op):
        self.P.op(eng, lambda e: e.tensor_tensor(out=dst_ap, in0=a_ap, in1=b_ap, op=op), reads=[a, b], writes=[dst])

    def ts(self, eng, dst, dst_ap, a, a_ap, s1, s2, op0, op1=None, sreads=()):
        if op1 is None:
            self.P.op(eng, lambda e: e.tensor_scalar(out=dst_ap, in0=a_ap, scalar1=s1, scalar2=None, op0=op0),
                      reads=[a] + list(sreads), writes=[dst])
        else:
            self.P.op(eng, lambda e: e.tensor_scalar(out=dst_ap, in0=a_ap, scalar1=s1, scalar2=s2, op0=op0, op1=op1),
                      reads=[a] + list(sreads), writes=[dst])

    def recip(self, dst, dst_ap, src, src_ap):
        self.P.op("dve", lambda e: e.reciprocal(out=dst_ap, in_=src_ap), reads=[src], writes=[dst])

    def memset(self, eng, dst, dst_ap, val):
        self.P.op(eng, lambda e: e.memset(dst_ap, val), writes=[dst])

    def transposes(self, dst_ps, src, src_aps, ident):
        def fn(e):
            for i, ap in enumerate(src_aps):
                ins = e.transpose(out=dst_ps.t[:, i, :], in_=ap, identity=ident.t[:])
            return ins
        self.P.op("pe", fn, reads=[src, ident], writes=[dst_ps])

    def matmuls(self, dst, specs, reads):
        def fn(e):
            for sp_ in specs:
                o, l, r, st, sp = sp_[:5]
                if len(sp_) > 5 and sp_[5]:
                    ins = e.matmul(o, lhsT=l, rhs=r, start=st, stop=sp, skip_group_check=True)
                else:
                    ins = e.matmul(o, lhsT=l, rhs=r, start=st, stop=sp)
            return ins
        self.P.op("pe", fn, reads=reads, writes=(dst if isinstance(dst, (list, tuple)) else [dst]))

    def setup_consts(self):
        P = self.P
        idf = P.sb([128, 128], F32, "idf")
        self.ident = P.sb([128, 128], BF16, "ident")
        d = self.din("ident", [128, 128])
        self.load(idf, idf.t[:], d, d.t)
        self.copy("dve", self.ident, self.ident.t[:], idf, idf.t[:])
        self.ones_row = P.sb([1, 128], F32, "ones_row")
        self.memset("dve", self.ones_row, self.ones_row.t[:], 1.0)

    def mod_vectors(self, l, need_ab, need_gate):
        P = self.P
        wmod = self.din(f"wmod{l}", [D, 3 * D])
        bT = self.din(f"bmodT{l}", [128, 24])
        res = {}
        if not hasattr(self, "silu_c"):
            cc = self.din("ccol", [128, 8])
            ct = P.sb([128, 8], F32, "ccol")
            self.load(ct, ct.t[:], cc, cc.t)
            self.silu_c = P.sb([128, 8], F32, "siluc")
            self.act(self.silu_c, self.silu_c.t[:], ct, ct.t[:], AF.Silu)
        sc = self.silu_c
        bt = P.sb([128, 24], F32, "bT")
        self.load(bt, bt.t[:], bT, bT.t)
        if need_ab:
            gT = self.din(f"ngT{l}", [128, 8])
            gt = P.sb([128, 8], F32, "gT")
            self.load(gt, gt.t[:], gT, gT.t)
            modT = P.sb([128, 16], F32, "modT")
            pg = self.pg[2]
            for g4 in range(8):
                st = self.wm_stage.next()
                self.load(st, st.t[:], wmod, wmod.t[:, g4 * 256:(g4 + 1) * 256].rearrange("(c p) n -> p c n", p=128),
                          q=self.dq.next())
                specs = []
                for jj in range(2):
                    j = g4 * 2 + jj
                    for k in range(8):
                        specs.append((pg.t[:, j:j + 1], st.t[:, k, jj * 128:(jj + 1) * 128], sc.t[:, k:k + 1], k == 0, k == 7))
                self.matmuls(pg, specs, [st, sc])
            self.tt("dve", modT, modT.t[:], pg, pg.t[:, 0:16], bt, bt.t[:, 0:16], ALU.add)
            A = P.sb([128, 8], F32, "A")
            self.P.op("dve", lambda e: e.scalar_tensor_tensor(out=A.t[:], in0=modT.t[:, 8:16], scalar=1.0, in1=gt.t[:],
                                                               op0=ALU.add, op1=ALU.mult), reads=[modT, gt], writes=[A])
            res["A"] = A
            res["B"] = modT
        if need_gate:
            bg = self.din(f"bmodg{l}", [1, D])
            bgt = P.sb([1, D], F32, "bg")
            self.load(bgt, bgt.t[:], bg, bg.t)
            grow = P.sb([1, D], F32, "grow")
            gbc = P.sb([128, D], F32, "gbc")
            for g2 in range(4):
                st = self.wm_stage.next()
                c0 = 2048 + g2 * 256
                self.load(st, st.t[:], wmod, wmod.t[:, c0:c0 + 256].rearrange("(c p) n -> p c n", p=128), q=self.dq.next())
                pg = self.pg[g2 % 2]
                specs = [(pg.t[0:1, 0:256], sc.t[:, k:k + 1], st.t[:, k, :], k == 0, k == 7) for k in range(8)]
                self.matmuls(pg, specs, [st, sc])
                self.tt("dve", grow, grow.t[:, g2 * 256:(g2 + 1) * 256], pg, pg.t[0:1, 0:256], bgt, bgt.t[:, g2 * 256:(g2 + 1) * 256], ALU.add)
            for g2 in range(2):
                pg = self.pg[g2]
                self.matmuls(pg, [(pg.t[:], self.ones_row.t[:], grow.t[:, g2 * 512:(g2 + 1) * 512], True, True)],
                             [self.ones_row, grow])
                self.copy("act", gbc, gbc.t[:, g2 * 512:(g2 + 1) * 512], pg, pg.t[:])
            res["gate_bc"] = gbc
        return res

    def load_weight(self, wsb, wdram, c0, ncols, scale_bc=None):
        P = self.P
        for g in range(0, ncols, 256):
            n = min(256, ncols - g)
            st = self.w_stage.next()
            self.load(st, st.t[:, :, 0:n], wdram, wdram.t[:, c0 + g:c0 + g + n].rearrange("(c p) n -> p c n", p=128),
                      q=self.dq.next())
            if scale_bc is None:
                self.copy("dve" if (g // 256) % 2 == 0 else "act", wsb.part(g), wsb.t[:, :, g:g + n], st, st.t[:, :, 0:n])
            else:
                for k in range(8):
                    self.tt("dve", wsb.part(g), wsb.t[:, k, g:g + n], st, st.t[:, k, 0:n], scale_bc, scale_bc.t[:, g:g + n], ALU.mult)

    def phase_a(self, ntiles, xsrc, xrow0, pdst, prow0, pcol0, wsb, ncols, A, Bm, cs, csrow0, hd, rope_cols, also=None, hcache=None):
        P = self.P
        if not hasattr(self, "pa_bufs"):
            self.pa_bufs = dict(
                tA=Rot([P.sb([128, 512], F32, "patA") for _ in range(3)]),
                tB=Rot([P.sb([128, 512], F32, "patB") for _ in range(3)]),
                csF=Rot([P.sb([128, 2, 512], F32, "pacsF") for _ in range(3)]),
            )
        bufs = self.pa_bufs
        h2 = hd // 2
        nhf = 512 // hd

        hc_tiles, hc_mode = hcache if hcache is not None else (None, None)

        def prep_a(i):
            if hc_mode == "r":
                csF = None
                if rope_cols > 0:
                    csF = bufs["csF"].next()
                    self.load(csF, csF.t[:], cs, cs.t[csrow0 + i * 128: csrow0 + (i + 1) * 128, :, :], q="sp")
                return None, csF, i
            xt = self.f32p.next()
            self.load(xt, xt.t[:], xsrc, xsrc.t[xrow0 + i * 128: xrow0 + (i + 1) * 128, :], q="sp")
            ss = self.ssp.next()
            sq = self.f32p.next()
            self.act(sq, sq.t[:], xt, xt.t[:], AF.Square, accum=ss.t[:, 0:1], extra_w=[ss])
            self.ts("dve", ss, ss.t[:, 1:2], ss, ss.t[:, 0:1], 1.0 / D, NORM_EPS, ALU.mult, ALU.add)
            self.act(ss, ss.t[:, 2:3], ss, ss.t[:, 1:2], AF.Sqrt)
            self.recip(ss, ss.t[:, 3:4], ss, ss.t[:, 2:3])
            xn = self.b1p.next()
            self.ts("dve", xn, xn.t[:], xt, xt.t[:], ss.t[:, 3:4], None, ALU.mult, sreads=[ss])
            csF = None
            if rope_cols > 0:
                csF = bufs["csF"].next()
                self.load(csF, csF.t[:], cs, cs.t[csrow0 + i * 128: csrow0 + (i + 1) * 128, :, :], q="sp")
            return xn, csF, i

        def prep_b(pa):
            xn, csF, i = pa
            if hc_mode == "r":
                hT = self.t3p.next()
                self.load(hT, hT.t[:], hc_tiles[i], hc_tiles[i].t, q="sp")
                cosF = sinS = None
                if csF is not None:
                    cosF = T(csF.t[:, 0, :], "cosF", buf=csF.b)
                    sinS = T(csF.t[:, 1, :], "sinS", buf=csF.b)
                return hT, cosF, sinS
            ptr = self.ptr.next()
            self.transposes(ptr, xn, [xn.t[:, c * 128:(c + 1) * 128] for c in range(8)], self.ident)
            hT = self.t3p.next()
            for c in range(8):
                if rope_cols == 0 and c % 2 == 1:
                    self.ts("dve", hT, hT.t[:, c, :], ptr, ptr.t[:, c, :], A.t[:, c:c + 1], Bm.t[:, c:c + 1],
                            ALU.mult, ALU.add, sreads=[A, Bm])
                else:
                    self._act_affine(hT, hT.t[:, c, :], ptr, ptr.t[:, c, :], A, A.t[:, c:c + 1], Bm, Bm.t[:, c:c + 1])
            if hc_mode == "w":
                self.store(hc_tiles[i], hc_tiles[i].t, hT, hT.t[:], q="pool")
            cosF = sinS = None
            if csF is not None:
                cosF = T(csF.t[:, 0, :], "cosF", buf=csF.b)
                sinS = T(csF.t[:, 1, :], "sinS", buf=csF.b)
            return hT, cosF, sinS

        def mm(i, st):
            hT, cosF, sinS = st
            for og in range(0, ncols, 2048):
                on = min(2048, ncols - og)
                ot = self.bfp.next()
                for g in range(og, og + on, 512):
                    pg = self.pgr.next()
                    specs = [(pg.t[:], hT.t[:, k, :], wsb.t[:, k, g:g + 512], k == 0, k == 7) for k in range(8)]
                    self.matmuls(pg, specs, [hT, wsb.part(g)])
                    lo = g - og
                    rc = max(0, min(512, rope_cols - g))
                    if rc > 0:
                        tA, tB = bufs["tA"].next(), bufs["tB"].next()
                        self.tt("dve", tA, tA.t[:, 0:rc], pg, pg.t[:, 0:rc], cosF, cosF.t[:, 0:rc], ALU.mult)
                        q4 = pg.t[:, 0:rc].rearrange("p (h two d) -> p h two d", two=2, d=h2)
                        b4 = tB.t[:, 0:rc].rearrange("p (h two d) -> p h two d", two=2, d=h2)
                        s4 = sinS.t[:, 0:rc].rearrange("p (h two d) -> p h two d", two=2, d=h2)
                        self.tt("dve", tB, b4[:, :, 0, :], pg, q4[:, :, 1, :], sinS, s4[:, :, 0, :], ALU.mult)
                        self.tt("dve", tB, b4[:, :, 1, :], pg, q4[:, :, 0, :], sinS, s4[:, :, 1, :], ALU.mult)
                        self.tt("pool", ot, ot.t[:, lo:lo + rc], tA, tA.t[:, 0:rc], tB, tB.t[:, 0:rc], ALU.add)
                        if rc < 512:
                            self.copy("act", ot, ot.t[:, lo + rc:lo + 512], pg, pg.t[:, rc:512])
                    else:
                        self.copy("act", ot, ot.t[:, lo:lo + 512], pg, pg.t[:])
                self.store(pdst, pdst.t[prow0 + i * 128: prow0 + (i + 1) * 128, pcol0 + og: pcol0 + og + on], ot, ot.t[:, 0:on])
                for (adst, atile, arow, c_lo, c_hi, dcol0) in (also or ()):
                    if atile != i:
                        continue
                    lo_, hi_ = max(c_lo, og), min(c_hi, og + on)
                    if hi_ > lo_:
                        self.store(None, adst.t[arow: arow + 128, dcol0 + lo_ - c_lo: dcol0 + hi_ - c_lo],
                                   ot, ot.t[:, lo_ - og:hi_ - og], q="pool")

        pa = {0: prep_a(0)}
        if ntiles > 1:
            pa[1] = prep_a(1)
        st = prep_b(pa.pop(0))
        for i in range(ntiles):
            if i + 2 < ntiles:
                pa[i + 2] = prep_a(i + 2)
            nxt = prep_b(pa.pop(i + 1)) if i + 1 < ntiles else None
            mm(i, st)
            st = nxt
            self.P.maybe_epoch()

    def hcache_tiles(self, name, ntiles):
        full = self.dint(name, [ntiles, 128, 1024], BF16)
        return [T(full.t[i].rearrange("p (c q) -> p c q", q=128), "hc", dram=False) for i in range(ntiles)]

    def _act_affine(self, dst, dst_ap, src, src_ap, a_t, a_ap, b_t, b_ap):
        self.P.op("act", lambda e: e.activation(out=dst_ap, in_=src_ap, func=AF.Identity, bias=b_ap, scale=a_ap),
                  reads=[src, a_t, b_t], writes=[dst])

    def phase_c_tile(self, ytile, xsrc, xrow, xdst, drow, wout, final_g=None, defer=False, banks=None, store_q="pool"):
        P = self.P
        if not hasattr(self, "pc_bufs"):
            self.pc_bufs = dict(yT=self.t3p, x=self.f32p, xo=self.f32p, ss=self.ssp, sq=self.f32p)
        bufs = self.pc_bufs
        banks = banks or self.pgr
        ptr = self.ptr.next()
        self.transposes(ptr, ytile, [ytile.t[:, c * 128:(c + 1) * 128] for c in range(8)], self.ident)
        yT = bufs["yT"].next()
        self.copy("act", yT, yT.t[:], ptr, ptr.t[:])
        xt = bufs["x"].next()
        self.load(xt, xt.t[:], xsrc, xsrc.t[xrow:xrow + 128, :], q="sp")
        pgs = []
        for half in range(2):
            pg = banks.next()
            specs = [(pg.t[:], yT.t[:, k, :], wout.t[:, k, half * 512:(half + 1) * 512], k == 0, k == 7) for k in range(8)]
            self.matmuls(pg, specs, [yT, wout.part(half * 512)])
            pgs.append(pg)

        def back():
            xo = bufs["xo"].next()
            for half in range(2):
                pg = pgs[half]
                self.tt("dve", xo, xo.t[:, half * 512:(half + 1) * 512], pg, pg.t[:], xt, xt.t[:, half * 512:(half + 1) * 512], ALU.add)
            if final_g is not None:
                ss = bufs["ss"].next()
                sq = bufs["sq"].next()
                self.act(sq, sq.t[:], xo, xo.t[:], AF.Square, accum=ss.t[:, 0:1], extra_w=[ss])
                self.ts("dve", ss, ss.t[:, 1:2], ss, ss.t[:, 0:1], 1.0 / D, NORM_EPS, ALU.mult, ALU.add)
                self.act(ss, ss.t[:, 2:3], ss, ss.t[:, 1:2], AF.Sqrt)
                self.recip(ss, ss.t[:, 3:4], ss, ss.t[:, 2:3])
                self.P.op("dve", lambda e: e.scalar_tensor_tensor(out=xt.t[:], in0=xo.t[:], scalar=ss.t[:, 3:4], in1=final_g.t[:],
                                                                   op0=ALU.mult, op1=ALU.mult), reads=[xo, ss, final_g], writes=[xt])
                self.store(xdst, xdst.t[drow:drow + 128, :], xt, xt.t[:], q=store_q)
            else:
                self.store(xdst, xdst.t[drow:drow + 128, :], xo, xo.t[:], q=store_q)
        if defer:
            return back
        back()
        return None

    def conv_layer(self, p0, convk_bc, vmask, xsrc, xdst, wout):
        P = self.P
        cx = [self.bfp] * 3
        tj = [self.f32p] * 3
        bz, sz, acc, yt = self.bfp, self.f32p, self.f32p, self.b1p
        pcb = Rot([self.pg[0], self.pg[1], self.pg[2], self.po[0]])
        prev_back = None
        for i in range(NT):
            r0 = 128 + i * 128
            ts_ = []
            for j in range(3):
                c = cx[j].next()
                self.load(c, c.t[:], p0, p0.t[r0 + j - 1: r0 + j - 1 + 128, 1024:3072], q="sp")
                t = tj[j].next()
                self.tt("pool", t, t.t[:], c, c.t[:, 0:1024], c, c.t[:, 1024:2048], ALU.mult)
                if (i == 0 and j == 0):
                    self.ts("pool", t, t.t[:], t, t.t[:], vmask.t[:, 0:1], None, ALU.mult, sreads=[vmask])
                if (i == NT - 1 and j == 2):
                    self.ts("pool", t, t.t[:], t, t.t[:], vmask.t[:, 1:2], None, ALU.mult, sreads=[vmask])
                ts_.append(t)
            b = bz.next()
            self.load(b, b.t[:, 0:1024], p0, p0.t[r0:r0 + 128, 0:1024], q="sp")
            self.load(b, b.t[:, 1024:2048], p0, p0.t[r0:r0 + 128, 3072:4096], q="sp")
            s = sz.next()
            self.act(s, s.t[:], b, b.t[:, 1024:2048], AF.Silu)
            a = acc.next()
            self.tt("dve", a, a.t[:], ts_[0], ts_[0].t[:], convk_bc, convk_bc.t[:, 0, :], ALU.mult)
            self.tt("pool", ts_[1], ts_[1].t[:], ts_[1], ts_[1].t[:], convk_bc, convk_bc.t[:, 1, :], ALU.mult)
            self.tt("dve", ts_[2], ts_[2].t[:], ts_[2], ts_[2].t[:], convk_bc, convk_bc.t[:, 2, :], ALU.mult)
            self.tt("pool", a, a.t[:], a, a.t[:], ts_[1], ts_[1].t[:], ALU.add)
            self.tt("dve", a, a.t[:], a, a.t[:], ts_[2], ts_[2].t[:], ALU.add)
            self.tt("dve", s, s.t[:], s, s.t[:], b, b.t[:, 0:1024], ALU.mult)
            y = yt.next()
            self.tt("dve", y, y.t[:], a, a.t[:], s, s.t[:], ALU.mult)
            bk = self.phase_c_tile(y, xsrc, r0, xdst, i * 128, wout, defer=True, banks=pcb, store_q="act")
            if prev_back is not None:
                prev_back()
            prev_back = bk
            self.P.maybe_epoch()
        prev_back()

    def band_attention(self, *, pq, kv, kvalid, masks, nheads, hd, ev, qcol0, kcol0, vcol0, nkvh, gq,
                       blocks, scale, finish):
        P = self.P
        qcols = nheads * hd
        kcols = nkvh * hd
        nqt = qcols // 128
        dup = (hd == 64)
        nkt = nkvh if dup else kcols // 128
        maxch = max(len(b["chunks"]) for b in blocks)
        hb = max(1, 512 // (128 * maxch))
        key = (qcols, nkt, nkvh, ev, maxch)
        if not hasattr(self, "_ba_cache"):
            self._ba_cache = {}
        if key not in self._ba_cache:
            self._ba_cache[key] = dict(
                qrow=Rot([P.sb([128, qcols], BF16, "ba_q") for _ in range(2)]),
                qT=Rot([P.sb([128, nqt, 128], BF16, "ba_qT") for _ in range(2)]),
                krow=Rot([P.sb([128, nkt * 128], BF16, "ba_k") for _ in range(2 * maxch)]),
                kT=Rot([P.sb([128, nkt, 128], BF16, "ba_kT") for _ in range(2 * maxch)]),
                vaug=Rot([P.sb([128, nkvh, ev + 1], BF16, "ba_v") for _ in range(2 * maxch)]),
                pt=Rot([P.sb([128, hb * maxch, 128], BF16, "ba_pt") for _ in range(3)]),
                kvc=Rot([P.sb([128, 1], F32, "ba_kvc") for _ in range(2 * maxch)]))
            for v_ in self._ba_cache[key]["vaug"].items:
                self.memset("pool", v_, v_.t[:], 1.0)
        c_ = self._ba_cache[key]
        kvcr = c_["kvc"]
        qrow, qT, krow, kT, vaug, pt = c_["qrow"], c_["qT"], c_["krow"], c_["kT"], c_["vaug"], c_["pt"]
        per_bank = 512 // (ev + 1)
        nch = maxch
        assert all(len(b_["chunks"]) == nch for b_ in blocks)
        if "maskb" not in c_:
            c_["maskb"] = P.sb([128, hb * nch, 128], BF16, "ba_maskb")
        maskb = c_["maskb"]
        self.memset("pool", maskb, maskb.t[:], 0.0)
        for ii in range(hb):
            for ci in range(nch):
                mid = blocks[0]["chunks"][ci][2]
                if mid is not None:
                    self.ts("pool", maskb, maskb.t[:, ii * nch + ci, :], masks, masks.t[:, mid, :], 30000.0, -30000.0, ALU.mult, ALU.add)
        oaps = []
        for h in range(nheads):
            bank = self.po[h // per_bank]
            sl = (h % per_bank) * (ev + 1)
            oaps.append((bank, bank.t[:, sl:sl + ev + 1]))

        def prep_loads(blk):
            qr = qrow.next()
            r0, st = blk["qrow0"], blk["qstep"]
            self.load(qr, qr.t[:], pq, pq.t[r0:r0 + 127 * st + 1:st, qcol0:qcol0 + qcols], q="sp")
            vas, kvcs, krs = [], [], []
            for (k0, kst, mid, halo) in blk["chunks"]:
                kr = krow.next()
                if dup:
                    ksrc = kv.t[k0:k0 + 127 * kst + 1:kst, kcol0:kcol0 + kcols].rearrange("p (h d) -> p h d", d=hd)
                    kr4 = kr.t[:].rearrange("p (h two d) -> p h two d", two=2, d=hd)
                    self.load(kr, kr4[:, :, 0, :], kv, ksrc, q="sp")
                    self.load(kr, kr4[:, :, 1, :], kv, ksrc, q="sp")
                else:
                    self.load(kr, kr.t[:], kv, kv.t[k0:k0 + 127 * kst + 1:kst, kcol0:kcol0 + kcols], q="sp")
                va = vaug.next()
                self.load(va, va.t[:, :, 0:ev], kv,
                          kv.t[k0:k0 + 127 * kst + 1:kst, vcol0:vcol0 + nkvh * ev].rearrange("p (h e) -> p h e", e=ev), q="sp")
                if halo:
                    kc_ = kvcr.next()
                    self.load(kc_, kc_.t[:], kvalid, kvalid.t[k0:k0 + 127 * kst + 1:kst, :], q="sp", slow=True)
                    kvcs.append(kc_)
                else:
                    kvcs.append(None)
                krs.append(kr)
                vas.append(va)
            return qr, krs, vas, kvcs

        def prep_tr(ld):
            qr, krs, vas, kvcs = ld
            kts = []
            ptr = self.ptr.next()
            self.transposes(ptr, qr, [qr.t[:, c * 128:(c + 1) * 128] for c in range(nqt)], self.ident)
            qt = qT.next()
            self.copy("dve", qt, qt.t[:], ptr, ptr.t[:, 0:nqt, :])
            for kr in krs:
                ptr = self.ptr.next()
                self.transposes(ptr, kr, [kr.t[:, c * 128:(c + 1) * 128] for c in range(nkt)], self.ident)
                kt = kT.next()
                self.copy("dve", kt, kt.t[:], ptr, ptr.t[:, 0:nkt, :])
                kts.append(kt)
            return qt, kts, vas, kvcs

        def emit_qk(st, hs):
            qt, kts, vas, kvcs = st
            pg = self.pgr.next()
            specs = []
            for ii, h in enumerate(hs):
                kvh = h // gq
                qd = h * hd
                po_ = qd % 128
                for ci in range(nch):
                    col = (ii * nch + ci) * 128
                    specs.append((pg.t[:, col:col + 128], kts[ci].t[po_: po_ + hd, kvh, :], qt.t[po_: po_ + hd, qd // 128, :],
                                  len(specs) == 0, False, True))
            ncol = len(hs) * nch
            specs.append((pg.t[:, 0:ncol * 128], self.ident.t[:], maskb.t[:, 0:ncol, :].rearrange("p c q -> p (c q)"), False, True, True))
            self.matmuls(pg, specs, kts + [qt, maskb, self.ident])
            p = pt.next()
            self.act(p, p.t[:, 0:ncol, :], pg, pg.t[:, 0:ncol * 128].rearrange("p (c q) -> p c q", q=128), AF.Exp, scale=scale)
            for ii, h in enumerate(hs):
                for ci in range(nch):
                    if kvcs[ci] is not None:
                        cc = ii * nch + ci
                        self.ts("dve", p, p.t[:, cc, :], p, p.t[:, cc, :], kvcs[ci].t[:, 0:1], None, ALU.mult, sreads=[kvcs[ci]])
            return p

        def emit_pv(st, hs, p):
            qt, kts, vas, kvcs = st
            banks = []
            per = {}
            for ii, h in enumerate(hs):
                kvh = h // gq
                bank, oap = oaps[h]
                if bank not in banks:
                    banks.append(bank)
                    per[id(bank)] = []
                for ci in range(nch):
                    cc = ii * nch + ci
                    per[id(bank)].append((oap, p.t[:, cc, :], vas[ci].t[:, kvh, :], ci == 0, ci == nch - 1))
            for bank in banks:
                self.matmuls(bank, per[id(bank)], [p] + vas)

        units = [list(range(h0, min(nheads, h0 + hb))) for h0 in range(0, nheads, hb)]
        prep_at = min(len(units) - 1, max(1, len(units) // 2))
        st = prep_tr(prep_loads(blocks[0]))
        deferred = None
        for bi, blk in enumerate(blocks):
            ld = prep_loads(blocks[bi + 1]) if bi + 1 < len(blocks) else None
            nxt = None
            ps_ = [emit_qk(st, units[0])]
            for ui, hs in enumerate(units):
                if ui + 1 < len(units):
                    ps_.append(emit_qk(st, units[ui + 1]))
                emit_pv(st, hs, ps_[ui])
                if ui == prep_at - 1 and ld is not None:
                    nxt = prep_tr(ld)
            if nxt is None and ld is not None:
                nxt = prep_tr(ld)
            d_ = finish(bi, blk, oaps)
            if deferred is not None:
                deferred()
            deferred = d_
            st = nxt
            self.P.maybe_epoch()
        if deferred is not None:
            deferred()


def _group(xs, n):
    return [xs[i:i + n] for i in range(0, len(xs), n)]


def d2d(kb, dst, dst_ap, src, src_ap, q="sp"):
    kb.load(dst, dst_ap, src, src_ap, q=q)


CC_MAX_BYTES = 2 << 20


def exchange_alloc(kb, name, pieces, cols):
    rmax = max(1, CC_MAX_BYTES // (cols * 2))
    jobs = []
    ci = 0
    for pi, (rows, d0, d1) in enumerate(pieces):
        for r0 in range(0, rows, rmax):
            n = min(rmax, rows - r0)
            sb_ = kb.dint(f"{name}_s{ci}", [n, cols], BF16)
            db_ = kb.dint(f"{name}_d{ci}", [2 * n, cols], BF16)
            sb_.name_is_dram = False
            db_.name_is_dram = False
            ci += 1
            jobs.append(dict(src=sb_, dst=db_, n=n, r0=r0, d0=d0, d1=d1, piece=pi))
    return jobs


def exchange_also(jobs, piece, tile0, c_lo, c_hi, dcol0):
    out = []
    for jb in jobs:
        if jb["piece"] != piece:
            continue
        for t in range(jb["n"] // 128):
            out.append((jb["src"], tile0 + jb["r0"] // 128 + t, t * 128, c_lo, c_hi, dcol0))
    return out


def exchange_cc(kb, jobs):
    groups = [[0, 1], [2, 3], [4, 5], [6, 7]]
    for jb in jobs:
        sb_, db_ = jb["src"], jb["dst"]
        kb.P.cc((lambda sb_=sb_, db_=db_: (lambda e: e.collective_compute(
            "AllGather", ALU.bypass, replica_groups=groups, ins=[sb_.t.opt()], outs=[db_.t.opt()])))(),
            reads=[sb_.b], writes=[db_.b])


def exchange_finish(kb, jobs):
    for jb in jobs:
        db_, n, r0 = jb["dst"], jb["n"], jb["r0"]
        if jb["d0"] is not None:
            d2d(kb, None, jb["d0"][r0:r0 + n, :], db_, db_.t[0:n, :], q=kb.dq.next())
        if jb["d1"] is not None:
            d2d(kb, None, jb["d1"][r0:r0 + n, :], db_, db_.t[n:2 * n, :], q=kb.dq.next())


def stage0(kb):
    P = kb.P
    xe = kb.din("xe", [TPC + 256, D])
    m0 = kb.mod_vectors(0, True, True)
    m1 = kb.mod_vectors(1, True, False)
    w_in0 = kb.din("w_in0", [D, 4096])
    w_out0 = kb.din("w_out0", [D, D])
    w_in1 = kb.din("w_in1", [D, 8192])
    convk = kb.din("convk_bc", [128, 3, D])
    vmask = kb.din("vmask", [128, 2])
    cs1 = kb.din("cs1", [TPC, 2, 512])
    X1 = kb.dint("X1", [TPC, D])
    P1 = kb.dint("P1", [TPC, 8192], BF16)
    p0 = kb.dint("p0", [TPC + 256, 4096], BF16)
    wsb = P.sb([128, 8, 2048], BF16, "wsb")
    wout = P.sb([128, 8, D], BF16, "wout")
    ck = P.sb([128, 3, D], F32, "convk")
    vm = P.sb([128, 2], F32, "vmask")
    kb.load(ck, ck.t[:], convk, convk.t)
    kb.load(vm, vm.t[:], vmask, vmask.t)
    kb.load_weight(wout, w_out0, 0, D, scale_bc=m0["gate_bc"])
    hc0 = kb.hcache_tiles("hc0", NT + 2)
    for ps_ in range(2):
        kb.load_weight(wsb, w_in0, ps_ * 2048, 2048)
        kb.phase_a(NT + 2, xe, 0, p0, 0, ps_ * 2048, wsb, 2048, m0["A"], m0["B"], None, 0, 128, 0,
                   hcache=(hc0, "w" if ps_ == 0 else "r"))
    P.barrier()
    kb.conv_layer(p0, ck, vm, xe, X1, wout)
    P.barrier()
    HALO = 1024
    kv = kb.dint("kv1", [TPC + 2 * HALO, 4096], BF16)
    hc1 = kb.hcache_tiles("hc1", NT)
    jobs = exchange_alloc(kb, "ex1", [
        (HALO, kv.t[0:HALO, :], None),
        (HALO, None, kv.t[HALO + TPC:, :]),
    ], 4096)
    for (c0, nc_, rc_) in ((3072, 2048, 2048), (5120, 2048, 1024)):
        kb.load_weight(wsb, w_in1, c0, nc_)
        also = [(kv, i, HALO + i * 128, 0, nc_, c0 - 3072) for i in range(NT)]
        also += exchange_also(jobs, 0, NT - HALO // 128, 0, nc_, c0 - 3072)
        also += exchange_also(jobs, 1, 0, 0, nc_, c0 - 3072)
        kb.phase_a(NT, X1, 0, P1, 0, c0, wsb, nc_, m1["A"], m1["B"], cs1, 0, 128, rc_, also=also,
                   hcache=(hc1, "w" if c0 == 3072 else "r"))
    P.barrier()
    exchange_cc(kb, jobs)
    for (c0, nc_, rc_) in ((0, 2048, 2048), (2048, 1024, 1024), (7168, 1024, 0)):
        kb.load_weight(wsb, w_in1, c0, nc_)
        kb.phase_a(NT, X1, 0, P1, 0, c0, wsb, nc_, m1["A"], m1["B"], cs1, 0, 128, rc_, hcache=(hc1, "r"))
    exchange_finish(kb, jobs)
    P.barrier()


def stage1(kb):
    P = kb.P
    HALO = 1024
    X1 = kb.dram["X1"]
    P1 = kb.dram["P1"]
    kv = kb.dram["kv1"]
    kvalid = kb.din("kvalid1", [TPC + 2 * HALO, 1])
    masksd = kb.din("masks", [128, 2, 128], BF16)
    m1 = kb.mod_vectors(1, False, True)
    m2 = kb.mod_vectors(2, True, False)
    w_out1 = kb.din("w_out1", [D, D])
    w_in2 = kb.din("w_in2", [D, 2560])
    cs2 = kb.din("cs2", [TPC, 2, 512])
    X2 = kb.dint("X2", [TPC, D])
    P2 = kb.dint("P2", [TPC, 2560], BF16)
    og = [kb.dint(f"og{g}", [TPC, 8 * 129]) for g in range(3)]
    wout = P.sb([128, 8, D], BF16, "wout")
    masks = P.sb([128, 2, 128], BF16, "masks")
    kb.load(masks, masks.t[:], masksd, masksd.t)
    kb.load_weight(wout, w_out1, 0, D, scale_bc=m1["gate_bc"])
    P.barrier()
    osb = Rot([P.sb([128, 8 * 129], F32, "osb") for _ in range(2)])
    for g, r in enumerate((1, 4, 16)):
        blocks = []
        for p in range(r):
            for j in range(TPC // (128 * r)):
                i0 = 128 * j
                q0 = p + r * i0
                ka = HALO + p + r * (i0 - 64)
                kb_ = HALO + p + r * (i0 + 64)
                ha = ka < HALO
                hb_ = kb_ + 127 * r >= HALO + TPC
                blocks.append(dict(qrow0=q0, qstep=r, chunks=[(ka, r, 0, ha), (kb_, r, 1, hb_)]))

        def finish(bi, blk, oaps, g=g, r=r):
            o = osb.next()
            for bnk in range(3):
                n = min(3, 8 - 3 * bnk) * 129
                kb.copy("act" if bnk == 1 else "dve", o, o.t[:, bnk * 387: bnk * 387 + n], kb.po[bnk], kb.po[bnk].t[:, 0:n])
            r0 = blk["qrow0"]
            kb.store(og[g], og[g].t[r0:r0 + 127 * r + 1:r, :], o, o.t[:])
        kb.band_attention(pq=P1, kv=kv, kvalid=kvalid, masks=masks, nheads=8, hd=128, ev=128,
                          qcol0=g * 1024, kcol0=g * 1024, vcol0=3072, nkvh=8, gq=1,
                          blocks=blocks, scale=128 ** -0.5, finish=finish)
    P.barrier()
    zt = Rot([P.sb([128, D], BF16, "zt") for _ in range(2)])
    rl = Rot([P.sb([128, 8], F32, "rl") for _ in range(2)])
    o3 = Rot(osb.items + [P.sb([128, 8 * 129], F32, "o3")])
    pcb = Rot([kb.pg[0], kb.pg[1], kb.pg[2], kb.po[0]])
    prev_back = None
    for i in range(NT):
        os_ = []
        for g in range(3):
            o = o3.next()
            kb.load(o, o.t[:], og[g], og[g].t[i * 128:(i + 1) * 128, :], q="sp")
            os_.append(o)
        z = zt.next()
        kb.load(z, z.t[:], P1, P1.t[i * 128:(i + 1) * 128, 7168:8192], q="sp")
        kb.tt("pool", os_[0], os_[0].t[:], os_[0], os_[0].t[:], os_[1], os_[1].t[:], ALU.add)
        kb.tt("dve", os_[0], os_[0].t[:], os_[0], os_[0].t[:], os_[2], os_[2].t[:], ALU.add)
        o4 = os_[0].t[:].rearrange("p (h e) -> p h e", e=129)
        rr = rl.next()
        kb.recip(rr, rr.t[:], os_[0], o4[:, :, 128])
        sz = kb.f32p.next()
        kb.act(sz, sz.t[:], z, z.t[:], AF.Silu)
        on = kb.f32p.next()
        kb.tt("dve", on, on.t[:].rearrange("p (h e) -> p h e", e=128), os_[0], o4[:, :, 0:128],
              rr, rr.t[:].unsqueeze(2).to_broadcast([128, 8, 128]), ALU.mult)
        y = kb.b1p.next()
        kb.tt("pool", y, y.t[:], on, on.t[:], sz, sz.t[:], ALU.mult)
        bk = kb.phase_c_tile(y, X1, i * 128, X2, i * 128, wout, defer=True, banks=pcb, store_q="act")
        if prev_back is not None:
            prev_back()
        prev_back = bk
        P.maybe_epoch()
    prev_back()
    P.barrier()
    wsb = P.sb([128, 8, 2048], BF16, "wsb")
    kb.load_weight(wsb, w_in2, 0, 2048)
    H2 = 128
    kv2 = kb.dint("kv2", [TPC + 2 * H2, 512], BF16)
    jobs = exchange_alloc(kb, "ex2", [(H2, kv2.t[0:H2, :], None), (H2, None, kv2.t[H2 + TPC:, :])], 512)
    also = [(kv2, i, H2 + i * 128, 1024, 1536, 0) for i in range(NT)]
    also += exchange_also(jobs, 0, NT - 1, 1024, 1536, 0)
    also += exchange_also(jobs, 1, 0, 1024, 1536, 0)
    hc2 = kb.hcache_tiles("hc2", NT)
    kb.phase_a(NT, X2, 0, P2, 0, 0, wsb, 2048, m2["A"], m2["B"], cs2, 0, 64, 1280, also=also, hcache=(hc2, "w"))
    P.barrier()
    exchange_cc(kb, jobs)
    kb.load_weight(wsb, w_in2, 2048, 512)
    kb.phase_a(NT, X2, 0, P2, 0, 2048, wsb, 512, m2["A"], m2["B"], cs2, 0, 64, 0, hcache=(hc2, "r"))
    exchange_finish(kb, jobs)
    P.barrier()


def stage2(kb):
    P = kb.P
    HALO = 128
    X2 = kb.dram["X2"]
    P2 = kb.dram["P2"]
    kv = kb.dram["kv2"]
    kvalid = kb.din("kvalid2", [TPC + 2 * HALO, 1])
    masksd = kb.din("masks", [128, 2, 128], BF16)
    sinkd = kb.din("sink_bc", [128, 16])
    m2 = kb.mod_vectors(2, False, True)
    m3 = kb.mod_vectors(3, True, False)
    w_out2 = kb.din("w_out2", [D, D])
    w_in3 = kb.din("w_in3", [D, 4096])
    cs3 = kb.din("cs2", [TPC, 2, 512])
    X3 = kb.dint("X3", [TPC, D])
    P3 = kb.dint("P3", [TPC, 4096], BF16)
    wout = P.sb([128, 8, D], BF16, "wout")
    masks = P.sb([128, 2, 128], BF16, "masks")
    kb.load(masks, masks.t[:], masksd, masksd.t)
    esink = P.sb([128, 16], F32, "esink")
    kb.load(esink, esink.t[:], sinkd, sinkd.t)
    kb.act(esink, esink.t[:], esink, esink.t[:], AF.Exp)
    kb.load_weight(wout, w_out2, 0, D, scale_bc=m2["gate_bc"])
    P.barrier()
    blocks = []
    for j in range(NT):
        q0 = 128 * j
        blocks.append(dict(qrow0=q0, qstep=1, chunks=[(HALO + q0 - 128, 1, 0, j == 0), (HALO + q0, 1, None, False),
                                                      (HALO + q0 + 128, 1, 1, j == NT - 1)]))
    zt = Rot([P.sb([128, D], BF16, "zt") for _ in range(2)])
    lt = Rot([P.sb([128, 16], F32, "lt") for _ in range(2)])

    oev = Rot([P.sb([128, 16 * 65], F32, "oev") for _ in range(2)])

    def finish(bi, blk, oaps):
        r0 = blk["qrow0"]
        oe = oev.next()
        for bnk in range(3):
            nh = min(7, 16 - 7 * bnk)
            kb.copy("act" if bnk == 1 else "dve", oe, oe.t[:, 7 * bnk * 65:(7 * bnk + nh) * 65], kb.po[bnk], kb.po[bnk].t[:, 0:nh * 65])

        def rest():
            z = zt.next()
            kb.load(z, z.t[:], P2, P2.t[r0:r0 + 128, 1536:2560], q="sp")
            sz = kb.f32p.next()
            kb.act(sz, sz.t[:], z, z.t[:], AF.Silu)
            l = lt.next()
            on = kb.f32p.next()
            o3_ = oe.t[:].rearrange("p (h e) -> p h e", e=65)
            kb.tt("dve", l, l.t[:], oe, o3_[:, :, 64], esink, esink.t[:], ALU.add)
            kb.recip(l, l.t[:], l, l.t[:])
            kb.tt("dve", on, on.t[:].rearrange("p (h e) -> p h e", e=64), oe, o3_[:, :, 0:64],
                  l, l.t[:].unsqueeze(2).to_broadcast([128, 16, 64]), ALU.mult)
            y = kb.b1p.next()
            kb.tt("pool", y, y.t[:], on, on.t[:], sz, sz.t[:], ALU.mult)
            kb.phase_c_tile(y, X2, r0, X3, r0, wout)
        return rest

    kb.band_attention(pq=P2, kv=kv, kvalid=kvalid, masks=masks, nheads=16, hd=64, ev=64,
                      qcol0=0, kcol0=0, vcol0=256, nkvh=4, gq=4, blocks=blocks, scale=64 ** -0.5, finish=finish)
    P.barrier()
    wsb = P.sb([128, 8, 2048], BF16, "wsb")
    kb.load_weight(wsb, w_in3, 1024, 2048)
    jobs = exchange_alloc(kb, "ex3", [(TPC, None, None)], 2048)
    kb.shared["ex3_jobs"] = jobs
    hc3 = kb.hcache_tiles("hc3", NT)
    kb.phase_a(NT, X3, 0, P3, 0, 1024, wsb, 2048, m3["A"], m3["B"], cs3, 0, 64, 1024,
               also=exchange_also(jobs, 0, 0, 0, 2048, 0), hcache=(hc3, "w"))
    P.barrier()
    exchange_cc(kb, jobs)
    kb.load_weight(wsb, w_in3, 0, 1024)
    kb.phase_a(NT, X3, 0, P3, 0, 0, wsb, 1024, m3["A"], m3["B"], cs3, 0, 64, 1024, hcache=(hc3, "r"))
    kb.load_weight(wsb, w_in3, 3072, 1024)
    kb.phase_a(NT, X3, 0, P3, 0, 3072, wsb, 1024, m3["A"], m3["B"], cs3, 0, 64, 0, hcache=(hc3, "r"))
    exchange_finish(kb, jobs)
    P.barrier()


def stage3(kb):
    P = kb.P
    X3 = kb.dram["X3"]
    P3 = kb.dram["P3"]
    ex3 = kb.shared["ex3_jobs"]
    lamd = kb.din("lam_bc", [128, 4, 64])
    sgd = kb.din("subln_bc", [128, 128])
    fgd = kb.din("finalg_bc", [128, D])
    m3 = kb.mod_vectors(3, False, True)
    w_out3 = kb.din("w_out3", [D, D])
    OUT = kb.dout("OUT", [TPC, D])
    Y = kb.dint("Y3", [TPC, D], BF16)
    wout = P.sb([128, 8, D], BF16, "wout")
    kb.load_weight(wout, w_out3, 0, D, scale_bc=m3["gate_bc"])
    lam_init = 0.8 - 0.6 * math.exp(-0.3 * 3)
    lv = P.sb([128, 4, 64], F32, "lv")
    kb.load(lv, lv.t[:], lamd, lamd.t)
    lp = P.sb([128, 2, 64], F32, "lp")
    ls = P.sb([128, 4], F32, "ls")
    kb.tt("dve", lp, lp.t[:, 0, :], lv, lv.t[:, 0, :], lv, lv.t[:, 1, :], ALU.mult)
    kb.tt("dve", lp, lp.t[:, 1, :], lv, lv.t[:, 2, :], lv, lv.t[:, 3, :], ALU.mult)
    P.op("dve", lambda e: e.reduce_sum(out=ls.t[:, 0:2], in_=lp.t[:], axis=AX.X), reads=[lp.b], writes=[ls.b])
    kb.act(ls, ls.t[:, 0:2], ls, ls.t[:, 0:2], AF.Exp)
    kb.tt("dve", ls, ls.t[:, 2:3], ls, ls.t[:, 0:1], ls, ls.t[:, 1:2], ALU.subtract)
    kb.ts("dve", ls, ls.t[:, 3:4], ls, ls.t[:, 2:3], lam_init, -1.0, ALU.add, ALU.mult)
    sg = P.sb([128, 128], F32, "sg")
    kb.load(sg, sg.t[:], sgd, sgd.t)
    kb.ts("dve", sg, sg.t[:], sg, sg.t[:], 1.0 - lam_init, None, ALU.mult)
    fg = P.sb([128, D], F32, "fg")
    kb.load(fg, fg.t[:], fgd, fgd.t)

    NKC = S // 128
    kTh = Rot([P.sb([128, S], BF16, "kTh") for _ in range(2)])
    vah = Rot([P.sb([128, NKC, 129], BF16, "vah") for _ in range(2)])
    for v_ in vah.items:
        kb.memset("pool", v_, v_.t[:], 1.0)
    krow = Rot([P.sb([128, 8, 128], BF16, "krow") for _ in range(2)])
    qrow = Rot([P.sb([128, 4, 128], BF16, "qrow") for _ in range(2)])
    qTt = Rot([P.sb([128, 512], BF16, "qTt") for _ in range(2)])
    ptile = Rot([P.sb([128, 2, 512], BF16, "ptile") for _ in range(3)])
    zt = Rot([P.sb([128, 4, 128], BF16, "zt") for _ in range(2)])
    szt = Rot([P.sb([128, 4, 128], F32, "szt") for _ in range(2)])
    ezt = Rot([P.sb([128, 4, 128], F32, "ezt") for _ in range(2)])
    yh = Rot([P.sb([128, 4, 128], BF16, "yh") for _ in range(2)])
    sm = Rot([P.sb([128, 24], F32, "sm") for _ in range(2)])
    A0r = Rot([P.sb([128, 4, 129], F32, "A0") for _ in range(2)])
    A1r = Rot([P.sb([128, 4, 129], F32, "A1") for _ in range(2)])
    w0 = Rot([P.sb([128, 4, 128], F32, "w0") for _ in range(2)])
    w1 = Rot([P.sb([128, 4, 128], F32, "w1") for _ in range(2)])
    scr = Rot(kb.sc)
    acc = kb.acc
    scale = 64 ** -0.5
    sgb = sg.t[:].unsqueeze(1).to_broadcast([128, 4, 128])

    def emit_qk(kT, qT, kc):
        sc = scr.next()
        specs = [(sc.t[:, comp, :], kT.t[64 * comp:64 * comp + 64, kc * 128:(kc + 1) * 128], qT.t[64 * comp:64 * comp + 64, :], True, True)
                 for comp in range(2)]
        kb.matmuls(sc, specs, [kT, qT])
        pt = ptile.next()
        kb.act(pt, pt.t[:], sc, sc.t[:], AF.Exp, scale=scale)
        return pt

    def emit_pv(pt, va, kc):
        specs = []
        for comp in range(2):
            for j in range(4):
                bank = acc[2 * comp + j // 2]
                specs.append((bank.t[:, (j % 2) * 129:(j % 2) * 129 + 129], pt.t[:, comp, j * 128:(j + 1) * 128],
                              va.t[:, kc, :], kc == 0 and j % 2 == 0, kc == NKC - 1 and j % 2 == 1, True))
        kb.matmuls(list(acc), specs, [pt, va])

    def k_prep(h, kT, c8):
        kr = krow.next()
        r = c8 // 4
        for j in range(2):
            db_ = ex3[2 * (c8 % 4) + j]["dst"]
            kb.load(kr, kr.t[:, 4 * j:4 * j + 4, :], db_,
                    db_.t[r * 512:(r + 1) * 512, h * 128:(h + 1) * 128].rearrange("(c p) e -> p c e", p=128), q="sp")
        ptr = kb.ptr.next()
        kb.transposes(ptr, kr, [kr.t[:, c, :] for c in range(8)], kb.ident)
        kb.copy("dve", kT, kT.t[:, c8 * 1024:(c8 + 1) * 1024], ptr, ptr.t[:].rearrange("p c q -> p (c q)"))

    def v_load(h, va):
        for r in range(2):
            for c in range(8):
                db_ = ex3[c]["dst"]
                b0 = r * 32 + c * 4
                kb.load(va, va.t[:, b0:b0 + 4, 0:128], db_,
                        db_.t[r * 512:(r + 1) * 512, 1024 + h * 128: 1024 + (h + 1) * 128].rearrange("(c p) e -> p c e", p=128), q="sp")

    heads = [(kTh.next(), vah.next()) for _ in range(8)]
    v_load(0, heads[0][1])
    for c8 in range(NKC // 8):
        k_prep(0, heads[0][0], c8)
    def q_prep(h, qc):
        qr = qrow.next()
        kb.load(qr, qr.t[:], P3, P3.t[qc * 512:(qc + 1) * 512, h * 128:(h + 1) * 128].rearrange("(c p) e -> p c e", p=128), q="sp")
        z = zt.next()
        kb.load(z, z.t[:], P3, P3.t[qc * 512:(qc + 1) * 512, 3072 + h * 128: 3072 + (h + 1) * 128].rearrange("(c p) e -> p c e", p=128), q="sp")
        ptr = kb.ptr.next()
        kb.transposes(ptr, qr, [qr.t[:, c, :] for c in range(4)], kb.ident)
        qT = qTt.next()
        kb.copy("dve", qT, qT.t[:], ptr, ptr.t[:, 0:4, :].rearrange("p c q -> p (c q)"))
        return qT, z

    def silu_z(z):
        ez = ezt.next()
        kb.act(ez, ez.t[:], z, z.t[:], AF.Exp, scale=-1.0)
        kb.ts("dve", ez, ez.t[:], ez, ez.t[:], 1.0, None, ALU.add)
        sz = szt.next()
        kb.recip(sz, sz.t[:], ez, ez.t[:])
        kb.tt("pool", sz, sz.t[:], sz, sz.t[:], z, z.t[:], ALU.mult)
        return sz

    def epilogue(h, qc, A0, A1, sz):
        s_ = sm.next()
        kb.recip(s_, s_.t[:, 0:4], A0, A0.t[:, :, 128])
        kb.recip(s_, s_.t[:, 4:8], A1, A1.t[:, :, 128])
        kb.ts("dve", s_, s_.t[:, 8:12], s_, s_.t[:, 4:8], ls.t[:, 3:4], None, ALU.mult, sreads=[ls])
        a0, a1 = w0.next(), w1.next()
        kb.tt("pool", a0, a0.t[:], A0, A0.t[:, :, 0:128], s_, s_.t[:, 0:4].unsqueeze(2).to_broadcast([128, 4, 128]), ALU.mult)
        kb.tt("dve", a1, a1.t[:], A1, A1.t[:, :, 0:128], s_, s_.t[:, 8:12].unsqueeze(2).to_broadcast([128, 4, 128]), ALU.mult)
        kb.tt("pool", a0, a0.t[:], a0, a0.t[:], a1, a1.t[:], ALU.add)
        kb.tt("dve", a1, a1.t[:], a0, a0.t[:], a0, a0.t[:], ALU.mult)
        P.op("dve", lambda e: e.reduce_sum(out=s_.t[:, 12:16], in_=a1.t[:], axis=AX.X), reads=[a1.b], writes=[s_.b])
        kb.ts("dve", s_, s_.t[:, 12:16], s_, s_.t[:, 12:16], 1.0 / 128, SUBLN_EPS, ALU.mult, ALU.add)
        kb.act(s_, s_.t[:, 16:20], s_, s_.t[:, 12:16], AF.Ln)
        kb.act(s_, s_.t[:, 20:24], s_, s_.t[:, 16:20], AF.Exp, scale=-0.5)
        kb.tt("pool", a0, a0.t[:], a0, a0.t[:], s_, s_.t[:, 20:24].unsqueeze(2).to_broadcast([128, 4, 128]), ALU.mult)
        kb.tt("dve", a0, a0.t[:], a0, a0.t[:], sg, sgb, ALU.mult)
        y = yh.next()
        kb.tt("pool", y, y.t[:], a0, a0.t[:], sz, sz.t[:], ALU.mult)
        kb.store(Y, Y.t[qc * 512:(qc + 1) * 512, h * 128:(h + 1) * 128].rearrange("(c p) e -> p c e", p=128), y, y.t[:])

    items = [(h, qc) for h in range(8) for qc in range(TPC // 512)]
    cur = q_prep(*items[0])
    for ii, (h, qc) in enumerate(items):
        kT, va = heads[h]
        qT, z = cur
        sz = silu_z(z)
        pts = [emit_qk(kT, qT, 0)]
        for kc in range(NKC):
            if kc + 1 < NKC:
                pts.append(emit_qk(kT, qT, kc + 1))
            emit_pv(pts[kc], va, kc)
        if h + 1 < 8:
            if qc == 0:
                v_load(h + 1, heads[h + 1][1])
            k_prep(h + 1, heads[h + 1][0], qc)
        A0, A1 = A0r.next(), A1r.next()
        for half in range(2):
            kb.copy("dve", A0, A0.t[:, 2 * half:2 * half + 2, :], acc[half], acc[half].t[:, 0:258].rearrange("p (j e) -> p j e", e=129))
            kb.copy("dve", A1, A1.t[:, 2 * half:2 * half + 2, :], acc[2 + half], acc[2 + half].t[:, 0:258].rearrange("p (j e) -> p j e", e=129))
        if ii + 1 < len(items):
            cur = q_prep(*items[ii + 1])
        epilogue(h, qc, A0, A1, sz)
        P.maybe_epoch()
    P.barrier()
    pcb = Rot(list(kb.acc))
    prev_back = None
    for i in range(NT):
        yt = kb.b1p.next()
        kb.load(yt, yt.t[:], Y, Y.t[i * 128:(i + 1) * 128, :], q="sp")
        bk = kb.phase_c_tile(yt, X3, i * 128, OUT, i * 128, wout, final_g=fg, defer=True, banks=pcb)
        if prev_back is not None:
            prev_back()
        prev_back = bk
        P.maybe_epoch()
    prev_back()
    P.barrier()


STAGES = [stage0, stage1, stage2, stage3]
_NC_CACHE = {}


def build_program(stages=(0, 1, 2, 3)):
    key = tuple(stages)
    if key in _NC_CACHE:
        return _NC_CACHE[key]
    nc = bass.Bass("TRN2", target_bir_lowering=False)
    shared = {"dram": {}}
    with contextlib.ExitStack() as st:
        P = Prog(nc, st)
        for si in stages:
            with contextlib.ExitStack() as sst:
                P.cur_stack = sst
                kb = KB(nc, P, shared, nf32=8 if si == 0 else 5, layout="l3" if si == 3 else "std", nbf={0: 8, 3: 1}.get(si, 4))
                kb.setup_consts()
                STAGES[si](kb)
                P.barrier()
                P.emit()
    _NC_CACHE[key] = (nc, shared)
    return _NC_CACHE[key]


def _pp(v, n):
    return np.ascontiguousarray(np.asarray(v, np.float32).reshape(n, 128).T)


def _bc(v):
    v = np.asarray(v, np.float32)
    return np.ascontiguousarray(np.broadcast_to(v[None], (128,) + v.shape))


def _rope_table(dim):
    inv = (10000.0 ** (-(np.arange(0, dim, 2, dtype=np.float32) / np.float32(dim)))).astype(np.float32)
    ang = np.arange(S, dtype=np.float32)[:, None] * inv[None, :]
    cos, sin = np.cos(ang).astype(np.float32), np.sin(ang).astype(np.float32)
    nh = 512 // dim
    cosF = np.tile(np.concatenate([cos, cos], axis=1), (1, nh))
    sinS = np.tile(np.concatenate([-sin, sin], axis=1), (1, nh))
    return np.ascontiguousarray(np.stack([cosF, sinS], axis=1))


def _ext(rows, lo, hi, total):
    out = np.zeros((hi - lo,) + rows.shape[1:], rows.dtype)
    a, b = max(lo, 0), min(hi, total)
    out[a - lo:b - lo] = rows[a:b]
    return out


def kernel(x, c, norm_g, w_mod, b_mod, conv_w_in, conv_k, conv_w_out, dil_w_in, dil_w_out,
           swa_w_in, swa_sink, swa_w_out, diff_w_in, diff_lambda, diff_subln_g, diff_w_out, final_g,
           _stages=(0, 1, 2, 3), _debug_outs=()):
    f = lambda a: np.ascontiguousarray(np.asarray(a, np.float32))
    x, c = f(x), f(c)
    bf = ml_dtypes.bfloat16
    masks = np.zeros((128, 2, 128), np.float32)
    kk, qq = np.meshgrid(np.arange(128), np.arange(128), indexing="ij")
    masks[:, 0, :] = (kk >= qq)
    masks[:, 1, :] = (kk <= qq)
    masks = masks.astype(bf)
    cs128, cs64 = _rope_table(128), _rope_table(64)
    nc, shared = build_program(_stages)
    need = set(shared["dram"].keys())

    def valid(lo, hi):
        t = np.arange(lo, hi)
        return np.ascontiguousarray(((t >= 0) & (t < S)).astype(np.float32)[:, None])

    shared_in = dict(ident=np.eye(128, dtype=np.float32), masks=masks,
                     w_in0=f(conv_w_in[0]), w_out0=f(conv_w_out[0]), w_in1=f(dil_w_in[0]), w_out1=f(dil_w_out[0]),
                     w_in2=f(swa_w_in[0]), w_out2=f(swa_w_out[0]), w_in3=f(diff_w_in[0]), w_out3=f(diff_w_out[0]),
                     convk_bc=_bc(f(conv_k[0])), sink_bc=_bc(f(swa_sink[0])), lam_bc=_bc(f(diff_lambda[0])),
                     subln_bc=_bc(f(diff_subln_g[0])), finalg_bc=_bc(f(final_g)))
    for l in range(4):
        shared_in[f"wmod{l}"] = f(w_mod[l])
        shared_in[f"bmodT{l}"] = _pp(b_mod[l], 24)
        shared_in[f"ngT{l}"] = _pp(norm_g[l], 8)
        shared_in[f"bmodg{l}"] = f(b_mod[l][2048:3072]).reshape(1, D)
    cores = [(b, h) for b in range(NB) for h in range(2)]
    maps = []
    for (b, h) in cores:
        t0 = h * TPC
        d = dict(shared_in)
        d["ccol"] = _pp(c[b], 8)
        d["xe"] = _ext(x[b], t0 - 128, t0 + TPC + 128, S)
        vm = np.ones((128, 2), np.float32)
        if h == 0:
            vm[0, 0] = 0.0
        if h == 1:
            vm[127, 1] = 0.0
        d["vmask"] = vm
        d["cs1"] = np.ascontiguousarray(cs128[t0:t0 + TPC])
        d["cs2"] = np.ascontiguousarray(cs64[t0:t0 + TPC])
        d["kvalid1"] = valid(t0 - 1024, t0 + TPC + 1024)
        d["kvalid2"] = valid(t0 - 128, t0 + TPC + 128)
        maps.append({k: v for k, v in d.items() if k in need})
    res = run_bass_kernel_spmd(nc, maps, core_ids=list(range(8)))
    if _debug_outs:
        return res.results
    out = np.zeros((NB, S, D), np.float32)
    for ci, (b, h) in enumerate(cores):
        out[b, h * TPC:(h + 1) * TPC] = res.results[ci]["OUT"]
    return out
```
